# Optimizing a Trainium2 kernel written in Bass

```python
import math
import jax, jax.numpy as jnp
from jax import lax
import numpy as np

D_MODEL = 1024
BATCH = 2
SEQ = 8192
DEPTH = 1

MIX_WIDTH = D_MODEL
DIFF_HEADS = 4
DIFF_HEAD_DIM = 64
DIFF_V_DIM = 2 * DIFF_HEAD_DIM
DIFF_WIDTH = DIFF_HEADS * DIFF_V_DIM
Q_BLOCK = 128
GLA_HEADS = 4
GLA_V_DIM = 128
GLA_K_DIM = GLA_V_DIM // 2
GLA_WIDTH = GLA_HEADS * GLA_V_DIM
GLA_GATE_RANK = 16
GLA_GATE_TEMP = 16.0
GLA_CHUNK = 64
IN_SPLITS = (
    DIFF_HEADS * 2 * DIFF_HEAD_DIM,
    DIFF_HEADS * 2 * DIFF_HEAD_DIM,
    DIFF_HEADS * DIFF_V_DIM,
    GLA_HEADS * GLA_K_DIM,
    GLA_HEADS * GLA_K_DIM,
    GLA_HEADS * GLA_V_DIM,
    GLA_HEADS * GLA_V_DIM,
    GLA_GATE_RANK,
)
IN_WIDTH = sum(IN_SPLITS)
D_FF = int(math.ceil(8 * D_MODEL / 3 / 256) * 256)
EPS = 1e-6

kernel_name = "hybrid_diffattn_gla_parallel_heads"


def rms_norm(x, g):
    xf = x.astype(jnp.float32)
    y = xf * lax.rsqrt(jnp.mean(xf * xf, axis=-1, keepdims=True) + EPS)
    return (y * g.astype(jnp.float32)).astype(x.dtype)


def diff_attention(q, k, v, lam):
    B, H, _, T, d = q.shape
    nb = T // Q_BLOCK
    scale = DIFF_HEAD_DIM ** -0.5
    slopes = 2.0 ** (-8.0 * (jnp.arange(H, dtype=jnp.float32) + 1.0) / H)
    qb = jnp.moveaxis(q.reshape(B, H, 2, nb, Q_BLOCK, d), 3, 0)
    kpos = jnp.arange(T, dtype=jnp.float32)

    def block(args):
        qblk, i = args
        qpos = (i * Q_BLOCK + jnp.arange(Q_BLOCK)).astype(jnp.float32)
        dist = qpos[:, None] - kpos[None, :]
        bias = jnp.where(dist[None] >= 0, -slopes[:, None, None] * dist[None], -jnp.inf)
        s = jnp.einsum('bhjqd,bhjkd->bhjqk', qblk, k).astype(jnp.float32) * scale
        p = jax.nn.softmax(s + bias[None, :, None], axis=-1)
        w = (p[:, :, 0] - lam * p[:, :, 1]).astype(v.dtype)
        return jnp.einsum('bhqk,bhkv->bhqv', w, v)

    out = lax.map(block, (qb, jnp.arange(nb)))
    return out.transpose(1, 0, 3, 2, 4).reshape(B, T, H, v.shape[-1])


def gla_chunked(q, k, v, g):
    B, T, H, dk = q.shape
    dv = v.shape[-1]
    n = T // GLA_CHUNK
    out_dtype = v.dtype

    def to_chunks(a):
        return a.astype(jnp.float32).reshape(B, n, GLA_CHUNK, H, a.shape[-1]).transpose(1, 0, 3, 2, 4)

    qc, kc, vc, gc = map(to_chunks, (q, k, v, g))
    causal = jnp.tril(jnp.ones((GLA_CHUNK, GLA_CHUNK), dtype=bool))

    def step(S, inp):
        qi, ki, vi, gi = inp
        b = jnp.cumsum(gi, axis=-2)
        o_inter = jnp.einsum('bhck,bhkv->bhcv', qi * jnp.exp(b), S)
        diff = b[:, :, :, None, :] - b[:, :, None, :, :]
        decay = jnp.exp(jnp.where(causal[None, None, :, :, None], diff, -jnp.inf))
        A = jnp.einsum('bhtk,bhsk,bhtsk->bhts', qi, ki, decay)
        o_intra = jnp.einsum('bhts,bhsv->bhtv', A, vi)
        b_last = b[:, :, -1:, :]
        S_new = jnp.exp(b_last[:, :, 0, :, None]) * S + jnp.einsum(
            'bhck,bhcv->bhkv', ki * jnp.exp(b_last - b), vi)
        return S_new, o_inter + o_intra

    S0 = jnp.zeros((B, H, dk, dv), jnp.float32)
    _, o = lax.scan(step, S0, (qc, kc, vc, gc))
    return o.transpose(1, 0, 3, 2, 4).reshape(B, T, H, dv).astype(out_dtype)


def setup_inputs(seed: int = 0) -> dict:
    key = jax.random.key(seed)
    ks = jax.random.split(key, 20)
    f32 = jnp.float32
    nrm = lambda k, shape, s: jax.random.normal(k, shape, f32) * s
    L = DEPTH
    return {
        "x": jax.random.normal(ks[0], (BATCH, SEQ, D_MODEL), f32),
        "attn_norm_gain": 1.0 + nrm(ks[1], (L, D_MODEL), 0.02),
        "w_in": nrm(ks[2], (L, D_MODEL, IN_WIDTH), D_MODEL ** -0.5),
        "q_norm_gain": 1.0 + nrm(ks[3], (L, DIFF_HEAD_DIM), 0.02),
        "k_norm_gain": 1.0 + nrm(ks[4], (L, DIFF_HEAD_DIM), 0.02),
        "lambda_q1": nrm(ks[5], (L, DIFF_HEAD_DIM), 0.1),
        "lambda_k1": nrm(ks[6], (L, DIFF_HEAD_DIM), 0.1),
        "lambda_q2": nrm(ks[7], (L, DIFF_HEAD_DIM), 0.1),
        "lambda_k2": nrm(ks[8], (L, DIFF_HEAD_DIM), 0.1),
        "diff_out_norm_gain": 1.0 + nrm(ks[9], (L, DIFF_V_DIM), 0.02),
        "w_gla_gate_up": nrm(ks[10], (L, GLA_GATE_RANK, GLA_HEADS * GLA_K_DIM), GLA_GATE_RANK ** -0.5),
        "b_gla_gate": nrm(ks[11], (L, GLA_HEADS * GLA_K_DIM), 0.01),
        "gla_out_norm_gain": 1.0 + nrm(ks[12], (L, GLA_V_DIM), 0.02),
        "w_out": nrm(ks[13], (L, MIX_WIDTH, D_MODEL), MIX_WIDTH ** -0.5),
        "ffn_norm_gain": 1.0 + nrm(ks[14], (L, D_MODEL), 0.02),
        "w_ffn_gate": nrm(ks[15], (L, D_MODEL, D_FF), D_MODEL ** -0.5),
        "w_ffn_up": nrm(ks[16], (L, D_MODEL, D_FF), D_MODEL ** -0.5),
        "w_ffn_down": nrm(ks[17], (L, D_FF, D_MODEL), D_FF ** -0.5),
    }


def reference(x, attn_norm_gain, w_in, q_norm_gain, k_norm_gain, lambda_q1, lambda_k1,
              lambda_q2, lambda_k2, diff_out_norm_gain, w_gla_gate_up, b_gla_gate,
              gla_out_norm_gain, w_out, ffn_norm_gain, w_ffn_gate, w_ffn_up, w_ffn_down):
    B, T, _ = x.shape
    split_pts = np.cumsum(IN_SPLITS)[:-1].tolist()
    h = x
    for l in range(DEPTH):
        lambda_init = 0.8 - 0.6 * math.exp(-0.3 * l)
        n = rms_norm(h, attn_norm_gain[l])
        proj = n @ w_in[l]
        dq, dk_, dv, gq, gk, gv, gout, glr = jnp.split(proj, split_pts, axis=-1)

        dq = rms_norm(dq.reshape(B, T, DIFF_HEADS, 2, DIFF_HEAD_DIM), q_norm_gain[l])
        dk_ = rms_norm(dk_.reshape(B, T, DIFF_HEADS, 2, DIFF_HEAD_DIM), k_norm_gain[l])
        dq = dq.transpose(0, 2, 3, 1, 4)
        dk_ = dk_.transpose(0, 2, 3, 1, 4)
        dv = dv.reshape(B, T, DIFF_HEADS, DIFF_V_DIM).transpose(0, 2, 1, 3)
        lam = (jnp.exp(jnp.sum(lambda_q1[l].astype(jnp.float32) * lambda_k1[l].astype(jnp.float32)))
               - jnp.exp(jnp.sum(lambda_q2[l].astype(jnp.float32) * lambda_k2[l].astype(jnp.float32)))
               + lambda_init)
        o_diff = diff_attention(dq, dk_, dv, lam)
        o_diff = rms_norm(o_diff, diff_out_norm_gain[l]) * (1.0 - lambda_init)

        gq = gq.reshape(B, T, GLA_HEADS, GLA_K_DIM) * (GLA_K_DIM ** -0.5)
        gk = gk.reshape(B, T, GLA_HEADS, GLA_K_DIM)
        gv = gv.reshape(B, T, GLA_HEADS, GLA_V_DIM)
        log_alpha = jax.nn.log_sigmoid(
            (glr @ w_gla_gate_up[l] + b_gla_gate[l]).astype(jnp.float32)) / GLA_GATE_TEMP
        log_alpha = log_alpha.reshape(B, T, GLA_HEADS, GLA_K_DIM)
        o_gla = gla_chunked(gq, gk, gv, log_alpha)
        o_gla = rms_norm(o_gla, gla_out_norm_gain[l]) * jax.nn.silu(
            gout.reshape(B, T, GLA_HEADS, GLA_V_DIM))

        mixed = jnp.concatenate([o_diff.reshape(B, T, DIFF_WIDTH),
                                 o_gla.reshape(B, T, GLA_WIDTH)], axis=-1)
        h = h + mixed @ w_out[l]

        m = rms_norm(h, ffn_norm_gain[l])
        h = h + (jax.nn.silu(m @ w_ffn_gate[l]) * (m @ w_ffn_up[l])) @ w_ffn_down[l]
    return h
```

```python
import contextlib
import numpy as np
import concourse.bass as bass
import concourse.mybir as mybir
from concourse.bass_utils import run_bass_kernel_spmd

F32 = mybir.dt.float32
BF16 = mybir.dt.bfloat16
AF = mybir.ActivationFunctionType
ALU = mybir.AluOpType
AX = mybir.AxisListType

D = 1024
DFF = 2816
NFF = DFF // 128
EPS = 1e-6
LAMBDA_INIT = 0.8 - 0.6 * 1.0
SLOPES = [2.0 ** (-8.0 * (h + 1.0) / 4.0) for h in range(4)]
NEG = -30000.0
SAME_ENGINE_SYNC = True


class ES:
    def __init__(self, name, eng, sem):
        self.name = name
        self.eng = eng
        self.sem = sem
        self.count = 0
        self.seen = {}


class Trk:
    def __init__(self, nc, stack):
        self.nc = nc
        self.stack = stack
        self.lw = {}
        self.rd = {}
        self.dsems = {}
        mk = lambda n, e: ES(n, e, stack.enter_context(nc.semaphore("s_" + n)))
        self.pe = mk("pe", nc.tensor)
        self.act = mk("act", nc.scalar)
        self.dve = mk("dve", nc.vector)
        self.pool = mk("pool", nc.gpsimd)
        self.sp = mk("sp", nc.sync)

    def dsem(self, name):
        if name not in self.dsems:
            self.dsems[name] = ES("d_" + name, None,
                                  self.stack.enter_context(self.nc.semaphore("d_" + name)))
        return self.dsems[name]

    def _deps(self, reads, writes):
        deps = {}

        def add(e, v):
            if deps.get(e, 0) < v:
                deps[e] = v
        for k in reads:
            if k in self.lw:
                add(*self.lw[k])
        for k in writes:
            if k in self.lw:
                add(*self.lw[k])
            for e, v in self.rd.get(k, {}).items():
                add(e, v)
        return deps

    def _wait(self, E, deps):
        for e, v in deps.items():
            if e is E and (not SAME_ENGINE_SYNC or E.name == "pe"):
                continue
            if E.seen.get(e, 0) >= v:
                continue
            assert v <= e.count, (E.name, e.name, v, e.count)
            E.eng.wait_ge(e.sem, v)
            E.seen[e] = v

    def _rec(self, E, val, reads, writes):
        for k in writes:
            self.lw[k] = (E, val)
            self.rd[k] = {}
        for k in reads:
            d = self.rd.setdefault(k, {})
            if d.get(E, 0) < val:
                d[E] = val

    def op(self, E, fn, reads=(), writes=(), inc=True):
        pr = [k for k in reads if k.startswith("pb")]
        if pr:
            reads = [k for k in reads if not k.startswith("pb")]
            writes = list(writes) + [k for k in pr if k not in writes]
        self._wait(E, self._deps(reads, writes))
        inst = fn()
        if inc:
            inst.then_inc(E.sem, 1)
            E.count += 1
            val = E.count
        else:
            val = E.count + 1
        self._rec(E, val, reads, writes)
        return inst

    def dma(self, Q, out, in_, reads, writes, sem):
        self._wait(Q, self._deps(reads, writes))
        Dm = self.dsem(sem)
        inst = Q.eng.dma_start(out=out, in_=in_)
        inst.then_inc(Dm.sem, 16)
        Dm.count += 16
        self._rec(Dm, Dm.count, reads, writes)
        return inst


def build(NU=4, dbg=False):
    T_ = NU * 2048
    NT = T_ // 128
    NO = NU * 4
    TO = NO * 128
    nc = bass.Bass("TRN2", target_bir_lowering=False)

    def din(name, shape, dt=F32):
        return nc.dram_tensor(name, shape, dt, kind="ExternalInput").ap()

    x_all = din("x_all", [T_, D])
    x_own = din("x_own", [TO, D])
    w_p0 = din("w_p0", [D, 1296])
    w_p1 = din("w_p1", [D, 512])
    w_ow = din("w_ow", [D, 2064])
    w_up = din("w_up", [16, 256])
    w_o = din("w_o", [D, D])
    w_g = din("w_g", [D, DFF])
    w_u = din("w_u", [D, DFF])
    w_d = din("w_d", [DFF, D])
    g_attn = din("g_attn", [128, D])
    g_ffn = din("g_ffn", [128, D])
    qk_rep = din("qk_rep", [128, 1024])
    b_rep = din("b_rep", [128, 256])
    gn_rep = din("gn_rep", [128, 256])
    lamv = din("lamv", [128, 256])
    cst = din("cst", [128, 514])
    tabs = din("tabs", [128, 260])
    msk = din("msk", [128, 16 * 512])
    out = nc.dram_tensor("out", [TO, D], F32, kind="ExternalOutput").ap()
    mix_scr = nc.dram_tensor("mix_scr", [TO, D], BF16, kind="Internal").ap()
    h_scr = nc.dram_tensor("h_scr", [TO, D], F32, kind="Internal").ap()
    dbg_outs = {}
    if dbg:
        dbg_outs["d_mix"] = nc.dram_tensor("d_mix", [TO, D], F32, kind="ExternalOutput").ap()
        dbg_outs["d_h"] = nc.dram_tensor("d_h", [TO, D], F32, kind="ExternalOutput").ap()

    with contextlib.ExitStack() as gst:
        T = Trk(nc, gst)
        PE, ACT, DVE, POOL, SP = T.pe, T.act, T.dve, T.pool, T.sp
        V = nc.vector
        A = nc.scalar
        G = nc.gpsimd
        PEe = nc.tensor

        def sbg(n, s, d):
            return gst.enter_context(nc.sbuf_tensor(n, s, d))

        pb = [gst.enter_context(nc.psum_tensor("pb%d" % i, [128, 512], F32)) for i in range(8)]
        pbb = [p[:].bitcast(BF16) for p in pb]

        cstf = sbg("cstf", [128, 514], F32)
        identb = sbg("identb", [128, 128], BF16)
        st = sbg("st", [128, 32], F32)
        junk = sbg("junk", [128, 1024], BF16)
        T.dma(SP, cstf[:], cst[:, :], [], ["cstf"], "c0")
        T.op(DVE, lambda: V.tensor_copy(out=identb[:], in_=cstf[:, 0:128]), ["cstf"], ["identb"])
        triS = cstf[:, 128:256]
        upS = cstf[:, 256:384]
        chI = cstf[:, 512:514]

        def rstd_from_ss(ss_ap, out_ap, n, keys):
            T.op(ACT, lambda: A.activation(out=out_ap, in_=ss_ap, func=AF.Ln, scale=1.0 / n, bias=EPS),
                 keys, keys)
            T.op(ACT, lambda: A.activation(out=out_ap, in_=out_ap, func=AF.Exp, scale=-0.5),
                 keys, keys)

        def norm_transpose(src_ap, xb, xk, nb, nbk, nTt, nTk, grep, grepk, dq, dsem, bank=5):
            T.dma(dq, xb[:], src_ap, [], [xk], dsem)
            T.op(DVE, lambda: V.scalar_tensor_tensor(out=junk[:], in0=xb[:], scalar=1.0, in1=xb[:],
                                                     op0=ALU.mult, op1=ALU.mult, accum_out=st[:, 0:1]),
                 [xk], ["junk", "st0"])
            rstd_from_ss(st[:, 0:1], st[:, 1:2], float(D), ["st0"])
            T.op(DVE, lambda: V.scalar_tensor_tensor(out=nb[:], in0=xb[:], scalar=st[:, 1:2], in1=grep[:],
                                                     op0=ALU.mult, op1=ALU.mult),
                 [xk, "st0", grepk], [nbk])
            bk = "pb%d" % bank
            for c in range(8):
                T.op(PE, lambda c=c: PEe.transpose(out=pbb[bank][:, c * 128:(c + 1) * 128],
                                                   in_=nb[:, c * 128:(c + 1) * 128], identity=identb[:]),
                     [nbk, "identb"], [bk], inc=(c == 7))
            T.op(ACT, lambda: A.copy(out=nTt[:], in_=pbb[bank][:, 0:1024]), [bk], [nTk])

        def project(nTt, nTk, W, Wk, groups):
            for (bank, pc0, wc0, ncol) in groups:
                bk = "pb%d" % bank
                for c in range(8):
                    T.op(PE, lambda c=c, bank=bank, pc0=pc0, wc0=wc0, ncol=ncol:
                         PEe.matmul(pb[bank][:, pc0:pc0 + ncol], lhsT=nTt[:, c * 128:(c + 1) * 128],
                                    rhs=W[:, c, wc0:wc0 + ncol], start=(c == 0), stop=(c == 7)),
                         [nTk, Wk], [bk], inc=(c == 7))

        with contextlib.ExitStack() as mst:
            def sb(n, s, d):
                return mst.enter_context(nc.sbuf_tensor(n, s, d))

            KT = sb("KT", [128, 2, T_], BF16)
            VE = sb("VE", [128, NT, 2, 130], BF16)
            QT = sb("QT", [128, 4, TO], BF16)
            WB = sb("WB", [128, 8, 2064], BF16)
            MK = sb("MK", [128, 16, 512], BF16)
            tabf = sb("tabf", [128, 260], F32)
            gat = sb("gat", [128, D], F32)
            qg = sb("qg", [128, 1024], F32)
            brep = sb("brep", [128, 256], F32)
            gn = sb("gn", [128, 256], F32)
            lam = sb("lam", [128, 256], F32)
            wupb = sb("wupb", [16, 256], BF16)
            maskA = sb("maskA", [128, 128], BF16)
            xbs = [sb("xb%d" % i, [128, D], F32) for i in range(2)]
            nbs = [sb("nb%d" % i, [128, D], BF16) for i in range(2)]
            nTs = [sb("nT%d" % i, [128, D], BF16) for i in range(2)]
            kf = sb("kf", [128, 512], F32)
            sq = sb("sq", [128, 512], F32)
            k16 = sb("k16", [128, 512], BF16)
            glr16 = sb("glr16", [128, 16], BF16)
            glrT = sb("glrT", [16, 128], BF16)
            zb = sb("zb", [128, 256], F32)
            lg = sb("lg", [128, 256], F32)
            eb = sb("eb", [128, 256], F32)
            enb = sb("enb", [128, 256], F32)
            ec = sb("ec", [128, 256], F32)
            dec = sb("dec", [128, 4], F32)
            qt16 = sb("qt16", [128, 256], BF16)
            kt16 = sb("kt16", [128, 256], BF16)
            kh16 = sb("kh16", [128, 256], BF16)
            vb16 = sb("vb16", [128, 512], BF16)
            qkT = sb("qkT", [128, 512], BF16)
            AT = sb("AT", [128, 512], BF16)
            S = sb("S", [128, 256], F32)
            Sb = [sb("Sb%d" % i, [128, 256], BF16) for i in range(2)]
            snap = sb("snap", [128, NU, 256], F32)
            sgt = sb("sgt", [128, 512], F32)
            gg = sb("gg", [128, 512], F32)
            onb = sb("onb", [128, 512], F32)
            ogl = sb("ogl", [128, 512], BF16)
            PTs = [sb("PT%d" % i, [128, 512], BF16) for i in range(4)]
            t1 = sb("t1", [128, 128], F32)
            of = sb("of", [128, 128], F32)
            og = sb("og", [128, 4, 128], BF16)

            T.dma(SP, tabf[:], tabs[:, :], [], ["tabf"], "c1")
            T.dma(SP, gat[:], g_attn[:, :], [], ["gat"], "c2")
            T.dma(SP, qg[:], qk_rep[:, :], [], ["qg"], "c3")
            T.dma(SP, brep[:], b_rep[:, :], [], ["brep"], "c4")
            T.dma(SP, gn[:], gn_rep[:, :], [], ["gn"], "c5")
            T.dma(SP, lam[:], lamv[:, :], [], ["lam"], "c6")
            T.dma(POOL, wupb[:], w_up[:, :], [], ["wupb"], "c7")
            T.dma(POOL, maskA[:], cst[:, 384:512], [], ["maskA"], "c8")
            T.dma(POOL, WB[:, :, 0:1296], w_p0.rearrange("(c p) n -> p c n", p=128), [], ["WB"], "w0")
            for e4 in range(4):
                T.dma(POOL, MK[:, e4 * 4:(e4 + 1) * 4, :],
                      msk[:, e4 * 2048:(e4 + 1) * 2048].rearrange("p (a b) -> p a b", b=512),
                      [], ["MK"], "c9")
            T.op(DVE, lambda: V.scalar_tensor_tensor(out=qg[:, 0:512], in0=qg[:, 0:512], scalar=0.125,
                                                     in1=qg[:, 512:1024], op0=ALU.mult, op1=ALU.mult),
                 ["qg"], ["qg"])
            T.op(DVE, lambda: V.tensor_scalar(out=gn[:, 0:128], in0=gn[:, 0:128], scalar1=1.0 - LAMBDA_INIT,
                                              scalar2=None, op0=ALU.mult), ["gn"], ["gn"])
            T.op(DVE, lambda: V.scalar_tensor_tensor(out=junk[:, 0:64], in0=lam[:, 0:64], scalar=1.0,
                                                     in1=lam[:, 64:128], op0=ALU.mult, op1=ALU.mult,
                                                     accum_out=st[:, 4:5]), ["lam"], ["junk", "stl"])
            T.op(DVE, lambda: V.scalar_tensor_tensor(out=junk[:, 0:64], in0=lam[:, 128:192], scalar=1.0,
                                                     in1=lam[:, 192:256], op0=ALU.mult, op1=ALU.mult,
                                                     accum_out=st[:, 5:6]), ["lam", "stl"], ["junk", "stl"])
            T.op(ACT, lambda: A.activation(out=st[:, 6:8], in_=st[:, 4:6], func=AF.Exp), ["stl"], ["stl"])
            T.op(DVE, lambda: V.tensor_tensor(out=st[:, 8:9], in0=st[:, 7:8], in1=st[:, 6:7], op=ALU.subtract),
                 ["stl"], ["stl"])
            T.op(DVE, lambda: V.tensor_scalar(out=st[:, 8:9], in0=st[:, 8:9], scalar1=-LAMBDA_INIT,
                                              scalar2=None, op0=ALU.add), ["stl"], ["stl"])
            neglam = st[:, 8:9]
            T.op(POOL, lambda: G.memset(VE[:, :, :, 128:130], 1.0), [], ["VE"])
            T.op(DVE, lambda: V.memset(S[:], 0.0), [], ["S"])

            def k_path(t, bank, c0, ngrp, dst_fn, gain=None):
                n = 64 * ngrp
                bk = "pb%d" % bank
                T.op(ACT, lambda: A.copy(out=kf[:, 0:n], in_=pb[bank][:, c0:c0 + n]), [bk], ["kf"])
                T.op(DVE, lambda: V.tensor_tensor(out=sq[:, 0:n], in0=kf[:, 0:n], in1=kf[:, 0:n], op=ALU.mult),
                     ["kf"], ["sq"])
                T.op(DVE, lambda: V.tensor_reduce(out=st[:, 16:16 + ngrp],
                                                  in_=sq[:, 0:n].rearrange("p (a b) -> p a b", b=64),
                                                  axis=AX.X, op=ALU.add), ["sq"], ["stk"])
                rstd_from_ss(st[:, 16:16 + ngrp], st[:, 24:24 + ngrp], 64.0, ["stk"])
                bc = bass.AP(st[:].tensor, st[:, 24:25].offset, [list(st[:].ap[0]), [1, ngrp], [0, 64]])
                if gain is None:
                    T.op(DVE, lambda: V.tensor_tensor(out=k16[:, 0:n].rearrange("p (a b) -> p a b", b=64),
                                                      in0=kf[:, 0:n].rearrange("p (a b) -> p a b", b=64),
                                                      in1=bc, op=ALU.mult), ["kf", "stk"], ["k16"])
                else:
                    T.op(DVE, lambda: V.tensor_tensor(out=sq[:, 0:n].rearrange("p (a b) -> p a b", b=64),
                                                      in0=kf[:, 0:n].rearrange("p (a b) -> p a b", b=64),
                                                      in1=bc, op=ALU.mult), ["kf", "stk", "sq"], ["sq"])
                    T.op(DVE, lambda: V.tensor_tensor(out=k16[:, 0:n], in0=sq[:, 0:n], in1=gain,
                                                      op=ALU.mult), ["sq", "qg"], ["k16"])
                nh = n // 128
                for hh in range(nh):
                    T.op(PE, lambda hh=hh: PEe.transpose(out=pbb[4][:, hh * 128:(hh + 1) * 128],
                                                         in_=k16[:, hh * 128:(hh + 1) * 128], identity=identb[:]),
                         ["k16", "identb"], ["pb4"], inc=(hh == nh - 1))
                dst_fn(pbb[4][:, 0:n].rearrange("p (a b) -> p a b", b=128))

            def gla_tile(own, t_idx, i_own):
                T.op(DVE, lambda: V.tensor_copy(out=glr16[:], in_=pb[6][:, 0:16]), ["pb6"], ["glr16"])
                T.op(PE, lambda: PEe.transpose(out=pbb[6][0:16, 64:192], in_=glr16[:], identity=identb[:]),
                     ["glr16", "identb"], ["pb6"])
                T.op(ACT, lambda: A.copy(out=glrT[:], in_=pbb[6][0:16, 64:192]), ["pb6"], ["glrT"])
                T.op(PE, lambda: PEe.matmul(pb[6][:, 128:384], lhsT=glrT[:], rhs=wupb[:], start=True, stop=True),
                     ["glrT", "wupb"], ["pb6"])
                T.op(DVE, lambda: V.tensor_tensor(out=zb[:], in0=pb[6][:, 128:384], in1=brep[:], op=ALU.add),
                     ["pb6", "brep"], ["zb"])
                T.op(ACT, lambda: A.activation(out=lg[:], in_=zb[:], func=AF.Exp, scale=-1.0), ["zb"], ["lg"])
                T.op(ACT, lambda: A.activation(out=lg[:], in_=lg[:], func=AF.Ln, bias=1.0), ["lg"], ["lg"])
                if own:
                    T.op(PE, lambda: PEe.matmul(pb[7][:, 0:256], lhsT=triS, rhs=lg[:], start=True, stop=True),
                         ["cstf", "lg"], ["pb7"], inc=False)
                T.op(PE, lambda: PEe.matmul(pb[7][:, 256:512], lhsT=upS, rhs=lg[:], start=True, stop=True),
                     ["cstf", "lg"], ["pb7"])
                for p in range(2):
                    T.op(PE, lambda p=p: PEe.matmul(pb[6][:, 96 + 2 * p:98 + 2 * p], lhsT=lg[:, p * 128:(p + 1) * 128],
                                                    rhs=chI, start=True, stop=True),
                         ["lg", "cstf"], ["pb6"], inc=(p == 1))
                T.op(ACT, lambda: A.activation(out=ec[:], in_=pb[7][:, 256:512], func=AF.Exp), ["pb7"], ["ec"])
                if own:
                    T.op(ACT, lambda: A.activation(out=eb[:], in_=pb[7][:, 0:256], func=AF.Exp), ["pb7"], ["eb"])
                    T.op(ACT, lambda: A.activation(out=enb[:], in_=pb[7][:, 0:256], func=AF.Exp, scale=-1.0),
                         ["pb7"], ["enb"])
                T.op(ACT, lambda: A.activation(out=dec[:], in_=pb[6][:, 96:100], func=AF.Exp), ["pb6"], ["dec"])
                T.op(DVE, lambda: V.tensor_tensor(out=kh16[:], in0=pb[1][:, 256:512], in1=ec[:], op=ALU.mult),
                     ["pb1", "ec"], ["kh16"])
                T.op(ACT, lambda: A.copy(out=vb16[:], in_=pb[2][:, :]), ["pb2"], ["vb16"])
                if own:
                    T.op(DVE, lambda: V.scalar_tensor_tensor(out=qt16[:], in0=pb[1][:, 0:256], scalar=0.125,
                                                             in1=eb[:], op0=ALU.mult, op1=ALU.mult),
                         ["pb1", "eb"], ["qt16"])
                    T.op(DVE, lambda: V.tensor_tensor(out=kt16[:], in0=pb[1][:, 256:512], in1=enb[:], op=ALU.mult),
                         ["pb1", "enb"], ["kt16"])
                    for p in range(2):
                        T.op(PE, lambda p=p: PEe.transpose(out=pbb[4][:, 512 + p * 128:512 + (p + 1) * 128],
                                                           in_=qt16[:, p * 128:(p + 1) * 128], identity=identb[:]),
                             ["qt16", "identb"], ["pb4"], inc=False)
                    for p in range(2):
                        T.op(PE, lambda p=p: PEe.transpose(out=pbb[4][:, 768 + p * 128:768 + (p + 1) * 128],
                                                           in_=kt16[:, p * 128:(p + 1) * 128], identity=identb[:]),
                             ["kt16", "identb"], ["pb4"], inc=(p == 1))
                    T.op(ACT, lambda: A.copy(out=qkT[:], in_=pbb[4][:, 512:1024]), ["pb4"], ["qkT"])
                    for hh in range(2):
                        bank = 0 if hh == 0 else 3
                        for p in range(2):
                            T.op(PE, lambda hh=hh, p=p, bank=bank:
                                 PEe.matmul(pb[bank][:, p * 128:(p + 1) * 128],
                                            lhsT=qkT[hh * 64:(hh + 1) * 64, 256 + p * 128:256 + (p + 1) * 128],
                                            rhs=qkT[hh * 64:(hh + 1) * 64, p * 128:(p + 1) * 128],
                                            start=True, stop=True),
                                 ["qkT"], ["pb%d" % bank], inc=(p == 1))
                    mA = bass.AP(maskA[:].tensor, maskA[:].offset, [list(maskA[:].ap[0]), [0, 2], [1, 128]])
                    for hh in range(2):
                        bank = 0 if hh == 0 else 3
                        outv = bass.AP(AT[:].tensor, AT[:, hh * 128:hh * 128 + 1].offset,
                                       [list(AT[:].ap[0]), [256, 2], [1, 128]])
                        T.op(DVE, lambda bank=bank, outv=outv: V.tensor_tensor(
                            out=outv, in0=pb[bank][:, 0:256].rearrange("p (a b) -> p a b", b=128), in1=mA, op=ALU.mult),
                            ["pb%d" % bank, "maskA"], ["AT"])
                for ch in range(2):
                    if own:
                        T.op(POOL, lambda ch=ch: G.tensor_copy(out=Sb[ch][:], in_=S[:]), ["S"], ["Sb%d" % ch])
                    sbank = 2 if ch == 0 else 5
                    sk = "pb%d" % sbank
                    for p in range(2):
                        for hh in range(2):
                            h = 2 * p + hh
                            T.op(PE, lambda ch=ch, p=p, hh=hh, h=h, sbank=sbank:
                                 PEe.matmul(pb[sbank][hh * 64:(hh + 1) * 64, p * 128:(p + 1) * 128],
                                            lhsT=kh16[ch * 64:(ch + 1) * 64, h * 64:(h + 1) * 64],
                                            rhs=vb16[ch * 64:(ch + 1) * 64, h * 128:(h + 1) * 128],
                                            start=True, stop=True),
                                 ["kh16", "vb16"], [sk], inc=(p == 1 and hh == 1))
                    for p in range(2):
                        T.op(DVE, lambda ch=ch, p=p, sbank=sbank:
                             V.scalar_tensor_tensor(out=S[:, p * 128:(p + 1) * 128], in0=S[:, p * 128:(p + 1) * 128],
                                                    scalar=dec[:, 2 * p + ch:2 * p + ch + 1],
                                                    in1=pb[sbank][:, p * 128:(p + 1) * 128],
                                                    op0=ALU.mult, op1=ALU.add),
                             ["S", "dec", sk] + (["Sb%d" % ch] if own else []), ["S"])
                if own:
                    for hh in range(2):
                        bank = 0 if hh == 0 else 3
                        bk = "pb%d" % bank
                        for p in range(2):
                            h = 2 * p + hh
                            oc = 256 + p * 128
                            T.op(PE, lambda h=h, bank=bank, oc=oc:
                                 PEe.matmul(pb[bank][:, oc:oc + 128], lhsT=AT[:, h * 128:(h + 1) * 128],
                                            rhs=vb16[:, h * 128:(h + 1) * 128], start=True, stop=False),
                                 ["AT", "vb16"], [bk], inc=False)
                            for ch in range(2):
                                T.op(PE, lambda h=h, hh=hh, p=p, ch=ch, bank=bank, oc=oc:
                                     PEe.matmul(pb[bank][ch * 64:(ch + 1) * 64, oc:oc + 128],
                                                lhsT=qkT[hh * 64:(hh + 1) * 64, p * 128 + ch * 64:p * 128 + (ch + 1) * 64],
                                                rhs=Sb[ch][hh * 64:(hh + 1) * 64, p * 128:(p + 1) * 128],
                                                start=False, stop=(ch == 1)),
                                     ["qkT", "Sb%d" % ch], [bk], inc=(ch == 1 and p == 1))
                    for hh in range(2):
                        bank = 0 if hh == 0 else 3
                        bk = "pb%d" % bank
                        T.op(ACT, lambda bank=bank: A.copy(out=kf[:, 0:256], in_=pb[bank][:, 256:512]), [bk], ["kf"])
                        T.op(DVE, lambda: V.tensor_tensor(out=sq[:, 0:256], in0=kf[:, 0:256], in1=kf[:, 0:256],
                                                          op=ALU.mult), ["kf"], ["sq"])
                        T.op(DVE, lambda: V.tensor_reduce(out=st[:, 16:18],
                                                          in_=sq[:, 0:256].rearrange("p (a b) -> p a b", b=128),
                                                          axis=AX.X, op=ALU.add), ["sq"], ["stk"])
                        rstd_from_ss(st[:, 16:18], st[:, 24:26], 128.0, ["stk"])
                        bc = bass.AP(st[:].tensor, st[:, 24:25].offset, [list(st[:].ap[0]), [1, 2], [0, 128]])
                        ggv = bass.AP(gg[:].tensor, gg[:, hh * 128:hh * 128 + 1].offset,
                                      [list(gg[:].ap[0]), [256, 2], [1, 128]])
                        oglv = bass.AP(ogl[:].tensor, ogl[:, hh * 128:hh * 128 + 1].offset,
                                       [list(ogl[:].ap[0]), [256, 2], [1, 128]])
                        T.op(DVE, lambda bc=bc: V.tensor_tensor(out=onb[:, 0:256].rearrange("p (a b) -> p a b", b=128),
                                                                in0=kf[:, 0:256].rearrange("p (a b) -> p a b", b=128),
                                                                in1=bc, op=ALU.mult), ["kf", "stk"], ["onb"])
                        T.op(DVE, lambda ggv=ggv, oglv=oglv: V.tensor_tensor(
                            out=oglv, in0=onb[:, 0:256].rearrange("p (a b) -> p a b", b=128), in1=ggv, op=ALU.mult),
                            ["onb", "gg"], ["ogl"])
                    T.dma(SP, mix_scr[i_own * 128:(i_own + 1) * 128, 512:1024], ogl[:], ["ogl"], ["mixg%d" % i_own], "mx")

            def kv_store(t):
                def dst(src3):
                    T.op(ACT, lambda: A.copy(out=KT[:, :, t * 128:(t + 1) * 128], in_=src3), ["pb4"], ["KT"])
                return dst

            for t in range(NT):
                s = t % 2
                norm_transpose(x_all[t * 128:(t + 1) * 128, :], xbs[s], "xb%d" % s, nbs[s], "nb%d" % s,
                               nTs[s], "nT%d" % s, gat, "gat", SP, "x%d" % s)
                project(nTs[s], "nT%d" % s, WB, "WB",
                        [(0, 0, 0, 512), (1, 256, 512, 256), (2, 0, 768, 512), (6, 0, 1280, 16)])
                k_path(t, 0, 0, 4, kv_store(t))
                T.op(DVE, lambda t=t: V.tensor_copy(out=VE[:, t, :, 0:128],
                                                    in_=pb[0][:, 256:512].rearrange("p (a b) -> p a b", b=128)),
                     ["pb0"], ["VE"])
                if t % 16 % 4 == 0:
                    u = t // 16
                    r = (t % 16) // 4
                    if r == 0:
                        T.op(DVE, lambda u=u, r=r: V.tensor_scalar(out=snap[:, u, :], in0=S[:],
                                                                   scalar1=tabf[:, 256 + r:257 + r], scalar2=None,
                                                                   op0=ALU.mult), ["S", "tabf"], ["snap"])
                    else:
                        T.op(DVE, lambda u=u, r=r: V.scalar_tensor_tensor(out=snap[:, u, :], in0=S[:],
                                                                          scalar=tabf[:, 256 + r:257 + r],
                                                                          in1=snap[:, u, :], op0=ALU.mult, op1=ALU.add),
                             ["S", "tabf", "snap"], ["snap"])
                gla_tile(False, t, None)

            T.dma(POOL, WB[:, :, :], w_ow.rearrange("(c p) n -> p c n", p=128), [], ["WB"], "w0")

            def q_store(i):
                def dst(src3):
                    T.op(ACT, lambda: A.copy(out=QT[:, :, i * 128:(i + 1) * 128], in_=src3), ["pb4"], ["QT"])
                return dst

            gnb = bass.AP(gn[:].tensor, gn[:, 128:129].offset, [list(gn[:].ap[0]), [0, 4], [1, 128]])
            for i in range(NO):
                s = i % 2
                if i % 4 == 0:
                    T.op(DVE, lambda i=i: V.tensor_copy(out=S[:], in_=snap[:, i // 4, :]), ["snap"], ["S"])
                norm_transpose(x_own[i * 128:(i + 1) * 128, :], xbs[s], "xb%d" % s, nbs[s], "nb%d" % s,
                               nTs[s], "nT%d" % s, gat, "gat", SP, "x%d" % s)
                project(nTs[s], "nT%d" % s, WB, "WB",
                        [(0, 0, 0, 512), (1, 0, 512, 512), (2, 0, 1024, 512), (3, 0, 1536, 512), (6, 0, 2048, 16)])
                k_path(i, 0, 0, 8, q_store(i), gain=qg[:, 0:512])
                T.op(ACT, lambda: A.activation(out=sgt[:], in_=pb[3][:, :], func=AF.Exp, scale=-1.0), ["pb3"], ["sgt"])
                T.op(DVE, lambda: V.tensor_scalar(out=sgt[:], in0=sgt[:], scalar1=1.0, scalar2=None, op0=ALU.add),
                     ["sgt"], ["sgt"])
                T.op(DVE, lambda: V.reciprocal(out=sgt[:], in_=sgt[:]), ["sgt"], ["sgt"])
                T.op(DVE, lambda: V.tensor_tensor(out=gg[:], in0=pb[3][:, :], in1=sgt[:], op=ALU.mult),
                     ["pb3", "sgt"], ["gg"])
                T.op(DVE, lambda: V.tensor_tensor(out=gg[:].rearrange("p (a b) -> p a b", b=128),
                                                  in0=gg[:].rearrange("p (a b) -> p a b", b=128), in1=gnb, op=ALU.mult),
                     ["gg", "gn"], ["gg"])
                gla_tile(True, None, i)

            def acc_ap(a):
                bank = 5 + a // 3
                col = (a % 3) * 130
                return bank, col

            def attention(hp):
                cnt = 0
                for u in range(NU):
                    nkb = 16 * u + 16
                    for hl in range(2):
                        h = 2 * hp + hl
                        for kb in range(nkb):
                            e = kb - 16 * u
                            ei = e + 48
                            for m in range(2):
                                sbank = 1 + (cnt % 4)
                                slot = cnt % 4
                                cnt += 1
                                sk = "pb%d" % sbank
                                T.op(PE, lambda m=m, hl=hl, kb=kb, h=h, u=u, sbank=sbank:
                                     PEe.matmul(pb[sbank][:, 0:512],
                                                lhsT=KT[m * 64:(m + 1) * 64, hl, kb * 128:(kb + 1) * 128],
                                                rhs=QT[m * 64:(m + 1) * 64, h, u * 512:(u + 1) * 512],
                                                start=True, stop=True),
                                     ["KT", "QT"], [sk])
                                T.op(ACT, lambda sbank=sbank, slot=slot, h=h, ei=ei:
                                     A.activation(out=PTs[slot][:], in_=pb[sbank][:, 0:512], func=AF.Exp,
                                                  bias=tabf[:, h * 64 + ei:h * 64 + ei + 1]),
                                     [sk, "tabf"], ["PT%d" % slot])
                                if e >= 0:
                                    T.op(POOL, lambda slot=slot, e=e:
                                         G.tensor_tensor(out=PTs[slot][:], in0=PTs[slot][:], in1=MK[:, e, :], op=ALU.mult),
                                         ["PT%d" % slot, "MK"], ["PT%d" % slot])
                                for c in range(4):
                                    bank, col = acc_ap(c * 2 + m)
                                    first = (kb == 0 and (c * 2 + m) in (0, 4, 6))
                                    T.op(PE, lambda c=c, bank=bank, col=col, slot=slot, kb=kb, hl=hl, nkb=nkb, first=first:
                                         PEe.matmul(pb[bank][:, col:col + 130],
                                                    lhsT=PTs[slot][:, c * 128:(c + 1) * 128],
                                                    rhs=VE[:, kb, hl, :], start=first, stop=(kb == nkb - 1),
                                                    skip_group_check=True),
                                         ["PT%d" % slot, "VE"], ["pb%d" % bank], inc=(c == 3))
                        for c in range(4):
                            b0, c0 = acc_ap(c * 2)
                            b1, c1 = acc_ap(c * 2 + 1)
                            k0 = "pb%d" % b0
                            k1 = "pb%d" % b1
                            T.op(DVE, lambda b0=b0, c0=c0: V.reciprocal(out=st[:, 10:11], in_=pb[b0][:, c0 + 128:c0 + 129]),
                                 [k0], ["sta"])
                            T.op(DVE, lambda b1=b1, c1=c1: V.reciprocal(out=st[:, 11:12], in_=pb[b1][:, c1 + 128:c1 + 129]),
                                 [k1, "sta"], ["sta"])
                            T.op(DVE, lambda: V.tensor_tensor(out=st[:, 11:12], in0=st[:, 11:12], in1=neglam, op=ALU.mult),
                                 ["sta", "stl"], ["sta"])
                            T.op(DVE, lambda b1=b1, c1=c1: V.tensor_scalar(out=t1[:], in0=pb[b1][:, c1:c1 + 128],
                                                                           scalar1=st[:, 11:12], scalar2=None, op0=ALU.mult),
                                 [k1, "sta"], ["t1"])
                            T.op(DVE, lambda b0=b0, c0=c0: V.scalar_tensor_tensor(out=of[:], in0=pb[b0][:, c0:c0 + 128],
                                                                                  scalar=st[:, 10:11], in1=t1[:],
                                                                                  op0=ALU.mult, op1=ALU.add),
                                 [k0, "sta", "t1"], ["of"])
                            T.op(DVE, lambda: V.scalar_tensor_tensor(out=junk[:, 0:128], in0=of[:], scalar=1.0, in1=of[:],
                                                                     op0=ALU.mult, op1=ALU.mult, accum_out=st[:, 12:13]),
                                 ["of"], ["junk", "stb"])
                            rstd_from_ss(st[:, 12:13], st[:, 13:14], 128.0, ["stb"])
                            T.op(DVE, lambda c=c: V.scalar_tensor_tensor(out=og[:, c, :], in0=of[:], scalar=st[:, 13:14],
                                                                         in1=gn[:, 0:128], op0=ALU.mult, op1=ALU.mult),
                                 ["of", "stb", "gn"], ["og"])
                        T.dma(SP, mix_scr[u * 512:(u + 1) * 512, h * 128:(h + 1) * 128].rearrange("(c p) n -> p c n", p=128),
                              og[:], ["og"], ["mixd%d_%d" % (u, h)], "mx2")

            attention(0)

            T.dma(POOL, WB[:, :, 0:512], w_p1.rearrange("(c p) n -> p c n", p=128), [], ["WB"], "w0")
            for t in range(NT):
                s = t % 2
                norm_transpose(x_all[t * 128:(t + 1) * 128, :], xbs[s], "xb%d" % s, nbs[s], "nb%d" % s,
                               nTs[s], "nT%d" % s, gat, "gat", SP, "x%d" % s)
                project(nTs[s], "nT%d" % s, WB, "WB", [(0, 0, 0, 512)])
                k_path(t, 0, 0, 4, kv_store(t))
                T.op(DVE, lambda t=t: V.tensor_copy(out=VE[:, t, :, 0:128],
                                                    in_=pb[0][:, 256:512].rearrange("p (a b) -> p a b", b=128)),
                     ["pb0"], ["VE"])
            attention(1)
            mix_keys = ["mixg%d" % i for i in range(NO)] + ["mixd%d_%d" % (u, h) for u in range(NU) for h in range(4)]

        with contextlib.ExitStack() as fst:
            def sb(n, s, d):
                return fst.enter_context(nc.sbuf_tensor(n, s, d))

            WG = sb("WG", [128, 8, DFF], BF16)
            WU = sb("WU", [128, 8, DFF], BF16)
            WD = sb("WD", [128, NFF, D], BF16)
            gff = sb("gff", [128, D], F32)
            hbs = [sb("hb%d" % i, [128, D], F32) for i in range(2)]
            nbs = [sb("fnb%d" % i, [128, D], BF16) for i in range(2)]
            obs = [sb("ob%d" % i, [128, D], F32) for i in range(2)]
            T.dma(SP, gff[:], g_ffn[:, :], [], ["gff"], "c2")
            with contextlib.ExitStack() as ost:
                def sbo(n, s, d):
                    return ost.enter_context(nc.sbuf_tensor(n, s, d))
                WO = sbo("WO", [128, 8, D], BF16)
                mxs = [sbo("mx%d" % i, [128, D], BF16) for i in range(2)]
                mxT = [sbo("mxT%d" % i, [128, D], BF16) for i in range(2)]
                xos = [sbo("xo%d" % i, [128, D], F32) for i in range(2)]
                T.dma(POOL, WO[:], w_o.rearrange("(c p) n -> p c n", p=128), [], ["WO"], "w1")
                T.dma(POOL, WG[:], w_g.rearrange("(c p) n -> p c n", p=128), [], ["WG"], "w2")
                T.dma(POOL, WU[:], w_u.rearrange("(c p) n -> p c n", p=128), [], ["WU"], "w3")
                T.dma(POOL, WD[:], w_d.rearrange("(c p) n -> p c n", p=128), [], ["WD"], "w4")
                for i in range(NO):
                    s = i % 2
                    T.dma(SP, mxs[s][:], mix_scr[i * 128:(i + 1) * 128, :], mix_keys if i < 2 else [], ["mx%d" % s], "m%d" % s)
                    T.dma(SP, xos[s][:], x_own[i * 128:(i + 1) * 128, :], [], ["xo%d" % s], "xo%d" % s)
                    for c in range(8):
                        T.op(PE, lambda c=c, s=s: PEe.transpose(out=pbb[0][:, c * 128:(c + 1) * 128],
                                                                in_=mxs[s][:, c * 128:(c + 1) * 128], identity=identb[:]),
                             ["mx%d" % s, "identb"], ["pb0"], inc=(c == 7))
                    T.op(ACT, lambda s=s: A.copy(out=mxT[s][:], in_=pbb[0][:, 0:1024]), ["pb0"], ["mxT%d" % s])
                    for nb_ in range(2):
                        bank = 1 + nb_
                        for c in range(8):
                            T.op(PE, lambda c=c, s=s, nb_=nb_, bank=bank:
                                 PEe.matmul(pb[bank][:, 0:512], lhsT=mxT[s][:, c * 128:(c + 1) * 128],
                                            rhs=WO[:, c, nb_ * 512:(nb_ + 1) * 512], start=(c == 0), stop=(c == 7)),
                                 ["mxT%d" % s, "WO"], ["pb%d" % bank], inc=(c == 7))
                        T.op(DVE, lambda s=s, nb_=nb_, bank=bank:
                             V.tensor_tensor(out=hbs[s][:, nb_ * 512:(nb_ + 1) * 512], in0=pb[bank][:, 0:512],
                                             in1=xos[s][:, nb_ * 512:(nb_ + 1) * 512], op=ALU.add),
                             ["pb%d" % bank, "xo%d" % s], ["hb%d" % s])
                    T.dma(SP, h_scr[i * 128:(i + 1) * 128, :], hbs[s][:], ["hb%d" % s], ["hscr%d" % i], "hs%d" % s)
                    if dbg:
                        T.dma(SP, dbg_outs["d_h"][i * 128:(i + 1) * 128, :], hbs[s][:], ["hb%d" % s], [], "dbg")
                        T.op(DVE, lambda s=s: V.tensor_copy(out=xos[s][:], in_=mxs[s][:]), ["mx%d" % s, "xo%d" % s], ["xo%d" % s])
                        T.dma(SP, dbg_outs["d_mix"][i * 128:(i + 1) * 128, :], xos[s][:], ["xo%d" % s], [], "dbg")

            mT = sb("mT", [128, 8, 512], BF16)
            actT = sb("actT", [128, NFF, 512], BF16)
            ee = sb("ee", [128, 512], F32)
            uS = sb("uS", [128, 512], F32)
            rr = sb("rr", [128, 512], F32)
            for gI in range(NU):
                for tt in range(4):
                    i = gI * 4 + tt
                    s = i % 2
                    T.dma(SP, hbs[s][:], h_scr[i * 128:(i + 1) * 128, :], ["hscr%d" % i], ["hb%d" % s], "hl%d" % s)
                    T.op(DVE, lambda s=s: V.scalar_tensor_tensor(out=junk[:], in0=hbs[s][:], scalar=1.0, in1=hbs[s][:],
                                                                 op0=ALU.mult, op1=ALU.mult, accum_out=st[:, 0:1]),
                         ["hb%d" % s], ["junk", "st0"])
                    rstd_from_ss(st[:, 0:1], st[:, 1:2], float(D), ["st0"])
                    T.op(DVE, lambda s=s: V.scalar_tensor_tensor(out=nbs[s][:], in0=hbs[s][:], scalar=st[:, 1:2],
                                                                 in1=gff[:], op0=ALU.mult, op1=ALU.mult),
                         ["hb%d" % s, "st0", "gff"], ["fnb%d" % s])
                    for c in range(8):
                        T.op(PE, lambda c=c, s=s: PEe.transpose(out=pbb[0][:, c * 128:(c + 1) * 128],
                                                                in_=nbs[s][:, c * 128:(c + 1) * 128], identity=identb[:]),
                             ["fnb%d" % s, "identb"], ["pb0"], inc=(c == 7))
                    T.op(ACT, lambda tt=tt: A.copy(out=mT[:, :, tt * 128:(tt + 1) * 128],
                                                   in_=pbb[0][:, 0:1024].rearrange("p (a b) -> p a b", b=128)),
                         ["pb0"], ["mT"])
                for f in range(NFF):
                    gb = 1 + (f % 2)
                    ub = 3 + (f % 2)
                    for c in range(8):
                        T.op(PE, lambda c=c, f=f, gb=gb: PEe.matmul(pb[gb][:, 0:512], lhsT=WG[:, c, f * 128:(f + 1) * 128],
                                                                    rhs=mT[:, c, :], start=(c == 0), stop=(c == 7)),
                             ["WG", "mT"], ["pb%d" % gb], inc=(c == 7))
                    for c in range(8):
                        T.op(PE, lambda c=c, f=f, ub=ub: PEe.matmul(pb[ub][:, 0:512], lhsT=WU[:, c, f * 128:(f + 1) * 128],
                                                                    rhs=mT[:, c, :], start=(c == 0), stop=(c == 7)),
                             ["WU", "mT"], ["pb%d" % ub], inc=(c == 7))
                    T.op(ACT, lambda gb=gb: A.activation(out=ee[:], in_=pb[gb][:, 0:512], func=AF.Exp, scale=-1.0),
                         ["pb%d" % gb], ["ee"])
                    T.op(ACT, lambda ub=ub: A.copy(out=uS[:], in_=pb[ub][:, 0:512]), ["pb%d" % ub], ["uS"])
                    T.op(DVE, lambda: V.tensor_scalar(out=ee[:], in0=ee[:], scalar1=1.0, scalar2=None, op0=ALU.add),
                         ["ee"], ["ee"])
                    T.op(DVE, lambda: V.reciprocal(out=rr[:], in_=ee[:]), ["ee"], ["rr"])
                    T.op(DVE, lambda gb=gb: V.tensor_tensor(out=rr[:], in0=pb[gb][:, 0:512], in1=rr[:], op=ALU.mult),
                         ["pb%d" % gb, "rr"], ["rr"])
                    T.op(DVE, lambda f=f: V.tensor_tensor(out=actT[:, f, :], in0=rr[:], in1=uS[:], op=ALU.mult),
                         ["rr", "uS"], ["actT"])
                for tt in range(4):
                    i = gI * 4 + tt
                    s = i % 2
                    T.dma(SP, hbs[s][:], h_scr[i * 128:(i + 1) * 128, :], ["hscr%d" % i], ["hb%d" % s], "hl%d" % s)
                    for nb_ in range(2):
                        bank = 5 + nb_
                        for f in range(NFF):
                            T.op(PE, lambda f=f, tt=tt, nb_=nb_, bank=bank:
                                 PEe.matmul(pb[bank][:, 0:512], lhsT=actT[:, f, tt * 128:(tt + 1) * 128],
                                            rhs=WD[:, f, nb_ * 512:(nb_ + 1) * 512], start=(f == 0), stop=(f == NFF - 1)),
                                 ["actT", "WD"], ["pb%d" % bank], inc=(f == NFF - 1))
                        T.op(DVE, lambda s=s, nb_=nb_, bank=bank:
                             V.tensor_tensor(out=obs[s][:, nb_ * 512:(nb_ + 1) * 512], in0=pb[bank][:, 0:512],
                                             in1=hbs[s][:, nb_ * 512:(nb_ + 1) * 512], op=ALU.add),
                             ["pb%d" % bank, "hb%d" % s], ["ob%d" % s])
                    T.dma(SP, out[i * 128:(i + 1) * 128, :], obs[s][:], ["ob%d" % s], [], "o%d" % s)

        for n in list(T.dsems):
            d = T.dsem(n)
            nc.sync.wait_ge(d.sem, d.count)
    return nc


def _consts():
    c = np.zeros((128, 514), np.float32)
    c[:, 0:128] = np.eye(128, dtype=np.float32)
    s = np.arange(128)[:, None]
    t = np.arange(128)[None, :]
    same = (s // 64) == (t // 64)
    c[:, 128:256] = np.where(same & (s <= t), -1.0 / 16.0, 0.0)
    c[:, 256:384] = np.where(same & (s > t), -1.0 / 16.0, 0.0)
    c[:, 384:512] = np.where(same & (s <= t), 1.0, 0.0)
    c[0:64, 512] = -1.0 / 16.0
    c[64:128, 513] = -1.0 / 16.0
    return c


def _tabs(j):
    tb = np.zeros((128, 260), np.float32)
    kl = np.arange(128, dtype=np.float64)
    for h in range(4):
        for ei in range(64):
            e = ei - 48
            if e <= 4 * j + 3:
                tb[:, h * 64 + ei] = SLOPES[h] * (128.0 * (e - 4 * j) + kl - 256.0)
            else:
                tb[:, h * 64 + ei] = NEG
    tb[:, 256 + j] = 1.0
    return tb


def _masks(j):
    m = np.zeros((128, 16, 4, 128), np.float32)
    k = np.arange(128)[:, None]
    q = np.arange(128)[None, :]
    tri = (k <= q).astype(np.float32)
    for e in range(16):
        for c in range(4):
            cs = 4 * j + c
            if e < cs:
                m[:, e, c, :] = 1.0
            elif e == cs:
                m[:, e, c, :] = tri
    return m.reshape(128, 16 * 512)


def _rep(v, n=128):
    return np.ascontiguousarray(np.broadcast_to(np.asarray(v, np.float32).reshape(1, -1), (n, v.size)))


def prep_inputs(inp, NU=4):
    T_ = NU * 2048
    f = lambda a: np.ascontiguousarray(np.asarray(a, dtype=np.float32))
    x = f(inp["x"])
    w_in = f(inp["w_in"])[0]
    cols = lambda a, b: list(range(a, b))
    dq, dk, dv = 0, 512, 1024
    gq, gk, gv, go, gl = 1536, 1792, 2048, 2560, 3072
    c_p0 = cols(dk, dk + 256) + cols(dv, dv + 256) + cols(gk, gk + 256) + cols(gv, gv + 512) + cols(gl, gl + 16)
    c_p1 = cols(dk + 256, dk + 512) + cols(dv + 256, dv + 512)
    c_ow = cols(dq, dq + 512) + cols(gq, gq + 256) + cols(gk, gk + 256) + cols(gv, gv + 512) + cols(go, go + 512) + cols(gl, gl + 16)
    shared = {
        "w_p0": np.ascontiguousarray(w_in[:, c_p0]),
        "w_p1": np.ascontiguousarray(w_in[:, c_p1]),
        "w_ow": np.ascontiguousarray(w_in[:, c_ow]),
        "w_up": f(inp["w_gla_gate_up"])[0],
        "w_o": f(inp["w_out"])[0],
        "w_g": f(inp["w_ffn_gate"])[0],
        "w_u": f(inp["w_ffn_up"])[0],
        "w_d": f(inp["w_ffn_down"])[0],
        "g_attn": _rep(f(inp["attn_norm_gain"])[0]),
        "g_ffn": _rep(f(inp["ffn_norm_gain"])[0]),
        "qk_rep": np.concatenate([_rep(np.tile(f(inp["q_norm_gain"])[0], 8)),
                                  _rep(np.tile(f(inp["k_norm_gain"])[0], 8))], axis=1),
        "b_rep": _rep(f(inp["b_gla_gate"])[0]),
        "gn_rep": np.concatenate([_rep(f(inp["diff_out_norm_gain"])[0]), _rep(f(inp["gla_out_norm_gain"])[0])], axis=1),
        "lamv": np.concatenate([_rep(f(inp[k])[0]) for k in ("lambda_q1", "lambda_k1", "lambda_q2", "lambda_k2")], axis=1),
        "cst": _consts(),
    }
    in_maps = []
    for core in range(8):
        b, j = core // 4, core % 4
        own_rows = np.concatenate([np.arange((4 * u + j) * 512, (4 * u + j + 1) * 512) for u in range(NU)])
        m = dict(shared)
        m["x_all"] = np.ascontiguousarray(x[b, :T_])
        m["x_own"] = np.ascontiguousarray(x[b, own_rows])
        m["tabs"] = _tabs(j)
        m["msk"] = _masks(j)
        in_maps.append(m)
    return in_maps


def assemble(results, NU=4, B=2, key="out"):
    T_ = NU * 2048
    outp = np.zeros((B, T_, D), np.float32)
    for core in range(8):
        b, j = core // 4, core % 4
        r = np.asarray(results[core][key])
        for u in range(NU):
            outp[b, (4 * u + j) * 512:(4 * u + j + 1) * 512] = r[u * 512:(u + 1) * 512]
    return outp


_NC_CACHE = {}


def kernel(**inputs):
    NU = 4
    if NU not in _NC_CACHE:
        _NC_CACHE[NU] = build(NU)
    nc = _NC_CACHE[NU]
    in_maps = prep_inputs(inputs, NU)
    res = run_bass_kernel_spmd(nc, in_maps, core_ids=list(range(8)))
    return assemble(res.results, NU)
```

```python
import contextlib
import numpy as np
import concourse.bass as bass
import concourse.mybir as mybir
from concourse.bass_utils import run_bass_kernel_spmd

F32 = mybir.dt.float32
BF16 = mybir.dt.bfloat16
AF = mybir.ActivationFunctionType
ALU = mybir.AluOpType
AX = mybir.AxisListType

D = 1024
DFF = 2816
NFF = DFF // 128
EPS = 1e-6
LAMBDA_INIT = 0.8 - 0.6 * 1.0
SLOPES = [2.0 ** (-8.0 * (h + 1.0) / 4.0) for h in range(4)]
NEG = -30000.0
SAME_ENGINE_SYNC = True


class ES:
    def __init__(self, name, eng, sem):
        self.name = name
        self.eng = eng
        self.sem = sem
        self.count = 0
        self.seen = {}


class Trk:
    def __init__(self, nc, stack):
        self.nc = nc
        self.stack = stack
        self.lw = {}
        self.rd = {}
        self.dsems = {}
        mk = lambda n, e: ES(n, e, stack.enter_context(nc.semaphore("s_" + n)))
        self.pe = mk("pe", nc.tensor)
        self.act = mk("act", nc.scalar)
        self.dve = mk("dve", nc.vector)
        self.pool = mk("pool", nc.gpsimd)
        self.sp = mk("sp", nc.sync)

    def dsem(self, name):
        if name not in self.dsems:
            self.dsems[name] = ES("d_" + name, None,
                                  self.stack.enter_context(self.nc.semaphore("d_" + name)))
        return self.dsems[name]

    def _deps(self, reads, writes):
        deps = {}

        def add(e, v):
            if deps.get(e, 0) < v:
                deps[e] = v
        for k in reads:
            if k in self.lw:
                add(*self.lw[k])
        for k in writes:
            if k in self.lw:
                add(*self.lw[k])
            for e, v in self.rd.get(k, {}).items():
                add(e, v)
        return deps

    def _wait(self, E, deps):
        for e, v in deps.items():
            if e is E and (not SAME_ENGINE_SYNC or E.name == "pe"):
                continue
            if E.seen.get(e, 0) >= v:
                continue
            assert v <= e.count, (E.name, e.name, v, e.count)
            E.eng.wait_ge(e.sem, v)
            E.seen[e] = v

    def _rec(self, E, val, reads, writes):
        for k in writes:
            self.lw[k] = (E, val)
            self.rd[k] = {}
        for k in reads:
            d = self.rd.setdefault(k, {})
            if d.get(E, 0) < val:
                d[E] = val

    def op(self, E, fn, reads=(), writes=(), inc=True):
        pr = [k for k in reads if k.startswith("pb")]
        if pr:
            reads = [k for k in reads if not k.startswith("pb")]
            writes = list(writes) + [k for k in pr if k not in writes]
        self._wait(E, self._deps(reads, writes))
        inst = fn()
        if inc:
            inst.then_inc(E.sem, 1)
            E.count += 1
            val = E.count
        else:
            val = E.count + 1
        self._rec(E, val, reads, writes)
        return inst

    def dma(self, Q, out, in_, reads, writes, sem):
        self._wait(Q, self._deps(reads, writes))
        Dm = self.dsem(sem)
        inst = Q.eng.dma_start(out=out, in_=in_)
        inst.then_inc(Dm.sem, 16)
        Dm.count += 16
        self._rec(Dm, Dm.count, reads, writes)
        return inst


def build(NU=4, dbg=False):
    T_ = NU * 2048
    NT = T_ // 128
    NO = NU * 4
    TO = NO * 128
    nc = bass.Bass("TRN2", target_bir_lowering=False)

    def din(name, shape, dt=F32):
        return nc.dram_tensor(name, shape, dt, kind="ExternalInput").ap()

    x_all = din("x_all", [T_, D])
    x_own = din("x_own", [TO, D])
    w_p0 = din("w_p0", [D, 1296])
    w_p1 = din("w_p1", [D, 512])
    w_ow = din("w_ow", [D, 2064])
    w_up = din("w_up", [16, 256])
    w_o = din("w_o", [D, D])
    w_g = din("w_g", [D, DFF])
    w_u = din("w_u", [D, DFF])
    w_d = din("w_d", [DFF, D])
    g_attn = din("g_attn", [128, D])
    g_ffn = din("g_ffn", [128, D])
    qk_rep = din("qk_rep", [128, 1024])
    b_rep = din("b_rep", [128, 256])
    gn_rep = din("gn_rep", [128, 256])
    lamv = din("lamv", [128, 256])
    cst = din("cst", [128, 514])
    tabs = din("tabs", [128, 260])
    msk = din("msk", [128, 16 * 512])
    out = nc.dram_tensor("out", [TO, D], F32, kind="ExternalOutput").ap()
    mix_scr = nc.dram_tensor("mix_scr", [TO, D], BF16, kind="Internal").ap()
    h_scr = nc.dram_tensor("h_scr", [TO, D], F32, kind="Internal").ap()
    dbg_outs = {}
    if dbg:
        dbg_outs["d_mix"] = nc.dram_tensor("d_mix", [TO, D], F32, kind="ExternalOutput").ap()
        dbg_outs["d_h"] = nc.dram_tensor("d_h", [TO, D], F32, kind="ExternalOutput").ap()

    with contextlib.ExitStack() as gst:
        T = Trk(nc, gst)
        PE, ACT, DVE, POOL, SP = T.pe, T.act, T.dve, T.pool, T.sp
        V = nc.vector
        A = nc.scalar
        G = nc.gpsimd
        PEe = nc.tensor

        def sbg(n, s, d):
            return gst.enter_context(nc.sbuf_tensor(n, s, d))

        pb = [gst.enter_context(nc.psum_tensor("pb%d" % i, [128, 512], F32)) for i in range(8)]
        pbb = [p[:].bitcast(BF16) for p in pb]

        cstf = sbg("cstf", [128, 514], F32)
        identb = sbg("identb", [128, 128], BF16)
        st = sbg("st", [128, 32], F32)
        junk = sbg("junk", [128, 1024], BF16)
        T.dma(SP, cstf[:], cst[:, :], [], ["cstf"], "c0")
        T.op(DVE, lambda: V.tensor_copy(out=identb[:], in_=cstf[:, 0:128]), ["cstf"], ["identb"])
        triS = cstf[:, 128:256]
        upS = cstf[:, 256:384]
        chI = cstf[:, 512:514]

        def rstd_from_ss(ss_ap, out_ap, n, keys):
            T.op(ACT, lambda: A.activation(out=out_ap, in_=ss_ap, func=AF.Ln, scale=1.0 / n, bias=EPS),
                 keys, keys)
            T.op(ACT, lambda: A.activation(out=out_ap, in_=out_ap, func=AF.Exp, scale=-0.5),
                 keys, keys)

        def norm_transpose(src_ap, xb, xk, nb, nbk, nTt, nTk, grep, grepk, dq, dsem, bank=5):
            T.dma(dq, xb[:], src_ap, [], [xk], dsem)
            T.op(DVE, lambda: V.scalar_tensor_tensor(out=junk[:], in0=xb[:], scalar=1.0, in1=xb[:],
                                                     op0=ALU.mult, op1=ALU.mult, accum_out=st[:, 0:1]),
                 [xk], ["junk", "st0"])
            rstd_from_ss(st[:, 0:1], st[:, 1:2], float(D), ["st0"])
            T.op(DVE, lambda: V.scalar_tensor_tensor(out=nb[:], in0=xb[:], scalar=st[:, 1:2], in1=grep[:],
                                                     op0=ALU.mult, op1=ALU.mult),
                 [xk, "st0", grepk], [nbk])
            bk = "pb%d" % bank
            for c in range(8):
                T.op(PE, lambda c=c: PEe.transpose(out=pbb[bank][:, c * 128:(c + 1) * 128],
                                                   in_=nb[:, c * 128:(c + 1) * 128], identity=identb[:]),
                     [nbk, "identb"], [bk], inc=(c == 7))
            T.op(ACT, lambda: A.copy(out=nTt[:], in_=pbb[bank][:, 0:1024]), [bk], [nTk])

        def project(nTt, nTk, W, Wk, groups):
            for (bank, pc0, wc0, ncol) in groups:
                bk = "pb%d" % bank
                for c in range(8):
                    T.op(PE, lambda c=c, bank=bank, pc0=pc0, wc0=wc0, ncol=ncol:
                         PEe.matmul(pb[bank][:, pc0:pc0 + ncol], lhsT=nTt[:, c * 128:(c + 1) * 128],
                                    rhs=W[:, c, wc0:wc0 + ncol], start=(c == 0), stop=(c == 7)),
                         [nTk, Wk], [bk], inc=(c == 7))

        with contextlib.ExitStack() as mst:
            def sb(n, s, d):
                return mst.enter_context(nc.sbuf_tensor(n, s, d))

            KT = sb("KT", [128, 2, T_], BF16)
            VE = sb("VE", [128, NT, 2, 130], BF16)
            QT = sb("QT", [128, 4, TO], BF16)
            WB = sb("WB", [128, 8, 2064], BF16)
            MK = sb("MK", [128, 16, 512], BF16)
            tabf = sb("tabf", [128, 260], F32)
            gat = sb("gat", [128, D], F32)
            qg = sb("qg", [128, 1024], F32)
            brep = sb("brep", [128, 256], F32)
            gn = sb("gn", [128, 256], F32)
            lam = sb("lam", [128, 256], F32)
            wupb = sb("wupb", [16, 256], BF16)
            maskA = sb("maskA", [128, 128], BF16)
            xbs = [sb("xb%d" % i, [128, D], F32) for i in range(2)]
            nbs = [sb("nb%d" % i, [128, D], BF16) for i in range(2)]
            nTs = [sb("nT%d" % i, [128, D], BF16) for i in range(2)]
            kf = sb("kf", [128, 512], F32)
            sq = sb("sq", [128, 512], F32)
            k16 = sb("k16", [128, 512], BF16)
            glr16 = sb("glr16", [128, 16], BF16)
            glrT = sb("glrT", [16, 128], BF16)
            zb = sb("zb", [128, 256], F32)
            lg = sb("lg", [128, 256], F32)
            eb = sb("eb", [128, 256], F32)
            enb = sb("enb", [128, 256], F32)
            ec = sb("ec", [128, 256], F32)
            dec = sb("dec", [128, 4], F32)
            qt16 = sb("qt16", [128, 256], BF16)
            kt16 = sb("kt16", [128, 256], BF16)
            kh16 = sb("kh16", [128, 256], BF16)
            vb16 = sb("vb16", [128, 512], BF16)
            qkT = sb("qkT", [128, 512], BF16)
            AT = sb("AT", [128, 512], BF16)
            S = sb("S", [128, 256], F32)
            Sb = [sb("Sb%d" % i, [128, 256], BF16) for i in range(2)]
            snap = sb("snap", [128, NU, 256], F32)
            sgt = sb("sgt", [128, 512], F32)
            gg = sb("gg", [128, 512], F32)
            onb = sb("onb", [128, 512], F32)
            ogl = sb("ogl", [128, 512], BF16)
            PTs = [sb("PT%d" % i, [128, 512], BF16) for i in range(6)]
            accS = sb("accS", [128, 3, 390], F32)
            t1 = sb("t1", [128, 128], F32)
            of = sb("of", [128, 128], F32)
            og = sb("og", [128, 4, 128], BF16)

            T.dma(SP, tabf[:], tabs[:, :], [], ["tabf"], "c1")
            T.dma(SP, gat[:], g_attn[:, :], [], ["gat"], "c2")
            T.dma(SP, qg[:], qk_rep[:, :], [], ["qg"], "c3")
            T.dma(SP, brep[:], b_rep[:, :], [], ["brep"], "c4")
            T.dma(SP, gn[:], gn_rep[:, :], [], ["gn"], "c5")
            T.dma(SP, lam[:], lamv[:, :], [], ["lam"], "c6")
            T.dma(POOL, wupb[:], w_up[:, :], [], ["wupb"], "c7")
            T.dma(POOL, maskA[:], cst[:, 384:512], [], ["maskA"], "c8")
            T.dma(POOL, WB[:, :, 0:1296], w_p0.rearrange("(c p) n -> p c n", p=128), [], ["WB"], "w0")
            for e4 in range(4):
                T.dma(POOL, MK[:, e4 * 4:(e4 + 1) * 4, :],
                      msk[:, e4 * 2048:(e4 + 1) * 2048].rearrange("p (a b) -> p a b", b=512),
                      [], ["MK"], "c9")
            T.op(DVE, lambda: V.scalar_tensor_tensor(out=qg[:, 0:512], in0=qg[:, 0:512], scalar=0.125,
                                                     in1=qg[:, 512:1024], op0=ALU.mult, op1=ALU.mult),
                 ["qg"], ["qg"])
            T.op(DVE, lambda: V.tensor_scalar(out=gn[:, 0:128], in0=gn[:, 0:128], scalar1=1.0 - LAMBDA_INIT,
                                              scalar2=None, op0=ALU.mult), ["gn"], ["gn"])
            T.op(DVE, lambda: V.scalar_tensor_tensor(out=junk[:, 0:64], in0=lam[:, 0:64], scalar=1.0,
                                                     in1=lam[:, 64:128], op0=ALU.mult, op1=ALU.mult,
                                                     accum_out=st[:, 4:5]), ["lam"], ["junk", "stl"])
            T.op(DVE, lambda: V.scalar_tensor_tensor(out=junk[:, 0:64], in0=lam[:, 128:192], scalar=1.0,
                                                     in1=lam[:, 192:256], op0=ALU.mult, op1=ALU.mult,
                                                     accum_out=st[:, 5:6]), ["lam", "stl"], ["junk", "stl"])
            T.op(ACT, lambda: A.activation(out=st[:, 6:8], in_=st[:, 4:6], func=AF.Exp), ["stl"], ["stl"])
            T.op(DVE, lambda: V.tensor_tensor(out=st[:, 8:9], in0=st[:, 7:8], in1=st[:, 6:7], op=ALU.subtract),
                 ["stl"], ["stl"])
            T.op(DVE, lambda: V.tensor_scalar(out=st[:, 8:9], in0=st[:, 8:9], scalar1=-LAMBDA_INIT,
                                              scalar2=None, op0=ALU.add), ["stl"], ["stl"])
            neglam = st[:, 8:9]
            T.op(POOL, lambda: G.memset(VE[:, :, :, 128:130], 1.0), [], ["VE"])
            T.op(DVE, lambda: V.memset(S[:], 0.0), [], ["S"])

            def k_path(t, bank, c0, ngrp, dst_fn, gain=None):
                n = 64 * ngrp
                bk = "pb%d" % bank
                T.op(ACT, lambda: A.copy(out=kf[:, 0:n], in_=pb[bank][:, c0:c0 + n]), [bk], ["kf"])
                T.op(DVE, lambda: V.tensor_tensor(out=sq[:, 0:n], in0=kf[:, 0:n], in1=kf[:, 0:n], op=ALU.mult),
                     ["kf"], ["sq"])
                T.op(DVE, lambda: V.tensor_reduce(out=st[:, 16:16 + ngrp],
                                                  in_=sq[:, 0:n].rearrange("p (a b) -> p a b", b=64),
                                                  axis=AX.X, op=ALU.add), ["sq"], ["stk"])
                rstd_from_ss(st[:, 16:16 + ngrp], st[:, 24:24 + ngrp], 64.0, ["stk"])
                bc = bass.AP(st[:].tensor, st[:, 24:25].offset, [list(st[:].ap[0]), [1, ngrp], [0, 64]])
                if gain is None:
                    T.op(DVE, lambda: V.tensor_tensor(out=k16[:, 0:n].rearrange("p (a b) -> p a b", b=64),
                                                      in0=kf[:, 0:n].rearrange("p (a b) -> p a b", b=64),
                                                      in1=bc, op=ALU.mult), ["kf", "stk"], ["k16"])
                else:
                    T.op(DVE, lambda: V.tensor_tensor(out=sq[:, 0:n].rearrange("p (a b) -> p a b", b=64),
                                                      in0=kf[:, 0:n].rearrange("p (a b) -> p a b", b=64),
                                                      in1=bc, op=ALU.mult), ["kf", "stk", "sq"], ["sq"])
                    T.op(DVE, lambda: V.tensor_tensor(out=k16[:, 0:n], in0=sq[:, 0:n], in1=gain,
                                                      op=ALU.mult), ["sq", "qg"], ["k16"])
                nh = n // 128
                for hh in range(nh):
                    T.op(PE, lambda hh=hh: PEe.transpose(out=pbb[4][:, hh * 128:(hh + 1) * 128],
                                                         in_=k16[:, hh * 128:(hh + 1) * 128], identity=identb[:]),
                         ["k16", "identb"], ["pb4"], inc=(hh == nh - 1))
                dst_fn(pbb[4][:, 0:n].rearrange("p (a b) -> p a b", b=128))

            def gla_tile(own, t_idx, i_own):
                T.op(DVE, lambda: V.tensor_copy(out=glr16[:], in_=pb[6][:, 0:16]), ["pb6"], ["glr16"])
                T.op(PE, lambda: PEe.transpose(out=pbb[6][0:16, 64:192], in_=glr16[:], identity=identb[:]),
                     ["glr16", "identb"], ["pb6"])
                T.op(ACT, lambda: A.copy(out=glrT[:], in_=pbb[6][0:16, 64:192]), ["pb6"], ["glrT"])
                T.op(PE, lambda: PEe.matmul(pb[6][:, 128:384], lhsT=glrT[:], rhs=wupb[:], start=True, stop=True),
                     ["glrT", "wupb"], ["pb6"])
                T.op(DVE, lambda: V.tensor_tensor(out=zb[:], in0=pb[6][:, 128:384], in1=brep[:], op=ALU.add),
                     ["pb6", "brep"], ["zb"])
                T.op(ACT, lambda: A.activation(out=lg[:], in_=zb[:], func=AF.Exp, scale=-1.0), ["zb"], ["lg"])
                T.op(ACT, lambda: A.activation(out=lg[:], in_=lg[:], func=AF.Ln, bias=1.0), ["lg"], ["lg"])
                if own:
                    T.op(PE, lambda: PEe.matmul(pb[7][:, 0:256], lhsT=triS, rhs=lg[:], start=True, stop=True),
                         ["cstf", "lg"], ["pb7"], inc=False)
                T.op(PE, lambda: PEe.matmul(pb[7][:, 256:512], lhsT=upS, rhs=lg[:], start=True, stop=True),
                     ["cstf", "lg"], ["pb7"])
                for p in range(2):
                    T.op(PE, lambda p=p: PEe.matmul(pb[6][:, 96 + 2 * p:98 + 2 * p], lhsT=lg[:, p * 128:(p + 1) * 128],
                                                    rhs=chI, start=True, stop=True),
                         ["lg", "cstf"], ["pb6"], inc=(p == 1))
                T.op(ACT, lambda: A.activation(out=ec[:], in_=pb[7][:, 256:512], func=AF.Exp), ["pb7"], ["ec"])
                if own:
                    T.op(ACT, lambda: A.activation(out=eb[:], in_=pb[7][:, 0:256], func=AF.Exp), ["pb7"], ["eb"])
                    T.op(ACT, lambda: A.activation(out=enb[:], in_=pb[7][:, 0:256], func=AF.Exp, scale=-1.0),
                         ["pb7"], ["enb"])
                T.op(ACT, lambda: A.activation(out=dec[:], in_=pb[6][:, 96:100], func=AF.Exp), ["pb6"], ["dec"])
                T.op(DVE, lambda: V.tensor_tensor(out=kh16[:], in0=pb[1][:, 256:512], in1=ec[:], op=ALU.mult),
                     ["pb1", "ec"], ["kh16"])
                T.op(ACT, lambda: A.copy(out=vb16[:], in_=pb[2][:, :]), ["pb2"], ["vb16"])
                if own:
                    T.op(DVE, lambda: V.scalar_tensor_tensor(out=qt16[:], in0=pb[1][:, 0:256], scalar=0.125,
                                                             in1=eb[:], op0=ALU.mult, op1=ALU.mult),
                         ["pb1", "eb"], ["qt16"])
                    T.op(DVE, lambda: V.tensor_tensor(out=kt16[:], in0=pb[1][:, 256:512], in1=enb[:], op=ALU.mult),
                         ["pb1", "enb"], ["kt16"])
                    for p in range(2):
                        T.op(PE, lambda p=p: PEe.transpose(out=pbb[4][:, 512 + p * 128:512 + (p + 1) * 128],
                                                           in_=qt16[:, p * 128:(p + 1) * 128], identity=identb[:]),
                             ["qt16", "identb"], ["pb4"], inc=False)
                    for p in range(2):
                        T.op(PE, lambda p=p: PEe.transpose(out=pbb[4][:, 768 + p * 128:768 + (p + 1) * 128],
                                                           in_=kt16[:, p * 128:(p + 1) * 128], identity=identb[:]),
                             ["kt16", "identb"], ["pb4"], inc=(p == 1))
                    T.op(ACT, lambda: A.copy(out=qkT[:], in_=pbb[4][:, 512:1024]), ["pb4"], ["qkT"])
                    for hh in range(2):
                        bank = 0 if hh == 0 else 3
                        for p in range(2):
                            T.op(PE, lambda hh=hh, p=p, bank=bank:
                                 PEe.matmul(pb[bank][:, p * 128:(p + 1) * 128],
                                            lhsT=qkT[hh * 64:(hh + 1) * 64, 256 + p * 128:256 + (p + 1) * 128],
                                            rhs=qkT[hh * 64:(hh + 1) * 64, p * 128:(p + 1) * 128],
                                            start=True, stop=True),
                                 ["qkT"], ["pb%d" % bank], inc=(p == 1))
                    mA = bass.AP(maskA[:].tensor, maskA[:].offset, [list(maskA[:].ap[0]), [0, 2], [1, 128]])
                    for hh in range(2):
                        bank = 0 if hh == 0 else 3
                        outv = bass.AP(AT[:].tensor, AT[:, hh * 128:hh * 128 + 1].offset,
                                       [list(AT[:].ap[0]), [256, 2], [1, 128]])
                        T.op(DVE, lambda bank=bank, outv=outv: V.tensor_tensor(
                            out=outv, in0=pb[bank][:, 0:256].rearrange("p (a b) -> p a b", b=128), in1=mA, op=ALU.mult),
                            ["pb%d" % bank, "maskA"], ["AT"])
                for ch in range(2):
                    if own:
                        T.op(POOL, lambda ch=ch: G.tensor_copy(out=Sb[ch][:], in_=S[:]), ["S"], ["Sb%d" % ch])
                    sbank = 2 if ch == 0 else 5
                    sk = "pb%d" % sbank
                    for p in range(2):
                        for hh in range(2):
                            h = 2 * p + hh
                            T.op(PE, lambda ch=ch, p=p, hh=hh, h=h, sbank=sbank:
                                 PEe.matmul(pb[sbank][hh * 64:(hh + 1) * 64, p * 128:(p + 1) * 128],
                                            lhsT=kh16[ch * 64:(ch + 1) * 64, h * 64:(h + 1) * 64],
                                            rhs=vb16[ch * 64:(ch + 1) * 64, h * 128:(h + 1) * 128],
                                            start=True, stop=True),
                                 ["kh16", "vb16"], [sk], inc=(p == 1 and hh == 1))
                    for p in range(2):
                        T.op(DVE, lambda ch=ch, p=p, sbank=sbank:
                             V.scalar_tensor_tensor(out=S[:, p * 128:(p + 1) * 128], in0=S[:, p * 128:(p + 1) * 128],
                                                    scalar=dec[:, 2 * p + ch:2 * p + ch + 1],
                                                    in1=pb[sbank][:, p * 128:(p + 1) * 128],
                                                    op0=ALU.mult, op1=ALU.add),
                             ["S", "dec", sk] + (["Sb%d" % ch] if own else []), ["S"])
                if own:
                    for hh in range(2):
                        bank = 0 if hh == 0 else 3
                        bk = "pb%d" % bank
                        for p in range(2):
                            h = 2 * p + hh
                            oc = 256 + p * 128
                            T.op(PE, lambda h=h, bank=bank, oc=oc:
                                 PEe.matmul(pb[bank][:, oc:oc + 128], lhsT=AT[:, h * 128:(h + 1) * 128],
                                            rhs=vb16[:, h * 128:(h + 1) * 128], start=True, stop=False),
                                 ["AT", "vb16"], [bk], inc=False)
                            for ch in range(2):
                                T.op(PE, lambda h=h, hh=hh, p=p, ch=ch, bank=bank, oc=oc:
                                     PEe.matmul(pb[bank][ch * 64:(ch + 1) * 64, oc:oc + 128],
                                                lhsT=qkT[hh * 64:(hh + 1) * 64, p * 128 + ch * 64:p * 128 + (ch + 1) * 64],
                                                rhs=Sb[ch][hh * 64:(hh + 1) * 64, p * 128:(p + 1) * 128],
                                                start=False, stop=True),
                                     ["qkT", "Sb%d" % ch], [bk], inc=(ch == 1 and p == 1))
                    for hh in range(2):
                        bank = 0 if hh == 0 else 3
                        bk = "pb%d" % bank
                        T.op(ACT, lambda bank=bank: A.copy(out=kf[:, 0:256], in_=pb[bank][:, 256:512]), [bk], ["kf"])
                        T.op(DVE, lambda: V.tensor_tensor(out=sq[:, 0:256], in0=kf[:, 0:256], in1=kf[:, 0:256],
                                                          op=ALU.mult), ["kf"], ["sq"])
                        T.op(DVE, lambda: V.tensor_reduce(out=st[:, 16:18],
                                                          in_=sq[:, 0:256].rearrange("p (a b) -> p a b", b=128),
                                                          axis=AX.X, op=ALU.add), ["sq"], ["stk"])
                        rstd_from_ss(st[:, 16:18], st[:, 24:26], 128.0, ["stk"])
                        bc = bass.AP(st[:].tensor, st[:, 24:25].offset, [list(st[:].ap[0]), [1, 2], [0, 128]])
                        ggv = bass.AP(gg[:].tensor, gg[:, hh * 128:hh * 128 + 1].offset,
                                      [list(gg[:].ap[0]), [256, 2], [1, 128]])
                        oglv = bass.AP(ogl[:].tensor, ogl[:, hh * 128:hh * 128 + 1].offset,
                                       [list(ogl[:].ap[0]), [256, 2], [1, 128]])
                        T.op(DVE, lambda bc=bc: V.tensor_tensor(out=onb[:, 0:256].rearrange("p (a b) -> p a b", b=128),
                                                                in0=kf[:, 0:256].rearrange("p (a b) -> p a b", b=128),
                                                                in1=bc, op=ALU.mult), ["kf", "stk"], ["onb"])
                        T.op(DVE, lambda ggv=ggv, oglv=oglv: V.tensor_tensor(
                            out=oglv, in0=onb[:, 0:256].rearrange("p (a b) -> p a b", b=128), in1=ggv, op=ALU.mult),
                            ["onb", "gg"], ["ogl"])
                    T.dma(SP, mix_scr[i_own * 128:(i_own + 1) * 128, 512:1024], ogl[:], ["ogl"], ["mixg%d" % i_own], "mx")

            def kv_store(t):
                def dst(src3):
                    T.op(ACT, lambda: A.copy(out=KT[:, :, t * 128:(t + 1) * 128], in_=src3), ["pb4"], ["KT"])
                return dst

            for t in range(NT):
                s = t % 2
                norm_transpose(x_all[t * 128:(t + 1) * 128, :], xbs[s], "xb%d" % s, nbs[s], "nb%d" % s,
                               nTs[s], "nT%d" % s, gat, "gat", SP, "x%d" % s)
                project(nTs[s], "nT%d" % s, WB, "WB",
                        [(0, 0, 0, 512), (1, 256, 512, 256), (2, 0, 768, 512), (6, 0, 1280, 16)])
                k_path(t, 0, 0, 4, kv_store(t))
                T.op(DVE, lambda t=t: V.tensor_copy(out=VE[:, t, :, 0:128],
                                                    in_=pb[0][:, 256:512].rearrange("p (a b) -> p a b", b=128)),
                     ["pb0"], ["VE"])
                if t % 16 % 4 == 0:
                    u = t // 16
                    r = (t % 16) // 4
                    if r == 0:
                        T.op(DVE, lambda u=u, r=r: V.tensor_scalar(out=snap[:, u, :], in0=S[:],
                                                                   scalar1=tabf[:, 256 + r:257 + r], scalar2=None,
                                                                   op0=ALU.mult), ["S", "tabf"], ["snap"])
                    else:
                        T.op(DVE, lambda u=u, r=r: V.scalar_tensor_tensor(out=snap[:, u, :], in0=S[:],
                                                                          scalar=tabf[:, 256 + r:257 + r],
                                                                          in1=snap[:, u, :], op0=ALU.mult, op1=ALU.add),
                             ["S", "tabf", "snap"], ["snap"])
                gla_tile(False, t, None)

            T.dma(POOL, WB[:, :, :], w_ow.rearrange("(c p) n -> p c n", p=128), [], ["WB"], "w0")

            def q_store(i):
                def dst(src3):
                    T.op(ACT, lambda: A.copy(out=QT[:, :, i * 128:(i + 1) * 128], in_=src3), ["pb4"], ["QT"])
                return dst

            gnb = bass.AP(gn[:].tensor, gn[:, 128:129].offset, [list(gn[:].ap[0]), [0, 4], [1, 128]])
            for i in range(NO):
                s = i % 2
                if i % 4 == 0:
                    T.op(DVE, lambda i=i: V.tensor_copy(out=S[:], in_=snap[:, i // 4, :]), ["snap"], ["S"])
                norm_transpose(x_own[i * 128:(i + 1) * 128, :], xbs[s], "xb%d" % s, nbs[s], "nb%d" % s,
                               nTs[s], "nT%d" % s, gat, "gat", SP, "x%d" % s)
                project(nTs[s], "nT%d" % s, WB, "WB",
                        [(0, 0, 0, 512), (1, 0, 512, 512), (2, 0, 1024, 512), (3, 0, 1536, 512), (6, 0, 2048, 16)])
                k_path(i, 0, 0, 8, q_store(i), gain=qg[:, 0:512])
                T.op(ACT, lambda: A.activation(out=sgt[:], in_=pb[3][:, :], func=AF.Exp, scale=-1.0), ["pb3"], ["sgt"])
                T.op(DVE, lambda: V.tensor_scalar(out=sgt[:], in0=sgt[:], scalar1=1.0, scalar2=None, op0=ALU.add),
                     ["sgt"], ["sgt"])
                T.op(DVE, lambda: V.reciprocal(out=sgt[:], in_=sgt[:]), ["sgt"], ["sgt"])
                T.op(DVE, lambda: V.tensor_tensor(out=gg[:], in0=pb[3][:, :], in1=sgt[:], op=ALU.mult),
                     ["pb3", "sgt"], ["gg"])
                T.op(DVE, lambda: V.tensor_tensor(out=gg[:].rearrange("p (a b) -> p a b", b=128),
                                                  in0=gg[:].rearrange("p (a b) -> p a b", b=128), in1=gnb, op=ALU.mult),
                     ["gg", "gn"], ["gg"])
                gla_tile(True, None, i)

            def acc_ap(a):
                bank = 5 + a // 3
                col = (a % 3) * 130
                return bank, col

            def attention(hp):
                steps = [(u, hl, kb) for u in range(NU) for hl in range(2) for kb in range(16 * u + 16)]
                LOOK = 2

                def emit_qk(idx):
                    u, hl, kb = steps[idx]
                    h = 2 * hp + hl
                    e = kb - 16 * u
                    ei = e + 48
                    for m in range(2):
                        sbank = 1 + (2 * idx + m) % 4
                        T.op(PE, lambda m=m, sbank=sbank:
                             PEe.matmul(pb[sbank][:, 0:512],
                                        lhsT=KT[m * 64:(m + 1) * 64, hl, kb * 128:(kb + 1) * 128],
                                        rhs=QT[m * 64:(m + 1) * 64, h, u * 512:(u + 1) * 512],
                                        start=True, stop=True),
                             ["KT", "QT"], ["pb%d" % sbank])
                    for m in range(2):
                        sbank = 1 + (2 * idx + m) % 4
                        slot = (2 * idx + m) % 6
                        T.op(ACT, lambda sbank=sbank, slot=slot:
                             A.activation(out=PTs[slot][:], in_=pb[sbank][:, 0:512], func=AF.Exp,
                                          bias=tabf[:, h * 64 + ei:h * 64 + ei + 1]),
                             ["pb%d" % sbank, "tabf"], ["PT%d" % slot])
                        if e >= 0:
                            T.op(POOL, lambda slot=slot:
                                 G.tensor_tensor(out=PTs[slot][:], in0=PTs[slot][:], in1=MK[:, e, :], op=ALU.mult),
                                 ["PT%d" % slot, "MK"], ["PT%d" % slot])

                def emit_pv(idx):
                    u, hl, kb = steps[idx]
                    h = 2 * hp + hl
                    nkb = 16 * u + 16
                    for m in range(2):
                        slot = (2 * idx + m) % 6
                        for c in range(4):
                            bank, col = acc_ap(c * 2 + m)
                            first = (kb == 0 and (c * 2 + m) in (0, 4, 6))
                            T.op(PE, lambda c=c, bank=bank, col=col, slot=slot, first=first:
                                 PEe.matmul(pb[bank][:, col:col + 130],
                                            lhsT=PTs[slot][:, c * 128:(c + 1) * 128],
                                            rhs=VE[:, kb, hl, :], start=first, stop=(kb == nkb - 1),
                                            skip_group_check=True),
                                 ["PT%d" % slot, "VE"], ["pb%d" % bank], inc=(c == 3 and m == 1))
                    if kb != nkb - 1:
                        return
                    for bi in range(3):
                        T.op(DVE, lambda bi=bi: V.tensor_copy(out=accS[:, bi, :], in_=pb[5 + bi][:, 0:390]),
                             ["pb%d" % (5 + bi)], ["accS"])
                    for c in range(4):
                        a0 = c * 2
                        a1 = c * 2 + 1
                        A0 = accS[:, a0 // 3, (a0 % 3) * 130:(a0 % 3) * 130 + 130]
                        A1 = accS[:, a1 // 3, (a1 % 3) * 130:(a1 % 3) * 130 + 130]
                        T.op(DVE, lambda A0=A0: V.reciprocal(out=st[:, 10:11], in_=A0[:, 128:129]), ["accS"], ["sta"])
                        T.op(DVE, lambda A1=A1: V.reciprocal(out=st[:, 11:12], in_=A1[:, 128:129]), ["accS", "sta"], ["sta"])
                        T.op(DVE, lambda: V.tensor_tensor(out=st[:, 11:12], in0=st[:, 11:12], in1=neglam, op=ALU.mult),
                             ["sta", "stl"], ["sta"])
                        T.op(DVE, lambda A1=A1: V.tensor_scalar(out=t1[:], in0=A1[:, 0:128], scalar1=st[:, 11:12],
                                                                scalar2=None, op0=ALU.mult), ["accS", "sta"], ["t1"])
                        T.op(DVE, lambda A0=A0: V.scalar_tensor_tensor(out=of[:], in0=A0[:, 0:128], scalar=st[:, 10:11],
                                                                       in1=t1[:], op0=ALU.mult, op1=ALU.add),
                             ["accS", "sta", "t1"], ["of"])
                        T.op(DVE, lambda: V.scalar_tensor_tensor(out=junk[:, 0:128], in0=of[:], scalar=1.0, in1=of[:],
                                                                 op0=ALU.mult, op1=ALU.mult, accum_out=st[:, 12:13]),
                             ["of"], ["junk", "stb"])
                        rstd_from_ss(st[:, 12:13], st[:, 13:14], 128.0, ["stb"])
                        T.op(DVE, lambda c=c: V.scalar_tensor_tensor(out=og[:, c, :], in0=of[:], scalar=st[:, 13:14],
                                                                     in1=gn[:, 0:128], op0=ALU.mult, op1=ALU.mult),
                             ["of", "stb", "gn"], ["og"])
                    T.dma(SP, mix_scr[u * 512:(u + 1) * 512, h * 128:(h + 1) * 128].rearrange("(c p) n -> p c n", p=128),
                          og[:], ["og"], ["mixd%d_%d" % (u, h)], "mx2")

                for i in range(len(steps) + LOOK):
                    if i < len(steps):
                        emit_qk(i)
                    if i - LOOK >= 0:
                        emit_pv(i - LOOK)

            attention(0)

            T.dma(POOL, WB[:, :, 0:512], w_p1.rearrange("(c p) n -> p c n", p=128), [], ["WB"], "w0")
            for t in range(NT):
                s = t % 2
                norm_transpose(x_all[t * 128:(t + 1) * 128, :], xbs[s], "xb%d" % s, nbs[s], "nb%d" % s,
                               nTs[s], "nT%d" % s, gat, "gat", SP, "x%d" % s)
                project(nTs[s], "nT%d" % s, WB, "WB", [(0, 0, 0, 512)])
                k_path(t, 0, 0, 4, kv_store(t))
                T.op(DVE, lambda t=t: V.tensor_copy(out=VE[:, t, :, 0:128],
                                                    in_=pb[0][:, 256:512].rearrange("p (a b) -> p a b", b=128)),
                     ["pb0"], ["VE"])
            attention(1)
            mix_keys = ["mixg%d" % i for i in range(NO)] + ["mixd%d_%d" % (u, h) for u in range(NU) for h in range(4)]

        with contextlib.ExitStack() as fst:
            def sb(n, s, d):
                return fst.enter_context(nc.sbuf_tensor(n, s, d))

            WG = sb("WG", [128, 8, DFF], BF16)
            WU = sb("WU", [128, 8, DFF], BF16)
            WD = sb("WD", [128, NFF, D], BF16)
            gff = sb("gff", [128, D], F32)
            hbs = [sb("hb%d" % i, [128, D], F32) for i in range(2)]
            nbs = [sb("fnb%d" % i, [128, D], BF16) for i in range(2)]
            obs = [sb("ob%d" % i, [128, D], F32) for i in range(2)]
            T.dma(SP, gff[:], g_ffn[:, :], [], ["gff"], "c2")
            with contextlib.ExitStack() as ost:
                def sbo(n, s, d):
                    return ost.enter_context(nc.sbuf_tensor(n, s, d))
                WO = sbo("WO", [128, 8, D], BF16)
                mxs = [sbo("mx%d" % i, [128, D], BF16) for i in range(2)]
                mxT = [sbo("mxT%d" % i, [128, D], BF16) for i in range(2)]
                xos = [sbo("xo%d" % i, [128, D], F32) for i in range(2)]
                T.dma(POOL, WO[:], w_o.rearrange("(c p) n -> p c n", p=128), [], ["WO"], "w1")
                T.dma(POOL, WG[:], w_g.rearrange("(c p) n -> p c n", p=128), [], ["WG"], "w2")
                T.dma(POOL, WU[:], w_u.rearrange("(c p) n -> p c n", p=128), [], ["WU"], "w3")
                T.dma(POOL, WD[:], w_d.rearrange("(c p) n -> p c n", p=128), [], ["WD"], "w4")
                for i in range(NO):
                    s = i % 2
                    T.dma(SP, mxs[s][:], mix_scr[i * 128:(i + 1) * 128, :], mix_keys if i < 2 else [], ["mx%d" % s], "m%d" % s)
                    T.dma(SP, xos[s][:], x_own[i * 128:(i + 1) * 128, :], [], ["xo%d" % s], "xo%d" % s)
                    for c in range(8):
                        T.op(PE, lambda c=c, s=s: PEe.transpose(out=pbb[0][:, c * 128:(c + 1) * 128],
                                                                in_=mxs[s][:, c * 128:(c + 1) * 128], identity=identb[:]),
                             ["mx%d" % s, "identb"], ["pb0"], inc=(c == 7))
                    T.op(ACT, lambda s=s: A.copy(out=mxT[s][:], in_=pbb[0][:, 0:1024]), ["pb0"], ["mxT%d" % s])
                    for nb_ in range(2):
                        bank = 1 + nb_
                        for c in range(8):
                            T.op(PE, lambda c=c, s=s, nb_=nb_, bank=bank:
                                 PEe.matmul(pb[bank][:, 0:512], lhsT=mxT[s][:, c * 128:(c + 1) * 128],
                                            rhs=WO[:, c, nb_ * 512:(nb_ + 1) * 512], start=(c == 0), stop=(c == 7)),
                                 ["mxT%d" % s, "WO"], ["pb%d" % bank], inc=(c == 7))
                        T.op(DVE, lambda s=s, nb_=nb_, bank=bank:
                             V.tensor_tensor(out=hbs[s][:, nb_ * 512:(nb_ + 1) * 512], in0=pb[bank][:, 0:512],
                                             in1=xos[s][:, nb_ * 512:(nb_ + 1) * 512], op=ALU.add),
                             ["pb%d" % bank, "xo%d" % s], ["hb%d" % s])
                    T.dma(SP, h_scr[i * 128:(i + 1) * 128, :], hbs[s][:], ["hb%d" % s], ["hscr%d" % i], "hs%d" % s)
                    if dbg:
                        T.dma(SP, dbg_outs["d_h"][i * 128:(i + 1) * 128, :], hbs[s][:], ["hb%d" % s], [], "dbg")
                        T.op(DVE, lambda s=s: V.tensor_copy(out=xos[s][:], in_=mxs[s][:]), ["mx%d" % s, "xo%d" % s], ["xo%d" % s])
                        T.dma(SP, dbg_outs["d_mix"][i * 128:(i + 1) * 128, :], xos[s][:], ["xo%d" % s], [], "dbg")

            mT = sb("mT", [128, 8, 512], BF16)
            actT = sb("actT", [128, NFF, 512], BF16)
            ee = sb("ee", [128, 512], F32)
            uS = sb("uS", [128, 512], F32)
            rr = sb("rr", [128, 512], F32)
            for gI in range(NU):
                for tt in range(4):
                    i = gI * 4 + tt
                    s = i % 2
                    T.dma(SP, hbs[s][:], h_scr[i * 128:(i + 1) * 128, :], ["hscr%d" % i], ["hb%d" % s], "hl%d" % s)
                    T.op(DVE, lambda s=s: V.scalar_tensor_tensor(out=junk[:], in0=hbs[s][:], scalar=1.0, in1=hbs[s][:],
                                                                 op0=ALU.mult, op1=ALU.mult, accum_out=st[:, 0:1]),
                         ["hb%d" % s], ["junk", "st0"])
                    rstd_from_ss(st[:, 0:1], st[:, 1:2], float(D), ["st0"])
                    T.op(DVE, lambda s=s: V.scalar_tensor_tensor(out=nbs[s][:], in0=hbs[s][:], scalar=st[:, 1:2],
                                                                 in1=gff[:], op0=ALU.mult, op1=ALU.mult),
                         ["hb%d" % s, "st0", "gff"], ["fnb%d" % s])
                    for c in range(8):
                        T.op(PE, lambda c=c, s=s: PEe.transpose(out=pbb[0][:, c * 128:(c + 1) * 128],
                                                                in_=nbs[s][:, c * 128:(c + 1) * 128], identity=identb[:]),
                             ["fnb%d" % s, "identb"], ["pb0"], inc=(c == 7))
                    T.op(ACT, lambda tt=tt: A.copy(out=mT[:, :, tt * 128:(tt + 1) * 128],
                                                   in_=pbb[0][:, 0:1024].rearrange("p (a b) -> p a b", b=128)),
                         ["pb0"], ["mT"])
                for f in range(NFF):
                    gb = 1 + (f % 2)
                    ub = 3 + (f % 2)
                    for c in range(8):
                        T.op(PE, lambda c=c, f=f, gb=gb: PEe.matmul(pb[gb][:, 0:512], lhsT=WG[:, c, f * 128:(f + 1) * 128],
                                                                    rhs=mT[:, c, :], start=(c == 0), stop=(c == 7)),
                             ["WG", "mT"], ["pb%d" % gb], inc=(c == 7))
                    for c in range(8):
                        T.op(PE, lambda c=c, f=f, ub=ub: PEe.matmul(pb[ub][:, 0:512], lhsT=WU[:, c, f * 128:(f + 1) * 128],
                                                                    rhs=mT[:, c, :], start=(c == 0), stop=(c == 7)),
                             ["WU", "mT"], ["pb%d" % ub], inc=(c == 7))
                    T.op(ACT, lambda gb=gb: A.activation(out=ee[:], in_=pb[gb][:, 0:512], func=AF.Exp, scale=-1.0),
                         ["pb%d" % gb], ["ee"])
                    T.op(ACT, lambda ub=ub: A.copy(out=uS[:], in_=pb[ub][:, 0:512]), ["pb%d" % ub], ["uS"])
                    T.op(DVE, lambda: V.tensor_scalar(out=ee[:], in0=ee[:], scalar1=1.0, scalar2=None, op0=ALU.add),
                         ["ee"], ["ee"])
                    T.op(DVE, lambda: V.reciprocal(out=rr[:], in_=ee[:]), ["ee"], ["rr"])
                    T.op(DVE, lambda gb=gb: V.tensor_tensor(out=rr[:], in0=pb[gb][:, 0:512], in1=rr[:], op=ALU.mult),
                         ["pb%d" % gb, "rr"], ["rr"])
                    T.op(DVE, lambda f=f: V.tensor_tensor(out=actT[:, f, :], in0=rr[:], in1=uS[:], op=ALU.mult),
                         ["rr", "uS"], ["actT"])
                for tt in range(4):
                    i = gI * 4 + tt
                    s = i % 2
                    T.dma(SP, hbs[s][:], h_scr[i * 128:(i + 1) * 128, :], ["hscr%d" % i], ["hb%d" % s], "hl%d" % s)
                    for nb_ in range(2):
                        bank = 5 + nb_
                        for f in range(NFF):
                            T.op(PE, lambda f=f, tt=tt, nb_=nb_, bank=bank:
                                 PEe.matmul(pb[bank][:, 0:512], lhsT=actT[:, f, tt * 128:(tt + 1) * 128],
                                            rhs=WD[:, f, nb_ * 512:(nb_ + 1) * 512], start=(f == 0), stop=(f == NFF - 1)),
                                 ["actT", "WD"], ["pb%d" % bank], inc=(f == NFF - 1))
                        T.op(DVE, lambda s=s, nb_=nb_, bank=bank:
                             V.tensor_tensor(out=obs[s][:, nb_ * 512:(nb_ + 1) * 512], in0=pb[bank][:, 0:512],
                                             in1=hbs[s][:, nb_ * 512:(nb_ + 1) * 512], op=ALU.add),
                             ["pb%d" % bank, "hb%d" % s], ["ob%d" % s])
                    T.dma(SP, out[i * 128:(i + 1) * 128, :], obs[s][:], ["ob%d" % s], [], "o%d" % s)

        for n in list(T.dsems):
            d = T.dsem(n)
            nc.sync.wait_ge(d.sem, d.count)
    return nc


def _consts():
    c = np.zeros((128, 514), np.float32)
    c[:, 0:128] = np.eye(128, dtype=np.float32)
    s = np.arange(128)[:, None]
    t = np.arange(128)[None, :]
    same = (s // 64) == (t // 64)
    c[:, 128:256] = np.where(same & (s <= t), -1.0 / 16.0, 0.0)
    c[:, 256:384] = np.where(same & (s > t), -1.0 / 16.0, 0.0)
    c[:, 384:512] = np.where(same & (s <= t), 1.0, 0.0)
    c[0:64, 512] = -1.0 / 16.0
    c[64:128, 513] = -1.0 / 16.0
    return c


def _tabs(j):
    tb = np.zeros((128, 260), np.float32)
    kl = np.arange(128, dtype=np.float64)
    for h in range(4):
        for ei in range(64):
            e = ei - 48
            if e <= 4 * j + 3:
                tb[:, h * 64 + ei] = SLOPES[h] * (128.0 * (e - 4 * j) + kl - 256.0)
            else:
                tb[:, h * 64 + ei] = NEG
    tb[:, 256 + j] = 1.0
    return tb


def _masks(j):
    m = np.zeros((128, 16, 4, 128), np.float32)
    k = np.arange(128)[:, None]
    q = np.arange(128)[None, :]
    tri = (k <= q).astype(np.float32)
    for e in range(16):
        for c in range(4):
            cs = 4 * j + c
            if e < cs:
                m[:, e, c, :] = 1.0
            elif e == cs:
                m[:, e, c, :] = tri
    return m.reshape(128, 16 * 512)


def _rep(v, n=128):
    return np.ascontiguousarray(np.broadcast_to(np.asarray(v, np.float32).reshape(1, -1), (n, v.size)))


def prep_inputs(inp, NU=4):
    T_ = NU * 2048
    f = lambda a: np.ascontiguousarray(np.asarray(a, dtype=np.float32))
    x = f(inp["x"])
    w_in = f(inp["w_in"])[0]
    cols = lambda a, b: list(range(a, b))
    dq, dk, dv = 0, 512, 1024
    gq, gk, gv, go, gl = 1536, 1792, 2048, 2560, 3072
    c_p0 = cols(dk, dk + 256) + cols(dv, dv + 256) + cols(gk, gk + 256) + cols(gv, gv + 512) + cols(gl, gl + 16)
    c_p1 = cols(dk + 256, dk + 512) + cols(dv + 256, dv + 512)
    c_ow = cols(dq, dq + 512) + cols(gq, gq + 256) + cols(gk, gk + 256) + cols(gv, gv + 512) + cols(go, go + 512) + cols(gl, gl + 16)
    shared = {
        "w_p0": np.ascontiguousarray(w_in[:, c_p0]),
        "w_p1": np.ascontiguousarray(w_in[:, c_p1]),
        "w_ow": np.ascontiguousarray(w_in[:, c_ow]),
        "w_up": f(inp["w_gla_gate_up"])[0],
        "w_o": f(inp["w_out"])[0],
        "w_g": f(inp["w_ffn_gate"])[0],
        "w_u": f(inp["w_ffn_up"])[0],
        "w_d": f(inp["w_ffn_down"])[0],
        "g_attn": _rep(f(inp["attn_norm_gain"])[0]),
        "g_ffn": _rep(f(inp["ffn_norm_gain"])[0]),
        "qk_rep": np.concatenate([_rep(np.tile(f(inp["q_norm_gain"])[0], 8)),
                                  _rep(np.tile(f(inp["k_norm_gain"])[0], 8))], axis=1),
        "b_rep": _rep(f(inp["b_gla_gate"])[0]),
        "gn_rep": np.concatenate([_rep(f(inp["diff_out_norm_gain"])[0]), _rep(f(inp["gla_out_norm_gain"])[0])], axis=1),
        "lamv": np.concatenate([_rep(f(inp[k])[0]) for k in ("lambda_q1", "lambda_k1", "lambda_q2", "lambda_k2")], axis=1),
        "cst": _consts(),
    }
    in_maps = []
    for core in range(8):
        b, j = core // 4, core % 4
        own_rows = np.concatenate([np.arange((4 * u + j) * 512, (4 * u + j + 1) * 512) for u in range(NU)])
        m = dict(shared)
        m["x_all"] = np.ascontiguousarray(x[b, :T_])
        m["x_own"] = np.ascontiguousarray(x[b, own_rows])
        m["tabs"] = _tabs(j)
        m["msk"] = _masks(j)
        in_maps.append(m)
    return in_maps


def assemble(results, NU=4, B=2, key="out"):
    T_ = NU * 2048
    outp = np.zeros((B, T_, D), np.float32)
    for core in range(8):
        b, j = core // 4, core % 4
        r = np.asarray(results[core][key])
        for u in range(NU):
            outp[b, (4 * u + j) * 512:(4 * u + j + 1) * 512] = r[u * 512:(u + 1) * 512]
    return outp


_NC_CACHE = {}


def kernel(**inputs):
    NU = 4
    if NU not in _NC_CACHE:
        _NC_CACHE[NU] = build(NU)
    nc = _NC_CACHE[NU]
    in_maps = prep_inputs(inputs, NU)
    res = run_bass_kernel_spmd(nc, in_maps, core_ids=list(range(8)))
    return assemble(res.results, NU)
```

```python
import contextlib
import numpy as np
import concourse.bass as bass
import concourse.mybir as mybir
from concourse.bass_utils import run_bass_kernel_spmd

F32 = mybir.dt.float32
BF16 = mybir.dt.bfloat16
AF = mybir.ActivationFunctionType
ALU = mybir.AluOpType
AX = mybir.AxisListType

D = 1024
DFF = 2816
NFF = DFF // 128
EPS = 1e-6
LAMBDA_INIT = 0.8 - 0.6 * 1.0
SLOPES = [2.0 ** (-8.0 * (h + 1.0) / 4.0) for h in range(4)]
NEG = -30000.0
SAME_ENGINE_SYNC = True


class ES:
    def __init__(self, name, eng, sem):
        self.name = name
        self.eng = eng
        self.sem = sem
        self.count = 0
        self.seen = {}


class Trk:
    def __init__(self, nc, stack):
        self.nc = nc
        self.stack = stack
        self.lw = {}
        self.rd = {}
        self.dsems = {}
        mk = lambda n, e: ES(n, e, stack.enter_context(nc.semaphore("s_" + n)))
        self.pe = mk("pe", nc.tensor)
        self.act = mk("act", nc.scalar)
        self.dve = mk("dve", nc.vector)
        self.pool = mk("pool", nc.gpsimd)
        self.sp = mk("sp", nc.sync)

    def dsem(self, name):
        if name not in self.dsems:
            self.dsems[name] = ES("d_" + name, None,
                                  self.stack.enter_context(self.nc.semaphore("d_" + name)))
        return self.dsems[name]

    def _deps(self, reads, writes):
        deps = {}

        def add(e, v):
            if deps.get(e, 0) < v:
                deps[e] = v
        for k in reads:
            if k in self.lw:
                add(*self.lw[k])
        for k in writes:
            if k in self.lw:
                add(*self.lw[k])
            for e, v in self.rd.get(k, {}).items():
                add(e, v)
        return deps

    def _wait(self, E, deps):
        for e, v in deps.items():
            if e is E and (not SAME_ENGINE_SYNC or E.name == "pe"):
                continue
            if E.seen.get(e, 0) >= v:
                continue
            assert v <= e.count, (E.name, e.name, v, e.count)
            E.eng.wait_ge(e.sem, v)
            E.seen[e] = v

    def _rec(self, E, val, reads, writes):
        for k in writes:
            self.lw[k] = (E, val)
            self.rd[k] = {}
        for k in reads:
            d = self.rd.setdefault(k, {})
            if d.get(E, 0) < val:
                d[E] = val

    def op(self, E, fn, reads=(), writes=(), inc=True):
        pr = [k for k in reads if k.startswith("pb")]
        if pr:
            reads = [k for k in reads if not k.startswith("pb")]
            writes = list(writes) + [k for k in pr if k not in writes]
        self._wait(E, self._deps(reads, writes))
        inst = fn()
        if inc:
            inst.then_inc(E.sem, 1)
            E.count += 1
            val = E.count
        else:
            val = E.count + 1
        self._rec(E, val, reads, writes)
        return inst

    def dma(self, Q, out, in_, reads, writes, sem):
        self._wait(Q, self._deps(reads, writes))
        Dm = self.dsem(sem)
        inst = Q.eng.dma_start(out=out, in_=in_)
        inst.then_inc(Dm.sem, 16)
        Dm.count += 16
        self._rec(Dm, Dm.count, reads, writes)
        return inst


def build(NU=4, dbg=False):
    T_ = NU * 2048
    NT = T_ // 128
    NO = NU * 4
    TO = NO * 128
    nc = bass.Bass("TRN2", target_bir_lowering=False)

    def din(name, shape, dt=F32):
        return nc.dram_tensor(name, shape, dt, kind="ExternalInput").ap()

    x_all = din("x_all", [T_, D])
    x_own = din("x_own", [TO, D])
    w_p0 = din("w_p0", [D, 1296])
    w_p1 = din("w_p1", [D, 512])
    w_ow = din("w_ow", [D, 2064])
    w_up = din("w_up", [16, 256])
    w_o = din("w_o", [D, D])
    w_g = din("w_g", [D, DFF])
    w_u = din("w_u", [D, DFF])
    w_d = din("w_d", [DFF, D])
    g_attn = din("g_attn", [128, D])
    g_ffn = din("g_ffn", [128, D])
    qk_rep = din("qk_rep", [128, 1024])
    b_rep = din("b_rep", [128, 256])
    gn_rep = din("gn_rep", [128, 256])
    lamv = din("lamv", [128, 256])
    cst = din("cst", [128, 514])
    tabs = din("tabs", [128, 260])
    msk = din("msk", [128, 16 * 512])
    out = nc.dram_tensor("out", [TO, D], F32, kind="ExternalOutput").ap()
    mix_scr = nc.dram_tensor("mix_scr", [TO, D], BF16, kind="Internal").ap()
    h_scr = nc.dram_tensor("h_scr", [TO, D], F32, kind="Internal").ap()
    dbg_outs = {}
    if dbg:
        dbg_outs["d_mix"] = nc.dram_tensor("d_mix", [TO, D], F32, kind="ExternalOutput").ap()
        dbg_outs["d_h"] = nc.dram_tensor("d_h", [TO, D], F32, kind="ExternalOutput").ap()

    with contextlib.ExitStack() as gst:
        T = Trk(nc, gst)
        PE, ACT, DVE, POOL, SP = T.pe, T.act, T.dve, T.pool, T.sp
        V = nc.vector
        A = nc.scalar
        G = nc.gpsimd
        PEe = nc.tensor

        def sbg(n, s, d):
            return gst.enter_context(nc.sbuf_tensor(n, s, d))

        pb = [gst.enter_context(nc.psum_tensor("pb%d" % i, [128, 512], F32)) for i in range(8)]
        pbb = [p[:].bitcast(BF16) for p in pb]

        cstf = sbg("cstf", [128, 514], F32)
        identb = sbg("identb", [128, 128], BF16)
        st = sbg("st", [128, 32], F32)
        junk = sbg("junk", [128, 1024], BF16)
        T.dma(SP, cstf[:], cst[:, :], [], ["cstf"], "c0")
        T.op(DVE, lambda: V.tensor_copy(out=identb[:], in_=cstf[:, 0:128]), ["cstf"], ["identb"])
        triS = cstf[:, 128:256]
        upS = cstf[:, 256:384]
        chI = cstf[:, 512:514]

        def rstd_from_ss(ss_ap, out_ap, n, keys):
            T.op(ACT, lambda: A.activation(out=out_ap, in_=ss_ap, func=AF.Ln, scale=1.0 / n, bias=EPS),
                 keys, keys)
            T.op(ACT, lambda: A.activation(out=out_ap, in_=out_ap, func=AF.Exp, scale=-0.5),
                 keys, keys)

        def norm_transpose(src_ap, xb, xk, nb, nbk, nTt, nTk, grep, grepk, dq, dsem, bank=5):
            T.dma(dq, xb[:], src_ap, [], [xk], dsem)
            T.op(DVE, lambda: V.scalar_tensor_tensor(out=junk[:], in0=xb[:], scalar=1.0, in1=xb[:],
                                                     op0=ALU.mult, op1=ALU.mult, accum_out=st[:, 0:1]),
                 [xk], ["junk", "st0"])
            rstd_from_ss(st[:, 0:1], st[:, 1:2], float(D), ["st0"])
            T.op(DVE, lambda: V.scalar_tensor_tensor(out=nb[:], in0=xb[:], scalar=st[:, 1:2], in1=grep[:],
                                                     op0=ALU.mult, op1=ALU.mult),
                 [xk, "st0", grepk], [nbk])
            bk = "pb%d" % bank
            for c in range(8):
                T.op(PE, lambda c=c: PEe.transpose(out=pbb[bank][:, c * 128:(c + 1) * 128],
                                                   in_=nb[:, c * 128:(c + 1) * 128], identity=identb[:]),
                     [nbk, "identb"], [bk], inc=(c == 7))
            T.op(ACT, lambda: A.copy(out=nTt[:], in_=pbb[bank][:, 0:1024]), [bk], [nTk])

        def project(nTt, nTk, W, Wk, groups):
            for (bank, pc0, wc0, ncol) in groups:
                bk = "pb%d" % bank
                for c in range(8):
                    T.op(PE, lambda c=c, bank=bank, pc0=pc0, wc0=wc0, ncol=ncol:
                         PEe.matmul(pb[bank][:, pc0:pc0 + ncol], lhsT=nTt[:, c * 128:(c + 1) * 128],
                                    rhs=W[:, c, wc0:wc0 + ncol], start=(c == 0), stop=(c == 7)),
                         [nTk, Wk], [bk], inc=(c == 7))

        with contextlib.ExitStack() as mst:
            def sb(n, s, d):
                return mst.enter_context(nc.sbuf_tensor(n, s, d))

            KT = sb("KT", [128, 2, T_], BF16)
            VE = sb("VE", [128, NT, 2, 130], BF16)
            QT = sb("QT", [128, 4, TO], BF16)
            WB = sb("WB", [128, 8, 2064], BF16)
            MK = sb("MK", [128, 16, 512], BF16)
            tabf = sb("tabf", [128, 260], F32)
            gat = sb("gat", [128, D], F32)
            qg = sb("qg", [128, 512], F32)
            brep = sb("brep", [128, 256], F32)
            gn = sb("gn", [128, 256], F32)
            wupb = sb("wupb", [16, 256], BF16)
            maskA = sb("maskA", [128, 128], BF16)
            xbs = [sb("xb%d" % i, [128, D], F32) for i in range(3)]
            ssx = sb("ssx", [128, 4], F32)
            rstd_all = sb("rstd_all", [128, NT], F32)
            nbs = [sb("nb%d" % i, [128, D], BF16) for i in range(2)]
            nTs = [sb("nT%d" % i, [128, D], BF16) for i in range(2)]
            kf = sb("kf", [128, 512], F32)
            sq = sb("sq", [128, 512], F32)
            k16s = [sb("k16_%d" % i, [128, 512], BF16) for i in range(2)]
            gkf = [sb("gkf%d" % i, [128, 256], F32) for i in range(2)]
            glr16s = [sb("glr16_%d" % i, [128, 16], BF16) for i in range(2)]
            glrT = sb("glrT", [16, 128], BF16)
            zb = sb("zb", [128, 256], F32)
            lg = sb("lg", [128, 256], F32)
            eb = sb("eb", [128, 256], F32)
            enb = sb("enb", [128, 256], F32)
            ec = sb("ec", [128, 256], F32)
            dec = sb("dec", [128, 4], F32)
            qt16 = sb("qt16", [128, 256], BF16)
            kt16 = sb("kt16", [128, 256], BF16)
            kh16 = sb("kh16", [128, 256], BF16)
            vb16s = [sb("vb16_%d" % i, [128, 512], BF16) for i in range(2)]
            qkT = sb("qkT", [128, 512], BF16)
            AT = sb("AT", [128, 512], BF16)
            S = sb("S", [128, 256], F32)
            Sb = [sb("Sb%d" % i, [128, 256], BF16) for i in range(2)]
            snap = sb("snap", [128, NU, 256], F32)
            sgt = sb("sgt", [128, 512], F32)
            gg = sb("gg", [128, 512], F32)
            ogl = sb("ogl", [128, 512], BF16)
            PTs = [sb("PT%d" % i, [128, 512], BF16) for i in range(6)]
            accS = sb("accS", [128, 3, 390], F32)
            t1 = sb("t1", [128, 128], F32)
            of = sb("of", [128, 128], F32)
            og = sb("og", [128, 4, 128], BF16)

            T.dma(SP, tabf[:], tabs[:, :], [], ["tabf"], "c1")
            T.dma(SP, gat[:], g_attn[:, :], [], ["gat"], "c2")
            T.dma(SP, kf[:], qk_rep[:, 0:512], [], ["kf"], "c3")
            T.dma(SP, sq[:], qk_rep[:, 512:1024], [], ["sq"], "c3b")
            T.dma(SP, brep[:], b_rep[:, :], [], ["brep"], "c4")
            T.dma(SP, gn[:], gn_rep[:, :], [], ["gn"], "c5")
            lam = zb
            T.dma(SP, lam[:], lamv[:, :], [], ["zb"], "c6")
            T.dma(POOL, wupb[:], w_up[:, :], [], ["wupb"], "c7")
            T.dma(POOL, maskA[:], cst[:, 384:512], [], ["maskA"], "c8")
            T.dma(POOL, WB[:, :, 0:1296], w_p0.rearrange("(c p) n -> p c n", p=128), [], ["WB"], "w0")
            for e4 in range(4):
                T.dma(POOL, MK[:, e4 * 4:(e4 + 1) * 4, :],
                      msk[:, e4 * 2048:(e4 + 1) * 2048].rearrange("p (a b) -> p a b", b=512),
                      [], ["MK"], "c9")
            T.op(DVE, lambda: V.scalar_tensor_tensor(out=qg[:, 0:512], in0=kf[:, 0:512], scalar=0.125,
                                                     in1=sq[:, 0:512], op0=ALU.mult, op1=ALU.mult),
                 ["kf", "sq"], ["qg"])
            T.op(DVE, lambda: V.tensor_scalar(out=gn[:, 0:128], in0=gn[:, 0:128], scalar1=1.0 - LAMBDA_INIT,
                                              scalar2=None, op0=ALU.mult), ["gn"], ["gn"])
            T.op(DVE, lambda: V.scalar_tensor_tensor(out=junk[:, 0:64], in0=lam[:, 0:64], scalar=1.0,
                                                     in1=lam[:, 64:128], op0=ALU.mult, op1=ALU.mult,
                                                     accum_out=st[:, 4:5]), ["zb"], ["junk", "stl"])
            T.op(DVE, lambda: V.scalar_tensor_tensor(out=junk[:, 0:64], in0=lam[:, 128:192], scalar=1.0,
                                                     in1=lam[:, 192:256], op0=ALU.mult, op1=ALU.mult,
                                                     accum_out=st[:, 5:6]), ["zb", "stl"], ["junk", "stl"])
            T.op(ACT, lambda: A.activation(out=st[:, 6:8], in_=st[:, 4:6], func=AF.Exp), ["stl"], ["stl"])
            T.op(DVE, lambda: V.tensor_tensor(out=st[:, 8:9], in0=st[:, 7:8], in1=st[:, 6:7], op=ALU.subtract),
                 ["stl"], ["stl"])
            T.op(DVE, lambda: V.tensor_scalar(out=st[:, 8:9], in0=st[:, 8:9], scalar1=-LAMBDA_INIT,
                                              scalar2=None, op0=ALU.add), ["stl"], ["stl"])
            neglam = st[:, 8:9]
            T.op(POOL, lambda: G.memset(VE[:, :, :, 128:130], 1.0), [], ["VE"])
            T.op(DVE, lambda: V.memset(S[:], 0.0), [], ["S"])

            def k_evac(bank, c0, ngrp, ks, gain=None):
                n = 64 * ngrp
                bk = "pb%d" % bank
                kk = "k16_%d" % ks
                T.op(ACT, lambda: A.copy(out=kf[:, 0:n], in_=pb[bank][:, c0:c0 + n]), [bk], ["kf"])
                T.op(DVE, lambda: V.tensor_tensor(out=sq[:, 0:n], in0=kf[:, 0:n], in1=kf[:, 0:n], op=ALU.mult),
                     ["kf"], ["sq"])
                T.op(DVE, lambda: V.tensor_reduce(out=st[:, 16:16 + ngrp],
                                                  in_=sq[:, 0:n].rearrange("p (a b) -> p a b", b=64),
                                                  axis=AX.X, op=ALU.add), ["sq"], ["stk"])
                rstd_from_ss(st[:, 16:16 + ngrp], st[:, 24:24 + ngrp], 64.0, ["stk"])
                bc = bass.AP(st[:].tensor, st[:, 24:25].offset, [list(st[:].ap[0]), [1, ngrp], [0, 64]])
                if gain is None:
                    T.op(DVE, lambda: V.tensor_tensor(out=k16s[ks][:, 0:n].rearrange("p (a b) -> p a b", b=64),
                                                      in0=kf[:, 0:n].rearrange("p (a b) -> p a b", b=64),
                                                      in1=bc, op=ALU.mult), ["kf", "stk"], [kk])
                else:
                    T.op(DVE, lambda: V.tensor_tensor(out=sq[:, 0:n].rearrange("p (a b) -> p a b", b=64),
                                                      in0=kf[:, 0:n].rearrange("p (a b) -> p a b", b=64),
                                                      in1=bc, op=ALU.mult), ["kf", "stk", "sq"], ["sq"])
                    T.op(DVE, lambda: V.tensor_tensor(out=k16s[ks][:, 0:n], in0=sq[:, 0:n], in1=gain,
                                                      op=ALU.mult), ["sq", "qg"], [kk])

            def k_tr(ks, n, dst_fn):
                nh = n // 128
                for hh in range(nh):
                    T.op(PE, lambda hh=hh: PEe.transpose(out=pbb[4][:, hh * 128:(hh + 1) * 128],
                                                         in_=k16s[ks][:, hh * 128:(hh + 1) * 128], identity=identb[:]),
                         ["k16_%d" % ks, "identb"], ["pb4"], inc=(hh == nh - 1))
                dst_fn(pbb[4][:, 0:n].rearrange("p (a b) -> p a b", b=128))

            def gla_segs(own, i_own, gk, gkk, gq, gqk, vb, vbk, g16, g16k, psS):
                def s_a():
                    T.op(PE, lambda: PEe.transpose(out=pbb[6][0:16, 64:192], in_=g16[:], identity=identb[:]),
                         [g16k, "identb"], ["pb6"])
                    T.op(ACT, lambda: A.copy(out=glrT[:], in_=pbb[6][0:16, 64:192]), ["pb6"], ["glrT"])

                def s_b():
                    T.op(PE, lambda: PEe.matmul(pb[6][:, 128:384], lhsT=glrT[:], rhs=wupb[:], start=True, stop=True),
                         ["glrT", "wupb"], ["pb6"])
                    T.op(DVE, lambda: V.tensor_tensor(out=zb[:], in0=pb[6][:, 128:384], in1=brep[:], op=ALU.add),
                         ["pb6", "brep"], ["zb"])
                    T.op(ACT, lambda: A.activation(out=lg[:], in_=zb[:], func=AF.Exp, scale=-1.0), ["zb"], ["lg"])
                    T.op(ACT, lambda: A.activation(out=lg[:], in_=lg[:], func=AF.Ln, bias=1.0), ["lg"], ["lg"])

                def s_c():
                    if own:
                        T.op(PE, lambda: PEe.matmul(pb[7][:, 0:256], lhsT=triS, rhs=lg[:], start=True, stop=True),
                             ["cstf", "lg"], ["pb7"], inc=False)
                    T.op(PE, lambda: PEe.matmul(pb[7][:, 256:512], lhsT=upS, rhs=lg[:], start=True, stop=True),
                         ["cstf", "lg"], ["pb7"])
                    for p in range(2):
                        T.op(PE, lambda p=p: PEe.matmul(pb[6][:, 96 + 2 * p:98 + 2 * p], lhsT=lg[:, p * 128:(p + 1) * 128],
                                                        rhs=chI, start=True, stop=True),
                             ["lg", "cstf"], ["pb6"], inc=(p == 1))
                    T.op(ACT, lambda: A.activation(out=ec[:], in_=pb[7][:, 256:512], func=AF.Exp), ["pb7"], ["ec"])
                    if own:
                        T.op(ACT, lambda: A.activation(out=eb[:], in_=pb[7][:, 0:256], func=AF.Exp), ["pb7"], ["eb"])
                        T.op(ACT, lambda: A.activation(out=enb[:], in_=pb[7][:, 0:256], func=AF.Exp, scale=-1.0),
                             ["pb7"], ["enb"])
                    T.op(ACT, lambda: A.activation(out=dec[:], in_=pb[6][:, 96:100], func=AF.Exp), ["pb6"], ["dec"])
                    T.op(DVE, lambda: V.tensor_tensor(out=kh16[:], in0=gk, in1=ec[:], op=ALU.mult),
                         [gkk, "ec"], ["kh16"])
                    if own:
                        T.op(DVE, lambda: V.scalar_tensor_tensor(out=qt16[:], in0=gq, scalar=0.125,
                                                                 in1=eb[:], op0=ALU.mult, op1=ALU.mult),
                             [gqk, "eb"], ["qt16"])
                        T.op(DVE, lambda: V.tensor_tensor(out=kt16[:], in0=gk, in1=enb[:], op=ALU.mult),
                             [gkk, "enb"], ["kt16"])
                        for p in range(2):
                            T.op(PE, lambda p=p: PEe.transpose(out=pbb[4][:, 512 + p * 128:512 + (p + 1) * 128],
                                                               in_=qt16[:, p * 128:(p + 1) * 128], identity=identb[:]),
                                 ["qt16", "identb"], ["pb4"], inc=False)
                        for p in range(2):
                            T.op(PE, lambda p=p: PEe.transpose(out=pbb[4][:, 768 + p * 128:768 + (p + 1) * 128],
                                                               in_=kt16[:, p * 128:(p + 1) * 128], identity=identb[:]),
                                 ["kt16", "identb"], ["pb4"], inc=(p == 1))
                        T.op(ACT, lambda: A.copy(out=qkT[:], in_=pbb[4][:, 512:1024]), ["pb4"], ["qkT"])

                def s_c2():
                    for hh in range(2):
                        bank = 0 if hh == 0 else 3
                        for p in range(2):
                            T.op(PE, lambda hh=hh, p=p, bank=bank:
                                 PEe.matmul(pb[bank][:, p * 128:(p + 1) * 128],
                                            lhsT=qkT[hh * 64:(hh + 1) * 64, 256 + p * 128:256 + (p + 1) * 128],
                                            rhs=qkT[hh * 64:(hh + 1) * 64, p * 128:(p + 1) * 128],
                                            start=True, stop=True),
                                 ["qkT"], ["pb%d" % bank], inc=(p == 1))
                    mA = bass.AP(maskA[:].tensor, maskA[:].offset, [list(maskA[:].ap[0]), [0, 2], [1, 128]])
                    for hh in range(2):
                        bank = 0 if hh == 0 else 3
                        outv = bass.AP(AT[:].tensor, AT[:, hh * 128:hh * 128 + 1].offset,
                                       [list(AT[:].ap[0]), [256, 2], [1, 128]])
                        T.op(DVE, lambda bank=bank, outv=outv: V.tensor_tensor(
                            out=outv, in0=pb[bank][:, 0:256].rearrange("p (a b) -> p a b", b=128), in1=mA, op=ALU.mult),
                            ["pb%d" % bank, "maskA"], ["AT"])

                def s_d(ch):
                    def f():
                        if own:
                            T.op(POOL, lambda: G.tensor_copy(out=Sb[ch][:], in_=S[:]), ["S"], ["Sb%d" % ch])
                        sbank, scol = psS[ch]
                        sk = "pb%d" % sbank
                        for p in range(2):
                            for hh in range(2):
                                h = 2 * p + hh
                                T.op(PE, lambda p=p, hh=hh, h=h:
                                     PEe.matmul(pb[sbank][hh * 64:(hh + 1) * 64, scol + p * 128:scol + (p + 1) * 128],
                                                lhsT=kh16[ch * 64:(ch + 1) * 64, h * 64:(h + 1) * 64],
                                                rhs=vb[ch * 64:(ch + 1) * 64, h * 128:(h + 1) * 128],
                                                start=True, stop=True),
                                     ["kh16", vbk], [sk], inc=(p == 1 and hh == 1))
                        for p in range(2):
                            T.op(DVE, lambda p=p:
                                 V.scalar_tensor_tensor(out=S[:, p * 128:(p + 1) * 128], in0=S[:, p * 128:(p + 1) * 128],
                                                        scalar=dec[:, 2 * p + ch:2 * p + ch + 1],
                                                        in1=pb[sbank][:, scol + p * 128:scol + (p + 1) * 128],
                                                        op0=ALU.mult, op1=ALU.add),
                                 ["S", "dec", sk], ["S"])
                    return f

                def s_o():
                    for hh in range(2):
                        bank = 0 if hh == 0 else 3
                        bk = "pb%d" % bank
                        for p in range(2):
                            h = 2 * p + hh
                            oc = 256 + p * 128
                            T.op(PE, lambda h=h, bank=bank, oc=oc:
                                 PEe.matmul(pb[bank][:, oc:oc + 128], lhsT=AT[:, h * 128:(h + 1) * 128],
                                            rhs=vb[:, h * 128:(h + 1) * 128], start=True, stop=False),
                                 ["AT", vbk], [bk], inc=False)
                            for ch in range(2):
                                T.op(PE, lambda h=h, hh=hh, p=p, ch=ch, bank=bank, oc=oc:
                                     PEe.matmul(pb[bank][ch * 64:(ch + 1) * 64, oc:oc + 128],
                                                lhsT=qkT[hh * 64:(hh + 1) * 64, p * 128 + ch * 64:p * 128 + (ch + 1) * 64],
                                                rhs=Sb[ch][hh * 64:(hh + 1) * 64, p * 128:(p + 1) * 128],
                                                start=False, stop=True),
                                     ["qkT", "Sb%d" % ch], [bk], inc=(ch == 1 and p == 1))
                    for hh in range(2):
                        bank = 0 if hh == 0 else 3
                        bk = "pb%d" % bank
                        T.op(ACT, lambda bank=bank: A.copy(out=kf[:, 0:256], in_=pb[bank][:, 256:512]), [bk], ["kf"])
                        T.op(DVE, lambda: V.tensor_tensor(out=sq[:, 0:256], in0=kf[:, 0:256], in1=kf[:, 0:256],
                                                          op=ALU.mult), ["kf"], ["sq"])
                        T.op(DVE, lambda: V.tensor_reduce(out=st[:, 16:18],
                                                          in_=sq[:, 0:256].rearrange("p (a b) -> p a b", b=128),
                                                          axis=AX.X, op=ALU.add), ["sq"], ["stk"])
                        rstd_from_ss(st[:, 16:18], st[:, 24:26], 128.0, ["stk"])
                        bc = bass.AP(st[:].tensor, st[:, 24:25].offset, [list(st[:].ap[0]), [1, 2], [0, 128]])
                        ggv = bass.AP(gg[:].tensor, gg[:, hh * 128:hh * 128 + 1].offset,
                                      [list(gg[:].ap[0]), [256, 2], [1, 128]])
                        oglv = bass.AP(ogl[:].tensor, ogl[:, hh * 128:hh * 128 + 1].offset,
                                       [list(ogl[:].ap[0]), [256, 2], [1, 128]])
                        T.op(DVE, lambda bc=bc: V.tensor_tensor(out=sq[:, 256:512].rearrange("p (a b) -> p a b", b=128),
                                                                in0=kf[:, 0:256].rearrange("p (a b) -> p a b", b=128),
                                                                in1=bc, op=ALU.mult), ["kf", "stk"], ["sq2"])
                        T.op(DVE, lambda ggv=ggv, oglv=oglv: V.tensor_tensor(
                            out=oglv, in0=sq[:, 256:512].rearrange("p (a b) -> p a b", b=128), in1=ggv, op=ALU.mult),
                            ["sq2", "gg"], ["ogl"])
                    T.dma(SP, mix_scr[i_own * 128:(i_own + 1) * 128, 512:1024], ogl[:], ["ogl"], ["mixg%d" % i_own], "mx")

                if own:
                    return [s_a, s_b, s_c, s_c2, s_d(0), s_d(1), s_o]
                return [s_a, s_b, s_c, s_d(0), s_d(1)]

            def seg_load(t):
                sl = t % 3
                T.dma(SP, xbs[sl][:], x_all[t * 128:(t + 1) * 128, :], [], ["xb%d" % sl], "x%d" % sl)

            def seg_stats(t):
                sl = t % 3
                T.op(DVE, lambda: V.scalar_tensor_tensor(out=junk[:], in0=xbs[sl][:], scalar=1.0, in1=xbs[sl][:],
                                                         op0=ALU.mult, op1=ALU.mult, accum_out=ssx[:, sl:sl + 1]),
                     ["xb%d" % sl], ["junk", "ssx%d" % sl])
                T.op(ACT, lambda: A.activation(out=rstd_all[:, t:t + 1], in_=ssx[:, sl:sl + 1], func=AF.Ln,
                                               scale=1.0 / D, bias=EPS), ["ssx%d" % sl], ["rs%d" % t])
                T.op(ACT, lambda: A.activation(out=rstd_all[:, t:t + 1], in_=rstd_all[:, t:t + 1], func=AF.Exp, scale=-0.5),
                     ["rs%d" % t], ["rs%d" % t])

            def seg_n(t):
                sl = t % 3
                ns = t % 2
                T.op(DVE, lambda: V.scalar_tensor_tensor(out=nbs[ns][:], in0=xbs[sl][:], scalar=rstd_all[:, t:t + 1],
                                                         in1=gat[:], op0=ALU.mult, op1=ALU.mult),
                     ["xb%d" % sl, "rs%d" % t, "gat"], ["nb%d" % ns])

            def seg_tr(t):
                ns = t % 2
                for c in range(8):
                    T.op(PE, lambda c=c: PEe.transpose(out=pbb[3][:, c * 128:(c + 1) * 128],
                                                       in_=nbs[ns][:, c * 128:(c + 1) * 128], identity=identb[:]),
                         ["nb%d" % ns, "identb"], ["pb3"], inc=(c == 7))
                T.op(ACT, lambda: A.copy(out=nTs[ns][:], in_=pbb[3][:, 0:1024]), ["pb3"], ["nT%d" % ns])

            def run_pipe(n_tiles, early_fn, main_fn, late_fn):
                seg_load(0)
                for it in range(-1, n_tiles + 1):
                    early = early_fn(it + 1) if 0 <= it + 1 < n_tiles else []
                    main = main_fn(it) if 0 <= it < n_tiles else []
                    late = late_fn(it - 1) if 0 <= it - 1 < n_tiles else []
                    for k in range(max(len(early), len(main), len(late))):
                        if k < len(main):
                            main[k]()
                        if k < len(late):
                            late[k]()
                        if k < len(early):
                            early[k]()

            def kv_store(t):
                def dst(src3):
                    T.op(ACT, lambda: A.copy(out=KT[:, :, t * 128:(t + 1) * 128], in_=src3), ["pb4"], ["KT"])
                return dst

            def v_store(t):
                T.op(DVE, lambda: V.tensor_copy(out=VE[:, t, :, 0:128],
                                                in_=pb[0][:, 256:512].rearrange("p (a b) -> p a b", b=128)),
                     ["pb0"], ["VE"])

            def p0_early(t):
                segs = []
                if t + 1 < NT:
                    segs.append(lambda: seg_load(t + 1))
                segs += [lambda: seg_stats(t), lambda: seg_n(t), lambda: seg_tr(t)]
                return segs

            def p0_main(t):
                ns = t % 2

                def m0():
                    project(nTs[ns], "nT%d" % ns, WB, "WB", [(0, 0, 0, 512)])
                    k_evac(0, 0, 4, ns)
                    v_store(t)

                def m1():
                    project(nTs[ns], "nT%d" % ns, WB, "WB", [(1, 0, 512, 272)])
                    T.op(DVE, lambda: V.tensor_copy(out=gkf[ns][:], in_=pb[1][:, 0:256]), ["pb1"], ["gkf%d" % ns])
                    T.op(DVE, lambda: V.tensor_copy(out=glr16s[ns][:], in_=pb[1][:, 256:272]), ["pb1"], ["glr16_%d" % ns])

                def m2():
                    project(nTs[ns], "nT%d" % ns, WB, "WB", [(2, 0, 784, 512)])
                    T.op(ACT, lambda: A.copy(out=vb16s[ns][:], in_=pb[2][:, :]), ["pb2"], ["vb16_%d" % ns])
                return [m0, m1, m2]

            def p0_late(t):
                ns = t % 2
                segs = gla_segs(False, None, gkf[ns][:], "gkf%d" % ns, None, None, vb16s[ns], "vb16_%d" % ns,
                                glr16s[ns], "glr16_%d" % ns, ((5, 0), (5, 256)))

                def l0():
                    k_tr(ns, 256, kv_store(t))
                    if t % 16 % 4 == 0:
                        u = t // 16
                        r = (t % 16) // 4
                        if r == 0:
                            T.op(DVE, lambda: V.tensor_scalar(out=snap[:, u, :], in0=S[:],
                                                              scalar1=tabf[:, 256 + r:257 + r], scalar2=None,
                                                              op0=ALU.mult), ["S", "tabf"], ["snap"])
                        else:
                            T.op(DVE, lambda: V.scalar_tensor_tensor(out=snap[:, u, :], in0=S[:],
                                                                     scalar=tabf[:, 256 + r:257 + r],
                                                                     in1=snap[:, u, :], op0=ALU.mult, op1=ALU.add),
                                 ["S", "tabf", "snap"], ["snap"])
                    segs[0]()
                return [l0] + segs[1:]

            run_pipe(NT, p0_early, p0_main, p0_late)

            T.dma(POOL, WB[:, :, :], w_ow.rearrange("(c p) n -> p c n", p=128), [], ["WB"], "w0")

            def q_store(i):
                def dst(src3):
                    T.op(ACT, lambda: A.copy(out=QT[:, :, i * 128:(i + 1) * 128], in_=src3), ["pb4"], ["QT"])
                return dst

            gnb = bass.AP(gn[:].tensor, gn[:, 128:129].offset, [list(gn[:].ap[0]), [0, 4], [1, 128]])
            for i in range(NO):
                s = i % 2
                if i % 4 == 0:
                    T.op(DVE, lambda i=i: V.tensor_copy(out=S[:], in_=snap[:, i // 4, :]), ["snap"], ["S"])
                norm_transpose(x_own[i * 128:(i + 1) * 128, :], xbs[s], "xb%d" % s, nbs[s], "nb%d" % s,
                               nTs[s], "nT%d" % s, gat, "gat", SP, "x%d" % s)
                project(nTs[s], "nT%d" % s, WB, "WB",
                        [(0, 0, 0, 512), (1, 0, 512, 512), (2, 0, 1024, 512), (3, 0, 1536, 512), (6, 0, 2048, 16)])
                k_evac(0, 0, 8, 0, gain=qg[:, 0:512])
                k_tr(0, 512, q_store(i))
                T.op(ACT, lambda: A.activation(out=sgt[:], in_=pb[3][:, :], func=AF.Exp, scale=-1.0), ["pb3"], ["sgt"])
                T.op(DVE, lambda: V.tensor_scalar(out=sgt[:], in0=sgt[:], scalar1=1.0, scalar2=None, op0=ALU.add),
                     ["sgt"], ["sgt"])
                T.op(DVE, lambda: V.reciprocal(out=sgt[:], in_=sgt[:]), ["sgt"], ["sgt"])
                T.op(DVE, lambda: V.tensor_tensor(out=gg[:], in0=pb[3][:, :], in1=sgt[:], op=ALU.mult),
                     ["pb3", "sgt"], ["gg"])
                T.op(DVE, lambda: V.tensor_tensor(out=gg[:].rearrange("p (a b) -> p a b", b=128),
                                                  in0=gg[:].rearrange("p (a b) -> p a b", b=128), in1=gnb, op=ALU.mult),
                     ["gg", "gn"], ["gg"])
                T.op(DVE, lambda: V.tensor_copy(out=glr16s[0][:], in_=pb[6][:, 0:16]), ["pb6"], ["glr16_0"])
                T.op(ACT, lambda: A.copy(out=vb16s[0][:], in_=pb[2][:, :]), ["pb2"], ["vb16_0"])
                for sg in gla_segs(True, i, pb[1][:, 256:512], "pb1", pb[1][:, 0:256], "pb1", vb16s[0], "vb16_0",
                                   glr16s[0], "glr16_0", ((2, 0), (5, 0))):
                    sg()

            def acc_ap(a):
                bank = 5 + a // 3
                col = (a % 3) * 130
                return bank, col

            def attention(hp):
                steps = [(u, hl, kb) for u in range(NU) for hl in range(2) for kb in range(16 * u + 16)]
                LOOK = 2

                def emit_qk(idx):
                    u, hl, kb = steps[idx]
                    h = 2 * hp + hl
                    e = kb - 16 * u
                    ei = e + 48
                    for m in range(2):
                        sbank = 1 + (2 * idx + m) % 4
                        T.op(PE, lambda m=m, sbank=sbank:
                             PEe.matmul(pb[sbank][:, 0:512],
                                        lhsT=KT[m * 64:(m + 1) * 64, hl, kb * 128:(kb + 1) * 128],
                                        rhs=QT[m * 64:(m + 1) * 64, h, u * 512:(u + 1) * 512],
                                        start=True, stop=True),
                             ["KT", "QT"], ["pb%d" % sbank])
                    for m in range(2):
                        sbank = 1 + (2 * idx + m) % 4
                        slot = (2 * idx + m) % 6
                        T.op(ACT, lambda sbank=sbank, slot=slot:
                             A.activation(out=PTs[slot][:], in_=pb[sbank][:, 0:512], func=AF.Exp,
                                          bias=tabf[:, h * 64 + ei:h * 64 + ei + 1]),
                             ["pb%d" % sbank, "tabf"], ["PT%d" % slot])
                        if e >= 0:
                            T.op(POOL, lambda slot=slot:
                                 G.tensor_tensor(out=PTs[slot][:], in0=PTs[slot][:], in1=MK[:, e, :], op=ALU.mult),
                                 ["PT%d" % slot, "MK"], ["PT%d" % slot])

                def emit_pv(idx):
                    u, hl, kb = steps[idx]
                    h = 2 * hp + hl
                    nkb = 16 * u + 16
                    for m in range(2):
                        slot = (2 * idx + m) % 6
                        for c in range(4):
                            bank, col = acc_ap(c * 2 + m)
                            first = (kb == 0 and (c * 2 + m) in (0, 4, 6))
                            T.op(PE, lambda c=c, bank=bank, col=col, slot=slot, first=first:
                                 PEe.matmul(pb[bank][:, col:col + 130],
                                            lhsT=PTs[slot][:, c * 128:(c + 1) * 128],
                                            rhs=VE[:, kb, hl, :], start=first, stop=(kb == nkb - 1),
                                            skip_group_check=True),
                                 ["PT%d" % slot, "VE"], ["pb%d" % bank], inc=(c == 3 and m == 1))
                    if kb != nkb - 1:
                        return
                    for bi in range(3):
                        T.op(DVE, lambda bi=bi: V.tensor_copy(out=accS[:, bi, :], in_=pb[5 + bi][:, 0:390]),
                             ["pb%d" % (5 + bi)], ["accS"])
                    for c in range(4):
                        a0 = c * 2
                        a1 = c * 2 + 1
                        A0 = accS[:, a0 // 3, (a0 % 3) * 130:(a0 % 3) * 130 + 130]
                        A1 = accS[:, a1 // 3, (a1 % 3) * 130:(a1 % 3) * 130 + 130]
                        T.op(DVE, lambda A0=A0: V.reciprocal(out=st[:, 10:11], in_=A0[:, 128:129]), ["accS"], ["sta"])
                        T.op(DVE, lambda A1=A1: V.reciprocal(out=st[:, 11:12], in_=A1[:, 128:129]), ["accS", "sta"], ["sta"])
                        T.op(DVE, lambda: V.tensor_tensor(out=st[:, 11:12], in0=st[:, 11:12], in1=neglam, op=ALU.mult),
                             ["sta", "stl"], ["sta"])
                        T.op(DVE, lambda A1=A1: V.tensor_scalar(out=t1[:], in0=A1[:, 0:128], scalar1=st[:, 11:12],
                                                                scalar2=None, op0=ALU.mult), ["accS", "sta"], ["t1"])
                        T.op(DVE, lambda A0=A0: V.scalar_tensor_tensor(out=of[:], in0=A0[:, 0:128], scalar=st[:, 10:11],
                                                                       in1=t1[:], op0=ALU.mult, op1=ALU.add),
                             ["accS", "sta", "t1"], ["of"])
                        T.op(DVE, lambda: V.scalar_tensor_tensor(out=junk[:, 0:128], in0=of[:], scalar=1.0, in1=of[:],
                                                                 op0=ALU.mult, op1=ALU.mult, accum_out=st[:, 12:13]),
                             ["of"], ["junk", "stb"])
                        rstd_from_ss(st[:, 12:13], st[:, 13:14], 128.0, ["stb"])
                        T.op(DVE, lambda c=c: V.scalar_tensor_tensor(out=og[:, c, :], in0=of[:], scalar=st[:, 13:14],
                                                                     in1=gn[:, 0:128], op0=ALU.mult, op1=ALU.mult),
                             ["of", "stb", "gn"], ["og"])
                    T.dma(SP, mix_scr[u * 512:(u + 1) * 512, h * 128:(h + 1) * 128].rearrange("(c p) n -> p c n", p=128),
                          og[:], ["og"], ["mixd%d_%d" % (u, h)], "mx2")

                for i in range(len(steps) + LOOK):
                    if i < len(steps):
                        emit_qk(i)
                    if i - LOOK >= 0:
                        emit_pv(i - LOOK)

            attention(0)

            T.dma(POOL, WB[:, :, 0:512], w_p1.rearrange("(c p) n -> p c n", p=128), [], ["WB"], "w0")

            def p1_early(t):
                segs = []
                if t + 1 < NT:
                    segs.append(lambda: seg_load(t + 1))
                segs += [lambda: seg_n(t), lambda: seg_tr(t)]
                return segs

            def p1_main(t):
                ns = t % 2

                def m0():
                    project(nTs[ns], "nT%d" % ns, WB, "WB", [(0, 0, 0, 512)])
                    k_evac(0, 0, 4, ns)
                    v_store(t)
                return [m0]

            def p1_late(t):
                return [lambda: k_tr(t % 2, 256, kv_store(t))]

            run_pipe(NT, p1_early, p1_main, p1_late)
            attention(1)
            mix_keys = ["mixg%d" % i for i in range(NO)] + ["mixd%d_%d" % (u, h) for u in range(NU) for h in range(4)]

        with contextlib.ExitStack() as fst:
            def sb(n, s, d):
                return fst.enter_context(nc.sbuf_tensor(n, s, d))

            WG = sb("WG", [128, 8, DFF], BF16)
            WU = sb("WU", [128, 8, DFF], BF16)
            WD = sb("WD", [128, NFF, D], BF16)
            gff = sb("gff", [128, D], F32)
            hbs = [sb("hb%d" % i, [128, D], F32) for i in range(2)]
            nbs = [sb("fnb%d" % i, [128, D], BF16) for i in range(2)]
            obs = [sb("ob%d" % i, [128, D], F32) for i in range(2)]
            T.dma(SP, gff[:], g_ffn[:, :], [], ["gff"], "c2")
            with contextlib.ExitStack() as ost:
                def sbo(n, s, d):
                    return ost.enter_context(nc.sbuf_tensor(n, s, d))
                WO = sbo("WO", [128, 8, D], BF16)
                mxs = [sbo("mx%d" % i, [128, D], BF16) for i in range(2)]
                mxT = [sbo("mxT%d" % i, [128, D], BF16) for i in range(2)]
                xos = [sbo("xo%d" % i, [128, D], F32) for i in range(2)]
                T.dma(POOL, WO[:], w_o.rearrange("(c p) n -> p c n", p=128), [], ["WO"], "w1")
                T.dma(POOL, WG[:], w_g.rearrange("(c p) n -> p c n", p=128), [], ["WG"], "w2")
                T.dma(POOL, WU[:], w_u.rearrange("(c p) n -> p c n", p=128), [], ["WU"], "w3")
                T.dma(POOL, WD[:], w_d.rearrange("(c p) n -> p c n", p=128), [], ["WD"], "w4")
                for i in range(NO):
                    s = i % 2
                    T.dma(SP, mxs[s][:], mix_scr[i * 128:(i + 1) * 128, :], mix_keys if i < 2 else [], ["mx%d" % s], "m%d" % s)
                    T.dma(SP, xos[s][:], x_own[i * 128:(i + 1) * 128, :], [], ["xo%d" % s], "xo%d" % s)
                    for c in range(8):
                        T.op(PE, lambda c=c, s=s: PEe.transpose(out=pbb[0][:, c * 128:(c + 1) * 128],
                                                                in_=mxs[s][:, c * 128:(c + 1) * 128], identity=identb[:]),
                             ["mx%d" % s, "identb"], ["pb0"], inc=(c == 7))
                    T.op(ACT, lambda s=s: A.copy(out=mxT[s][:], in_=pbb[0][:, 0:1024]), ["pb0"], ["mxT%d" % s])
                    for nb_ in range(2):
                        bank = 1 + nb_
                        for c in range(8):
                            T.op(PE, lambda c=c, s=s, nb_=nb_, bank=bank:
                                 PEe.matmul(pb[bank][:, 0:512], lhsT=mxT[s][:, c * 128:(c + 1) * 128],
                                            rhs=WO[:, c, nb_ * 512:(nb_ + 1) * 512], start=(c == 0), stop=(c == 7)),
                                 ["mxT%d" % s, "WO"], ["pb%d" % bank], inc=(c == 7))
                        T.op(DVE, lambda s=s, nb_=nb_, bank=bank:
                             V.tensor_tensor(out=hbs[s][:, nb_ * 512:(nb_ + 1) * 512], in0=pb[bank][:, 0:512],
                                             in1=xos[s][:, nb_ * 512:(nb_ + 1) * 512], op=ALU.add),
                             ["pb%d" % bank, "xo%d" % s], ["hb%d" % s])
                    T.dma(SP, h_scr[i * 128:(i + 1) * 128, :], hbs[s][:], ["hb%d" % s], ["hscr%d" % i], "hs%d" % s)
                    if dbg:
                        T.dma(SP, dbg_outs["d_h"][i * 128:(i + 1) * 128, :], hbs[s][:], ["hb%d" % s], [], "dbg")
                        T.op(DVE, lambda s=s: V.tensor_copy(out=xos[s][:], in_=mxs[s][:]), ["mx%d" % s, "xo%d" % s], ["xo%d" % s])
                        T.dma(SP, dbg_outs["d_mix"][i * 128:(i + 1) * 128, :], xos[s][:], ["xo%d" % s], [], "dbg")

            mT = sb("mT", [128, 8, 512], BF16)
            actT = sb("actT", [128, NFF, 512], BF16)
            ee = sb("ee", [128, 512], F32)
            uS = sb("uS", [128, 512], F32)
            rr = sb("rr", [128, 512], F32)
            for gI in range(NU):
                for tt in range(4):
                    i = gI * 4 + tt
                    s = i % 2
                    T.dma(SP, hbs[s][:], h_scr[i * 128:(i + 1) * 128, :], ["hscr%d" % i], ["hb%d" % s], "hl%d" % s)
                    T.op(DVE, lambda s=s: V.scalar_tensor_tensor(out=junk[:], in0=hbs[s][:], scalar=1.0, in1=hbs[s][:],
                                                                 op0=ALU.mult, op1=ALU.mult, accum_out=st[:, 0:1]),
                         ["hb%d" % s], ["junk", "st0"])
                    rstd_from_ss(st[:, 0:1], st[:, 1:2], float(D), ["st0"])
                    T.op(DVE, lambda s=s: V.scalar_tensor_tensor(out=nbs[s][:], in0=hbs[s][:], scalar=st[:, 1:2],
                                                                 in1=gff[:], op0=ALU.mult, op1=ALU.mult),
                         ["hb%d" % s, "st0", "gff"], ["fnb%d" % s])
                    for c in range(8):
                        T.op(PE, lambda c=c, s=s: PEe.transpose(out=pbb[0][:, c * 128:(c + 1) * 128],
                                                                in_=nbs[s][:, c * 128:(c + 1) * 128], identity=identb[:]),
                             ["fnb%d" % s, "identb"], ["pb0"], inc=(c == 7))
                    T.op(ACT, lambda tt=tt: A.copy(out=mT[:, :, tt * 128:(tt + 1) * 128],
                                                   in_=pbb[0][:, 0:1024].rearrange("p (a b) -> p a b", b=128)),
                         ["pb0"], ["mT"])
                for f in range(NFF):
                    gb = 1 + (f % 2)
                    ub = 3 + (f % 2)
                    for c in range(8):
                        T.op(PE, lambda c=c, f=f, gb=gb: PEe.matmul(pb[gb][:, 0:512], lhsT=WG[:, c, f * 128:(f + 1) * 128],
                                                                    rhs=mT[:, c, :], start=(c == 0), stop=(c == 7)),
                             ["WG", "mT"], ["pb%d" % gb], inc=(c == 7))
                    for c in range(8):
                        T.op(PE, lambda c=c, f=f, ub=ub: PEe.matmul(pb[ub][:, 0:512], lhsT=WU[:, c, f * 128:(f + 1) * 128],
                                                                    rhs=mT[:, c, :], start=(c == 0), stop=(c == 7)),
                             ["WU", "mT"], ["pb%d" % ub], inc=(c == 7))
                    T.op(ACT, lambda gb=gb: A.activation(out=ee[:], in_=pb[gb][:, 0:512], func=AF.Exp, scale=-1.0),
                         ["pb%d" % gb], ["ee"])
                    T.op(ACT, lambda ub=ub: A.copy(out=uS[:], in_=pb[ub][:, 0:512]), ["pb%d" % ub], ["uS"])
                    T.op(DVE, lambda: V.tensor_scalar(out=ee[:], in0=ee[:], scalar1=1.0, scalar2=None, op0=ALU.add),
                         ["ee"], ["ee"])
                    T.op(DVE, lambda: V.reciprocal(out=rr[:], in_=ee[:]), ["ee"], ["rr"])
                    T.op(DVE, lambda gb=gb: V.tensor_tensor(out=rr[:], in0=pb[gb][:, 0:512], in1=rr[:], op=ALU.mult),
                         ["pb%d" % gb, "rr"], ["rr"])
                    T.op(DVE, lambda f=f: V.tensor_tensor(out=actT[:, f, :], in0=rr[:], in1=uS[:], op=ALU.mult),
                         ["rr", "uS"], ["actT"])
                for tt in range(4):
                    i = gI * 4 + tt
                    s = i % 2
                    T.dma(SP, hbs[s][:], h_scr[i * 128:(i + 1) * 128, :], ["hscr%d" % i], ["hb%d" % s], "hl%d" % s)
                    for nb_ in range(2):
                        bank = 5 + nb_
                        for f in range(NFF):
                            T.op(PE, lambda f=f, tt=tt, nb_=nb_, bank=bank:
                                 PEe.matmul(pb[bank][:, 0:512], lhsT=actT[:, f, tt * 128:(tt + 1) * 128],
                                            rhs=WD[:, f, nb_ * 512:(nb_ + 1) * 512], start=(f == 0), stop=(f == NFF - 1)),
                                 ["actT", "WD"], ["pb%d" % bank], inc=(f == NFF - 1))
                        T.op(DVE, lambda s=s, nb_=nb_, bank=bank:
                             V.tensor_tensor(out=obs[s][:, nb_ * 512:(nb_ + 1) * 512], in0=pb[bank][:, 0:512],
                                             in1=hbs[s][:, nb_ * 512:(nb_ + 1) * 512], op=ALU.add),
                             ["pb%d" % bank, "hb%d" % s], ["ob%d" % s])
                    T.dma(SP, out[i * 128:(i + 1) * 128, :], obs[s][:], ["ob%d" % s], [], "o%d" % s)

        for n in list(T.dsems):
            d = T.dsem(n)
            nc.sync.wait_ge(d.sem, d.count)
    return nc


def _consts():
    c = np.zeros((128, 514), np.float32)
    c[:, 0:128] = np.eye(128, dtype=np.float32)
    s = np.arange(128)[:, None]
    t = np.arange(128)[None, :]
    same = (s // 64) == (t // 64)
    c[:, 128:256] = np.where(same & (s <= t), -1.0 / 16.0, 0.0)
    c[:, 256:384] = np.where(same & (s > t), -1.0 / 16.0, 0.0)
    c[:, 384:512] = np.where(same & (s <= t), 1.0, 0.0)
    c[0:64, 512] = -1.0 / 16.0
    c[64:128, 513] = -1.0 / 16.0
    return c


def _tabs(j):
    tb = np.zeros((128, 260), np.float32)
    kl = np.arange(128, dtype=np.float64)
    for h in range(4):
        for ei in range(64):
            e = ei - 48
            if e <= 4 * j + 3:
                tb[:, h * 64 + ei] = SLOPES[h] * (128.0 * (e - 4 * j) + kl - 256.0)
            else:
                tb[:, h * 64 + ei] = NEG
    tb[:, 256 + j] = 1.0
    return tb


def _masks(j):
    m = np.zeros((128, 16, 4, 128), np.float32)
    k = np.arange(128)[:, None]
    q = np.arange(128)[None, :]
    tri = (k <= q).astype(np.float32)
    for e in range(16):
        for c in range(4):
            cs = 4 * j + c
            if e < cs:
                m[:, e, c, :] = 1.0
            elif e == cs:
                m[:, e, c, :] = tri
    return m.reshape(128, 16 * 512)


def _rep(v, n=128):
    return np.ascontiguousarray(np.broadcast_to(np.asarray(v, np.float32).reshape(1, -1), (n, v.size)))


def prep_inputs(inp, NU=4):
    T_ = NU * 2048
    f = lambda a: np.ascontiguousarray(np.asarray(a, dtype=np.float32))
    x = f(inp["x"])
    w_in = f(inp["w_in"])[0]
    cols = lambda a, b: list(range(a, b))
    dq, dk, dv = 0, 512, 1024
    gq, gk, gv, go, gl = 1536, 1792, 2048, 2560, 3072
    c_p0 = cols(dk, dk + 256) + cols(dv, dv + 256) + cols(gk, gk + 256) + cols(gl, gl + 16) + cols(gv, gv + 512)
    c_p1 = cols(dk + 256, dk + 512) + cols(dv + 256, dv + 512)
    c_ow = cols(dq, dq + 512) + cols(gq, gq + 256) + cols(gk, gk + 256) + cols(gv, gv + 512) + cols(go, go + 512) + cols(gl, gl + 16)
    shared = {
        "w_p0": np.ascontiguousarray(w_in[:, c_p0]),
        "w_p1": np.ascontiguousarray(w_in[:, c_p1]),
        "w_ow": np.ascontiguousarray(w_in[:, c_ow]),
        "w_up": f(inp["w_gla_gate_up"])[0],
        "w_o": f(inp["w_out"])[0],
        "w_g": f(inp["w_ffn_gate"])[0],
        "w_u": f(inp["w_ffn_up"])[0],
        "w_d": f(inp["w_ffn_down"])[0],
        "g_attn": _rep(f(inp["attn_norm_gain"])[0]),
        "g_ffn": _rep(f(inp["ffn_norm_gain"])[0]),
        "qk_rep": np.concatenate([_rep(np.tile(f(inp["q_norm_gain"])[0], 8)),
                                  _rep(np.tile(f(inp["k_norm_gain"])[0], 8))], axis=1),
        "b_rep": _rep(f(inp["b_gla_gate"])[0]),
        "gn_rep": np.concatenate([_rep(f(inp["diff_out_norm_gain"])[0]), _rep(f(inp["gla_out_norm_gain"])[0])], axis=1),
        "lamv": np.concatenate([_rep(f(inp[k])[0]) for k in ("lambda_q1", "lambda_k1", "lambda_q2", "lambda_k2")], axis=1),
        "cst": _consts(),
    }
    in_maps = []
    for core in range(8):
        b, j = core // 4, core % 4
        own_rows = np.concatenate([np.arange((4 * u + j) * 512, (4 * u + j + 1) * 512) for u in range(NU)])
        m = dict(shared)
        m["x_all"] = np.ascontiguousarray(x[b, :T_])
        m["x_own"] = np.ascontiguousarray(x[b, own_rows])
        m["tabs"] = _tabs(j)
        m["msk"] = _masks(j)
        in_maps.append(m)
    return in_maps


def assemble(results, NU=4, B=2, key="out"):
    T_ = NU * 2048
    outp = np.zeros((B, T_, D), np.float32)
    for core in range(8):
        b, j = core // 4, core % 4
        r = np.asarray(results[core][key])
        for u in range(NU):
            outp[b, (4 * u + j) * 512:(4 * u + j + 1) * 512] = r[u * 512:(u + 1) * 512]
    return outp


_NC_CACHE = {}


def kernel(**inputs):
    NU = 4
    if NU not in _NC_CACHE:
        _NC_CACHE[NU] = build(NU)
    nc = _NC_CACHE[NU]
    in_maps = prep_inputs(inputs, NU)
    res = run_bass_kernel_spmd(nc, in_maps, core_ids=list(range(8)))
    return assemble(res.results, NU)
```

```python
import contextlib
import numpy as np
import concourse.bass as bass
import concourse.mybir as mybir
from concourse.bass_utils import run_bass_kernel_spmd

F32 = mybir.dt.float32
BF16 = mybir.dt.bfloat16
AF = mybir.ActivationFunctionType
ALU = mybir.AluOpType
AX = mybir.AxisListType

D = 1024
DFF = 2816
NFF = DFF // 128
EPS = 1e-6
LAMBDA_INIT = 0.8 - 0.6 * 1.0
SLOPES = [2.0 ** (-8.0 * (h + 1.0) / 4.0) for h in range(4)]
NEG = -30000.0
SAME_ENGINE_SYNC = True


class ES:
    def __init__(self, name, eng, sem):
        self.name = name
        self.eng = eng
        self.sem = sem
        self.count = 0
        self.seen = {}


class Trk:
    def __init__(self, nc, stack):
        self.nc = nc
        self.stack = stack
        self.lw = {}
        self.rd = {}
        self.dsems = {}
        mk = lambda n, e: ES(n, e, stack.enter_context(nc.semaphore("s_" + n)))
        self.pe = mk("pe", nc.tensor)
        self.act = mk("act", nc.scalar)
        self.dve = mk("dve", nc.vector)
        self.pool = mk("pool", nc.gpsimd)
        self.sp = mk("sp", nc.sync)

    def dsem(self, name):
        if name not in self.dsems:
            self.dsems[name] = ES("d_" + name, None,
                                  self.stack.enter_context(self.nc.semaphore("d_" + name)))
        return self.dsems[name]

    def _deps(self, reads, writes):
        deps = {}

        def add(e, v):
            if deps.get(e, 0) < v:
                deps[e] = v
        for k in reads:
            if k in self.lw:
                add(*self.lw[k])
        for k in writes:
            if k in self.lw:
                add(*self.lw[k])
            for e, v in self.rd.get(k, {}).items():
                add(e, v)
        return deps

    def _wait(self, E, deps):
        for e, v in deps.items():
            if e is E and (not SAME_ENGINE_SYNC or E.name == "pe"):
                continue
            if E.seen.get(e, 0) >= v:
                continue
            assert v <= e.count, (E.name, e.name, v, e.count)
            E.eng.wait_ge(e.sem, v)
            E.seen[e] = v

    def _rec(self, E, val, reads, writes):
        for k in writes:
            self.lw[k] = (E, val)
            self.rd[k] = {}
        for k in reads:
            d = self.rd.setdefault(k, {})
            if d.get(E, 0) < val:
                d[E] = val

    def op(self, E, fn, reads=(), writes=(), inc=True):
        pr = [k for k in reads if k.startswith("pb")]
        if pr:
            reads = [k for k in reads if not k.startswith("pb")]
            writes = list(writes) + [k for k in pr if k not in writes]
        self._wait(E, self._deps(reads, writes))
        inst = fn()
        if inc:
            inst.then_inc(E.sem, 1)
            E.count += 1
            val = E.count
        else:
            val = E.count + 1
        self._rec(E, val, reads, writes)
        return inst

    def dma(self, Q, out, in_, reads, writes, sem):
        self._wait(Q, self._deps(reads, writes))
        Dm = self.dsem(sem)
        inst = Q.eng.dma_start(out=out, in_=in_)
        inst.then_inc(Dm.sem, 16)
        Dm.count += 16
        self._rec(Dm, Dm.count, reads, writes)
        return inst


def build(NU=4, dbg=False):
    T_ = NU * 2048
    NT = T_ // 128
    NO = NU * 4
    TO = NO * 128
    nc = bass.Bass("TRN2", target_bir_lowering=False)

    def din(name, shape, dt=F32):
        return nc.dram_tensor(name, shape, dt, kind="ExternalInput").ap()

    x_all = din("x_all", [T_, D])
    x_own = din("x_own", [TO, D])
    w_p0 = din("w_p0", [D, 1296])
    w_p1 = din("w_p1", [D, 512])
    w_ow = din("w_ow", [D, 2064])
    w_up = din("w_up", [16, 256])
    w_o = din("w_o", [D, D])
    w_g = din("w_g", [D, DFF])
    w_u = din("w_u", [D, DFF])
    w_d = din("w_d", [DFF, D])
    g_attn = din("g_attn", [128, D])
    g_ffn = din("g_ffn", [128, D])
    qk_rep = din("qk_rep", [128, 1024])
    b_rep = din("b_rep", [128, 256])
    gn_rep = din("gn_rep", [128, 256])
    lamv = din("lamv", [128, 256])
    cst = din("cst", [128, 514])
    tabs = din("tabs", [128, 260])
    msk = din("msk", [128, 16 * 512])
    out = nc.dram_tensor("out", [TO, D], F32, kind="ExternalOutput").ap()
    mix_scr = nc.dram_tensor("mix_scr", [TO, D], BF16, kind="Internal").ap()
    h_scr = nc.dram_tensor("h_scr", [TO, D], F32, kind="Internal").ap()
    dbg_outs = {}
    if dbg:
        dbg_outs["d_mix"] = nc.dram_tensor("d_mix", [TO, D], F32, kind="ExternalOutput").ap()
        dbg_outs["d_h"] = nc.dram_tensor("d_h", [TO, D], F32, kind="ExternalOutput").ap()

    with contextlib.ExitStack() as gst:
        T = Trk(nc, gst)
        PE, ACT, DVE, POOL, SP = T.pe, T.act, T.dve, T.pool, T.sp
        V = nc.vector
        A = nc.scalar
        G = nc.gpsimd
        PEe = nc.tensor

        def sbg(n, s, d):
            return gst.enter_context(nc.sbuf_tensor(n, s, d))

        pb = [gst.enter_context(nc.psum_tensor("pb%d" % i, [128, 512], F32)) for i in range(8)]
        pbb = [p[:].bitcast(BF16) for p in pb]

        cstf = sbg("cstf", [128, 514], F32)
        identb = sbg("identb", [128, 128], BF16)
        st = sbg("st", [128, 32], F32)
        junk = sbg("junk", [128, 1024], BF16)
        T.dma(SP, cstf[:], cst[:, :], [], ["cstf"], "c0")
        T.op(DVE, lambda: V.tensor_copy(out=identb[:], in_=cstf[:, 0:128]), ["cstf"], ["identb"])
        triS = cstf[:, 128:256]
        upS = cstf[:, 256:384]
        chI = cstf[:, 512:514]

        def rstd_from_ss(ss_ap, out_ap, n, keys):
            T.op(ACT, lambda: A.activation(out=out_ap, in_=ss_ap, func=AF.Ln, scale=1.0 / n, bias=EPS),
                 keys, keys)
            T.op(ACT, lambda: A.activation(out=out_ap, in_=out_ap, func=AF.Exp, scale=-0.5),
                 keys, keys)

        def norm_transpose(src_ap, xb, xk, nb, nbk, nTt, nTk, grep, grepk, dq, dsem, bank=5):
            T.dma(dq, xb[:], src_ap, [], [xk], dsem)
            T.op(DVE, lambda: V.scalar_tensor_tensor(out=junk[:], in0=xb[:], scalar=1.0, in1=xb[:],
                                                     op0=ALU.mult, op1=ALU.mult, accum_out=st[:, 0:1]),
                 [xk], ["junk", "st0"])
            rstd_from_ss(st[:, 0:1], st[:, 1:2], float(D), ["st0"])
            T.op(DVE, lambda: V.scalar_tensor_tensor(out=nb[:], in0=xb[:], scalar=st[:, 1:2], in1=grep[:],
                                                     op0=ALU.mult, op1=ALU.mult),
                 [xk, "st0", grepk], [nbk])
            bk = "pb%d" % bank
            for c in range(8):
                T.op(PE, lambda c=c: PEe.transpose(out=pbb[bank][:, c * 128:(c + 1) * 128],
                                                   in_=nb[:, c * 128:(c + 1) * 128], identity=identb[:]),
                     [nbk, "identb"], [bk], inc=(c == 7))
            T.op(ACT, lambda: A.copy(out=nTt[:], in_=pbb[bank][:, 0:1024]), [bk], [nTk])

        def project(nTt, nTk, W, Wk, groups):
            for (bank, pc0, wc0, ncol) in groups:
                bk = "pb%d" % bank
                for c in range(8):
                    T.op(PE, lambda c=c, bank=bank, pc0=pc0, wc0=wc0, ncol=ncol:
                         PEe.matmul(pb[bank][:, pc0:pc0 + ncol], lhsT=nTt[:, c * 128:(c + 1) * 128],
                                    rhs=W[:, c, wc0:wc0 + ncol], start=(c == 0), stop=(c == 7)),
                         [nTk, Wk], [bk], inc=(c == 7))

        with contextlib.ExitStack() as mst:
            def sb(n, s, d):
                return mst.enter_context(nc.sbuf_tensor(n, s, d))

            KT = sb("KT", [128, 2, T_], BF16)
            VE = sb("VE", [128, NT, 2, 130], BF16)
            QT = sb("QT", [128, 4, TO], BF16)
            WB = sb("WB", [128, 8, 2064], BF16)
            MK = sb("MK", [128, 16, 512], BF16)
            tabf = sb("tabf", [128, 260], F32)
            gat = sb("gat", [128, D], F32)
            qg = sb("qg", [128, 512], F32)
            brep = sb("brep", [128, 256], F32)
            gn = sb("gn", [128, 256], F32)
            wupb = sb("wupb", [16, 256], BF16)
            maskA = sb("maskA", [128, 128], BF16)
            xbs = [sb("xb%d" % i, [128, D], F32) for i in range(3)]
            ssx = sb("ssx", [128, 4], F32)
            rstd_all = sb("rstd_all", [128, NT], F32)
            nbs = [sb("nb%d" % i, [128, D], BF16) for i in range(2)]
            nTs = [sb("nT%d" % i, [128, D], BF16) for i in range(2)]
            kf = sb("kf", [128, 512], F32)
            sq = sb("sq", [128, 512], F32)
            k16s = [sb("k16_%d" % i, [128, 512], BF16) for i in range(2)]
            gkf = [sb("gkf%d" % i, [128, 256], F32) for i in range(2)]
            glr16s = [sb("glr16_%d" % i, [128, 16], BF16) for i in range(2)]
            glrT = sb("glrT", [16, 128], BF16)
            zb = sb("zb", [128, 256], F32)
            lg = sb("lg", [128, 256], F32)
            eb = sb("eb", [128, 256], F32)
            enb = sb("enb", [128, 256], F32)
            ec = sb("ec", [128, 256], F32)
            dec = sb("dec", [128, 4], F32)
            qt16 = sb("qt16", [128, 256], BF16)
            kt16 = sb("kt16", [128, 256], BF16)
            kh16 = sb("kh16", [128, 256], BF16)
            vb16s = [sb("vb16_%d" % i, [128, 512], BF16) for i in range(2)]
            qkT = sb("qkT", [128, 512], BF16)
            AT = sb("AT", [128, 512], BF16)
            S = sb("S", [128, 256], F32)
            Sb = [sb("Sb%d" % i, [128, 256], BF16) for i in range(2)]
            snap = sb("snap", [128, NU, 256], F32)
            sgt = sb("sgt", [128, 512], F32)
            gg = sb("gg", [128, 512], F32)
            ogl = sb("ogl", [128, 512], BF16)
            PTs = [sb("PT%d" % i, [128, 512], BF16) for i in range(6)]
            accS = sb("accS", [128, 3, 390], F32)
            t1 = sb("t1", [128, 128], F32)
            of = sb("of", [128, 128], F32)
            og = sb("og", [128, 4, 128], BF16)

            T.dma(SP, tabf[:], tabs[:, :], [], ["tabf"], "c1")
            T.dma(SP, gat[:], g_attn[:, :], [], ["gat"], "c2")
            T.dma(SP, kf[:], qk_rep[:, 0:512], [], ["kf"], "c3")
            T.dma(SP, sq[:], qk_rep[:, 512:1024], [], ["sq"], "c3b")
            T.dma(SP, brep[:], b_rep[:, :], [], ["brep"], "c4")
            T.dma(SP, gn[:], gn_rep[:, :], [], ["gn"], "c5")
            lam = zb
            T.dma(SP, lam[:], lamv[:, :], [], ["zb"], "c6")
            T.dma(POOL, wupb[:], w_up[:, :], [], ["wupb"], "c7")
            T.dma(POOL, maskA[:], cst[:, 384:512], [], ["maskA"], "c8")
            T.dma(POOL, WB[:, :, 0:1296], w_p0.rearrange("(c p) n -> p c n", p=128), [], ["WB"], "w0")
            for e4 in range(4):
                T.dma(POOL, MK[:, e4 * 4:(e4 + 1) * 4, :],
                      msk[:, e4 * 2048:(e4 + 1) * 2048].rearrange("p (a b) -> p a b", b=512),
                      [], ["MK"], "c9")
            T.op(DVE, lambda: V.scalar_tensor_tensor(out=qg[:, 0:512], in0=kf[:, 0:512], scalar=0.125,
                                                     in1=sq[:, 0:512], op0=ALU.mult, op1=ALU.mult),
                 ["kf", "sq"], ["qg"])
            T.op(DVE, lambda: V.tensor_scalar(out=gn[:, 0:128], in0=gn[:, 0:128], scalar1=1.0 - LAMBDA_INIT,
                                              scalar2=None, op0=ALU.mult), ["gn"], ["gn"])
            T.op(DVE, lambda: V.scalar_tensor_tensor(out=junk[:, 0:64], in0=lam[:, 0:64], scalar=1.0,
                                                     in1=lam[:, 64:128], op0=ALU.mult, op1=ALU.mult,
                                                     accum_out=st[:, 4:5]), ["zb"], ["junk", "stl"])
            T.op(DVE, lambda: V.scalar_tensor_tensor(out=junk[:, 0:64], in0=lam[:, 128:192], scalar=1.0,
                                                     in1=lam[:, 192:256], op0=ALU.mult, op1=ALU.mult,
                                                     accum_out=st[:, 5:6]), ["zb", "stl"], ["junk", "stl"])
            T.op(ACT, lambda: A.activation(out=st[:, 6:8], in_=st[:, 4:6], func=AF.Exp), ["stl"], ["stl"])
            T.op(DVE, lambda: V.tensor_tensor(out=st[:, 8:9], in0=st[:, 7:8], in1=st[:, 6:7], op=ALU.subtract),
                 ["stl"], ["stl"])
            T.op(DVE, lambda: V.tensor_scalar(out=st[:, 8:9], in0=st[:, 8:9], scalar1=-LAMBDA_INIT,
                                              scalar2=None, op0=ALU.add), ["stl"], ["stl"])
            neglam = st[:, 8:9]
            T.op(POOL, lambda: G.memset(VE[:, :, :, 128:130], 1.0), [], ["VE"])
            T.op(DVE, lambda: V.memset(S[:], 0.0), [], ["S"])

            def k_evac_a(bank, c0, ngrp):
                n = 64 * ngrp
                T.op(ACT, lambda: A.copy(out=kf[:, 0:n], in_=pb[bank][:, c0:c0 + n]), ["pb%d" % bank], ["kf"])

            def k_evac_b(ngrp):
                n = 64 * ngrp
                T.op(DVE, lambda: V.tensor_tensor(out=sq[:, 0:n], in0=kf[:, 0:n], in1=kf[:, 0:n], op=ALU.mult),
                     ["kf"], ["sq"])
                T.op(DVE, lambda: V.tensor_reduce(out=st[:, 16:16 + ngrp],
                                                  in_=sq[:, 0:n].rearrange("p (a b) -> p a b", b=64),
                                                  axis=AX.X, op=ALU.add), ["sq"], ["stk"])
                rstd_from_ss(st[:, 16:16 + ngrp], st[:, 24:24 + ngrp], 64.0, ["stk"])

            def k_evac_c(ngrp, ks, gain=None):
                n = 64 * ngrp
                kk = "k16_%d" % ks
                bc = bass.AP(st[:].tensor, st[:, 24:25].offset, [list(st[:].ap[0]), [1, ngrp], [0, 64]])
                if gain is None:
                    T.op(DVE, lambda: V.tensor_tensor(out=k16s[ks][:, 0:n].rearrange("p (a b) -> p a b", b=64),
                                                      in0=kf[:, 0:n].rearrange("p (a b) -> p a b", b=64),
                                                      in1=bc, op=ALU.mult), ["kf", "stk"], [kk])
                else:
                    T.op(DVE, lambda: V.tensor_tensor(out=sq[:, 0:n].rearrange("p (a b) -> p a b", b=64),
                                                      in0=kf[:, 0:n].rearrange("p (a b) -> p a b", b=64),
                                                      in1=bc, op=ALU.mult), ["kf", "stk", "sq"], ["sq"])
                    T.op(DVE, lambda: V.tensor_tensor(out=k16s[ks][:, 0:n], in0=sq[:, 0:n], in1=gain,
                                                      op=ALU.mult), ["sq", "qg"], [kk])

            def k_evac(bank, c0, ngrp, ks, gain=None):
                k_evac_a(bank, c0, ngrp)
                k_evac_b(ngrp)
                k_evac_c(ngrp, ks, gain)

            def k_tr(ks, n, dst_fn):
                nh = n // 128
                for hh in range(nh):
                    T.op(PE, lambda hh=hh: PEe.transpose(out=pbb[4][:, hh * 128:(hh + 1) * 128],
                                                         in_=k16s[ks][:, hh * 128:(hh + 1) * 128], identity=identb[:]),
                         ["k16_%d" % ks, "identb"], ["pb4"], inc=(hh == nh - 1))
                dst_fn(pbb[4][:, 0:n].rearrange("p (a b) -> p a b", b=128))

            def gla_segs(own, i_own, gk, gkk, gq, gqk, vb, vbk, g16, g16k, psS):
                def s_a():
                    T.op(PE, lambda: PEe.transpose(out=pbb[6][0:16, 64:192], in_=g16[:], identity=identb[:]),
                         [g16k, "identb"], ["pb6"])
                    T.op(ACT, lambda: A.copy(out=glrT[:], in_=pbb[6][0:16, 64:192]), ["pb6"], ["glrT"])

                def s_b():
                    T.op(PE, lambda: PEe.matmul(pb[6][:, 128:384], lhsT=glrT[:], rhs=wupb[:], start=True, stop=True),
                         ["glrT", "wupb"], ["pb6"])
                    T.op(DVE, lambda: V.tensor_tensor(out=zb[:], in0=pb[6][:, 128:384], in1=brep[:], op=ALU.add),
                         ["pb6", "brep"], ["zb"])
                    T.op(ACT, lambda: A.activation(out=lg[:], in_=zb[:], func=AF.Exp, scale=-1.0), ["zb"], ["lg"])
                    T.op(ACT, lambda: A.activation(out=lg[:], in_=lg[:], func=AF.Ln, bias=1.0), ["lg"], ["lg"])

                def s_c():
                    if own:
                        T.op(PE, lambda: PEe.matmul(pb[7][:, 0:256], lhsT=triS, rhs=lg[:], start=True, stop=True),
                             ["cstf", "lg"], ["pb7"], inc=False)
                    T.op(PE, lambda: PEe.matmul(pb[7][:, 256:512], lhsT=upS, rhs=lg[:], start=True, stop=True),
                         ["cstf", "lg"], ["pb7"])
                    for p in range(2):
                        T.op(PE, lambda p=p: PEe.matmul(pb[6][:, 96 + 2 * p:98 + 2 * p], lhsT=lg[:, p * 128:(p + 1) * 128],
                                                        rhs=chI, start=True, stop=True),
                             ["lg", "cstf"], ["pb6"], inc=(p == 1))
                    T.op(ACT, lambda: A.activation(out=ec[:], in_=pb[7][:, 256:512], func=AF.Exp), ["pb7"], ["ec"])
                    if own:
                        T.op(ACT, lambda: A.activation(out=eb[:], in_=pb[7][:, 0:256], func=AF.Exp), ["pb7"], ["eb"])
                        T.op(ACT, lambda: A.activation(out=enb[:], in_=pb[7][:, 0:256], func=AF.Exp, scale=-1.0),
                             ["pb7"], ["enb"])
                    T.op(ACT, lambda: A.activation(out=dec[:], in_=pb[6][:, 96:100], func=AF.Exp), ["pb6"], ["dec"])
                    T.op(DVE, lambda: V.tensor_tensor(out=kh16[:], in0=gk, in1=ec[:], op=ALU.mult),
                         [gkk, "ec"], ["kh16"])
                    if own:
                        T.op(DVE, lambda: V.scalar_tensor_tensor(out=qt16[:], in0=gq, scalar=0.125,
                                                                 in1=eb[:], op0=ALU.mult, op1=ALU.mult),
                             [gqk, "eb"], ["qt16"])
                        T.op(DVE, lambda: V.tensor_tensor(out=kt16[:], in0=gk, in1=enb[:], op=ALU.mult),
                             [gkk, "enb"], ["kt16"])
                        for p in range(2):
                            T.op(PE, lambda p=p: PEe.transpose(out=pbb[4][:, 512 + p * 128:512 + (p + 1) * 128],
                                                               in_=qt16[:, p * 128:(p + 1) * 128], identity=identb[:]),
                                 ["qt16", "identb"], ["pb4"], inc=False)
                        for p in range(2):
                            T.op(PE, lambda p=p: PEe.transpose(out=pbb[4][:, 768 + p * 128:768 + (p + 1) * 128],
                                                               in_=kt16[:, p * 128:(p + 1) * 128], identity=identb[:]),
                                 ["kt16", "identb"], ["pb4"], inc=(p == 1))
                        T.op(ACT, lambda: A.copy(out=qkT[:], in_=pbb[4][:, 512:1024]), ["pb4"], ["qkT"])

                def s_c2():
                    for hh in range(2):
                        bank = 0 if hh == 0 else 3
                        for p in range(2):
                            T.op(PE, lambda hh=hh, p=p, bank=bank:
                                 PEe.matmul(pb[bank][:, p * 128:(p + 1) * 128],
                                            lhsT=qkT[hh * 64:(hh + 1) * 64, 256 + p * 128:256 + (p + 1) * 128],
                                            rhs=qkT[hh * 64:(hh + 1) * 64, p * 128:(p + 1) * 128],
                                            start=True, stop=True),
                                 ["qkT"], ["pb%d" % bank], inc=(p == 1))
                    mA = bass.AP(maskA[:].tensor, maskA[:].offset, [list(maskA[:].ap[0]), [0, 2], [1, 128]])
                    for hh in range(2):
                        bank = 0 if hh == 0 else 3
                        outv = bass.AP(AT[:].tensor, AT[:, hh * 128:hh * 128 + 1].offset,
                                       [list(AT[:].ap[0]), [256, 2], [1, 128]])
                        T.op(DVE, lambda bank=bank, outv=outv: V.tensor_tensor(
                            out=outv, in0=pb[bank][:, 0:256].rearrange("p (a b) -> p a b", b=128), in1=mA, op=ALU.mult),
                            ["pb%d" % bank, "maskA"], ["AT"])

                def s_d(ch):
                    def f():
                        if own:
                            T.op(POOL, lambda: G.tensor_copy(out=Sb[ch][:], in_=S[:]), ["S"], ["Sb%d" % ch])
                        sbank, scol = psS[ch]
                        sk = "pb%d" % sbank
                        for p in range(2):
                            for hh in range(2):
                                h = 2 * p + hh
                                T.op(PE, lambda p=p, hh=hh, h=h:
                                     PEe.matmul(pb[sbank][hh * 64:(hh + 1) * 64, scol + p * 128:scol + (p + 1) * 128],
                                                lhsT=kh16[ch * 64:(ch + 1) * 64, h * 64:(h + 1) * 64],
                                                rhs=vb[ch * 64:(ch + 1) * 64, h * 128:(h + 1) * 128],
                                                start=True, stop=True),
                                     ["kh16", vbk], [sk], inc=(p == 1 and hh == 1))
                        for p in range(2):
                            T.op(DVE, lambda p=p:
                                 V.scalar_tensor_tensor(out=S[:, p * 128:(p + 1) * 128], in0=S[:, p * 128:(p + 1) * 128],
                                                        scalar=dec[:, 2 * p + ch:2 * p + ch + 1],
                                                        in1=pb[sbank][:, scol + p * 128:scol + (p + 1) * 128],
                                                        op0=ALU.mult, op1=ALU.add),
                                 ["S", "dec", sk], ["S"])
                    return f

                def s_o():
                    for hh in range(2):
                        bank = 0 if hh == 0 else 3
                        bk = "pb%d" % bank
                        for p in range(2):
                            h = 2 * p + hh
                            oc = 256 + p * 128
                            T.op(PE, lambda h=h, bank=bank, oc=oc:
                                 PEe.matmul(pb[bank][:, oc:oc + 128], lhsT=AT[:, h * 128:(h + 1) * 128],
                                            rhs=vb[:, h * 128:(h + 1) * 128], start=True, stop=False),
                                 ["AT", vbk], [bk], inc=False)
                            for ch in range(2):
                                T.op(PE, lambda h=h, hh=hh, p=p, ch=ch, bank=bank, oc=oc:
                                     PEe.matmul(pb[bank][ch * 64:(ch + 1) * 64, oc:oc + 128],
                                                lhsT=qkT[hh * 64:(hh + 1) * 64, p * 128 + ch * 64:p * 128 + (ch + 1) * 64],
                                                rhs=Sb[ch][hh * 64:(hh + 1) * 64, p * 128:(p + 1) * 128],
                                                start=False, stop=True),
                                     ["qkT", "Sb%d" % ch], [bk], inc=(ch == 1 and p == 1))
                    for hh in range(2):
                        bank = 0 if hh == 0 else 3
                        bk = "pb%d" % bank
                        T.op(ACT, lambda bank=bank: A.copy(out=kf[:, 0:256], in_=pb[bank][:, 256:512]), [bk], ["kf"])
                        T.op(DVE, lambda: V.tensor_tensor(out=sq[:, 0:256], in0=kf[:, 0:256], in1=kf[:, 0:256],
                                                          op=ALU.mult), ["kf"], ["sq"])
                        T.op(DVE, lambda: V.tensor_reduce(out=st[:, 16:18],
                                                          in_=sq[:, 0:256].rearrange("p (a b) -> p a b", b=128),
                                                          axis=AX.X, op=ALU.add), ["sq"], ["stk"])
                        rstd_from_ss(st[:, 16:18], st[:, 24:26], 128.0, ["stk"])
                        bc = bass.AP(st[:].tensor, st[:, 24:25].offset, [list(st[:].ap[0]), [1, 2], [0, 128]])
                        ggv = bass.AP(gg[:].tensor, gg[:, hh * 128:hh * 128 + 1].offset,
                                      [list(gg[:].ap[0]), [256, 2], [1, 128]])
                        oglv = bass.AP(ogl[:].tensor, ogl[:, hh * 128:hh * 128 + 1].offset,
                                       [list(ogl[:].ap[0]), [256, 2], [1, 128]])
                        T.op(DVE, lambda bc=bc: V.tensor_tensor(out=sq[:, 256:512].rearrange("p (a b) -> p a b", b=128),
                                                                in0=kf[:, 0:256].rearrange("p (a b) -> p a b", b=128),
                                                                in1=bc, op=ALU.mult), ["kf", "stk"], ["sq2"])
                        T.op(DVE, lambda ggv=ggv, oglv=oglv: V.tensor_tensor(
                            out=oglv, in0=sq[:, 256:512].rearrange("p (a b) -> p a b", b=128), in1=ggv, op=ALU.mult),
                            ["sq2", "gg"], ["ogl"])
                    T.dma(SP, mix_scr[i_own * 128:(i_own + 1) * 128, 512:1024], ogl[:], ["ogl"], ["mixg%d" % i_own], "mx")

                if own:
                    return [s_a, s_b, s_c, s_c2, s_d(0), s_d(1), s_o]
                return [s_a, s_b, s_c, s_d(0), s_d(1)]

            def seg_load(t):
                sl = t % 3
                T.dma(SP, xbs[sl][:], x_all[t * 128:(t + 1) * 128, :], [], ["xb%d" % sl], "x%d" % sl)

            def seg_stats(t):
                sl = t % 3
                T.op(DVE, lambda: V.scalar_tensor_tensor(out=junk[:], in0=xbs[sl][:], scalar=1.0, in1=xbs[sl][:],
                                                         op0=ALU.mult, op1=ALU.mult, accum_out=ssx[:, sl:sl + 1]),
                     ["xb%d" % sl], ["junk", "ssx%d" % sl])
                T.op(ACT, lambda: A.activation(out=rstd_all[:, t:t + 1], in_=ssx[:, sl:sl + 1], func=AF.Ln,
                                               scale=1.0 / D, bias=EPS), ["ssx%d" % sl], ["rs%d" % t])
                T.op(ACT, lambda: A.activation(out=rstd_all[:, t:t + 1], in_=rstd_all[:, t:t + 1], func=AF.Exp, scale=-0.5),
                     ["rs%d" % t], ["rs%d" % t])

            def seg_n(t):
                sl = t % 3
                ns = t % 2
                T.op(DVE, lambda: V.scalar_tensor_tensor(out=nbs[ns][:], in0=xbs[sl][:], scalar=rstd_all[:, t:t + 1],
                                                         in1=gat[:], op0=ALU.mult, op1=ALU.mult),
                     ["xb%d" % sl, "rs%d" % t, "gat"], ["nb%d" % ns])

            def seg_tr(t):
                ns = t % 2
                for c in range(8):
                    T.op(PE, lambda c=c: PEe.transpose(out=pbb[3][:, c * 128:(c + 1) * 128],
                                                       in_=nbs[ns][:, c * 128:(c + 1) * 128], identity=identb[:]),
                         ["nb%d" % ns, "identb"], ["pb3"], inc=(c == 7))
                T.op(ACT, lambda: A.copy(out=nTs[ns][:], in_=pbb[3][:, 0:1024]), ["pb3"], ["nT%d" % ns])

            def run_pipe(n_tiles, early_fn, main_fn, late_fn):
                seg_load(0)
                for it in range(-1, n_tiles + 1):
                    early = early_fn(it + 1) if 0 <= it + 1 < n_tiles else []
                    main = main_fn(it) if 0 <= it < n_tiles else []
                    late = late_fn(it - 1) if 0 <= it - 1 < n_tiles else []
                    for k in range(max(len(early), len(main), len(late))):
                        if k < len(main):
                            main[k]()
                        if k < len(late):
                            late[k]()
                        if k < len(early):
                            early[k]()

            def kv_store(t):
                def dst(src3):
                    T.op(ACT, lambda: A.copy(out=KT[:, :, t * 128:(t + 1) * 128], in_=src3), ["pb4"], ["KT"])
                return dst

            def v_store(t):
                T.op(DVE, lambda: V.tensor_copy(out=VE[:, t, :, 0:128],
                                                in_=pb[0][:, 256:512].rearrange("p (a b) -> p a b", b=128)),
                     ["pb0"], ["VE"])

            def p0_iter(it):
                ok = lambda t: 0 <= t < NT
                if ok(it + 2):
                    seg_stats(it + 2)
                if ok(it + 1):
                    seg_tr(it + 1)
                late = []
                if ok(it - 1):
                    t1 = it - 1
                    n1 = t1 % 2
                    late = gla_segs(False, None, gkf[n1][:], "gkf%d" % n1, None, None, vb16s[n1], "vb16_%d" % n1,
                                    glr16s[n1], "glr16_%d" % n1, ((5, 0), (5, 256)))
                ns = it % 2
                if ok(it):
                    project(nTs[ns], "nT%d" % ns, WB, "WB", [(0, 0, 0, 512)])
                    k_evac_a(0, 0, 4)
                    v_store(it)
                    k_evac_b(4)
                if ok(it - 1):
                    t1 = it - 1
                    k_tr(t1 % 2, 256, kv_store(t1))
                    if t1 % 16 % 4 == 0:
                        u = t1 // 16
                        r = (t1 % 16) // 4
                        if r == 0:
                            T.op(DVE, lambda: V.tensor_scalar(out=snap[:, u, :], in0=S[:],
                                                              scalar1=tabf[:, 256 + r:257 + r], scalar2=None,
                                                              op0=ALU.mult), ["S", "tabf"], ["snap"])
                        else:
                            T.op(DVE, lambda: V.scalar_tensor_tensor(out=snap[:, u, :], in0=S[:],
                                                                     scalar=tabf[:, 256 + r:257 + r],
                                                                     in1=snap[:, u, :], op0=ALU.mult, op1=ALU.add),
                                 ["S", "tabf", "snap"], ["snap"])
                    late[0]()
                if ok(it + 2):
                    seg_n(it + 2)
                if ok(it):
                    project(nTs[ns], "nT%d" % ns, WB, "WB", [(1, 0, 512, 272)])
                    T.op(DVE, lambda: V.tensor_copy(out=gkf[ns][:], in_=pb[1][:, 0:256]), ["pb1"], ["gkf%d" % ns])
                    T.op(DVE, lambda: V.tensor_copy(out=glr16s[ns][:], in_=pb[1][:, 256:272]), ["pb1"], ["glr16_%d" % ns])
                    k_evac_c(4, ns)
                if ok(it - 1):
                    late[1]()
                if ok(it):
                    project(nTs[ns], "nT%d" % ns, WB, "WB", [(2, 0, 784, 512)])
                    T.op(ACT, lambda: A.copy(out=vb16s[ns][:], in_=pb[2][:, :]), ["pb2"], ["vb16_%d" % ns])
                if ok(it - 1):
                    late[2]()
                    late[3]()
                    late[4]()
                if ok(it + 3):
                    seg_load(it + 3)

            for t in range(min(3, NT)):
                seg_load(t)
            for it in range(-2, NT + 1):
                p0_iter(it)

            T.dma(POOL, WB[:, :, :], w_ow.rearrange("(c p) n -> p c n", p=128), [], ["WB"], "w0")

            def q_store(i):
                def dst(src3):
                    T.op(ACT, lambda: A.copy(out=QT[:, :, i * 128:(i + 1) * 128], in_=src3), ["pb4"], ["QT"])
                return dst

            gnb = bass.AP(gn[:].tensor, gn[:, 128:129].offset, [list(gn[:].ap[0]), [0, 4], [1, 128]])
            for i in range(NO):
                s = i % 2
                if i % 4 == 0:
                    T.op(DVE, lambda i=i: V.tensor_copy(out=S[:], in_=snap[:, i // 4, :]), ["snap"], ["S"])
                norm_transpose(x_own[i * 128:(i + 1) * 128, :], xbs[s], "xb%d" % s, nbs[s], "nb%d" % s,
                               nTs[s], "nT%d" % s, gat, "gat", SP, "x%d" % s)
                project(nTs[s], "nT%d" % s, WB, "WB",
                        [(0, 0, 0, 512), (1, 0, 512, 512), (2, 0, 1024, 512), (3, 0, 1536, 512), (6, 0, 2048, 16)])
                k_evac(0, 0, 8, 0, gain=qg[:, 0:512])
                k_tr(0, 512, q_store(i))
                T.op(ACT, lambda: A.activation(out=sgt[:], in_=pb[3][:, :], func=AF.Exp, scale=-1.0), ["pb3"], ["sgt"])
                T.op(DVE, lambda: V.tensor_scalar(out=sgt[:], in0=sgt[:], scalar1=1.0, scalar2=None, op0=ALU.add),
                     ["sgt"], ["sgt"])
                T.op(DVE, lambda: V.reciprocal(out=sgt[:], in_=sgt[:]), ["sgt"], ["sgt"])
                T.op(DVE, lambda: V.tensor_tensor(out=gg[:], in0=pb[3][:, :], in1=sgt[:], op=ALU.mult),
                     ["pb3", "sgt"], ["gg"])
                T.op(DVE, lambda: V.tensor_tensor(out=gg[:].rearrange("p (a b) -> p a b", b=128),
                                                  in0=gg[:].rearrange("p (a b) -> p a b", b=128), in1=gnb, op=ALU.mult),
                     ["gg", "gn"], ["gg"])
                T.op(DVE, lambda: V.tensor_copy(out=glr16s[0][:], in_=pb[6][:, 0:16]), ["pb6"], ["glr16_0"])
                T.op(ACT, lambda: A.copy(out=vb16s[0][:], in_=pb[2][:, :]), ["pb2"], ["vb16_0"])
                for sg in gla_segs(True, i, pb[1][:, 256:512], "pb1", pb[1][:, 0:256], "pb1", vb16s[0], "vb16_0",
                                   glr16s[0], "glr16_0", ((2, 0), (5, 0))):
                    sg()

            def acc_ap(a):
                bank = 5 + a // 3
                col = (a % 3) * 130
                return bank, col

            def attention(hp):
                steps = [(u, hl, kb) for u in range(NU) for hl in range(2) for kb in range(16 * u + 16)]
                LOOK = 2

                def emit_qk(idx):
                    u, hl, kb = steps[idx]
                    h = 2 * hp + hl
                    e = kb - 16 * u
                    ei = e + 48
                    for m in range(2):
                        sbank = 1 + (2 * idx + m) % 4
                        T.op(PE, lambda m=m, sbank=sbank:
                             PEe.matmul(pb[sbank][:, 0:512],
                                        lhsT=KT[m * 64:(m + 1) * 64, hl, kb * 128:(kb + 1) * 128],
                                        rhs=QT[m * 64:(m + 1) * 64, h, u * 512:(u + 1) * 512],
                                        start=True, stop=True),
                             ["KT", "QT"], ["pb%d" % sbank])
                    for m in range(2):
                        sbank = 1 + (2 * idx + m) % 4
                        slot = (2 * idx + m) % 6
                        T.op(ACT, lambda sbank=sbank, slot=slot:
                             A.activation(out=PTs[slot][:], in_=pb[sbank][:, 0:512], func=AF.Exp,
                                          bias=tabf[:, h * 64 + ei:h * 64 + ei + 1]),
                             ["pb%d" % sbank, "tabf"], ["PT%d" % slot])
                        if e >= 0:
                            T.op(POOL, lambda slot=slot:
                                 G.tensor_tensor(out=PTs[slot][:], in0=PTs[slot][:], in1=MK[:, e, :], op=ALU.mult),
                                 ["PT%d" % slot, "MK"], ["PT%d" % slot])

                def emit_pv(idx):
                    u, hl, kb = steps[idx]
                    h = 2 * hp + hl
                    nkb = 16 * u + 16
                    for m in range(2):
                        slot = (2 * idx + m) % 6
                        for c in range(4):
                            bank, col = acc_ap(c * 2 + m)
                            first = (kb == 0 and (c * 2 + m) in (0, 4, 6))
                            T.op(PE, lambda c=c, bank=bank, col=col, slot=slot, first=first:
                                 PEe.matmul(pb[bank][:, col:col + 130],
                                            lhsT=PTs[slot][:, c * 128:(c + 1) * 128],
                                            rhs=VE[:, kb, hl, :], start=first, stop=(kb == nkb - 1),
                                            skip_group_check=True),
                                 ["PT%d" % slot, "VE"], ["pb%d" % bank], inc=(c == 3 and m == 1))
                    if kb != nkb - 1:
                        return
                    for bi in range(3):
                        T.op(DVE, lambda bi=bi: V.tensor_copy(out=accS[:, bi, :], in_=pb[5 + bi][:, 0:390]),
                             ["pb%d" % (5 + bi)], ["accS"])
                    for c in range(4):
                        a0 = c * 2
                        a1 = c * 2 + 1
                        A0 = accS[:, a0 // 3, (a0 % 3) * 130:(a0 % 3) * 130 + 130]
                        A1 = accS[:, a1 // 3, (a1 % 3) * 130:(a1 % 3) * 130 + 130]
                        T.op(DVE, lambda A0=A0: V.reciprocal(out=st[:, 10:11], in_=A0[:, 128:129]), ["accS"], ["sta"])
                        T.op(DVE, lambda A1=A1: V.reciprocal(out=st[:, 11:12], in_=A1[:, 128:129]), ["accS", "sta"], ["sta"])
                        T.op(DVE, lambda: V.tensor_tensor(out=st[:, 11:12], in0=st[:, 11:12], in1=neglam, op=ALU.mult),
                             ["sta", "stl"], ["sta"])
                        T.op(DVE, lambda A1=A1: V.tensor_scalar(out=t1[:], in0=A1[:, 0:128], scalar1=st[:, 11:12],
                                                                scalar2=None, op0=ALU.mult), ["accS", "sta"], ["t1"])
                        T.op(DVE, lambda A0=A0: V.scalar_tensor_tensor(out=of[:], in0=A0[:, 0:128], scalar=st[:, 10:11],
                                                                       in1=t1[:], op0=ALU.mult, op1=ALU.add),
                             ["accS", "sta", "t1"], ["of"])
                        T.op(DVE, lambda: V.scalar_tensor_tensor(out=junk[:, 0:128], in0=of[:], scalar=1.0, in1=of[:],
                                                                 op0=ALU.mult, op1=ALU.mult, accum_out=st[:, 12:13]),
                             ["of"], ["junk", "stb"])
                        rstd_from_ss(st[:, 12:13], st[:, 13:14], 128.0, ["stb"])
                        T.op(DVE, lambda c=c: V.scalar_tensor_tensor(out=og[:, c, :], in0=of[:], scalar=st[:, 13:14],
                                                                     in1=gn[:, 0:128], op0=ALU.mult, op1=ALU.mult),
                             ["of", "stb", "gn"], ["og"])
                    T.dma(SP, mix_scr[u * 512:(u + 1) * 512, h * 128:(h + 1) * 128].rearrange("(c p) n -> p c n", p=128),
                          og[:], ["og"], ["mixd%d_%d" % (u, h)], "mx2")

                for i in range(len(steps) + LOOK):
                    if i < len(steps):
                        emit_qk(i)
                    if i - LOOK >= 0:
                        emit_pv(i - LOOK)

            attention(0)

            T.dma(POOL, WB[:, :, 0:512], w_p1.rearrange("(c p) n -> p c n", p=128), [], ["WB"], "w0")

            def p1_iter(it):
                ok = lambda t: 0 <= t < NT
                if ok(it + 2):
                    seg_n(it + 2)
                if ok(it + 1):
                    seg_tr(it + 1)
                ns = it % 2
                if ok(it):
                    project(nTs[ns], "nT%d" % ns, WB, "WB", [(0, 0, 0, 512)])
                    k_evac_a(0, 0, 4)
                    v_store(it)
                    k_evac_b(4)
                if ok(it - 1):
                    k_tr((it - 1) % 2, 256, kv_store(it - 1))
                if ok(it):
                    k_evac_c(4, ns)
                if ok(it + 3):
                    seg_load(it + 3)

            for t in range(min(3, NT)):
                seg_load(t)
            for it in range(-2, NT + 1):
                p1_iter(it)
            attention(1)
            mix_keys = ["mixg%d" % i for i in range(NO)] + ["mixd%d_%d" % (u, h) for u in range(NU) for h in range(4)]

        with contextlib.ExitStack() as fst:
            def sb(n, s, d):
                return fst.enter_context(nc.sbuf_tensor(n, s, d))

            WG = sb("WG", [128, 8, DFF], BF16)
            WU = sb("WU", [128, 8, DFF], BF16)
            WD = sb("WD", [128, NFF, D], BF16)
            gff = sb("gff", [128, D], F32)
            hbs = [sb("hb%d" % i, [128, D], F32) for i in range(2)]
            nbs = [sb("fnb%d" % i, [128, D], BF16) for i in range(2)]
            obs = [sb("ob%d" % i, [128, D], F32) for i in range(2)]
            T.dma(SP, gff[:], g_ffn[:, :], [], ["gff"], "c2")
            with contextlib.ExitStack() as ost:
                def sbo(n, s, d):
                    return ost.enter_context(nc.sbuf_tensor(n, s, d))
                WO = sbo("WO", [128, 8, D], BF16)
                mxs = [sbo("mx%d" % i, [128, D], BF16) for i in range(2)]
                mxT = [sbo("mxT%d" % i, [128, D], BF16) for i in range(2)]
                xos = [sbo("xo%d" % i, [128, D], F32) for i in range(2)]
                T.dma(POOL, WO[:], w_o.rearrange("(c p) n -> p c n", p=128), [], ["WO"], "w1")
                T.dma(POOL, WG[:], w_g.rearrange("(c p) n -> p c n", p=128), [], ["WG"], "w2")
                T.dma(POOL, WU[:], w_u.rearrange("(c p) n -> p c n", p=128), [], ["WU"], "w3")
                T.dma(POOL, WD[:], w_d.rearrange("(c p) n -> p c n", p=128), [], ["WD"], "w4")
                for i in range(NO):
                    s = i % 2
                    T.dma(SP, mxs[s][:], mix_scr[i * 128:(i + 1) * 128, :], mix_keys if i < 2 else [], ["mx%d" % s], "m%d" % s)
                    T.dma(SP, xos[s][:], x_own[i * 128:(i + 1) * 128, :], [], ["xo%d" % s], "xo%d" % s)
                    for c in range(8):
                        T.op(PE, lambda c=c, s=s: PEe.transpose(out=pbb[0][:, c * 128:(c + 1) * 128],
                                                                in_=mxs[s][:, c * 128:(c + 1) * 128], identity=identb[:]),
                             ["mx%d" % s, "identb"], ["pb0"], inc=(c == 7))
                    T.op(ACT, lambda s=s: A.copy(out=mxT[s][:], in_=pbb[0][:, 0:1024]), ["pb0"], ["mxT%d" % s])
                    for nb_ in range(2):
                        bank = 1 + nb_
                        for c in range(8):
                            T.op(PE, lambda c=c, s=s, nb_=nb_, bank=bank:
                                 PEe.matmul(pb[bank][:, 0:512], lhsT=mxT[s][:, c * 128:(c + 1) * 128],
                                            rhs=WO[:, c, nb_ * 512:(nb_ + 1) * 512], start=(c == 0), stop=(c == 7)),
                                 ["mxT%d" % s, "WO"], ["pb%d" % bank], inc=(c == 7))
                        T.op(DVE, lambda s=s, nb_=nb_, bank=bank:
                             V.tensor_tensor(out=hbs[s][:, nb_ * 512:(nb_ + 1) * 512], in0=pb[bank][:, 0:512],
                                             in1=xos[s][:, nb_ * 512:(nb_ + 1) * 512], op=ALU.add),
                             ["pb%d" % bank, "xo%d" % s], ["hb%d" % s])
                    T.dma(SP, h_scr[i * 128:(i + 1) * 128, :], hbs[s][:], ["hb%d" % s], ["hscr%d" % i], "hs%d" % s)
                    if dbg:
                        T.dma(SP, dbg_outs["d_h"][i * 128:(i + 1) * 128, :], hbs[s][:], ["hb%d" % s], [], "dbg")
                        T.op(DVE, lambda s=s: V.tensor_copy(out=xos[s][:], in_=mxs[s][:]), ["mx%d" % s, "xo%d" % s], ["xo%d" % s])
                        T.dma(SP, dbg_outs["d_mix"][i * 128:(i + 1) * 128, :], xos[s][:], ["xo%d" % s], [], "dbg")

            mT = sb("mT", [128, 8, 512], BF16)
            actT = sb("actT", [128, NFF, 512], BF16)
            ee = sb("ee", [128, 512], F32)
            uS = sb("uS", [128, 512], F32)
            rr = sb("rr", [128, 512], F32)
            for gI in range(NU):
                for tt in range(4):
                    i = gI * 4 + tt
                    s = i % 2
                    T.dma(SP, hbs[s][:], h_scr[i * 128:(i + 1) * 128, :], ["hscr%d" % i], ["hb%d" % s], "hl%d" % s)
                    T.op(DVE, lambda s=s: V.scalar_tensor_tensor(out=junk[:], in0=hbs[s][:], scalar=1.0, in1=hbs[s][:],
                                                                 op0=ALU.mult, op1=ALU.mult, accum_out=st[:, 0:1]),
                         ["hb%d" % s], ["junk", "st0"])
                    rstd_from_ss(st[:, 0:1], st[:, 1:2], float(D), ["st0"])
                    T.op(DVE, lambda s=s: V.scalar_tensor_tensor(out=nbs[s][:], in0=hbs[s][:], scalar=st[:, 1:2],
                                                                 in1=gff[:], op0=ALU.mult, op1=ALU.mult),
                         ["hb%d" % s, "st0", "gff"], ["fnb%d" % s])
                    for c in range(8):
                        T.op(PE, lambda c=c, s=s: PEe.transpose(out=pbb[0][:, c * 128:(c + 1) * 128],
                                                                in_=nbs[s][:, c * 128:(c + 1) * 128], identity=identb[:]),
                             ["fnb%d" % s, "identb"], ["pb0"], inc=(c == 7))
                    T.op(ACT, lambda tt=tt: A.copy(out=mT[:, :, tt * 128:(tt + 1) * 128],
                                                   in_=pbb[0][:, 0:1024].rearrange("p (a b) -> p a b", b=128)),
                         ["pb0"], ["mT"])
                for f in range(NFF):
                    gb = 1 + (f % 2)
                    ub = 3 + (f % 2)
                    for c in range(8):
                        T.op(PE, lambda c=c, f=f, gb=gb: PEe.matmul(pb[gb][:, 0:512], lhsT=WG[:, c, f * 128:(f + 1) * 128],
                                                                    rhs=mT[:, c, :], start=(c == 0), stop=(c == 7)),
                             ["WG", "mT"], ["pb%d" % gb], inc=(c == 7))
                    for c in range(8):
                        T.op(PE, lambda c=c, f=f, ub=ub: PEe.matmul(pb[ub][:, 0:512], lhsT=WU[:, c, f * 128:(f + 1) * 128],
                                                                    rhs=mT[:, c, :], start=(c == 0), stop=(c == 7)),
                             ["WU", "mT"], ["pb%d" % ub], inc=(c == 7))
                    T.op(ACT, lambda gb=gb: A.activation(out=ee[:], in_=pb[gb][:, 0:512], func=AF.Exp, scale=-1.0),
                         ["pb%d" % gb], ["ee"])
                    T.op(ACT, lambda ub=ub: A.copy(out=uS[:], in_=pb[ub][:, 0:512]), ["pb%d" % ub], ["uS"])
                    T.op(DVE, lambda: V.tensor_scalar(out=ee[:], in0=ee[:], scalar1=1.0, scalar2=None, op0=ALU.add),
                         ["ee"], ["ee"])
                    T.op(DVE, lambda: V.reciprocal(out=rr[:], in_=ee[:]), ["ee"], ["rr"])
                    T.op(DVE, lambda gb=gb: V.tensor_tensor(out=rr[:], in0=pb[gb][:, 0:512], in1=rr[:], op=ALU.mult),
                         ["pb%d" % gb, "rr"], ["rr"])
                    T.op(DVE, lambda f=f: V.tensor_tensor(out=actT[:, f, :], in0=rr[:], in1=uS[:], op=ALU.mult),
                         ["rr", "uS"], ["actT"])
                for tt in range(4):
                    i = gI * 4 + tt
                    s = i % 2
                    T.dma(SP, hbs[s][:], h_scr[i * 128:(i + 1) * 128, :], ["hscr%d" % i], ["hb%d" % s], "hl%d" % s)
                    for nb_ in range(2):
                        bank = 5 + nb_
                        for f in range(NFF):
                            T.op(PE, lambda f=f, tt=tt, nb_=nb_, bank=bank:
                                 PEe.matmul(pb[bank][:, 0:512], lhsT=actT[:, f, tt * 128:(tt + 1) * 128],
                                            rhs=WD[:, f, nb_ * 512:(nb_ + 1) * 512], start=(f == 0), stop=(f == NFF - 1)),
                                 ["actT", "WD"], ["pb%d" % bank], inc=(f == NFF - 1))
                        T.op(DVE, lambda s=s, nb_=nb_, bank=bank:
                             V.tensor_tensor(out=obs[s][:, nb_ * 512:(nb_ + 1) * 512], in0=pb[bank][:, 0:512],
                                             in1=hbs[s][:, nb_ * 512:(nb_ + 1) * 512], op=ALU.add),
                             ["pb%d" % bank, "hb%d" % s], ["ob%d" % s])
                    T.dma(SP, out[i * 128:(i + 1) * 128, :], obs[s][:], ["ob%d" % s], [], "o%d" % s)

        for n in list(T.dsems):
            d = T.dsem(n)
            nc.sync.wait_ge(d.sem, d.count)
    return nc


def _consts():
    c = np.zeros((128, 514), np.float32)
    c[:, 0:128] = np.eye(128, dtype=np.float32)
    s = np.arange(128)[:, None]
    t = np.arange(128)[None, :]
    same = (s // 64) == (t // 64)
    c[:, 128:256] = np.where(same & (s <= t), -1.0 / 16.0, 0.0)
    c[:, 256:384] = np.where(same & (s > t), -1.0 / 16.0, 0.0)
    c[:, 384:512] = np.where(same & (s <= t), 1.0, 0.0)
    c[0:64, 512] = -1.0 / 16.0
    c[64:128, 513] = -1.0 / 16.0
    return c


def _tabs(j):
    tb = np.zeros((128, 260), np.float32)
    kl = np.arange(128, dtype=np.float64)
    for h in range(4):
        for ei in range(64):
            e = ei - 48
            if e <= 4 * j + 3:
                tb[:, h * 64 + ei] = SLOPES[h] * (128.0 * (e - 4 * j) + kl - 256.0)
            else:
                tb[:, h * 64 + ei] = NEG
    tb[:, 256 + j] = 1.0
    return tb


def _masks(j):
    m = np.zeros((128, 16, 4, 128), np.float32)
    k = np.arange(128)[:, None]
    q = np.arange(128)[None, :]
    tri = (k <= q).astype(np.float32)
    for e in range(16):
        for c in range(4):
            cs = 4 * j + c
            if e < cs:
                m[:, e, c, :] = 1.0
            elif e == cs:
                m[:, e, c, :] = tri
    return m.reshape(128, 16 * 512)


def _rep(v, n=128):
    return np.ascontiguousarray(np.broadcast_to(np.asarray(v, np.float32).reshape(1, -1), (n, v.size)))


def prep_inputs(inp, NU=4):
    T_ = NU * 2048
    f = lambda a: np.ascontiguousarray(np.asarray(a, dtype=np.float32))
    x = f(inp["x"])
    w_in = f(inp["w_in"])[0]
    cols = lambda a, b: list(range(a, b))
    dq, dk, dv = 0, 512, 1024
    gq, gk, gv, go, gl = 1536, 1792, 2048, 2560, 3072
    c_p0 = cols(dk, dk + 256) + cols(dv, dv + 256) + cols(gk, gk + 256) + cols(gl, gl + 16) + cols(gv, gv + 512)
    c_p1 = cols(dk + 256, dk + 512) + cols(dv + 256, dv + 512)
    c_ow = cols(dq, dq + 512) + cols(gq, gq + 256) + cols(gk, gk + 256) + cols(gv, gv + 512) + cols(go, go + 512) + cols(gl, gl + 16)
    shared = {
        "w_p0": np.ascontiguousarray(w_in[:, c_p0]),
        "w_p1": np.ascontiguousarray(w_in[:, c_p1]),
        "w_ow": np.ascontiguousarray(w_in[:, c_ow]),
        "w_up": f(inp["w_gla_gate_up"])[0],
        "w_o": f(inp["w_out"])[0],
        "w_g": f(inp["w_ffn_gate"])[0],
        "w_u": f(inp["w_ffn_up"])[0],
        "w_d": f(inp["w_ffn_down"])[0],
        "g_attn": _rep(f(inp["attn_norm_gain"])[0]),
        "g_ffn": _rep(f(inp["ffn_norm_gain"])[0]),
        "qk_rep": np.concatenate([_rep(np.tile(f(inp["q_norm_gain"])[0], 8)),
                                  _rep(np.tile(f(inp["k_norm_gain"])[0], 8))], axis=1),
        "b_rep": _rep(f(inp["b_gla_gate"])[0]),
        "gn_rep": np.concatenate([_rep(f(inp["diff_out_norm_gain"])[0]), _rep(f(inp["gla_out_norm_gain"])[0])], axis=1),
        "lamv": np.concatenate([_rep(f(inp[k])[0]) for k in ("lambda_q1", "lambda_k1", "lambda_q2", "lambda_k2")], axis=1),
        "cst": _consts(),
    }
    in_maps = []
    for core in range(8):
        b, j = core // 4, core % 4
        own_rows = np.concatenate([np.arange((4 * u + j) * 512, (4 * u + j + 1) * 512) for u in range(NU)])
        m = dict(shared)
        m["x_all"] = np.ascontiguousarray(x[b, :T_])
        m["x_own"] = np.ascontiguousarray(x[b, own_rows])
        m["tabs"] = _tabs(j)
        m["msk"] = _masks(j)
        in_maps.append(m)
    return in_maps


def assemble(results, NU=4, B=2, key="out"):
    T_ = NU * 2048
    outp = np.zeros((B, T_, D), np.float32)
    for core in range(8):
        b, j = core // 4, core % 4
        r = np.asarray(results[core][key])
        for u in range(NU):
            outp[b, (4 * u + j) * 512:(4 * u + j + 1) * 512] = r[u * 512:(u + 1) * 512]
    return outp


_NC_CACHE = {}


def kernel(**inputs):
    NU = 4
    if NU not in _NC_CACHE:
        _NC_CACHE[NU] = build(NU)
    nc = _NC_CACHE[NU]
    in_maps = prep_inputs(inputs, NU)
    res = run_bass_kernel_spmd(nc, in_maps, core_ids=list(range(8)))
    return assemble(res.results, NU)
```

```python
import contextlib
import numpy as np
import concourse.bass as bass
import concourse.mybir as mybir
from concourse.bass_utils import run_bass_kernel_spmd

F32 = mybir.dt.float32
BF16 = mybir.dt.bfloat16
AF = mybir.ActivationFunctionType
ALU = mybir.AluOpType
AX = mybir.AxisListType

D = 1024
DFF = 2816
NFF = DFF // 128
EPS = 1e-6
LAMBDA_INIT = 0.8 - 0.6 * 1.0
SLOPES = [2.0 ** (-8.0 * (h + 1.0) / 4.0) for h in range(4)]
NEG = -30000.0
SAME_ENGINE_SYNC = True


class ES:
    def __init__(self, name, eng, sem):
        self.name = name
        self.eng = eng
        self.sem = sem
        self.count = 0
        self.seen = {}


class Trk:
    def __init__(self, nc, stack):
        self.nc = nc
        self.stack = stack
        self.lw = {}
        self.rd = {}
        self.dsems = {}
        mk = lambda n, e: ES(n, e, stack.enter_context(nc.semaphore("s_" + n)))
        self.pe = mk("pe", nc.tensor)
        self.act = mk("act", nc.scalar)
        self.dve = mk("dve", nc.vector)
        self.pool = mk("pool", nc.gpsimd)
        self.sp = mk("sp", nc.sync)

    def dsem(self, name):
        if name not in self.dsems:
            self.dsems[name] = ES("d_" + name, None,
                                  self.stack.enter_context(self.nc.semaphore("d_" + name)))
        return self.dsems[name]

    def _deps(self, reads, writes):
        raw = {}
        other = {}

        def add(d, e, v):
            if d.get(e, 0) < v:
                d[e] = v
        for k in reads:
            if k in self.lw:
                add(raw, *self.lw[k])
        for k in writes:
            if k in self.lw:
                add(other, *self.lw[k])
            for e, v in self.rd.get(k, {}).items():
                add(other, e, v)
        return raw, other

    def _wait(self, E, deps):
        raw, other = deps
        allv = dict(other)
        for e, v in raw.items():
            if allv.get(e, 0) < v:
                allv[e] = v
        for e, v in allv.items():
            if e is E:
                if E.name == "pe" or not SAME_ENGINE_SYNC:
                    continue
                v = raw.get(e, 0)
                if v == 0:
                    continue
            if E.seen.get(e, 0) >= v:
                continue
            assert v <= e.count, (E.name, e.name, v, e.count)
            E.eng.wait_ge(e.sem, v)
            E.seen[e] = v

    def _rec(self, E, val, reads, writes):
        for k in writes:
            self.lw[k] = (E, val)
            self.rd[k] = {}
        for k in reads:
            d = self.rd.setdefault(k, {})
            if d.get(E, 0) < val:
                d[E] = val

    def op(self, E, fn, reads=(), writes=(), inc=True):
        pr = [k for k in reads if k.startswith("pb")]
        if pr:
            reads = [k for k in reads if not k.startswith("pb")]
            writes = list(writes) + [k for k in pr if k not in writes]
        self._wait(E, self._deps(reads, writes))
        inst = fn()
        if inc:
            inst.then_inc(E.sem, 1)
            E.count += 1
            val = E.count
        else:
            val = E.count + 1
        self._rec(E, val, reads, writes)
        return inst

    def barrier(self):
        engs = [self.pe, self.act, self.dve, self.pool, self.sp]
        srcs = engs + list(self.dsems.values())
        for E in engs:
            for e in srcs:
                if e is E or e.count == 0 or E.seen.get(e, 0) >= e.count:
                    continue
                E.eng.wait_ge(e.sem, e.count)
                E.seen[e] = e.count

    def dma(self, Q, out, in_, reads, writes, sem):
        self._wait(Q, self._deps(reads, writes))
        Dm = self.dsem(sem)
        inst = Q.eng.dma_start(out=out, in_=in_)
        inst.then_inc(Dm.sem, 16)
        Dm.count += 16
        self._rec(Dm, Dm.count, reads, writes)
        return inst


def build(NU=4, dbg=False):
    T_ = NU * 2048
    NT = T_ // 128
    NO = NU * 4
    TO = NO * 128
    nc = bass.Bass("TRN2", target_bir_lowering=False)

    def din(name, shape, dt=F32):
        return nc.dram_tensor(name, shape, dt, kind="ExternalInput").ap()

    x_all = din("x_all", [T_, D])
    x_own = din("x_own", [TO, D])
    w_p0 = din("w_p0", [D, 1296])
    w_p1 = din("w_p1", [D, 512])
    w_ow = din("w_ow", [D, 2064])
    w_up = din("w_up", [16, 256])
    w_o = din("w_o", [D, D])
    w_g = din("w_g", [D, DFF])
    w_u = din("w_u", [D, DFF])
    w_d = din("w_d", [DFF, D])
    g_attn = din("g_attn", [128, D])
    g_ffn = din("g_ffn", [128, D])
    qk_rep = din("qk_rep", [128, 1024])
    b_rep = din("b_rep", [128, 256])
    gn_rep = din("gn_rep", [128, 256])
    lamv = din("lamv", [128, 256])
    cst = din("cst", [128, 514])
    tabs = din("tabs", [128, 260])
    msk = din("msk", [128, 16 * 512])
    out = nc.dram_tensor("out", [TO, D], F32, kind="ExternalOutput").ap()
    mix_scr = nc.dram_tensor("mix_scr", [TO, D], BF16, kind="Internal").ap()
    h_scr = nc.dram_tensor("h_scr", [TO, D], F32, kind="Internal").ap()
    dbg_outs = {}
    if dbg:
        dbg_outs["d_mix"] = nc.dram_tensor("d_mix", [TO, D], F32, kind="ExternalOutput").ap()
        dbg_outs["d_h"] = nc.dram_tensor("d_h", [TO, D], F32, kind="ExternalOutput").ap()

    with contextlib.ExitStack() as gst:
        T = Trk(nc, gst)
        PE, ACT, DVE, POOL, SP = T.pe, T.act, T.dve, T.pool, T.sp
        V = nc.vector
        A = nc.scalar
        G = nc.gpsimd
        PEe = nc.tensor

        def sbg(n, s, d):
            return gst.enter_context(nc.sbuf_tensor(n, s, d))

        pb = [gst.enter_context(nc.psum_tensor("pb%d" % i, [128, 512], F32)) for i in range(8)]
        pbb = [p[:].bitcast(BF16) for p in pb]

        cstf = sbg("cstf", [128, 514], F32)
        identb = sbg("identb", [128, 128], BF16)
        st = sbg("st", [128, 32], F32)
        junk = sbg("junk", [128, 1024], BF16)
        T.dma(SP, cstf[:], cst[:, :], [], ["cstf"], "c0")
        T.op(DVE, lambda: V.tensor_copy(out=identb[:], in_=cstf[:, 0:128]), ["cstf"], ["identb"])
        triS = cstf[:, 128:256]
        upS = cstf[:, 256:384]
        chI = cstf[:, 512:514]

        def rstd_from_ss(ss_ap, out_ap, n, keys):
            T.op(ACT, lambda: A.activation(out=out_ap, in_=ss_ap, func=AF.Ln, scale=1.0 / n, bias=EPS),
                 keys, keys)
            T.op(ACT, lambda: A.activation(out=out_ap, in_=out_ap, func=AF.Exp, scale=-0.5),
                 keys, keys)

        def norm_transpose(src_ap, xb, xk, nb, nbk, nTt, nTk, grep, grepk, dq, dsem, bank=5):
            T.dma(dq, xb[:], src_ap, [], [xk], dsem)
            T.op(DVE, lambda: V.scalar_tensor_tensor(out=junk[:], in0=xb[:], scalar=1.0, in1=xb[:],
                                                     op0=ALU.mult, op1=ALU.mult, accum_out=st[:, 0:1]),
                 [xk], ["junk", "st0"])
            rstd_from_ss(st[:, 0:1], st[:, 1:2], float(D), ["st0"])
            T.op(DVE, lambda: V.scalar_tensor_tensor(out=nb[:], in0=xb[:], scalar=st[:, 1:2], in1=grep[:],
                                                     op0=ALU.mult, op1=ALU.mult),
                 [xk, "st0", grepk], [nbk])
            bk = "pb%d" % bank
            for c in range(8):
                T.op(PE, lambda c=c: PEe.transpose(out=pbb[bank][:, c * 128:(c + 1) * 128],
                                                   in_=nb[:, c * 128:(c + 1) * 128], identity=identb[:]),
                     [nbk, "identb"], [bk], inc=(c == 7))
            T.op(ACT, lambda: A.copy(out=nTt[:], in_=pbb[bank][:, 0:1024]), [bk], [nTk])

        def project(nTt, nTk, W, Wk, groups):
            for (bank, pc0, wc0, ncol) in groups:
                bk = "pb%d" % bank
                for c in range(8):
                    T.op(PE, lambda c=c, bank=bank, pc0=pc0, wc0=wc0, ncol=ncol:
                         PEe.matmul(pb[bank][:, pc0:pc0 + ncol], lhsT=nTt[:, c * 128:(c + 1) * 128],
                                    rhs=W[:, c, wc0:wc0 + ncol], start=(c == 0), stop=(c == 7)),
                         [nTk, Wk], [bk], inc=(c == 7))

        with contextlib.ExitStack() as mst:
            def sb(n, s, d):
                return mst.enter_context(nc.sbuf_tensor(n, s, d))

            KT = sb("KT", [128, 2, T_], BF16)
            VE = sb("VE", [128, NT, 2, 130], BF16)
            QT = sb("QT", [128, 4, TO], BF16)
            WB = sb("WB", [128, 8, 2064], BF16)
            MK = sb("MK", [128, 16, 512], BF16)
            tabf = sb("tabf", [128, 260], F32)
            gat = sb("gat", [128, D], F32)
            qg = sb("qg", [128, 512], F32)
            brep = sb("brep", [128, 256], F32)
            gn = sb("gn", [128, 256], F32)
            wupb = sb("wupb", [16, 256], BF16)
            maskA = sb("maskA", [128, 128], BF16)
            xbs = [sb("xb%d" % i, [128, D], F32) for i in range(3)]
            ssx = sb("ssx", [128, 4], F32)
            rstd_all = sb("rstd_all", [128, NT], F32)
            nbs = [sb("nb%d" % i, [128, D], BF16) for i in range(2)]
            nTs = [sb("nT%d" % i, [128, D], BF16) for i in range(2)]
            kf = sb("kf", [128, 512], F32)
            sq = sb("sq", [128, 512], F32)
            k16s = [sb("k16_%d" % i, [128, 512], BF16) for i in range(2)]
            gkf = [sb("gkf%d" % i, [128, 256], F32) for i in range(2)]
            glr16s = [sb("glr16_%d" % i, [128, 16], BF16) for i in range(2)]
            glrT = sb("glrT", [16, 128], BF16)
            zb = sb("zb", [128, 256], F32)
            lg = sb("lg", [128, 256], F32)
            eb = sb("eb", [128, 256], F32)
            enb = sb("enb", [128, 256], F32)
            ec = sb("ec", [128, 256], F32)
            dec = sb("dec", [128, 4], F32)
            qt16 = sb("qt16", [128, 256], BF16)
            kt16 = sb("kt16", [128, 256], BF16)
            kh16 = sb("kh16", [128, 256], BF16)
            vb16s = [sb("vb16_%d" % i, [128, 512], BF16) for i in range(2)]
            qkT = sb("qkT", [128, 512], BF16)
            AT = sb("AT", [128, 512], BF16)
            S = sb("S", [128, 256], F32)
            Sb = [sb("Sb%d" % i, [128, 256], BF16) for i in range(2)]
            snap = sb("snap", [128, NU, 256], F32)
            sgt = sb("sgt", [128, 512], F32)
            gg = sb("gg", [128, 512], F32)
            ogl = sb("ogl", [128, 512], BF16)
            PTs = [sb("PT%d" % i, [128, 512], BF16) for i in range(6)]
            accS = sb("accS", [128, 3, 390], F32)
            t1 = sb("t1", [128, 128], F32)
            of = sb("of", [128, 128], F32)
            og = sb("og", [128, 4, 128], BF16)

            T.dma(SP, tabf[:], tabs[:, :], [], ["tabf"], "c1")
            T.dma(SP, gat[:], g_attn[:, :], [], ["gat"], "c2")
            T.dma(SP, kf[:], qk_rep[:, 0:512], [], ["kf"], "c3")
            T.dma(SP, sq[:], qk_rep[:, 512:1024], [], ["sq"], "c3b")
            T.dma(SP, brep[:], b_rep[:, :], [], ["brep"], "c4")
            T.dma(SP, gn[:], gn_rep[:, :], [], ["gn"], "c5")
            lam = zb
            T.dma(SP, lam[:], lamv[:, :], [], ["zb"], "c6")
            T.dma(POOL, wupb[:], w_up[:, :], [], ["wupb"], "c7")
            T.dma(POOL, maskA[:], cst[:, 384:512], [], ["maskA"], "c8")
            T.dma(POOL, WB[:, :, 0:1296], w_p0.rearrange("(c p) n -> p c n", p=128), [], ["WB"], "w0")
            for e4 in range(4):
                T.dma(POOL, MK[:, e4 * 4:(e4 + 1) * 4, :],
                      msk[:, e4 * 2048:(e4 + 1) * 2048].rearrange("p (a b) -> p a b", b=512),
                      [], ["MK"], "c9")
            T.op(DVE, lambda: V.scalar_tensor_tensor(out=qg[:, 0:512], in0=kf[:, 0:512], scalar=0.125,
                                                     in1=sq[:, 0:512], op0=ALU.mult, op1=ALU.mult),
                 ["kf", "sq"], ["qg"])
            T.op(DVE, lambda: V.tensor_scalar(out=gn[:, 0:128], in0=gn[:, 0:128], scalar1=1.0 - LAMBDA_INIT,
                                              scalar2=None, op0=ALU.mult), ["gn"], ["gn"])
            T.op(DVE, lambda: V.scalar_tensor_tensor(out=junk[:, 0:64], in0=lam[:, 0:64], scalar=1.0,
                                                     in1=lam[:, 64:128], op0=ALU.mult, op1=ALU.mult,
                                                     accum_out=st[:, 4:5]), ["zb"], ["junk", "stl"])
            T.op(DVE, lambda: V.scalar_tensor_tensor(out=junk[:, 0:64], in0=lam[:, 128:192], scalar=1.0,
                                                     in1=lam[:, 192:256], op0=ALU.mult, op1=ALU.mult,
                                                     accum_out=st[:, 5:6]), ["zb", "stl"], ["junk", "stl"])
            T.op(ACT, lambda: A.activation(out=st[:, 6:8], in_=st[:, 4:6], func=AF.Exp), ["stl"], ["stl"])
            T.op(DVE, lambda: V.tensor_tensor(out=st[:, 8:9], in0=st[:, 7:8], in1=st[:, 6:7], op=ALU.subtract),
                 ["stl"], ["stl"])
            T.op(DVE, lambda: V.tensor_scalar(out=st[:, 8:9], in0=st[:, 8:9], scalar1=-LAMBDA_INIT,
                                              scalar2=None, op0=ALU.add), ["stl"], ["stl"])
            neglam = st[:, 8:9]
            T.op(POOL, lambda: G.memset(VE[:, :, :, 128:130], 1.0), [], ["VE"])
            T.op(DVE, lambda: V.memset(S[:], 0.0), [], ["S"])

            def k_evac_a(bank, c0, ngrp):
                n = 64 * ngrp
                T.op(ACT, lambda: A.copy(out=kf[:, 0:n], in_=pb[bank][:, c0:c0 + n]), ["pb%d" % bank], ["kf"])

            def k_evac_b(ngrp):
                n = 64 * ngrp
                T.op(DVE, lambda: V.tensor_tensor(out=sq[:, 0:n], in0=kf[:, 0:n], in1=kf[:, 0:n], op=ALU.mult),
                     ["kf"], ["sq"])
                T.op(DVE, lambda: V.tensor_reduce(out=st[:, 16:16 + ngrp],
                                                  in_=sq[:, 0:n].rearrange("p (a b) -> p a b", b=64),
                                                  axis=AX.X, op=ALU.add), ["sq"], ["stk"])
                rstd_from_ss(st[:, 16:16 + ngrp], st[:, 24:24 + ngrp], 64.0, ["stk"])

            def k_evac_c(ngrp, ks, gain=None):
                n = 64 * ngrp
                kk = "k16_%d" % ks
                bc = bass.AP(st[:].tensor, st[:, 24:25].offset, [list(st[:].ap[0]), [1, ngrp], [0, 64]])
                if gain is None:
                    T.op(DVE, lambda: V.tensor_tensor(out=k16s[ks][:, 0:n].rearrange("p (a b) -> p a b", b=64),
                                                      in0=kf[:, 0:n].rearrange("p (a b) -> p a b", b=64),
                                                      in1=bc, op=ALU.mult), ["kf", "stk"], [kk])
                else:
                    T.op(DVE, lambda: V.tensor_tensor(out=sq[:, 0:n].rearrange("p (a b) -> p a b", b=64),
                                                      in0=kf[:, 0:n].rearrange("p (a b) -> p a b", b=64),
                                                      in1=bc, op=ALU.mult), ["kf", "stk", "sq"], ["sq"])
                    T.op(DVE, lambda: V.tensor_tensor(out=k16s[ks][:, 0:n], in0=sq[:, 0:n], in1=gain,
                                                      op=ALU.mult), ["sq", "qg"], [kk])

            def k_evac(bank, c0, ngrp, ks, gain=None):
                k_evac_a(bank, c0, ngrp)
                k_evac_b(ngrp)
                k_evac_c(ngrp, ks, gain)

            def k_tr(ks, n, dst_fn):
                nh = n // 128
                for hh in range(nh):
                    T.op(PE, lambda hh=hh: PEe.transpose(out=pbb[4][:, hh * 128:(hh + 1) * 128],
                                                         in_=k16s[ks][:, hh * 128:(hh + 1) * 128], identity=identb[:]),
                         ["k16_%d" % ks, "identb"], ["pb4"], inc=(hh == nh - 1))
                dst_fn(pbb[4][:, 0:n].rearrange("p (a b) -> p a b", b=128))

            def gla_segs(own, i_own, gk, gkk, gq, gqk, vb, vbk, g16, g16k, psS):
                def s_a():
                    T.op(PE, lambda: PEe.transpose(out=pbb[6][0:16, 64:192], in_=g16[:], identity=identb[:]),
                         [g16k, "identb"], ["pb6"])
                    T.op(ACT, lambda: A.copy(out=glrT[:], in_=pbb[6][0:16, 64:192]), ["pb6"], ["glrT"])

                def s_b():
                    T.op(PE, lambda: PEe.matmul(pb[6][:, 128:384], lhsT=glrT[:], rhs=wupb[:], start=True, stop=True),
                         ["glrT", "wupb"], ["pb6"])
                    T.op(DVE, lambda: V.tensor_tensor(out=zb[:], in0=pb[6][:, 128:384], in1=brep[:], op=ALU.add),
                         ["pb6", "brep"], ["zb"])
                    T.op(ACT, lambda: A.activation(out=lg[:], in_=zb[:], func=AF.Exp, scale=-1.0), ["zb"], ["lg"])
                    T.op(ACT, lambda: A.activation(out=lg[:], in_=lg[:], func=AF.Ln, bias=1.0), ["lg"], ["lg"])

                def s_c():
                    if own:
                        T.op(PE, lambda: PEe.matmul(pb[7][:, 0:256], lhsT=triS, rhs=lg[:], start=True, stop=True),
                             ["cstf", "lg"], ["pb7"], inc=False)
                    T.op(PE, lambda: PEe.matmul(pb[7][:, 256:512], lhsT=upS, rhs=lg[:], start=True, stop=True),
                         ["cstf", "lg"], ["pb7"])
                    for p in range(2):
                        T.op(PE, lambda p=p: PEe.matmul(pb[6][:, 96 + 2 * p:98 + 2 * p], lhsT=lg[:, p * 128:(p + 1) * 128],
                                                        rhs=chI, start=True, stop=True),
                             ["lg", "cstf"], ["pb6"], inc=(p == 1))
                    T.op(ACT, lambda: A.activation(out=ec[:], in_=pb[7][:, 256:512], func=AF.Exp), ["pb7"], ["ec"])
                    if own:
                        T.op(ACT, lambda: A.activation(out=eb[:], in_=pb[7][:, 0:256], func=AF.Exp), ["pb7"], ["eb"])
                        T.op(ACT, lambda: A.activation(out=enb[:], in_=pb[7][:, 0:256], func=AF.Exp, scale=-1.0),
                             ["pb7"], ["enb"])
                    T.op(ACT, lambda: A.activation(out=dec[:], in_=pb[6][:, 96:100], func=AF.Exp), ["pb6"], ["dec"])
                    T.op(DVE, lambda: V.tensor_tensor(out=kh16[:], in0=gk, in1=ec[:], op=ALU.mult),
                         [gkk, "ec"], ["kh16"])
                    if own:
                        T.op(DVE, lambda: V.scalar_tensor_tensor(out=qt16[:], in0=gq, scalar=0.125,
                                                                 in1=eb[:], op0=ALU.mult, op1=ALU.mult),
                             [gqk, "eb"], ["qt16"])
                        T.op(DVE, lambda: V.tensor_tensor(out=kt16[:], in0=gk, in1=enb[:], op=ALU.mult),
                             [gkk, "enb"], ["kt16"])
                        for p in range(2):
                            T.op(PE, lambda p=p: PEe.transpose(out=pbb[4][:, 512 + p * 128:512 + (p + 1) * 128],
                                                               in_=qt16[:, p * 128:(p + 1) * 128], identity=identb[:]),
                                 ["qt16", "identb"], ["pb4"], inc=False)
                        for p in range(2):
                            T.op(PE, lambda p=p: PEe.transpose(out=pbb[4][:, 768 + p * 128:768 + (p + 1) * 128],
                                                               in_=kt16[:, p * 128:(p + 1) * 128], identity=identb[:]),
                                 ["kt16", "identb"], ["pb4"], inc=(p == 1))
                        T.op(ACT, lambda: A.copy(out=qkT[:], in_=pbb[4][:, 512:1024]), ["pb4"], ["qkT"])

                def s_c2():
                    for hh in range(2):
                        bank = 0 if hh == 0 else 3
                        for p in range(2):
                            T.op(PE, lambda hh=hh, p=p, bank=bank:
                                 PEe.matmul(pb[bank][:, p * 128:(p + 1) * 128],
                                            lhsT=qkT[hh * 64:(hh + 1) * 64, 256 + p * 128:256 + (p + 1) * 128],
                                            rhs=qkT[hh * 64:(hh + 1) * 64, p * 128:(p + 1) * 128],
                                            start=True, stop=True),
                                 ["qkT"], ["pb%d" % bank], inc=(p == 1))
                    mA = bass.AP(maskA[:].tensor, maskA[:].offset, [list(maskA[:].ap[0]), [0, 2], [1, 128]])
                    for hh in range(2):
                        bank = 0 if hh == 0 else 3
                        outv = bass.AP(AT[:].tensor, AT[:, hh * 128:hh * 128 + 1].offset,
                                       [list(AT[:].ap[0]), [256, 2], [1, 128]])
                        T.op(DVE, lambda bank=bank, outv=outv: V.tensor_tensor(
                            out=outv, in0=pb[bank][:, 0:256].rearrange("p (a b) -> p a b", b=128), in1=mA, op=ALU.mult),
                            ["pb%d" % bank, "maskA"], ["AT"])

                def s_d(ch):
                    def f():
                        if own:
                            T.op(POOL, lambda: G.tensor_copy(out=Sb[ch][:], in_=S[:]), ["S"], ["Sb%d" % ch])
                        sbank, scol = psS[ch]
                        sk = "pb%d" % sbank
                        for p in range(2):
                            for hh in range(2):
                                h = 2 * p + hh
                                T.op(PE, lambda p=p, hh=hh, h=h:
                                     PEe.matmul(pb[sbank][hh * 64:(hh + 1) * 64, scol + p * 128:scol + (p + 1) * 128],
                                                lhsT=kh16[ch * 64:(ch + 1) * 64, h * 64:(h + 1) * 64],
                                                rhs=vb[ch * 64:(ch + 1) * 64, h * 128:(h + 1) * 128],
                                                start=True, stop=True),
                                     ["kh16", vbk], [sk], inc=(p == 1 and hh == 1))
                        for p in range(2):
                            T.op(DVE, lambda p=p:
                                 V.scalar_tensor_tensor(out=S[:, p * 128:(p + 1) * 128], in0=S[:, p * 128:(p + 1) * 128],
                                                        scalar=dec[:, 2 * p + ch:2 * p + ch + 1],
                                                        in1=pb[sbank][:, scol + p * 128:scol + (p + 1) * 128],
                                                        op0=ALU.mult, op1=ALU.add),
                                 ["S", "dec", sk], ["S"])
                    return f

                def s_o():
                    for hh in range(2):
                        bank = 0 if hh == 0 else 3
                        bk = "pb%d" % bank
                        for p in range(2):
                            h = 2 * p + hh
                            oc = 256 + p * 128
                            T.op(PE, lambda h=h, bank=bank, oc=oc:
                                 PEe.matmul(pb[bank][:, oc:oc + 128], lhsT=AT[:, h * 128:(h + 1) * 128],
                                            rhs=vb[:, h * 128:(h + 1) * 128], start=True, stop=False),
                                 ["AT", vbk], [bk], inc=False)
                            for ch in range(2):
                                T.op(PE, lambda h=h, hh=hh, p=p, ch=ch, bank=bank, oc=oc:
                                     PEe.matmul(pb[bank][ch * 64:(ch + 1) * 64, oc:oc + 128],
                                                lhsT=qkT[hh * 64:(hh + 1) * 64, p * 128 + ch * 64:p * 128 + (ch + 1) * 64],
                                                rhs=Sb[ch][hh * 64:(hh + 1) * 64, p * 128:(p + 1) * 128],
                                                start=False, stop=True),
                                     ["qkT", "Sb%d" % ch], [bk], inc=(ch == 1 and p == 1))
                    for hh in range(2):
                        bank = 0 if hh == 0 else 3
                        bk = "pb%d" % bank
                        T.op(ACT, lambda bank=bank: A.copy(out=kf[:, 0:256], in_=pb[bank][:, 256:512]), [bk], ["kf"])
                        T.op(DVE, lambda: V.tensor_tensor(out=sq[:, 0:256], in0=kf[:, 0:256], in1=kf[:, 0:256],
                                                          op=ALU.mult), ["kf"], ["sq"])
                        T.op(DVE, lambda: V.tensor_reduce(out=st[:, 16:18],
                                                          in_=sq[:, 0:256].rearrange("p (a b) -> p a b", b=128),
                                                          axis=AX.X, op=ALU.add), ["sq"], ["stk"])
                        rstd_from_ss(st[:, 16:18], st[:, 24:26], 128.0, ["stk"])
                        bc = bass.AP(st[:].tensor, st[:, 24:25].offset, [list(st[:].ap[0]), [1, 2], [0, 128]])
                        ggv = bass.AP(gg[:].tensor, gg[:, hh * 128:hh * 128 + 1].offset,
                                      [list(gg[:].ap[0]), [256, 2], [1, 128]])
                        oglv = bass.AP(ogl[:].tensor, ogl[:, hh * 128:hh * 128 + 1].offset,
                                       [list(ogl[:].ap[0]), [256, 2], [1, 128]])
                        T.op(DVE, lambda bc=bc: V.tensor_tensor(out=sq[:, 256:512].rearrange("p (a b) -> p a b", b=128),
                                                                in0=kf[:, 0:256].rearrange("p (a b) -> p a b", b=128),
                                                                in1=bc, op=ALU.mult), ["kf", "stk"], ["sq2"])
                        T.op(DVE, lambda ggv=ggv, oglv=oglv: V.tensor_tensor(
                            out=oglv, in0=sq[:, 256:512].rearrange("p (a b) -> p a b", b=128), in1=ggv, op=ALU.mult),
                            ["sq2", "gg"], ["ogl"])
                    T.dma(SP, mix_scr[i_own * 128:(i_own + 1) * 128, 512:1024], ogl[:], ["ogl"], ["mixg%d" % i_own], "mx")

                if own:
                    return [s_a, s_b, s_c, s_c2, s_d(0), s_d(1), s_o]
                return [s_a, s_b, s_c, s_d(0), s_d(1)]

            def seg_load(t):
                sl = t % 3
                T.dma(SP, xbs[sl][:], x_all[t * 128:(t + 1) * 128, :], [], ["xb%d" % sl], "x%d" % sl)

            def seg_stats(t):
                sl = t % 3
                T.op(DVE, lambda: V.scalar_tensor_tensor(out=junk[:], in0=xbs[sl][:], scalar=1.0, in1=xbs[sl][:],
                                                         op0=ALU.mult, op1=ALU.mult, accum_out=ssx[:, sl:sl + 1]),
                     ["xb%d" % sl], ["junk", "ssx%d" % sl])
                T.op(ACT, lambda: A.activation(out=rstd_all[:, t:t + 1], in_=ssx[:, sl:sl + 1], func=AF.Ln,
                                               scale=1.0 / D, bias=EPS), ["ssx%d" % sl], ["rs%d" % t])
                T.op(ACT, lambda: A.activation(out=rstd_all[:, t:t + 1], in_=rstd_all[:, t:t + 1], func=AF.Exp, scale=-0.5),
                     ["rs%d" % t], ["rs%d" % t])

            def seg_n(t):
                sl = t % 3
                ns = t % 2
                T.op(DVE, lambda: V.scalar_tensor_tensor(out=nbs[ns][:], in0=xbs[sl][:], scalar=rstd_all[:, t:t + 1],
                                                         in1=gat[:], op0=ALU.mult, op1=ALU.mult),
                     ["xb%d" % sl, "rs%d" % t, "gat"], ["nb%d" % ns])

            def seg_tr(t):
                ns = t % 2
                for c in range(8):
                    T.op(PE, lambda c=c: PEe.transpose(out=pbb[3][:, c * 128:(c + 1) * 128],
                                                       in_=nbs[ns][:, c * 128:(c + 1) * 128], identity=identb[:]),
                         ["nb%d" % ns, "identb"], ["pb3"], inc=(c == 7))
                T.op(ACT, lambda: A.copy(out=nTs[ns][:], in_=pbb[3][:, 0:1024]), ["pb3"], ["nT%d" % ns])

            def run_pipe(n_tiles, early_fn, main_fn, late_fn):
                seg_load(0)
                for it in range(-1, n_tiles + 1):
                    early = early_fn(it + 1) if 0 <= it + 1 < n_tiles else []
                    main = main_fn(it) if 0 <= it < n_tiles else []
                    late = late_fn(it - 1) if 0 <= it - 1 < n_tiles else []
                    for k in range(max(len(early), len(main), len(late))):
                        if k < len(main):
                            main[k]()
                        if k < len(late):
                            late[k]()
                        if k < len(early):
                            early[k]()

            def kv_store(t):
                def dst(src3):
                    T.op(ACT, lambda: A.copy(out=KT[:, :, t * 128:(t + 1) * 128], in_=src3), ["pb4"], ["KT"])
                return dst

            def v_store(t):
                T.op(DVE, lambda: V.tensor_copy(out=VE[:, t, :, 0:128],
                                                in_=pb[0][:, 256:512].rearrange("p (a b) -> p a b", b=128)),
                     ["pb0"], ["VE"])

            def p0_iter(it):
                ok = lambda t: 0 <= t < NT
                if ok(it + 2):
                    seg_stats(it + 2)
                if ok(it + 1):
                    seg_tr(it + 1)
                late = []
                if ok(it - 1):
                    t1 = it - 1
                    n1 = t1 % 2
                    late = gla_segs(False, None, gkf[n1][:], "gkf%d" % n1, None, None, vb16s[n1], "vb16_%d" % n1,
                                    glr16s[n1], "glr16_%d" % n1, ((5, 0), (5, 256)))
                ns = it % 2
                if ok(it):
                    project(nTs[ns], "nT%d" % ns, WB, "WB", [(0, 0, 0, 512)])
                    k_evac_a(0, 0, 4)
                    v_store(it)
                    k_evac_b(4)
                if ok(it - 1):
                    t1 = it - 1
                    k_tr(t1 % 2, 256, kv_store(t1))
                    if t1 % 16 % 4 == 0:
                        u = t1 // 16
                        r = (t1 % 16) // 4
                        if r == 0:
                            T.op(DVE, lambda: V.tensor_scalar(out=snap[:, u, :], in0=S[:],
                                                              scalar1=tabf[:, 256 + r:257 + r], scalar2=None,
                                                              op0=ALU.mult), ["S", "tabf"], ["snap"])
                        else:
                            T.op(DVE, lambda: V.scalar_tensor_tensor(out=snap[:, u, :], in0=S[:],
                                                                     scalar=tabf[:, 256 + r:257 + r],
                                                                     in1=snap[:, u, :], op0=ALU.mult, op1=ALU.add),
                                 ["S", "tabf", "snap"], ["snap"])
                    late[0]()
                if ok(it + 2):
                    seg_n(it + 2)
                if ok(it):
                    project(nTs[ns], "nT%d" % ns, WB, "WB", [(1, 0, 512, 272)])
                    T.op(DVE, lambda: V.tensor_copy(out=gkf[ns][:], in_=pb[1][:, 0:256]), ["pb1"], ["gkf%d" % ns])
                    T.op(DVE, lambda: V.tensor_copy(out=glr16s[ns][:], in_=pb[1][:, 256:272]), ["pb1"], ["glr16_%d" % ns])
                    k_evac_c(4, ns)
                if ok(it - 1):
                    late[1]()
                if ok(it):
                    project(nTs[ns], "nT%d" % ns, WB, "WB", [(2, 0, 784, 512)])
                    T.op(ACT, lambda: A.copy(out=vb16s[ns][:], in_=pb[2][:, :]), ["pb2"], ["vb16_%d" % ns])
                if ok(it - 1):
                    late[2]()
                    late[3]()
                    late[4]()
                if ok(it + 3):
                    seg_load(it + 3)

            for t in range(min(3, NT)):
                seg_load(t)
            for it in range(-2, NT + 1):
                p0_iter(it)

            T.dma(POOL, WB[:, :, :], w_ow.rearrange("(c p) n -> p c n", p=128), [], ["WB"], "w0")

            def q_store(i):
                def dst(src3):
                    T.op(ACT, lambda: A.copy(out=QT[:, :, i * 128:(i + 1) * 128], in_=src3), ["pb4"], ["QT"])
                return dst

            gnb = bass.AP(gn[:].tensor, gn[:, 128:129].offset, [list(gn[:].ap[0]), [0, 4], [1, 128]])
            for i in range(NO):
                s = i % 2
                if i % 4 == 0:
                    T.op(DVE, lambda i=i: V.tensor_copy(out=S[:], in_=snap[:, i // 4, :]), ["snap"], ["S"])
                norm_transpose(x_own[i * 128:(i + 1) * 128, :], xbs[s], "xb%d" % s, nbs[s], "nb%d" % s,
                               nTs[s], "nT%d" % s, gat, "gat", SP, "x%d" % s)
                project(nTs[s], "nT%d" % s, WB, "WB",
                        [(0, 0, 0, 512), (1, 0, 512, 512), (2, 0, 1024, 512), (3, 0, 1536, 512), (6, 0, 2048, 16)])
                k_evac(0, 0, 8, 0, gain=qg[:, 0:512])
                k_tr(0, 512, q_store(i))
                T.op(ACT, lambda: A.activation(out=sgt[:], in_=pb[3][:, :], func=AF.Exp, scale=-1.0), ["pb3"], ["sgt"])
                T.op(DVE, lambda: V.tensor_scalar(out=sgt[:], in0=sgt[:], scalar1=1.0, scalar2=None, op0=ALU.add),
                     ["sgt"], ["sgt"])
                T.op(DVE, lambda: V.reciprocal(out=sgt[:], in_=sgt[:]), ["sgt"], ["sgt"])
                T.op(DVE, lambda: V.tensor_tensor(out=gg[:], in0=pb[3][:, :], in1=sgt[:], op=ALU.mult),
                     ["pb3", "sgt"], ["gg"])
                T.op(DVE, lambda: V.tensor_tensor(out=gg[:].rearrange("p (a b) -> p a b", b=128),
                                                  in0=gg[:].rearrange("p (a b) -> p a b", b=128), in1=gnb, op=ALU.mult),
                     ["gg", "gn"], ["gg"])
                T.op(DVE, lambda: V.tensor_copy(out=glr16s[0][:], in_=pb[6][:, 0:16]), ["pb6"], ["glr16_0"])
                T.op(ACT, lambda: A.copy(out=vb16s[0][:], in_=pb[2][:, :]), ["pb2"], ["vb16_0"])
                for sg in gla_segs(True, i, pb[1][:, 256:512], "pb1", pb[1][:, 0:256], "pb1", vb16s[0], "vb16_0",
                                   glr16s[0], "glr16_0", ((2, 0), (5, 0))):
                    sg()

            def acc_ap(a):
                bank = 5 + a // 3
                col = (a % 3) * 130
                return bank, col

            def attention(hp):
                steps = [(u, hl, kb) for u in range(NU) for hl in range(2) for kb in range(16 * u + 16)]
                LOOK = 2

                def emit_qk(idx):
                    u, hl, kb = steps[idx]
                    h = 2 * hp + hl
                    e = kb - 16 * u
                    ei = e + 48
                    for m in range(2):
                        sbank = 1 + (2 * idx + m) % 4
                        T.op(PE, lambda m=m, sbank=sbank:
                             PEe.matmul(pb[sbank][:, 0:512],
                                        lhsT=KT[m * 64:(m + 1) * 64, hl, kb * 128:(kb + 1) * 128],
                                        rhs=QT[m * 64:(m + 1) * 64, h, u * 512:(u + 1) * 512],
                                        start=True, stop=True),
                             ["KT", "QT"], ["pb%d" % sbank])
                    for m in range(2):
                        sbank = 1 + (2 * idx + m) % 4
                        slot = (2 * idx + m) % 6
                        T.op(ACT, lambda sbank=sbank, slot=slot:
                             A.activation(out=PTs[slot][:], in_=pb[sbank][:, 0:512], func=AF.Exp,
                                          bias=tabf[:, h * 64 + ei:h * 64 + ei + 1]),
                             ["pb%d" % sbank, "tabf"], ["PT%d" % slot])
                        if e >= 0:
                            T.op(POOL, lambda slot=slot:
                                 G.tensor_tensor(out=PTs[slot][:], in0=PTs[slot][:], in1=MK[:, e, :], op=ALU.mult),
                                 ["PT%d" % slot, "MK"], ["PT%d" % slot])

                def emit_pv(idx):
                    u, hl, kb = steps[idx]
                    h = 2 * hp + hl
                    nkb = 16 * u + 16
                    for m in range(2):
                        slot = (2 * idx + m) % 6
                        for c in range(4):
                            bank, col = acc_ap(c * 2 + m)
                            first = (kb == 0 and (c * 2 + m) in (0, 4, 6))
                            T.op(PE, lambda c=c, bank=bank, col=col, slot=slot, first=first:
                                 PEe.matmul(pb[bank][:, col:col + 130],
                                            lhsT=PTs[slot][:, c * 128:(c + 1) * 128],
                                            rhs=VE[:, kb, hl, :], start=first, stop=(kb == nkb - 1),
                                            skip_group_check=True),
                                 ["PT%d" % slot, "VE"], ["pb%d" % bank], inc=(c == 3 and m == 1))
                    if kb != nkb - 1:
                        return
                    for bi in range(3):
                        T.op(DVE, lambda bi=bi: V.tensor_copy(out=accS[:, bi, :], in_=pb[5 + bi][:, 0:390]),
                             ["pb%d" % (5 + bi)], ["accS"])
                    for c in range(4):
                        a0 = c * 2
                        a1 = c * 2 + 1
                        A0 = accS[:, a0 // 3, (a0 % 3) * 130:(a0 % 3) * 130 + 130]
                        A1 = accS[:, a1 // 3, (a1 % 3) * 130:(a1 % 3) * 130 + 130]
                        T.op(DVE, lambda A0=A0: V.reciprocal(out=st[:, 10:11], in_=A0[:, 128:129]), ["accS"], ["sta"])
                        T.op(DVE, lambda A1=A1: V.reciprocal(out=st[:, 11:12], in_=A1[:, 128:129]), ["accS", "sta"], ["sta"])
                        T.op(DVE, lambda: V.tensor_tensor(out=st[:, 11:12], in0=st[:, 11:12], in1=neglam, op=ALU.mult),
                             ["sta", "stl"], ["sta"])
                        T.op(DVE, lambda A1=A1: V.tensor_scalar(out=t1[:], in0=A1[:, 0:128], scalar1=st[:, 11:12],
                                                                scalar2=None, op0=ALU.mult), ["accS", "sta"], ["t1"])
                        T.op(DVE, lambda A0=A0: V.scalar_tensor_tensor(out=of[:], in0=A0[:, 0:128], scalar=st[:, 10:11],
                                                                       in1=t1[:], op0=ALU.mult, op1=ALU.add),
                             ["accS", "sta", "t1"], ["of"])
                        T.op(DVE, lambda: V.scalar_tensor_tensor(out=junk[:, 0:128], in0=of[:], scalar=1.0, in1=of[:],
                                                                 op0=ALU.mult, op1=ALU.mult, accum_out=st[:, 12:13]),
                             ["of"], ["junk", "stb"])
                        rstd_from_ss(st[:, 12:13], st[:, 13:14], 128.0, ["stb"])
                        T.op(DVE, lambda c=c: V.scalar_tensor_tensor(out=og[:, c, :], in0=of[:], scalar=st[:, 13:14],
                                                                     in1=gn[:, 0:128], op0=ALU.mult, op1=ALU.mult),
                             ["of", "stb", "gn"], ["og"])
                    T.dma(SP, mix_scr[u * 512:(u + 1) * 512, h * 128:(h + 1) * 128].rearrange("(c p) n -> p c n", p=128),
                          og[:], ["og"], ["mixd%d_%d" % (u, h)], "mx2")

                for i in range(len(steps) + LOOK):
                    if i < len(steps):
                        emit_qk(i)
                    if i - LOOK >= 0:
                        emit_pv(i - LOOK)

            attention(0)

            T.dma(POOL, WB[:, :, 0:512], w_p1.rearrange("(c p) n -> p c n", p=128), [], ["WB"], "w0")

            def p1_iter(it):
                ok = lambda t: 0 <= t < NT
                if ok(it + 2):
                    seg_n(it + 2)
                if ok(it + 1):
                    seg_tr(it + 1)
                ns = it % 2
                if ok(it):
                    project(nTs[ns], "nT%d" % ns, WB, "WB", [(0, 0, 0, 512)])
                    k_evac_a(0, 0, 4)
                    v_store(it)
                    k_evac_b(4)
                if ok(it - 1):
                    k_tr((it - 1) % 2, 256, kv_store(it - 1))
                if ok(it):
                    k_evac_c(4, ns)
                if ok(it + 3):
                    seg_load(it + 3)

            for t in range(min(3, NT)):
                seg_load(t)
            for it in range(-2, NT + 1):
                p1_iter(it)
            attention(1)
            T.barrier()
            mix_keys = ["mixg%d" % i for i in range(NO)] + ["mixd%d_%d" % (u, h) for u in range(NU) for h in range(4)]

        with contextlib.ExitStack() as fst:
            def sb(n, s, d):
                return fst.enter_context(nc.sbuf_tensor(n, s, d))

            WG = sb("WG", [128, 8, DFF], BF16)
            WU = sb("WU", [128, 8, DFF], BF16)
            WD = sb("WD", [128, NFF, D], BF16)
            gff = sb("gff", [128, D], F32)
            hbs = [sb("hb%d" % i, [128, D], F32) for i in range(2)]
            nbs = [sb("fnb%d" % i, [128, D], BF16) for i in range(2)]
            obs = [sb("ob%d" % i, [128, D], F32) for i in range(2)]
            T.dma(SP, gff[:], g_ffn[:, :], [], ["gff"], "c2")
            with contextlib.ExitStack() as ost:
                def sbo(n, s, d):
                    return ost.enter_context(nc.sbuf_tensor(n, s, d))
                WO = sbo("WO", [128, 8, D], BF16)
                mxs = [sbo("mx%d" % i, [128, D], BF16) for i in range(2)]
                mxT = [sbo("mxT%d" % i, [128, D], BF16) for i in range(2)]
                xos = [sbo("xo%d" % i, [128, D], F32) for i in range(2)]
                T.dma(POOL, WO[:], w_o.rearrange("(c p) n -> p c n", p=128), [], ["WO"], "w1")
                T.dma(POOL, WG[:], w_g.rearrange("(c p) n -> p c n", p=128), [], ["WG"], "w2")
                T.dma(POOL, WU[:], w_u.rearrange("(c p) n -> p c n", p=128), [], ["WU"], "w3")
                T.dma(POOL, WD[:], w_d.rearrange("(c p) n -> p c n", p=128), [], ["WD"], "w4")
                for i in range(NO):
                    s = i % 2
                    T.dma(SP, mxs[s][:], mix_scr[i * 128:(i + 1) * 128, :], mix_keys if i < 2 else [], ["mx%d" % s], "m%d" % s)
                    T.dma(SP, xos[s][:], x_own[i * 128:(i + 1) * 128, :], [], ["xo%d" % s], "xo%d" % s)
                    for c in range(8):
                        T.op(PE, lambda c=c, s=s: PEe.transpose(out=pbb[0][:, c * 128:(c + 1) * 128],
                                                                in_=mxs[s][:, c * 128:(c + 1) * 128], identity=identb[:]),
                             ["mx%d" % s, "identb"], ["pb0"], inc=(c == 7))
                    T.op(ACT, lambda s=s: A.copy(out=mxT[s][:], in_=pbb[0][:, 0:1024]), ["pb0"], ["mxT%d" % s])
                    for nb_ in range(2):
                        bank = 1 + nb_
                        for c in range(8):
                            T.op(PE, lambda c=c, s=s, nb_=nb_, bank=bank:
                                 PEe.matmul(pb[bank][:, 0:512], lhsT=mxT[s][:, c * 128:(c + 1) * 128],
                                            rhs=WO[:, c, nb_ * 512:(nb_ + 1) * 512], start=(c == 0), stop=(c == 7)),
                                 ["mxT%d" % s, "WO"], ["pb%d" % bank], inc=(c == 7))
                        T.op(DVE, lambda s=s, nb_=nb_, bank=bank:
                             V.tensor_tensor(out=hbs[s][:, nb_ * 512:(nb_ + 1) * 512], in0=pb[bank][:, 0:512],
                                             in1=xos[s][:, nb_ * 512:(nb_ + 1) * 512], op=ALU.add),
                             ["pb%d" % bank, "xo%d" % s], ["hb%d" % s])
                    T.dma(SP, h_scr[i * 128:(i + 1) * 128, :], hbs[s][:], ["hb%d" % s], ["hscr%d" % i], "hs%d" % s)
                    if dbg:
                        T.dma(SP, dbg_outs["d_h"][i * 128:(i + 1) * 128, :], hbs[s][:], ["hb%d" % s], [], "dbg")
                        T.op(DVE, lambda s=s: V.tensor_copy(out=xos[s][:], in_=mxs[s][:]), ["mx%d" % s, "xo%d" % s], ["xo%d" % s])
                        T.dma(SP, dbg_outs["d_mix"][i * 128:(i + 1) * 128, :], xos[s][:], ["xo%d" % s], [], "dbg")

            T.barrier()
            mT = sb("mT", [128, 8, 512], BF16)
            actT = sb("actT", [128, NFF, 512], BF16)
            ee = sb("ee", [128, 512], F32)
            uS = sb("uS", [128, 512], F32)
            rr = sb("rr", [128, 512], F32)
            for gI in range(NU):
                for tt in range(4):
                    i = gI * 4 + tt
                    s = i % 2
                    T.dma(SP, hbs[s][:], h_scr[i * 128:(i + 1) * 128, :], ["hscr%d" % i], ["hb%d" % s], "hl%d" % s)
                    T.op(DVE, lambda s=s: V.scalar_tensor_tensor(out=junk[:], in0=hbs[s][:], scalar=1.0, in1=hbs[s][:],
                                                                 op0=ALU.mult, op1=ALU.mult, accum_out=st[:, 0:1]),
                         ["hb%d" % s], ["junk", "st0"])
                    rstd_from_ss(st[:, 0:1], st[:, 1:2], float(D), ["st0"])
                    T.op(DVE, lambda s=s: V.scalar_tensor_tensor(out=nbs[s][:], in0=hbs[s][:], scalar=st[:, 1:2],
                                                                 in1=gff[:], op0=ALU.mult, op1=ALU.mult),
                         ["hb%d" % s, "st0", "gff"], ["fnb%d" % s])
                    for c in range(8):
                        T.op(PE, lambda c=c, s=s: PEe.transpose(out=pbb[0][:, c * 128:(c + 1) * 128],
                                                                in_=nbs[s][:, c * 128:(c + 1) * 128], identity=identb[:]),
                             ["fnb%d" % s, "identb"], ["pb0"], inc=(c == 7))
                    T.op(ACT, lambda tt=tt: A.copy(out=mT[:, :, tt * 128:(tt + 1) * 128],
                                                   in_=pbb[0][:, 0:1024].rearrange("p (a b) -> p a b", b=128)),
                         ["pb0"], ["mT"])
                for f in range(NFF):
                    gb = 1 + (f % 2)
                    ub = 3 + (f % 2)
                    for c in range(8):
                        T.op(PE, lambda c=c, f=f, gb=gb: PEe.matmul(pb[gb][:, 0:512], lhsT=WG[:, c, f * 128:(f + 1) * 128],
                                                                    rhs=mT[:, c, :], start=(c == 0), stop=(c == 7)),
                             ["WG", "mT"], ["pb%d" % gb], inc=(c == 7))
                    for c in range(8):
                        T.op(PE, lambda c=c, f=f, ub=ub: PEe.matmul(pb[ub][:, 0:512], lhsT=WU[:, c, f * 128:(f + 1) * 128],
                                                                    rhs=mT[:, c, :], start=(c == 0), stop=(c == 7)),
                             ["WU", "mT"], ["pb%d" % ub], inc=(c == 7))
                    T.op(ACT, lambda gb=gb: A.activation(out=ee[:], in_=pb[gb][:, 0:512], func=AF.Exp, scale=-1.0),
                         ["pb%d" % gb], ["ee"])
                    T.op(ACT, lambda ub=ub: A.copy(out=uS[:], in_=pb[ub][:, 0:512]), ["pb%d" % ub], ["uS"])
                    T.op(DVE, lambda: V.tensor_scalar(out=ee[:], in0=ee[:], scalar1=1.0, scalar2=None, op0=ALU.add),
                         ["ee"], ["ee"])
                    T.op(DVE, lambda: V.reciprocal(out=rr[:], in_=ee[:]), ["ee"], ["rr"])
                    T.op(DVE, lambda gb=gb: V.tensor_tensor(out=rr[:], in0=pb[gb][:, 0:512], in1=rr[:], op=ALU.mult),
                         ["pb%d" % gb, "rr"], ["rr"])
                    T.op(DVE, lambda f=f: V.tensor_tensor(out=actT[:, f, :], in0=rr[:], in1=uS[:], op=ALU.mult),
                         ["rr", "uS"], ["actT"])
                for tt in range(4):
                    i = gI * 4 + tt
                    s = i % 2
                    T.dma(SP, hbs[s][:], h_scr[i * 128:(i + 1) * 128, :], ["hscr%d" % i], ["hb%d" % s], "hl%d" % s)
                    for nb_ in range(2):
                        bank = 5 + nb_
                        for f in range(NFF):
                            T.op(PE, lambda f=f, tt=tt, nb_=nb_, bank=bank:
                                 PEe.matmul(pb[bank][:, 0:512], lhsT=actT[:, f, tt * 128:(tt + 1) * 128],
                                            rhs=WD[:, f, nb_ * 512:(nb_ + 1) * 512], start=(f == 0), stop=(f == NFF - 1)),
                                 ["actT", "WD"], ["pb%d" % bank], inc=(f == NFF - 1))
                        T.op(DVE, lambda s=s, nb_=nb_, bank=bank:
                             V.tensor_tensor(out=obs[s][:, nb_ * 512:(nb_ + 1) * 512], in0=pb[bank][:, 0:512],
                                             in1=hbs[s][:, nb_ * 512:(nb_ + 1) * 512], op=ALU.add),
                             ["pb%d" % bank, "hb%d" % s], ["ob%d" % s])
                    T.dma(SP, out[i * 128:(i + 1) * 128, :], obs[s][:], ["ob%d" % s], [], "o%d" % s)

        for n in list(T.dsems):
            d = T.dsem(n)
            nc.sync.wait_ge(d.sem, d.count)
    return nc


def _consts():
    c = np.zeros((128, 514), np.float32)
    c[:, 0:128] = np.eye(128, dtype=np.float32)
    s = np.arange(128)[:, None]
    t = np.arange(128)[None, :]
    same = (s // 64) == (t // 64)
    c[:, 128:256] = np.where(same & (s <= t), -1.0 / 16.0, 0.0)
    c[:, 256:384] = np.where(same & (s > t), -1.0 / 16.0, 0.0)
    c[:, 384:512] = np.where(same & (s <= t), 1.0, 0.0)
    c[0:64, 512] = -1.0 / 16.0
    c[64:128, 513] = -1.0 / 16.0
    return c


def _tabs(j):
    tb = np.zeros((128, 260), np.float32)
    kl = np.arange(128, dtype=np.float64)
    for h in range(4):
        for ei in range(64):
            e = ei - 48
            if e <= 4 * j + 3:
                tb[:, h * 64 + ei] = SLOPES[h] * (128.0 * (e - 4 * j) + kl - 256.0)
            else:
                tb[:, h * 64 + ei] = NEG
    tb[:, 256 + j] = 1.0
    return tb


def _masks(j):
    m = np.zeros((128, 16, 4, 128), np.float32)
    k = np.arange(128)[:, None]
    q = np.arange(128)[None, :]
    tri = (k <= q).astype(np.float32)
    for e in range(16):
        for c in range(4):
            cs = 4 * j + c
            if e < cs:
                m[:, e, c, :] = 1.0
            elif e == cs:
                m[:, e, c, :] = tri
    return m.reshape(128, 16 * 512)


def _rep(v, n=128):
    return np.ascontiguousarray(np.broadcast_to(np.asarray(v, np.float32).reshape(1, -1), (n, v.size)))


def prep_inputs(inp, NU=4):
    T_ = NU * 2048
    f = lambda a: np.ascontiguousarray(np.asarray(a, dtype=np.float32))
    x = f(inp["x"])
    w_in = f(inp["w_in"])[0]
    cols = lambda a, b: list(range(a, b))
    dq, dk, dv = 0, 512, 1024
    gq, gk, gv, go, gl = 1536, 1792, 2048, 2560, 3072
    c_p0 = cols(dk, dk + 256) + cols(dv, dv + 256) + cols(gk, gk + 256) + cols(gl, gl + 16) + cols(gv, gv + 512)
    c_p1 = cols(dk + 256, dk + 512) + cols(dv + 256, dv + 512)
    c_ow = cols(dq, dq + 512) + cols(gq, gq + 256) + cols(gk, gk + 256) + cols(gv, gv + 512) + cols(go, go + 512) + cols(gl, gl + 16)
    shared = {
        "w_p0": np.ascontiguousarray(w_in[:, c_p0]),
        "w_p1": np.ascontiguousarray(w_in[:, c_p1]),
        "w_ow": np.ascontiguousarray(w_in[:, c_ow]),
        "w_up": f(inp["w_gla_gate_up"])[0],
        "w_o": f(inp["w_out"])[0],
        "w_g": f(inp["w_ffn_gate"])[0],
        "w_u": f(inp["w_ffn_up"])[0],
        "w_d": f(inp["w_ffn_down"])[0],
        "g_attn": _rep(f(inp["attn_norm_gain"])[0]),
        "g_ffn": _rep(f(inp["ffn_norm_gain"])[0]),
        "qk_rep": np.concatenate([_rep(np.tile(f(inp["q_norm_gain"])[0], 8)),
                                  _rep(np.tile(f(inp["k_norm_gain"])[0], 8))], axis=1),
        "b_rep": _rep(f(inp["b_gla_gate"])[0]),
        "gn_rep": np.concatenate([_rep(f(inp["diff_out_norm_gain"])[0]), _rep(f(inp["gla_out_norm_gain"])[0])], axis=1),
        "lamv": np.concatenate([_rep(f(inp[k])[0]) for k in ("lambda_q1", "lambda_k1", "lambda_q2", "lambda_k2")], axis=1),
        "cst": _consts(),
    }
    in_maps = []
    for core in range(8):
        b, j = core // 4, core % 4
        own_rows = np.concatenate([np.arange((4 * u + j) * 512, (4 * u + j + 1) * 512) for u in range(NU)])
        m = dict(shared)
        m["x_all"] = np.ascontiguousarray(x[b, :T_])
        m["x_own"] = np.ascontiguousarray(x[b, own_rows])
        m["tabs"] = _tabs(j)
        m["msk"] = _masks(j)
        in_maps.append(m)
    return in_maps


def assemble(results, NU=4, B=2, key="out"):
    T_ = NU * 2048
    outp = np.zeros((B, T_, D), np.float32)
    for core in range(8):
        b, j = core // 4, core % 4
        r = np.asarray(results[core][key])
        for u in range(NU):
            outp[b, (4 * u + j) * 512:(4 * u + j + 1) * 512] = r[u * 512:(u + 1) * 512]
    return outp


_NC_CACHE = {}


def kernel(**inputs):
    NU = 4
    if NU not in _NC_CACHE:
        _NC_CACHE[NU] = build(NU)
    nc = _NC_CACHE[NU]
    in_maps = prep_inputs(inputs, NU)
    res = run_bass_kernel_spmd(nc, in_maps, core_ids=list(range(8)))
    return assemble(res.results, NU)
```

```python
import contextlib
import numpy as np
import concourse.bass as bass
import concourse.mybir as mybir
from concourse.bass_utils import run_bass_kernel_spmd

F32 = mybir.dt.float32
BF16 = mybir.dt.bfloat16
AF = mybir.ActivationFunctionType
ALU = mybir.AluOpType
AX = mybir.AxisListType

D = 1024
DFF = 2816
NFF = DFF // 128
EPS = 1e-6
LAMBDA_INIT = 0.8 - 0.6 * 1.0
SLOPES = [2.0 ** (-8.0 * (h + 1.0) / 4.0) for h in range(4)]
NEG = -30000.0
SAME_ENGINE_SYNC = True


class ES:
    def __init__(self, name, eng, sem):
        self.name = name
        self.eng = eng
        self.sem = sem
        self.count = 0
        self.seen = {}


class Trk:
    def __init__(self, nc, stack):
        self.nc = nc
        self.stack = stack
        self.lw = {}
        self.rd = {}
        self.dsems = {}
        mk = lambda n, e: ES(n, e, stack.enter_context(nc.semaphore("s_" + n)))
        self.pe = mk("pe", nc.tensor)
        self.act = mk("act", nc.scalar)
        self.dve = mk("dve", nc.vector)
        self.pool = mk("pool", nc.gpsimd)
        self.sp = mk("sp", nc.sync)

    def dsem(self, name):
        if name not in self.dsems:
            self.dsems[name] = ES("d_" + name, None,
                                  self.stack.enter_context(self.nc.semaphore("d_" + name)))
        return self.dsems[name]

    def _deps(self, reads, writes):
        raw = {}
        other = {}

        def add(d, e, v):
            if d.get(e, 0) < v:
                d[e] = v
        for k in reads:
            if k in self.lw:
                add(raw, *self.lw[k])
        for k in writes:
            if k in self.lw:
                add(other, *self.lw[k])
            for e, v in self.rd.get(k, {}).items():
                add(other, e, v)
        return raw, other

    def _wait(self, E, deps):
        raw, other = deps
        allv = dict(other)
        for e, v in raw.items():
            if allv.get(e, 0) < v:
                allv[e] = v
        for e, v in allv.items():
            if e is E:
                if E.name == "pe" or not SAME_ENGINE_SYNC:
                    continue
                v = raw.get(e, 0)
                if v == 0:
                    continue
            if E.seen.get(e, 0) >= v:
                continue
            assert v <= e.count, (E.name, e.name, v, e.count)
            E.eng.wait_ge(e.sem, v)
            E.seen[e] = v

    def _rec(self, E, val, reads, writes):
        for k in writes:
            self.lw[k] = (E, val)
            self.rd[k] = {}
        for k in reads:
            d = self.rd.setdefault(k, {})
            if d.get(E, 0) < val:
                d[E] = val

    def op(self, E, fn, reads=(), writes=(), inc=True):
        pr = [k for k in reads if k.startswith("pb")]
        if pr:
            reads = [k for k in reads if not k.startswith("pb")]
            writes = list(writes) + [k for k in pr if k not in writes]
        self._wait(E, self._deps(reads, writes))
        inst = fn()
        if inc:
            inst.then_inc(E.sem, 1)
            E.count += 1
            val = E.count
        else:
            val = E.count + 1
        self._rec(E, val, reads, writes)
        return inst

    def barrier(self):
        engs = [self.pe, self.act, self.dve, self.pool, self.sp]
        srcs = engs + list(self.dsems.values())
        for E in engs:
            for e in srcs:
                if e is E or e.count == 0 or E.seen.get(e, 0) >= e.count:
                    continue
                E.eng.wait_ge(e.sem, e.count)
                E.seen[e] = e.count

    def dma(self, Q, out, in_, reads, writes, sem):
        self._wait(Q, self._deps(reads, writes))
        Dm = self.dsem(sem)
        inst = Q.eng.dma_start(out=out, in_=in_)
        inst.then_inc(Dm.sem, 16)
        Dm.count += 16
        self._rec(Dm, Dm.count, reads, writes)
        return inst


def build(NU=4, dbg=False):
    T_ = NU * 2048
    NT = T_ // 128
    NO = NU * 4
    TO = NO * 128
    nc = bass.Bass("TRN2", target_bir_lowering=False)

    def din(name, shape, dt=F32):
        return nc.dram_tensor(name, shape, dt, kind="ExternalInput").ap()

    x_all = din("x_all", [T_, D])
    x_own = din("x_own", [TO, D])
    w_p0 = din("w_p0", [D, 1296])
    w_p1 = din("w_p1", [D, 512])
    w_ow = din("w_ow", [D, 2064])
    w_up = din("w_up", [16, 256])
    w_o = din("w_o", [D, D])
    w_g = din("w_g", [D, DFF])
    w_u = din("w_u", [D, DFF])
    w_d = din("w_d", [DFF, D])
    g_attn = din("g_attn", [128, D])
    g_ffn = din("g_ffn", [128, D])
    qk_rep = din("qk_rep", [128, 1024])
    b_rep = din("b_rep", [128, 256])
    gn_rep = din("gn_rep", [128, 256])
    lamv = din("lamv", [128, 256])
    cst = din("cst", [128, 514])
    tabs = din("tabs", [128, 260])
    msk = din("msk", [128, 16 * 512])
    out = nc.dram_tensor("out", [TO, D], F32, kind="ExternalOutput").ap()
    mix_scr = nc.dram_tensor("mix_scr", [TO, D], BF16, kind="Internal").ap()
    h_scr = nc.dram_tensor("h_scr", [TO, D], F32, kind="Internal").ap()
    dbg_outs = {}
    if dbg:
        dbg_outs["d_mix"] = nc.dram_tensor("d_mix", [TO, D], F32, kind="ExternalOutput").ap()
        dbg_outs["d_h"] = nc.dram_tensor("d_h", [TO, D], F32, kind="ExternalOutput").ap()

    with contextlib.ExitStack() as gst:
        T = Trk(nc, gst)
        PE, ACT, DVE, POOL, SP = T.pe, T.act, T.dve, T.pool, T.sp
        V = nc.vector
        A = nc.scalar
        G = nc.gpsimd
        PEe = nc.tensor

        def sbg(n, s, d):
            return gst.enter_context(nc.sbuf_tensor(n, s, d))

        pb = [gst.enter_context(nc.psum_tensor("pb%d" % i, [128, 512], F32)) for i in range(8)]
        pbb = [p[:].bitcast(BF16) for p in pb]

        cstf = sbg("cstf", [128, 514], F32)
        identb = sbg("identb", [128, 128], BF16)
        st = sbg("st", [128, 32], F32)
        junk = sbg("junk", [128, 1024], BF16)
        T.dma(SP, cstf[:], cst[:, :], [], ["cstf"], "c0")
        T.op(DVE, lambda: V.tensor_copy(out=identb[:], in_=cstf[:, 0:128]), ["cstf"], ["identb"])
        triS = cstf[:, 128:256]
        upS = cstf[:, 256:384]
        chI = cstf[:, 512:514]

        def rstd_from_ss(ss_ap, out_ap, n, keys):
            T.op(ACT, lambda: A.activation(out=out_ap, in_=ss_ap, func=AF.Ln, scale=1.0 / n, bias=EPS),
                 keys, keys)
            T.op(ACT, lambda: A.activation(out=out_ap, in_=out_ap, func=AF.Exp, scale=-0.5),
                 keys, keys)

        def norm_transpose(src_ap, xb, xk, nb, nbk, nTt, nTk, grep, grepk, dq, dsem, bank=5):
            T.dma(dq, xb[:], src_ap, [], [xk], dsem)
            T.op(DVE, lambda: V.scalar_tensor_tensor(out=junk[:], in0=xb[:], scalar=1.0, in1=xb[:],
                                                     op0=ALU.mult, op1=ALU.mult, accum_out=st[:, 0:1]),
                 [xk], ["junk", "st0"])
            rstd_from_ss(st[:, 0:1], st[:, 1:2], float(D), ["st0"])
            T.op(DVE, lambda: V.scalar_tensor_tensor(out=nb[:], in0=xb[:], scalar=st[:, 1:2], in1=grep[:],
                                                     op0=ALU.mult, op1=ALU.mult),
                 [xk, "st0", grepk], [nbk])
            bk = "pb%d" % bank
            for c in range(8):
                T.op(PE, lambda c=c: PEe.transpose(out=pbb[bank][:, c * 128:(c + 1) * 128],
                                                   in_=nb[:, c * 128:(c + 1) * 128], identity=identb[:]),
                     [nbk, "identb"], [bk], inc=(c == 7))
            T.op(ACT, lambda: A.copy(out=nTt[:], in_=pbb[bank][:, 0:1024]), [bk], [nTk])

        def project(nTt, nTk, W, Wk, groups):
            for (bank, pc0, wc0, ncol) in groups:
                bk = "pb%d" % bank
                for c in range(8):
                    T.op(PE, lambda c=c, bank=bank, pc0=pc0, wc0=wc0, ncol=ncol:
                         PEe.matmul(pb[bank][:, pc0:pc0 + ncol], lhsT=nTt[:, c * 128:(c + 1) * 128],
                                    rhs=W[:, c, wc0:wc0 + ncol], start=(c == 0), stop=(c == 7)),
                         [nTk, Wk], [bk], inc=(c == 7))

        with contextlib.ExitStack() as mst:
            def sb(n, s, d):
                return mst.enter_context(nc.sbuf_tensor(n, s, d))

            KT = sb("KT", [128, 2, T_], BF16)
            VE = sb("VE", [128, NT, 2, 130], BF16)
            QT = sb("QT", [128, 4, TO], BF16)
            WB = sb("WB", [128, 8, 2064], BF16)
            MK = sb("MK", [128, 16, 512], BF16)
            tabf = sb("tabf", [128, 260], F32)
            gat = sb("gat", [128, D], F32)
            qg = sb("qg", [128, 512], F32)
            brep = sb("brep", [128, 256], F32)
            gn = sb("gn", [128, 256], F32)
            wupb = sb("wupb", [16, 256], BF16)
            maskA = sb("maskA", [128, 128], BF16)
            xbs = [sb("xb%d" % i, [128, D], F32) for i in range(3)]
            ssx = sb("ssx", [128, 4], F32)
            rstd_all = sb("rstd_all", [128, NT], F32)
            nbs = [sb("nb%d" % i, [128, D], BF16) for i in range(2)]
            nTs = [sb("nT%d" % i, [128, D], BF16) for i in range(2)]
            kf = sb("kf", [128, 512], F32)
            sq = sb("sq", [128, 512], F32)
            k16s = [sb("k16_%d" % i, [128, 512], BF16) for i in range(2)]
            gkf = [sb("gkf%d" % i, [128, 256], F32) for i in range(2)]
            glr16s = [sb("glr16_%d" % i, [128, 16], BF16) for i in range(2)]
            glrT = sb("glrT", [16, 128], BF16)
            zb = sb("zb", [128, 256], F32)
            lg = sb("lg", [128, 256], F32)
            eb = sb("eb", [128, 256], F32)
            enb = sb("enb", [128, 256], F32)
            ec = sb("ec", [128, 256], F32)
            dec = sb("dec", [128, 4], F32)
            qt16 = sb("qt16", [128, 256], BF16)
            kt16 = sb("kt16", [128, 256], BF16)
            kh16 = sb("kh16", [128, 256], BF16)
            vb16s = [sb("vb16_%d" % i, [128, 512], BF16) for i in range(2)]
            qkT = sb("qkT", [128, 512], BF16)
            AT = sb("AT", [128, 512], BF16)
            S = sb("S", [128, 256], F32)
            Sb = [sb("Sb%d" % i, [128, 256], BF16) for i in range(2)]
            snap = sb("snap", [128, NU, 256], F32)
            sgt = sb("sgt", [128, 512], F32)
            gg = sb("gg", [128, 512], F32)
            ogl = sb("ogl", [128, 512], BF16)
            PTs = [sb("PT%d" % i, [128, 512], BF16) for i in range(6)]
            accS = sb("accS", [128, 3, 390], F32)
            t1 = sb("t1", [128, 128], F32)
            of = sb("of", [128, 128], F32)
            og = sb("og", [128, 4, 128], BF16)

            T.dma(SP, tabf[:], tabs[:, :], [], ["tabf"], "c1")
            T.dma(SP, gat[:], g_attn[:, :], [], ["gat"], "c2")
            T.dma(SP, kf[:], qk_rep[:, 0:512], [], ["kf"], "c3")
            T.dma(SP, sq[:], qk_rep[:, 512:1024], [], ["sq"], "c3b")
            T.dma(SP, brep[:], b_rep[:, :], [], ["brep"], "c4")
            T.dma(SP, gn[:], gn_rep[:, :], [], ["gn"], "c5")
            lam = zb
            T.dma(SP, lam[:], lamv[:, :], [], ["zb"], "c6")
            T.dma(POOL, wupb[:], w_up[:, :], [], ["wupb"], "c7")
            T.dma(POOL, maskA[:], cst[:, 384:512], [], ["maskA"], "c8")
            T.dma(POOL, WB[:, :, 0:1296], w_p0.rearrange("(c p) n -> p c n", p=128), [], ["WB"], "w0")
            for e4 in range(4):
                T.dma(POOL, MK[:, e4 * 4:(e4 + 1) * 4, :],
                      msk[:, e4 * 2048:(e4 + 1) * 2048].rearrange("p (a b) -> p a b", b=512),
                      [], ["MK"], "c9")
            T.op(DVE, lambda: V.scalar_tensor_tensor(out=qg[:, 0:512], in0=kf[:, 0:512], scalar=0.125,
                                                     in1=sq[:, 0:512], op0=ALU.mult, op1=ALU.mult),
                 ["kf", "sq"], ["qg"])
            T.op(DVE, lambda: V.tensor_scalar(out=gn[:, 0:128], in0=gn[:, 0:128], scalar1=1.0 - LAMBDA_INIT,
                                              scalar2=None, op0=ALU.mult), ["gn"], ["gn"])
            T.op(DVE, lambda: V.scalar_tensor_tensor(out=junk[:, 0:64], in0=lam[:, 0:64], scalar=1.0,
                                                     in1=lam[:, 64:128], op0=ALU.mult, op1=ALU.mult,
                                                     accum_out=st[:, 4:5]), ["zb"], ["junk", "stl"])
            T.op(DVE, lambda: V.scalar_tensor_tensor(out=junk[:, 0:64], in0=lam[:, 128:192], scalar=1.0,
                                                     in1=lam[:, 192:256], op0=ALU.mult, op1=ALU.mult,
                                                     accum_out=st[:, 5:6]), ["zb", "stl"], ["junk", "stl"])
            T.op(ACT, lambda: A.activation(out=st[:, 6:8], in_=st[:, 4:6], func=AF.Exp), ["stl"], ["stl"])
            T.op(DVE, lambda: V.tensor_tensor(out=st[:, 8:9], in0=st[:, 7:8], in1=st[:, 6:7], op=ALU.subtract),
                 ["stl"], ["stl"])
            T.op(DVE, lambda: V.tensor_scalar(out=st[:, 8:9], in0=st[:, 8:9], scalar1=-LAMBDA_INIT,
                                              scalar2=None, op0=ALU.add), ["stl"], ["stl"])
            neglam = st[:, 8:9]
            T.op(POOL, lambda: G.memset(VE[:, :, :, 128:130], 1.0), [], ["VE"])
            T.op(DVE, lambda: V.memset(S[:], 0.0), [], ["S"])

            def k_evac_a(bank, c0, ngrp):
                n = 64 * ngrp
                T.op(ACT, lambda: A.copy(out=kf[:, 0:n], in_=pb[bank][:, c0:c0 + n]), ["pb%d" % bank], ["kf"])

            def k_evac_b(ngrp):
                n = 64 * ngrp
                T.op(DVE, lambda: V.tensor_tensor(out=sq[:, 0:n], in0=kf[:, 0:n], in1=kf[:, 0:n], op=ALU.mult),
                     ["kf"], ["sq"])
                T.op(DVE, lambda: V.tensor_reduce(out=st[:, 16:16 + ngrp],
                                                  in_=sq[:, 0:n].rearrange("p (a b) -> p a b", b=64),
                                                  axis=AX.X, op=ALU.add), ["sq"], ["stk"])
                rstd_from_ss(st[:, 16:16 + ngrp], st[:, 24:24 + ngrp], 64.0, ["stk"])

            def k_evac_c(ngrp, ks, gain=None):
                n = 64 * ngrp
                kk = "k16_%d" % ks
                bc = bass.AP(st[:].tensor, st[:, 24:25].offset, [list(st[:].ap[0]), [1, ngrp], [0, 64]])
                if gain is None:
                    T.op(DVE, lambda: V.tensor_tensor(out=k16s[ks][:, 0:n].rearrange("p (a b) -> p a b", b=64),
                                                      in0=kf[:, 0:n].rearrange("p (a b) -> p a b", b=64),
                                                      in1=bc, op=ALU.mult), ["kf", "stk"], [kk])
                else:
                    T.op(DVE, lambda: V.tensor_tensor(out=sq[:, 0:n].rearrange("p (a b) -> p a b", b=64),
                                                      in0=kf[:, 0:n].rearrange("p (a b) -> p a b", b=64),
                                                      in1=bc, op=ALU.mult), ["kf", "stk", "sq"], ["sq"])
                    T.op(DVE, lambda: V.tensor_tensor(out=k16s[ks][:, 0:n], in0=sq[:, 0:n], in1=gain,
                                                      op=ALU.mult), ["sq", "qg"], [kk])

            def k_evac(bank, c0, ngrp, ks, gain=None):
                k_evac_a(bank, c0, ngrp)
                k_evac_b(ngrp)
                k_evac_c(ngrp, ks, gain)

            def k_tr(ks, n, dst_fn):
                nh = n // 128
                for hh in range(nh):
                    T.op(PE, lambda hh=hh: PEe.transpose(out=pbb[4][:, hh * 128:(hh + 1) * 128],
                                                         in_=k16s[ks][:, hh * 128:(hh + 1) * 128], identity=identb[:]),
                         ["k16_%d" % ks, "identb"], ["pb4"], inc=(hh == nh - 1))
                dst_fn(pbb[4][:, 0:n].rearrange("p (a b) -> p a b", b=128))

            def gla_segs(own, i_own, gk, gkk, gq, gqk, vb, vbk, g16, g16k, psS):
                def s_a():
                    T.op(PE, lambda: PEe.transpose(out=pbb[6][0:16, 64:192], in_=g16[:], identity=identb[:]),
                         [g16k, "identb"], ["pb6"])
                    T.op(ACT, lambda: A.copy(out=glrT[:], in_=pbb[6][0:16, 64:192]), ["pb6"], ["glrT"])

                def s_b():
                    T.op(PE, lambda: PEe.matmul(pb[6][:, 128:384], lhsT=glrT[:], rhs=wupb[:], start=True, stop=True),
                         ["glrT", "wupb"], ["pb6"])
                    T.op(DVE, lambda: V.tensor_tensor(out=zb[:], in0=pb[6][:, 128:384], in1=brep[:], op=ALU.add),
                         ["pb6", "brep"], ["zb"])
                    T.op(ACT, lambda: A.activation(out=lg[:], in_=zb[:], func=AF.Exp, scale=-1.0), ["zb"], ["lg"])
                    T.op(ACT, lambda: A.activation(out=lg[:], in_=lg[:], func=AF.Ln, bias=1.0), ["lg"], ["lg"])

                def s_c():
                    if own:
                        T.op(PE, lambda: PEe.matmul(pb[7][:, 0:256], lhsT=triS, rhs=lg[:], start=True, stop=True),
                             ["cstf", "lg"], ["pb7"], inc=False)
                    T.op(PE, lambda: PEe.matmul(pb[7][:, 256:512], lhsT=upS, rhs=lg[:], start=True, stop=True),
                         ["cstf", "lg"], ["pb7"])
                    for p in range(2):
                        T.op(PE, lambda p=p: PEe.matmul(pb[6][:, 96 + 2 * p:98 + 2 * p], lhsT=lg[:, p * 128:(p + 1) * 128],
                                                        rhs=chI, start=True, stop=True),
                             ["lg", "cstf"], ["pb6"], inc=(p == 1))
                    T.op(ACT, lambda: A.activation(out=ec[:], in_=pb[7][:, 256:512], func=AF.Exp), ["pb7"], ["ec"])
                    if own:
                        T.op(ACT, lambda: A.activation(out=eb[:], in_=pb[7][:, 0:256], func=AF.Exp), ["pb7"], ["eb"])
                        T.op(ACT, lambda: A.activation(out=enb[:], in_=pb[7][:, 0:256], func=AF.Exp, scale=-1.0),
                             ["pb7"], ["enb"])
                    T.op(ACT, lambda: A.activation(out=dec[:], in_=pb[6][:, 96:100], func=AF.Exp), ["pb6"], ["dec"])
                    T.op(DVE, lambda: V.tensor_tensor(out=kh16[:], in0=gk, in1=ec[:], op=ALU.mult),
                         [gkk, "ec"], ["kh16"])
                    if own:
                        T.op(DVE, lambda: V.scalar_tensor_tensor(out=qt16[:], in0=gq, scalar=0.125,
                                                                 in1=eb[:], op0=ALU.mult, op1=ALU.mult),
                             [gqk, "eb"], ["qt16"])
                        T.op(DVE, lambda: V.tensor_tensor(out=kt16[:], in0=gk, in1=enb[:], op=ALU.mult),
                             [gkk, "enb"], ["kt16"])
                        for p in range(2):
                            T.op(PE, lambda p=p: PEe.transpose(out=pbb[4][:, 512 + p * 128:512 + (p + 1) * 128],
                                                               in_=qt16[:, p * 128:(p + 1) * 128], identity=identb[:]),
                                 ["qt16", "identb"], ["pb4"], inc=False)
                        for p in range(2):
                            T.op(PE, lambda p=p: PEe.transpose(out=pbb[4][:, 768 + p * 128:768 + (p + 1) * 128],
                                                               in_=kt16[:, p * 128:(p + 1) * 128], identity=identb[:]),
                                 ["kt16", "identb"], ["pb4"], inc=(p == 1))
                        T.op(ACT, lambda: A.copy(out=qkT[:], in_=pbb[4][:, 512:1024]), ["pb4"], ["qkT"])

                def s_c2():
                    for hh in range(2):
                        bank = 0 if hh == 0 else 3
                        for p in range(2):
                            T.op(PE, lambda hh=hh, p=p, bank=bank:
                                 PEe.matmul(pb[bank][:, p * 128:(p + 1) * 128],
                                            lhsT=qkT[hh * 64:(hh + 1) * 64, 256 + p * 128:256 + (p + 1) * 128],
                                            rhs=qkT[hh * 64:(hh + 1) * 64, p * 128:(p + 1) * 128],
                                            start=True, stop=True),
                                 ["qkT"], ["pb%d" % bank], inc=(p == 1))
                    mA = bass.AP(maskA[:].tensor, maskA[:].offset, [list(maskA[:].ap[0]), [0, 2], [1, 128]])
                    for hh in range(2):
                        bank = 0 if hh == 0 else 3
                        outv = bass.AP(AT[:].tensor, AT[:, hh * 128:hh * 128 + 1].offset,
                                       [list(AT[:].ap[0]), [256, 2], [1, 128]])
                        T.op(DVE, lambda bank=bank, outv=outv: V.tensor_tensor(
                            out=outv, in0=pb[bank][:, 0:256].rearrange("p (a b) -> p a b", b=128), in1=mA, op=ALU.mult),
                            ["pb%d" % bank, "maskA"], ["AT"])

                def s_d(ch):
                    def f():
                        if own:
                            T.op(DVE, lambda: V.tensor_copy(out=Sb[ch][:], in_=S[:]), ["S"], ["Sb%d" % ch])
                        sbank, scol = psS[ch]
                        sk = "pb%d" % sbank
                        for p in range(2):
                            for hh in range(2):
                                h = 2 * p + hh
                                T.op(PE, lambda p=p, hh=hh, h=h:
                                     PEe.matmul(pb[sbank][hh * 64:(hh + 1) * 64, scol + p * 128:scol + (p + 1) * 128],
                                                lhsT=kh16[ch * 64:(ch + 1) * 64, h * 64:(h + 1) * 64],
                                                rhs=vb[ch * 64:(ch + 1) * 64, h * 128:(h + 1) * 128],
                                                start=True, stop=True),
                                     ["kh16", vbk], [sk], inc=(p == 1 and hh == 1))
                        for p in range(2):
                            T.op(DVE, lambda p=p:
                                 V.scalar_tensor_tensor(out=S[:, p * 128:(p + 1) * 128], in0=S[:, p * 128:(p + 1) * 128],
                                                        scalar=dec[:, 2 * p + ch:2 * p + ch + 1],
                                                        in1=pb[sbank][:, scol + p * 128:scol + (p + 1) * 128],
                                                        op0=ALU.mult, op1=ALU.add),
                                 ["S", "dec", sk], ["S"])
                    return f

                def s_o():
                    for hh in range(2):
                        bank = 0 if hh == 0 else 3
                        bk = "pb%d" % bank
                        for p in range(2):
                            h = 2 * p + hh
                            oc = 256 + p * 128
                            T.op(PE, lambda h=h, bank=bank, oc=oc:
                                 PEe.matmul(pb[bank][:, oc:oc + 128], lhsT=AT[:, h * 128:(h + 1) * 128],
                                            rhs=vb[:, h * 128:(h + 1) * 128], start=True, stop=False),
                                 ["AT", vbk], [bk], inc=False)
                            for ch in range(2):
                                T.op(PE, lambda h=h, hh=hh, p=p, ch=ch, bank=bank, oc=oc:
                                     PEe.matmul(pb[bank][ch * 64:(ch + 1) * 64, oc:oc + 128],
                                                lhsT=qkT[hh * 64:(hh + 1) * 64, p * 128 + ch * 64:p * 128 + (ch + 1) * 64],
                                                rhs=Sb[ch][hh * 64:(hh + 1) * 64, p * 128:(p + 1) * 128],
                                                start=False, stop=True),
                                     ["qkT", "Sb%d" % ch], [bk], inc=(ch == 1 and p == 1))
                    for hh in range(2):
                        bank = 0 if hh == 0 else 3
                        bk = "pb%d" % bank
                        T.op(ACT, lambda bank=bank: A.copy(out=kf[:, 0:256], in_=pb[bank][:, 256:512]), [bk], ["kf"])
                        T.op(DVE, lambda: V.tensor_tensor(out=sq[:, 0:256], in0=kf[:, 0:256], in1=kf[:, 0:256],
                                                          op=ALU.mult), ["kf"], ["sq"])
                        T.op(DVE, lambda: V.tensor_reduce(out=st[:, 16:18],
                                                          in_=sq[:, 0:256].rearrange("p (a b) -> p a b", b=128),
                                                          axis=AX.X, op=ALU.add), ["sq"], ["stk"])
                        rstd_from_ss(st[:, 16:18], st[:, 24:26], 128.0, ["stk"])
                        bc = bass.AP(st[:].tensor, st[:, 24:25].offset, [list(st[:].ap[0]), [1, 2], [0, 128]])
                        ggv = bass.AP(gg[:].tensor, gg[:, hh * 128:hh * 128 + 1].offset,
                                      [list(gg[:].ap[0]), [256, 2], [1, 128]])
                        oglv = bass.AP(ogl[:].tensor, ogl[:, hh * 128:hh * 128 + 1].offset,
                                       [list(ogl[:].ap[0]), [256, 2], [1, 128]])
                        T.op(DVE, lambda bc=bc: V.tensor_tensor(out=sq[:, 256:512].rearrange("p (a b) -> p a b", b=128),
                                                                in0=kf[:, 0:256].rearrange("p (a b) -> p a b", b=128),
                                                                in1=bc, op=ALU.mult), ["kf", "stk"], ["sq2"])
                        T.op(DVE, lambda ggv=ggv, oglv=oglv: V.tensor_tensor(
                            out=oglv, in0=sq[:, 256:512].rearrange("p (a b) -> p a b", b=128), in1=ggv, op=ALU.mult),
                            ["sq2", "gg"], ["ogl"])
                    T.dma(SP, mix_scr[i_own * 128:(i_own + 1) * 128, 512:1024], ogl[:], ["ogl"], ["mixg%d" % i_own], "mx")

                if own:
                    return [s_a, s_b, s_c, s_c2, s_d(0), s_d(1), s_o]
                return [s_a, s_b, s_c, s_d(0), s_d(1)]

            def seg_load(t):
                sl = t % 3
                T.dma(SP, xbs[sl][:], x_all[t * 128:(t + 1) * 128, :], [], ["xb%d" % sl], "x%d" % sl)

            def seg_stats(t):
                sl = t % 3
                T.op(DVE, lambda: V.scalar_tensor_tensor(out=junk[:], in0=xbs[sl][:], scalar=1.0, in1=xbs[sl][:],
                                                         op0=ALU.mult, op1=ALU.mult, accum_out=ssx[:, sl:sl + 1]),
                     ["xb%d" % sl], ["junk", "ssx%d" % sl])
                T.op(ACT, lambda: A.activation(out=rstd_all[:, t:t + 1], in_=ssx[:, sl:sl + 1], func=AF.Ln,
                                               scale=1.0 / D, bias=EPS), ["ssx%d" % sl], ["rs%d" % t])
                T.op(ACT, lambda: A.activation(out=rstd_all[:, t:t + 1], in_=rstd_all[:, t:t + 1], func=AF.Exp, scale=-0.5),
                     ["rs%d" % t], ["rs%d" % t])

            def seg_n(t):
                sl = t % 3
                ns = t % 2
                T.op(DVE, lambda: V.scalar_tensor_tensor(out=nbs[ns][:], in0=xbs[sl][:], scalar=rstd_all[:, t:t + 1],
                                                         in1=gat[:], op0=ALU.mult, op1=ALU.mult),
                     ["xb%d" % sl, "rs%d" % t, "gat"], ["nb%d" % ns])

            def seg_tr(t):
                ns = t % 2
                for c in range(8):
                    T.op(PE, lambda c=c: PEe.transpose(out=pbb[3][:, c * 128:(c + 1) * 128],
                                                       in_=nbs[ns][:, c * 128:(c + 1) * 128], identity=identb[:]),
                         ["nb%d" % ns, "identb"], ["pb3"], inc=(c == 7))
                T.op(ACT, lambda: A.copy(out=nTs[ns][:], in_=pbb[3][:, 0:1024]), ["pb3"], ["nT%d" % ns])

            def run_pipe(n_tiles, early_fn, main_fn, late_fn):
                seg_load(0)
                for it in range(-1, n_tiles + 1):
                    early = early_fn(it + 1) if 0 <= it + 1 < n_tiles else []
                    main = main_fn(it) if 0 <= it < n_tiles else []
                    late = late_fn(it - 1) if 0 <= it - 1 < n_tiles else []
                    for k in range(max(len(early), len(main), len(late))):
                        if k < len(main):
                            main[k]()
                        if k < len(late):
                            late[k]()
                        if k < len(early):
                            early[k]()

            def kv_store(t):
                def dst(src3):
                    T.op(ACT, lambda: A.copy(out=KT[:, :, t * 128:(t + 1) * 128], in_=src3), ["pb4"], ["KT"])
                return dst

            def v_store(t):
                T.op(DVE, lambda: V.tensor_copy(out=VE[:, t, :, 0:128],
                                                in_=pb[0][:, 256:512].rearrange("p (a b) -> p a b", b=128)),
                     ["pb0"], ["VE"])

            def p0_iter(it):
                ok = lambda t: 0 <= t < NT
                if ok(it + 2):
                    seg_stats(it + 2)
                if ok(it + 1):
                    seg_tr(it + 1)
                late = []
                if ok(it - 1):
                    t1 = it - 1
                    n1 = t1 % 2
                    late = gla_segs(False, None, gkf[n1][:], "gkf%d" % n1, None, None, vb16s[n1], "vb16_%d" % n1,
                                    glr16s[n1], "glr16_%d" % n1, ((5, 0), (5, 256)))
                ns = it % 2
                if ok(it):
                    project(nTs[ns], "nT%d" % ns, WB, "WB", [(0, 0, 0, 512)])
                    k_evac_a(0, 0, 4)
                    v_store(it)
                    k_evac_b(4)
                if ok(it - 1):
                    t1 = it - 1
                    k_tr(t1 % 2, 256, kv_store(t1))
                    if t1 % 16 % 4 == 0:
                        u = t1 // 16
                        r = (t1 % 16) // 4
                        if r == 0:
                            T.op(DVE, lambda: V.tensor_scalar(out=snap[:, u, :], in0=S[:],
                                                              scalar1=tabf[:, 256 + r:257 + r], scalar2=None,
                                                              op0=ALU.mult), ["S", "tabf"], ["snap"])
                        else:
                            T.op(DVE, lambda: V.scalar_tensor_tensor(out=snap[:, u, :], in0=S[:],
                                                                     scalar=tabf[:, 256 + r:257 + r],
                                                                     in1=snap[:, u, :], op0=ALU.mult, op1=ALU.add),
                                 ["S", "tabf", "snap"], ["snap"])
                    late[0]()
                if ok(it + 2):
                    seg_n(it + 2)
                if ok(it):
                    project(nTs[ns], "nT%d" % ns, WB, "WB", [(1, 0, 512, 272)])
                    T.op(DVE, lambda: V.tensor_copy(out=gkf[ns][:], in_=pb[1][:, 0:256]), ["pb1"], ["gkf%d" % ns])
                    T.op(DVE, lambda: V.tensor_copy(out=glr16s[ns][:], in_=pb[1][:, 256:272]), ["pb1"], ["glr16_%d" % ns])
                    k_evac_c(4, ns)
                if ok(it - 1):
                    late[1]()
                if ok(it):
                    project(nTs[ns], "nT%d" % ns, WB, "WB", [(2, 0, 784, 512)])
                    T.op(ACT, lambda: A.copy(out=vb16s[ns][:], in_=pb[2][:, :]), ["pb2"], ["vb16_%d" % ns])
                if ok(it - 1):
                    late[2]()
                    late[3]()
                    late[4]()
                if ok(it + 3):
                    seg_load(it + 3)

            for t in range(min(3, NT)):
                seg_load(t)
            for it in range(-2, NT + 1):
                p0_iter(it)

            T.dma(POOL, WB[:, :, :], w_ow.rearrange("(c p) n -> p c n", p=128), [], ["WB"], "w0")

            def q_store(i):
                def dst(src3):
                    T.op(ACT, lambda: A.copy(out=QT[:, :, i * 128:(i + 1) * 128], in_=src3), ["pb4"], ["QT"])
                return dst

            gnb = bass.AP(gn[:].tensor, gn[:, 128:129].offset, [list(gn[:].ap[0]), [0, 4], [1, 128]])

            def o_load(i):
                sl = i % 3
                T.dma(SP, xbs[sl][:], x_own[i * 128:(i + 1) * 128, :], [], ["xb%d" % sl], "x%d" % sl)

            def o_stats(i):
                sl = i % 3
                T.op(DVE, lambda: V.scalar_tensor_tensor(out=junk[:], in0=xbs[sl][:], scalar=1.0, in1=xbs[sl][:],
                                                         op0=ALU.mult, op1=ALU.mult, accum_out=ssx[:, sl:sl + 1]),
                     ["xb%d" % sl], ["junk", "ssx%d" % sl])
                rstd_from_ss(ssx[:, sl:sl + 1], ssx[:, 3:4] if False else st[:, 1 + (i % 2):2 + (i % 2)], float(D), ["ssx%d" % sl, "sto%d" % (i % 2)])

            def o_n(i):
                sl = i % 3
                ns = i % 2
                T.op(DVE, lambda: V.scalar_tensor_tensor(out=nbs[ns][:], in0=xbs[sl][:], scalar=st[:, 1 + ns:2 + ns],
                                                         in1=gat[:], op0=ALU.mult, op1=ALU.mult),
                     ["xb%d" % sl, "sto%d" % ns, "gat"], ["nb%d" % ns])

            def o_tr(i):
                ns = i % 2
                for c in range(8):
                    T.op(PE, lambda c=c: PEe.transpose(out=pbb[5][:, c * 128:(c + 1) * 128],
                                                       in_=nbs[ns][:, c * 128:(c + 1) * 128], identity=identb[:]),
                         ["nb%d" % ns, "identb"], ["pb5"], inc=(c == 7))
                T.op(ACT, lambda: A.copy(out=nTs[ns][:], in_=pbb[5][:, 0:1024]), ["pb5"], ["nT%d" % ns])

            for i in range(min(2, NO)):
                o_load(i)
            o_stats(0)
            o_n(0)
            o_tr(0)
            for i in range(NO):
                s = i % 2
                if i + 2 < NO:
                    o_load(i + 2)
                if i + 1 < NO:
                    o_stats(i + 1)
                if i % 4 == 0:
                    T.op(DVE, lambda i=i: V.tensor_copy(out=S[:], in_=snap[:, i // 4, :]), ["snap"], ["S"])
                project(nTs[s], "nT%d" % s, WB, "WB", [(0, 0, 0, 512), (3, 0, 1536, 512)])
                if i + 1 < NO:
                    o_n(i + 1)
                k_evac_a(0, 0, 8)
                T.op(ACT, lambda: A.activation(out=sgt[:], in_=pb[3][:, :], func=AF.Exp, scale=-1.0), ["pb3"], ["sgt"])
                T.op(ACT, lambda: A.activation(out=sgt[:], in_=sgt[:], func=AF.Ln, bias=1.0), ["sgt"], ["sgt"])
                T.op(ACT, lambda: A.activation(out=sgt[:], in_=sgt[:], func=AF.Exp, scale=-1.0), ["sgt"], ["sgt"])
                project(nTs[s], "nT%d" % s, WB, "WB", [(1, 0, 512, 512), (2, 0, 1024, 512), (6, 0, 2048, 16)])
                if i + 1 < NO:
                    o_tr(i + 1)
                k_evac_b(8)
                T.op(DVE, lambda: V.tensor_tensor(out=gg[:], in0=pb[3][:, :], in1=sgt[:], op=ALU.mult),
                     ["pb3", "sgt"], ["gg"])
                T.op(DVE, lambda: V.tensor_copy(out=glr16s[0][:], in_=pb[6][:, 0:16]), ["pb6"], ["glr16_0"])
                T.op(ACT, lambda: A.copy(out=vb16s[0][:], in_=pb[2][:, :]), ["pb2"], ["vb16_0"])
                k_evac_c(8, 0, gain=qg[:, 0:512])
                T.op(DVE, lambda: V.tensor_tensor(out=gg[:].rearrange("p (a b) -> p a b", b=128),
                                                  in0=gg[:].rearrange("p (a b) -> p a b", b=128), in1=gnb, op=ALU.mult),
                     ["gg", "gn"], ["gg"])
                segs = gla_segs(True, i, pb[1][:, 256:512], "pb1", pb[1][:, 0:256], "pb1", vb16s[0], "vb16_0",
                                glr16s[0], "glr16_0", ((2, 0), (2, 256)))
                segs[0]()
                k_tr(0, 512, q_store(i))
                for sg in segs[1:]:
                    sg()

            def acc_ap(a):
                bank = 5 + a // 3
                col = (a % 3) * 130
                return bank, col

            def attention(hp):
                steps = [(u, hl, kb) for u in range(NU) for hl in range(2) for kb in range(16 * u + 16)]
                LOOK = 2

                def emit_qk(idx):
                    u, hl, kb = steps[idx]
                    h = 2 * hp + hl
                    e = kb - 16 * u
                    ei = e + 48
                    for m in range(2):
                        sbank = 1 + (2 * idx + m) % 4
                        T.op(PE, lambda m=m, sbank=sbank:
                             PEe.matmul(pb[sbank][:, 0:512],
                                        lhsT=KT[m * 64:(m + 1) * 64, hl, kb * 128:(kb + 1) * 128],
                                        rhs=QT[m * 64:(m + 1) * 64, h, u * 512:(u + 1) * 512],
                                        start=True, stop=True),
                             ["KT", "QT"], ["pb%d" % sbank])
                    for m in range(2):
                        sbank = 1 + (2 * idx + m) % 4
                        slot = (2 * idx + m) % 6
                        T.op(ACT, lambda sbank=sbank, slot=slot:
                             A.activation(out=PTs[slot][:], in_=pb[sbank][:, 0:512], func=AF.Exp,
                                          bias=tabf[:, h * 64 + ei:h * 64 + ei + 1]),
                             ["pb%d" % sbank, "tabf"], ["PT%d" % slot])
                        if e >= 0:
                            T.op(POOL, lambda slot=slot:
                                 G.tensor_tensor(out=PTs[slot][:], in0=PTs[slot][:], in1=MK[:, e, :], op=ALU.mult),
                                 ["PT%d" % slot, "MK"], ["PT%d" % slot])

                def emit_pv(idx):
                    u, hl, kb = steps[idx]
                    h = 2 * hp + hl
                    nkb = 16 * u + 16
                    for m in range(2):
                        slot = (2 * idx + m) % 6
                        for c in range(4):
                            bank, col = acc_ap(c * 2 + m)
                            first = (kb == 0 and (c * 2 + m) in (0, 4, 6))
                            T.op(PE, lambda c=c, bank=bank, col=col, slot=slot, first=first:
                                 PEe.matmul(pb[bank][:, col:col + 130],
                                            lhsT=PTs[slot][:, c * 128:(c + 1) * 128],
                                            rhs=VE[:, kb, hl, :], start=first, stop=(kb == nkb - 1),
                                            skip_group_check=True),
                                 ["PT%d" % slot, "VE"], ["pb%d" % bank], inc=(c == 3 and m == 1))
                    if kb != nkb - 1:
                        return
                    for bi in range(3):
                        T.op(DVE, lambda bi=bi: V.tensor_copy(out=accS[:, bi, :], in_=pb[5 + bi][:, 0:390]),
                             ["pb%d" % (5 + bi)], ["accS"])
                    for c in range(4):
                        a0 = c * 2
                        a1 = c * 2 + 1
                        A0 = accS[:, a0 // 3, (a0 % 3) * 130:(a0 % 3) * 130 + 130]
                        A1 = accS[:, a1 // 3, (a1 % 3) * 130:(a1 % 3) * 130 + 130]
                        T.op(DVE, lambda A0=A0: V.reciprocal(out=st[:, 10:11], in_=A0[:, 128:129]), ["accS"], ["sta"])
                        T.op(DVE, lambda A1=A1: V.reciprocal(out=st[:, 11:12], in_=A1[:, 128:129]), ["accS", "sta"], ["sta"])
                        T.op(DVE, lambda: V.tensor_tensor(out=st[:, 11:12], in0=st[:, 11:12], in1=neglam, op=ALU.mult),
                             ["sta", "stl"], ["sta"])
                        T.op(DVE, lambda A1=A1: V.tensor_scalar(out=t1[:], in0=A1[:, 0:128], scalar1=st[:, 11:12],
                                                                scalar2=None, op0=ALU.mult), ["accS", "sta"], ["t1"])
                        T.op(DVE, lambda A0=A0: V.scalar_tensor_tensor(out=of[:], in0=A0[:, 0:128], scalar=st[:, 10:11],
                                                                       in1=t1[:], op0=ALU.mult, op1=ALU.add),
                             ["accS", "sta", "t1"], ["of"])
                        T.op(DVE, lambda: V.scalar_tensor_tensor(out=junk[:, 0:128], in0=of[:], scalar=1.0, in1=of[:],
                                                                 op0=ALU.mult, op1=ALU.mult, accum_out=st[:, 12:13]),
                             ["of"], ["junk", "stb"])
                        rstd_from_ss(st[:, 12:13], st[:, 13:14], 128.0, ["stb"])
                        T.op(DVE, lambda c=c: V.scalar_tensor_tensor(out=og[:, c, :], in0=of[:], scalar=st[:, 13:14],
                                                                     in1=gn[:, 0:128], op0=ALU.mult, op1=ALU.mult),
                             ["of", "stb", "gn"], ["og"])
                    T.dma(SP, mix_scr[u * 512:(u + 1) * 512, h * 128:(h + 1) * 128].rearrange("(c p) n -> p c n", p=128),
                          og[:], ["og"], ["mixd%d_%d" % (u, h)], "mx2")

                for i in range(len(steps) + LOOK):
                    if i < len(steps):
                        emit_qk(i)
                    if i - LOOK >= 0:
                        emit_pv(i - LOOK)

            attention(0)

            T.dma(POOL, WB[:, :, 0:512], w_p1.rearrange("(c p) n -> p c n", p=128), [], ["WB"], "w0")

            def p1_iter(it):
                ok = lambda t: 0 <= t < NT
                if ok(it + 2):
                    seg_n(it + 2)
                if ok(it + 1):
                    seg_tr(it + 1)
                ns = it % 2
                if ok(it):
                    project(nTs[ns], "nT%d" % ns, WB, "WB", [(0, 0, 0, 512)])
                    k_evac_a(0, 0, 4)
                    v_store(it)
                    k_evac_b(4)
                if ok(it - 1):
                    k_tr((it - 1) % 2, 256, kv_store(it - 1))
                if ok(it):
                    k_evac_c(4, ns)
                if ok(it + 3):
                    seg_load(it + 3)

            for t in range(min(3, NT)):
                seg_load(t)
            for it in range(-2, NT + 1):
                p1_iter(it)
            attention(1)
            T.barrier()
            mix_keys = ["mixg%d" % i for i in range(NO)] + ["mixd%d_%d" % (u, h) for u in range(NU) for h in range(4)]

        with contextlib.ExitStack() as fst:
            def sb(n, s, d):
                return fst.enter_context(nc.sbuf_tensor(n, s, d))

            WG = sb("WG", [128, 8, DFF], BF16)
            WU = sb("WU", [128, 8, DFF], BF16)
            WD = sb("WD", [128, NFF, D], BF16)
            gff = sb("gff", [128, D], F32)
            hbs = [sb("hb%d" % i, [128, D], F32) for i in range(2)]
            nbs = [sb("fnb%d" % i, [128, D], BF16) for i in range(2)]
            obs = [sb("ob%d" % i, [128, D], F32) for i in range(2)]
            T.dma(SP, gff[:], g_ffn[:, :], [], ["gff"], "c2")
            with contextlib.ExitStack() as ost:
                def sbo(n, s, d):
                    return ost.enter_context(nc.sbuf_tensor(n, s, d))
                WO = sbo("WO", [128, 8, D], BF16)
                mxs = [sbo("mx%d" % i, [128, D], BF16) for i in range(2)]
                mxT = [sbo("mxT%d" % i, [128, D], BF16) for i in range(2)]
                xos = [sbo("xo%d" % i, [128, D], F32) for i in range(2)]
                T.dma(POOL, WO[:], w_o.rearrange("(c p) n -> p c n", p=128), [], ["WO"], "w1")
                T.dma(POOL, WG[:], w_g.rearrange("(c p) n -> p c n", p=128), [], ["WG"], "w2")
                T.dma(POOL, WU[:], w_u.rearrange("(c p) n -> p c n", p=128), [], ["WU"], "w3")
                T.dma(POOL, WD[:], w_d.rearrange("(c p) n -> p c n", p=128), [], ["WD"], "w4")
                for i in range(NO):
                    s = i % 2
                    T.dma(SP, mxs[s][:], mix_scr[i * 128:(i + 1) * 128, :], mix_keys if i < 2 else [], ["mx%d" % s], "m%d" % s)
                    T.dma(SP, xos[s][:], x_own[i * 128:(i + 1) * 128, :], [], ["xo%d" % s], "xo%d" % s)
                    for c in range(8):
                        T.op(PE, lambda c=c, s=s: PEe.transpose(out=pbb[0][:, c * 128:(c + 1) * 128],
                                                                in_=mxs[s][:, c * 128:(c + 1) * 128], identity=identb[:]),
                             ["mx%d" % s, "identb"], ["pb0"], inc=(c == 7))
                    T.op(ACT, lambda s=s: A.copy(out=mxT[s][:], in_=pbb[0][:, 0:1024]), ["pb0"], ["mxT%d" % s])
                    for nb_ in range(2):
                        bank = 1 + nb_
                        for c in range(8):
                            T.op(PE, lambda c=c, s=s, nb_=nb_, bank=bank:
                                 PEe.matmul(pb[bank][:, 0:512], lhsT=mxT[s][:, c * 128:(c + 1) * 128],
                                            rhs=WO[:, c, nb_ * 512:(nb_ + 1) * 512], start=(c == 0), stop=(c == 7)),
                                 ["mxT%d" % s, "WO"], ["pb%d" % bank], inc=(c == 7))
                        T.op(DVE, lambda s=s, nb_=nb_, bank=bank:
                             V.tensor_tensor(out=hbs[s][:, nb_ * 512:(nb_ + 1) * 512], in0=pb[bank][:, 0:512],
                                             in1=xos[s][:, nb_ * 512:(nb_ + 1) * 512], op=ALU.add),
                             ["pb%d" % bank, "xo%d" % s], ["hb%d" % s])
                    T.dma(SP, h_scr[i * 128:(i + 1) * 128, :], hbs[s][:], ["hb%d" % s], ["hscr%d" % i], "hs%d" % s)
                    if dbg:
                        T.dma(SP, dbg_outs["d_h"][i * 128:(i + 1) * 128, :], hbs[s][:], ["hb%d" % s], [], "dbg")
                        T.op(DVE, lambda s=s: V.tensor_copy(out=xos[s][:], in_=mxs[s][:]), ["mx%d" % s, "xo%d" % s], ["xo%d" % s])
                        T.dma(SP, dbg_outs["d_mix"][i * 128:(i + 1) * 128, :], xos[s][:], ["xo%d" % s], [], "dbg")

            T.barrier()
            mT = sb("mT", [128, 8, 512], BF16)
            actT = sb("actT", [128, NFF, 512], BF16)
            ee = sb("ee", [128, 512], F32)
            for gI in range(NU):
                for tt in range(4):
                    i = gI * 4 + tt
                    s = i % 2
                    T.dma(SP, hbs[s][:], h_scr[i * 128:(i + 1) * 128, :], ["hscr%d" % i], ["hb%d" % s], "hl%d" % s)
                    T.op(DVE, lambda s=s: V.scalar_tensor_tensor(out=junk[:], in0=hbs[s][:], scalar=1.0, in1=hbs[s][:],
                                                                 op0=ALU.mult, op1=ALU.mult, accum_out=st[:, 0:1]),
                         ["hb%d" % s], ["junk", "st0"])
                    rstd_from_ss(st[:, 0:1], st[:, 1:2], float(D), ["st0"])
                    T.op(DVE, lambda s=s: V.scalar_tensor_tensor(out=nbs[s][:], in0=hbs[s][:], scalar=st[:, 1:2],
                                                                 in1=gff[:], op0=ALU.mult, op1=ALU.mult),
                         ["hb%d" % s, "st0", "gff"], ["fnb%d" % s])
                    for c in range(8):
                        T.op(PE, lambda c=c, s=s: PEe.transpose(out=pbb[0][:, c * 128:(c + 1) * 128],
                                                                in_=nbs[s][:, c * 128:(c + 1) * 128], identity=identb[:]),
                             ["fnb%d" % s, "identb"], ["pb0"], inc=(c == 7))
                    T.op(ACT, lambda tt=tt: A.copy(out=mT[:, :, tt * 128:(tt + 1) * 128],
                                                   in_=pbb[0][:, 0:1024].rearrange("p (a b) -> p a b", b=128)),
                         ["pb0"], ["mT"])
                for f in range(NFF):
                    gb = 1 + (f % 2)
                    ub = 3 + (f % 2)
                    for c in range(8):
                        T.op(PE, lambda c=c, f=f, gb=gb: PEe.matmul(pb[gb][:, 0:512], lhsT=WG[:, c, f * 128:(f + 1) * 128],
                                                                    rhs=mT[:, c, :], start=(c == 0), stop=(c == 7)),
                             ["WG", "mT"], ["pb%d" % gb], inc=(c == 7))
                    for c in range(8):
                        T.op(PE, lambda c=c, f=f, ub=ub: PEe.matmul(pb[ub][:, 0:512], lhsT=WU[:, c, f * 128:(f + 1) * 128],
                                                                    rhs=mT[:, c, :], start=(c == 0), stop=(c == 7)),
                             ["WU", "mT"], ["pb%d" % ub], inc=(c == 7))
                    T.op(ACT, lambda gb=gb: A.activation(out=ee[:], in_=pb[gb][:, 0:512], func=AF.Silu),
                         ["pb%d" % gb], ["ee"])
                    T.op(DVE, lambda f=f, ub=ub: V.tensor_tensor(out=actT[:, f, :], in0=pb[ub][:, 0:512], in1=ee[:], op=ALU.mult),
                         ["pb%d" % ub, "ee"], ["actT"])
                for tt in range(4):
                    i = gI * 4 + tt
                    s = i % 2
                    T.dma(SP, hbs[s][:], h_scr[i * 128:(i + 1) * 128, :], ["hscr%d" % i], ["hb%d" % s], "hl%d" % s)
                    for nb_ in range(2):
                        bank = 5 + nb_
                        for f in range(NFF):
                            T.op(PE, lambda f=f, tt=tt, nb_=nb_, bank=bank:
                                 PEe.matmul(pb[bank][:, 0:512], lhsT=actT[:, f, tt * 128:(tt + 1) * 128],
                                            rhs=WD[:, f, nb_ * 512:(nb_ + 1) * 512], start=(f == 0), stop=(f == NFF - 1)),
                                 ["actT", "WD"], ["pb%d" % bank], inc=(f == NFF - 1))
                        T.op(DVE, lambda s=s, nb_=nb_, bank=bank:
                             V.tensor_tensor(out=obs[s][:, nb_ * 512:(nb_ + 1) * 512], in0=pb[bank][:, 0:512],
                                             in1=hbs[s][:, nb_ * 512:(nb_ + 1) * 512], op=ALU.add),
                             ["pb%d" % bank, "hb%d" % s], ["ob%d" % s])
                    T.dma(SP, out[i * 128:(i + 1) * 128, :], obs[s][:], ["ob%d" % s], [], "o%d" % s)

        for n in list(T.dsems):
            d = T.dsem(n)
            nc.sync.wait_ge(d.sem, d.count)
    return nc


def _consts():
    c = np.zeros((128, 514), np.float32)
    c[:, 0:128] = np.eye(128, dtype=np.float32)
    s = np.arange(128)[:, None]
    t = np.arange(128)[None, :]
    same = (s // 64) == (t // 64)
    c[:, 128:256] = np.where(same & (s <= t), -1.0 / 16.0, 0.0)
    c[:, 256:384] = np.where(same & (s > t), -1.0 / 16.0, 0.0)
    c[:, 384:512] = np.where(same & (s <= t), 1.0, 0.0)
    c[0:64, 512] = -1.0 / 16.0
    c[64:128, 513] = -1.0 / 16.0
    return c


def _tabs(j):
    tb = np.zeros((128, 260), np.float32)
    kl = np.arange(128, dtype=np.float64)
    for h in range(4):
        for ei in range(64):
            e = ei - 48
            if e <= 4 * j + 3:
                tb[:, h * 64 + ei] = SLOPES[h] * (128.0 * (e - 4 * j) + kl - 256.0)
            else:
                tb[:, h * 64 + ei] = NEG
    tb[:, 256 + j] = 1.0
    return tb


def _masks(j):
    m = np.zeros((128, 16, 4, 128), np.float32)
    k = np.arange(128)[:, None]
    q = np.arange(128)[None, :]
    tri = (k <= q).astype(np.float32)
    for e in range(16):
        for c in range(4):
            cs = 4 * j + c
            if e < cs:
                m[:, e, c, :] = 1.0
            elif e == cs:
                m[:, e, c, :] = tri
    return m.reshape(128, 16 * 512)


def _rep(v, n=128):
    return np.ascontiguousarray(np.broadcast_to(np.asarray(v, np.float32).reshape(1, -1), (n, v.size)))


def prep_inputs(inp, NU=4):
    T_ = NU * 2048
    f = lambda a: np.ascontiguousarray(np.asarray(a, dtype=np.float32))
    x = f(inp["x"])
    w_in = f(inp["w_in"])[0]
    cols = lambda a, b: list(range(a, b))
    dq, dk, dv = 0, 512, 1024
    gq, gk, gv, go, gl = 1536, 1792, 2048, 2560, 3072
    c_p0 = cols(dk, dk + 256) + cols(dv, dv + 256) + cols(gk, gk + 256) + cols(gl, gl + 16) + cols(gv, gv + 512)
    c_p1 = cols(dk + 256, dk + 512) + cols(dv + 256, dv + 512)
    c_ow = cols(dq, dq + 512) + cols(gq, gq + 256) + cols(gk, gk + 256) + cols(gv, gv + 512) + cols(go, go + 512) + cols(gl, gl + 16)
    shared = {
        "w_p0": np.ascontiguousarray(w_in[:, c_p0]),
        "w_p1": np.ascontiguousarray(w_in[:, c_p1]),
        "w_ow": np.ascontiguousarray(w_in[:, c_ow]),
        "w_up": f(inp["w_gla_gate_up"])[0],
        "w_o": f(inp["w_out"])[0],
        "w_g": f(inp["w_ffn_gate"])[0],
        "w_u": f(inp["w_ffn_up"])[0],
        "w_d": f(inp["w_ffn_down"])[0],
        "g_attn": _rep(f(inp["attn_norm_gain"])[0]),
        "g_ffn": _rep(f(inp["ffn_norm_gain"])[0]),
        "qk_rep": np.concatenate([_rep(np.tile(f(inp["q_norm_gain"])[0], 8)),
                                  _rep(np.tile(f(inp["k_norm_gain"])[0], 8))], axis=1),
        "b_rep": _rep(f(inp["b_gla_gate"])[0]),
        "gn_rep": np.concatenate([_rep(f(inp["diff_out_norm_gain"])[0]), _rep(f(inp["gla_out_norm_gain"])[0])], axis=1),
        "lamv": np.concatenate([_rep(f(inp[k])[0]) for k in ("lambda_q1", "lambda_k1", "lambda_q2", "lambda_k2")], axis=1),
        "cst": _consts(),
    }
    in_maps = []
    for core in range(8):
        b, j = core // 4, core % 4
        own_rows = np.concatenate([np.arange((4 * u + j) * 512, (4 * u + j + 1) * 512) for u in range(NU)])
        m = dict(shared)
        m["x_all"] = np.ascontiguousarray(x[b, :T_])
        m["x_own"] = np.ascontiguousarray(x[b, own_rows])
        m["tabs"] = _tabs(j)
        m["msk"] = _masks(j)
        in_maps.append(m)
    return in_maps


def assemble(results, NU=4, B=2, key="out"):
    T_ = NU * 2048
    outp = np.zeros((B, T_, D), np.float32)
    for core in range(8):
        b, j = core // 4, core % 4
        r = np.asarray(results[core][key])
        for u in range(NU):
            outp[b, (4 * u + j) * 512:(4 * u + j + 1) * 512] = r[u * 512:(u + 1) * 512]
    return outp


_NC_CACHE = {}


def kernel(**inputs):
    NU = 4
    if NU not in _NC_CACHE:
        _NC_CACHE[NU] = build(NU)
    nc = _NC_CACHE[NU]
    in_maps = prep_inputs(inputs, NU)
    res = run_bass_kernel_spmd(nc, in_maps, core_ids=list(range(8)))
    return assemble(res.results, NU)
```

```python
import contextlib
import numpy as np
import concourse.bass as bass
import concourse.mybir as mybir
from concourse.bass_utils import run_bass_kernel_spmd

F32 = mybir.dt.float32
BF16 = mybir.dt.bfloat16
AF = mybir.ActivationFunctionType
ALU = mybir.AluOpType
AX = mybir.AxisListType

D = 1024
DFF = 2816
NFF = DFF // 128
EPS = 1e-6
LAMBDA_INIT = 0.8 - 0.6 * 1.0
SLOPES = [2.0 ** (-8.0 * (h + 1.0) / 4.0) for h in range(4)]
NEG = -30000.0
S_MAX = 8.0 * 1.25
UNDERFLOW = 104.0
SAME_ENGINE_SYNC = True


class ES:
    def __init__(self, name, eng, sem):
        self.name = name
        self.eng = eng
        self.sem = sem
        self.count = 0
        self.seen = {}


class Trk:
    def __init__(self, nc, stack):
        self.nc = nc
        self.stack = stack
        self.lw = {}
        self.rd = {}
        self.dsems = {}
        mk = lambda n, e: ES(n, e, stack.enter_context(nc.semaphore("s_" + n)))
        self.pe = mk("pe", nc.tensor)
        self.act = mk("act", nc.scalar)
        self.dve = mk("dve", nc.vector)
        self.pool = mk("pool", nc.gpsimd)
        self.sp = mk("sp", nc.sync)

    def dsem(self, name):
        if name not in self.dsems:
            self.dsems[name] = ES("d_" + name, None,
                                  self.stack.enter_context(self.nc.semaphore("d_" + name)))
        return self.dsems[name]

    def _deps(self, reads, writes):
        raw = {}
        other = {}

        def add(d, e, v):
            if d.get(e, 0) < v:
                d[e] = v
        for k in reads:
            if k in self.lw:
                add(raw, *self.lw[k])
        for k in writes:
            if k in self.lw:
                add(other, *self.lw[k])
            for e, v in self.rd.get(k, {}).items():
                add(other, e, v)
        return raw, other

    def _wait(self, E, deps):
        raw, other = deps
        allv = dict(other)
        for e, v in raw.items():
            if allv.get(e, 0) < v:
                allv[e] = v
        for e, v in allv.items():
            if e is E:
                if E.name == "pe" or not SAME_ENGINE_SYNC:
                    continue
                v = raw.get(e, 0)
                if v == 0:
                    continue
            if E.seen.get(e, 0) >= v:
                continue
            assert v <= e.count, (E.name, e.name, v, e.count)
            E.eng.wait_ge(e.sem, v)
            E.seen[e] = v

    def _rec(self, E, val, reads, writes):
        for k in writes:
            self.lw[k] = (E, val)
            self.rd[k] = {}
        for k in reads:
            d = self.rd.setdefault(k, {})
            if d.get(E, 0) < val:
                d[E] = val

    def op(self, E, fn, reads=(), writes=(), inc=True):
        pr = [k for k in reads if k.startswith("pb")]
        if pr:
            reads = [k for k in reads if not k.startswith("pb")]
            writes = list(writes) + [k for k in pr if k not in writes]
        self._wait(E, self._deps(reads, writes))
        inst = fn()
        if inc:
            inst.then_inc(E.sem, 1)
            E.count += 1
            val = E.count
        else:
            val = E.count + 1
        self._rec(E, val, reads, writes)
        return inst

    def barrier(self):
        engs = [self.pe, self.act, self.dve, self.pool, self.sp]
        srcs = engs + list(self.dsems.values())
        for E in engs:
            for e in srcs:
                if e is E or e.count == 0 or E.seen.get(e, 0) >= e.count:
                    continue
                E.eng.wait_ge(e.sem, e.count)
                E.seen[e] = e.count

    def dma(self, Q, out, in_, reads, writes, sem):
        self._wait(Q, self._deps(reads, writes))
        Dm = self.dsem(sem)
        inst = Q.eng.dma_start(out=out, in_=in_)
        inst.then_inc(Dm.sem, 16)
        Dm.count += 16
        self._rec(Dm, Dm.count, reads, writes)
        return inst


def build(NU=4, dbg=False):
    T_ = NU * 2048
    NT = T_ // 128
    NO = NU * 4
    TO = NO * 128
    nc = bass.Bass("TRN2", target_bir_lowering=False)

    def din(name, shape, dt=F32):
        return nc.dram_tensor(name, shape, dt, kind="ExternalInput").ap()

    x_all = din("x_all", [T_, D])
    x_own = din("x_own", [TO, D])
    w_p0 = din("w_p0", [D, 1296])
    w_p1 = din("w_p1", [D, 512])
    w_ow = din("w_ow", [D, 2064])
    w_up = din("w_up", [16, 256])
    w_o = din("w_o", [D, D])
    w_g = din("w_g", [D, DFF])
    w_u = din("w_u", [D, DFF])
    w_d = din("w_d", [DFF, D])
    g_attn = din("g_attn", [128, D])
    g_ffn = din("g_ffn", [128, D])
    qk_rep = din("qk_rep", [128, 1024])
    b_rep = din("b_rep", [128, 256])
    gn_rep = din("gn_rep", [128, 256])
    lamv = din("lamv", [128, 256])
    cst = din("cst", [128, 514])
    tabs = din("tabs", [128, 260])
    msk = din("msk", [128, 16 * 512])
    out = nc.dram_tensor("out", [TO, D], F32, kind="ExternalOutput").ap()
    mix_scr = nc.dram_tensor("mix_scr", [TO, D], BF16, kind="Internal").ap()
    h_scr = nc.dram_tensor("h_scr", [TO, D], F32, kind="Internal").ap()
    dbg_outs = {}
    if dbg:
        dbg_outs["d_mix"] = nc.dram_tensor("d_mix", [TO, D], F32, kind="ExternalOutput").ap()
        dbg_outs["d_h"] = nc.dram_tensor("d_h", [TO, D], F32, kind="ExternalOutput").ap()

    with contextlib.ExitStack() as gst:
        T = Trk(nc, gst)
        PE, ACT, DVE, POOL, SP = T.pe, T.act, T.dve, T.pool, T.sp
        V = nc.vector
        A = nc.scalar
        G = nc.gpsimd
        PEe = nc.tensor

        def sbg(n, s, d):
            return gst.enter_context(nc.sbuf_tensor(n, s, d))

        pb = [gst.enter_context(nc.psum_tensor("pb%d" % i, [128, 512], F32)) for i in range(8)]
        pbb = [p[:].bitcast(BF16) for p in pb]

        cstf = sbg("cstf", [128, 514], F32)
        identb = sbg("identb", [128, 128], BF16)
        st = sbg("st", [128, 32], F32)
        junk = sbg("junk", [128, 1024], BF16)
        arena = sbg("arena", [128, 16512 + 3 * 2048], BF16)
        WB = arena[:, 0:16512].rearrange("p (c n) -> p c n", n=2064)
        xbs = [arena[:, 16512 + k * 2048:16512 + (k + 1) * 2048].bitcast(F32) for k in range(3)]
        WG = arena[:, 0:8 * DFF].rearrange("p (c n) -> p c n", n=DFF)
        T.dma(SP, cstf[:], cst[:, :], [], ["cstf"], "c0")
        T.op(DVE, lambda: V.tensor_copy(out=identb[:], in_=cstf[:, 0:128]), ["cstf"], ["identb"])
        triS = cstf[:, 128:256]
        upS = cstf[:, 256:384]
        chI = cstf[:, 512:514]

        def rstd_from_ss(ss_ap, out_ap, n, keys):
            T.op(ACT, lambda: A.activation(out=out_ap, in_=ss_ap, func=AF.Ln, scale=1.0 / n, bias=EPS),
                 keys, keys)
            T.op(ACT, lambda: A.activation(out=out_ap, in_=out_ap, func=AF.Exp, scale=-0.5),
                 keys, keys)

        def norm_transpose(src_ap, xb, xk, nb, nbk, nTt, nTk, grep, grepk, dq, dsem, bank=5):
            T.dma(dq, xb[:], src_ap, [], [xk], dsem)
            T.op(DVE, lambda: V.scalar_tensor_tensor(out=junk[:], in0=xb[:], scalar=1.0, in1=xb[:],
                                                     op0=ALU.mult, op1=ALU.mult, accum_out=st[:, 0:1]),
                 [xk], ["junk", "st0"])
            rstd_from_ss(st[:, 0:1], st[:, 1:2], float(D), ["st0"])
            T.op(DVE, lambda: V.scalar_tensor_tensor(out=nb[:], in0=xb[:], scalar=st[:, 1:2], in1=grep[:],
                                                     op0=ALU.mult, op1=ALU.mult),
                 [xk, "st0", grepk], [nbk])
            bk = "pb%d" % bank
            for c in range(8):
                T.op(PE, lambda c=c: PEe.transpose(out=pbb[bank][:, c * 128:(c + 1) * 128],
                                                   in_=nb[:, c * 128:(c + 1) * 128], identity=identb[:]),
                     [nbk, "identb"], [bk], inc=(c == 7))
            T.op(ACT, lambda: A.copy(out=nTt[:], in_=pbb[bank][:, 0:1024]), [bk], [nTk])

        def project(nTt, nTk, W, Wk, groups):
            for (bank, pc0, wc0, ncol) in groups:
                bk = "pb%d" % bank
                for c in range(8):
                    T.op(PE, lambda c=c, bank=bank, pc0=pc0, wc0=wc0, ncol=ncol:
                         PEe.matmul(pb[bank][:, pc0:pc0 + ncol], lhsT=nTt[:, c * 128:(c + 1) * 128],
                                    rhs=W[:, c, wc0:wc0 + ncol], start=(c == 0), stop=(c == 7)),
                         [nTk, Wk], [bk], inc=(c == 7))

        with contextlib.ExitStack() as mst:
            def sb(n, s, d):
                return mst.enter_context(nc.sbuf_tensor(n, s, d))

            KT = sb("KT", [128, 2, T_], BF16)
            VE = sb("VE", [128, NT, 2, 130], BF16)
            QT = sb("QT", [128, 4, TO], BF16)
            MK = sb("MK", [128, 16, 512], BF16)
            tabf = sb("tabf", [128, 260], F32)
            gat = sb("gat", [128, D], F32)
            qg = sb("qg", [128, 512], F32)
            brep = sb("brep", [128, 256], F32)
            gn = sb("gn", [128, 256], F32)
            wupb = sb("wupb", [16, 256], BF16)
            maskA = sb("maskA", [128, 128], BF16)
            ssx = sb("ssx", [128, 4], F32)
            rstd_all = sb("rstd_all", [128, NT], F32)
            nbs = [sb("nb%d" % i, [128, D], BF16) for i in range(2)]
            nTs = [sb("nT%d" % i, [128, D], BF16) for i in range(2)]
            kf = sb("kf", [128, 512], F32)
            sq = sb("sq", [128, 512], F32)
            k16s = [sb("k16_%d" % i, [128, 512], BF16) for i in range(2)]
            gkf = [sb("gkf%d" % i, [128, 256], F32) for i in range(2)]
            glr16s = [sb("glr16_%d" % i, [128, 16], BF16) for i in range(2)]
            glrT = sb("glrT", [16, 128], BF16)
            zb = sb("zb", [128, 256], F32)
            lg = sb("lg", [128, 256], F32)
            eb = sb("eb", [128, 256], F32)
            enb = sb("enb", [128, 256], F32)
            ec = sb("ec", [128, 256], F32)
            dec = sb("dec", [128, 4], F32)
            qt16 = sb("qt16", [128, 256], BF16)
            kt16 = sb("kt16", [128, 256], BF16)
            kh16 = sb("kh16", [128, 256], BF16)
            vb16s = [sb("vb16_%d" % i, [128, 512], BF16) for i in range(2)]
            qkT = sb("qkT", [128, 512], BF16)
            AT = sb("AT", [128, 512], BF16)
            S = sb("S", [128, 256], F32)
            Sb = [sb("Sb%d" % i, [128, 256], BF16) for i in range(2)]
            snap = sb("snap", [128, NU, 256], F32)
            sgt = sb("sgt", [128, 512], F32)
            gg = sb("gg", [128, 512], F32)
            ogl = sb("ogl", [128, 512], BF16)
            PTs = [sb("PT%d" % i, [128, 512], BF16) for i in range(6)]
            accS = sb("accS", [128, 3, 390], F32)
            t1 = sb("t1", [128, 128], F32)
            of = sb("of", [128, 128], F32)
            og = sb("og", [128, 4, 128], BF16)

            T.dma(SP, tabf[:], tabs[:, :], [], ["tabf"], "c1")
            T.dma(SP, gat[:], g_attn[:, :], [], ["gat"], "c2")
            T.dma(SP, kf[:], qk_rep[:, 0:512], [], ["kf"], "c3")
            T.dma(SP, sq[:], qk_rep[:, 512:1024], [], ["sq"], "c3b")
            T.dma(SP, brep[:], b_rep[:, :], [], ["brep"], "c4")
            T.dma(SP, gn[:], gn_rep[:, :], [], ["gn"], "c5")
            lam = zb
            T.dma(SP, lam[:], lamv[:, :], [], ["zb"], "c6")
            T.dma(POOL, wupb[:], w_up[:, :], [], ["wupb"], "c7")
            T.dma(POOL, maskA[:], cst[:, 384:512], [], ["maskA"], "c8")
            T.dma(POOL, WB[:, :, 0:1296], w_p0.rearrange("(c p) n -> p c n", p=128), [], ["WB"], "w0")
            for e4 in range(4):
                T.dma(POOL, MK[:, e4 * 4:(e4 + 1) * 4, :],
                      msk[:, e4 * 2048:(e4 + 1) * 2048].rearrange("p (a b) -> p a b", b=512),
                      [], ["MK"], "c9")
            T.op(DVE, lambda: V.scalar_tensor_tensor(out=qg[:, 0:512], in0=kf[:, 0:512], scalar=0.125,
                                                     in1=sq[:, 0:512], op0=ALU.mult, op1=ALU.mult),
                 ["kf", "sq"], ["qg"])
            T.op(DVE, lambda: V.tensor_scalar(out=gn[:, 0:128], in0=gn[:, 0:128], scalar1=1.0 - LAMBDA_INIT,
                                              scalar2=None, op0=ALU.mult), ["gn"], ["gn"])
            T.op(DVE, lambda: V.scalar_tensor_tensor(out=junk[:, 0:64], in0=lam[:, 0:64], scalar=1.0,
                                                     in1=lam[:, 64:128], op0=ALU.mult, op1=ALU.mult,
                                                     accum_out=st[:, 4:5]), ["zb"], ["junk", "stl"])
            T.op(DVE, lambda: V.scalar_tensor_tensor(out=junk[:, 0:64], in0=lam[:, 128:192], scalar=1.0,
                                                     in1=lam[:, 192:256], op0=ALU.mult, op1=ALU.mult,
                                                     accum_out=st[:, 5:6]), ["zb", "stl"], ["junk", "stl"])
            T.op(ACT, lambda: A.activation(out=st[:, 6:8], in_=st[:, 4:6], func=AF.Exp), ["stl"], ["stl"])
            T.op(DVE, lambda: V.tensor_tensor(out=st[:, 8:9], in0=st[:, 7:8], in1=st[:, 6:7], op=ALU.subtract),
                 ["stl"], ["stl"])
            T.op(DVE, lambda: V.tensor_scalar(out=st[:, 8:9], in0=st[:, 8:9], scalar1=-LAMBDA_INIT,
                                              scalar2=None, op0=ALU.add), ["stl"], ["stl"])
            neglam = st[:, 8:9]
            T.op(POOL, lambda: G.memset(VE[:, :, :, 128:130], 1.0), [], ["VE"])
            T.op(DVE, lambda: V.memset(S[:], 0.0), [], ["S"])

            def k_evac_a(bank, c0, ngrp):
                n = 64 * ngrp
                T.op(ACT, lambda: A.copy(out=kf[:, 0:n], in_=pb[bank][:, c0:c0 + n]), ["pb%d" % bank], ["kf"])

            def k_evac_b(ngrp):
                n = 64 * ngrp
                T.op(DVE, lambda: V.tensor_tensor(out=sq[:, 0:n], in0=kf[:, 0:n], in1=kf[:, 0:n], op=ALU.mult),
                     ["kf"], ["sq"])
                T.op(DVE, lambda: V.tensor_reduce(out=st[:, 16:16 + ngrp],
                                                  in_=sq[:, 0:n].rearrange("p (a b) -> p a b", b=64),
                                                  axis=AX.X, op=ALU.add), ["sq"], ["stk"])
                rstd_from_ss(st[:, 16:16 + ngrp], st[:, 24:24 + ngrp], 64.0, ["stk"])

            def k_evac_c(ngrp, ks, gain=None):
                n = 64 * ngrp
                kk = "k16_%d" % ks
                bc = bass.AP(st[:].tensor, st[:, 24:25].offset, [list(st[:].ap[0]), [1, ngrp], [0, 64]])
                if gain is None:
                    T.op(DVE, lambda: V.tensor_tensor(out=k16s[ks][:, 0:n].rearrange("p (a b) -> p a b", b=64),
                                                      in0=kf[:, 0:n].rearrange("p (a b) -> p a b", b=64),
                                                      in1=bc, op=ALU.mult), ["kf", "stk"], [kk])
                else:
                    T.op(DVE, lambda: V.tensor_tensor(out=sq[:, 0:n].rearrange("p (a b) -> p a b", b=64),
                                                      in0=kf[:, 0:n].rearrange("p (a b) -> p a b", b=64),
                                                      in1=bc, op=ALU.mult), ["kf", "stk", "sq"], ["sq"])
                    T.op(DVE, lambda: V.tensor_tensor(out=k16s[ks][:, 0:n], in0=sq[:, 0:n], in1=gain,
                                                      op=ALU.mult), ["sq", "qg"], [kk])

            def k_evac(bank, c0, ngrp, ks, gain=None):
                k_evac_a(bank, c0, ngrp)
                k_evac_b(ngrp)
                k_evac_c(ngrp, ks, gain)

            def k_tr(ks, n, dst_fn):
                nh = n // 128
                for hh in range(nh):
                    T.op(PE, lambda hh=hh: PEe.transpose(out=pbb[4][:, hh * 128:(hh + 1) * 128],
                                                         in_=k16s[ks][:, hh * 128:(hh + 1) * 128], identity=identb[:]),
                         ["k16_%d" % ks, "identb"], ["pb4"], inc=(hh == nh - 1))
                dst_fn(pbb[4][:, 0:n].rearrange("p (a b) -> p a b", b=128))

            def gla_segs(own, i_own, gk, gkk, gq, gqk, vb, vbk, g16, g16k, psS):
                def s_a():
                    T.op(PE, lambda: PEe.transpose(out=pbb[6][0:16, 64:192], in_=g16[:], identity=identb[:]),
                         [g16k, "identb"], ["pb6"])
                    T.op(ACT, lambda: A.copy(out=glrT[:], in_=pbb[6][0:16, 64:192]), ["pb6"], ["glrT"])

                def s_b():
                    T.op(PE, lambda: PEe.matmul(pb[6][:, 128:384], lhsT=glrT[:], rhs=wupb[:], start=True, stop=True),
                         ["glrT", "wupb"], ["pb6"])
                    T.op(DVE, lambda: V.tensor_tensor(out=zb[:], in0=pb[6][:, 128:384], in1=brep[:], op=ALU.add),
                         ["pb6", "brep"], ["zb"])
                    T.op(ACT, lambda: A.activation(out=lg[:], in_=zb[:], func=AF.Exp, scale=-1.0), ["zb"], ["lg"])
                    T.op(ACT, lambda: A.activation(out=lg[:], in_=lg[:], func=AF.Ln, bias=1.0), ["lg"], ["lg"])

                def s_c():
                    if own:
                        T.op(PE, lambda: PEe.matmul(pb[7][:, 0:256], lhsT=triS, rhs=lg[:], start=True, stop=True),
                             ["cstf", "lg"], ["pb7"], inc=False)
                    T.op(PE, lambda: PEe.matmul(pb[7][:, 256:512], lhsT=upS, rhs=lg[:], start=True, stop=True),
                         ["cstf", "lg"], ["pb7"])
                    for p in range(2):
                        T.op(PE, lambda p=p: PEe.matmul(pb[6][:, 96 + 2 * p:98 + 2 * p], lhsT=lg[:, p * 128:(p + 1) * 128],
                                                        rhs=chI, start=True, stop=True),
                             ["lg", "cstf"], ["pb6"], inc=(p == 1))
                    T.op(ACT, lambda: A.activation(out=ec[:], in_=pb[7][:, 256:512], func=AF.Exp), ["pb7"], ["ec"])
                    if own:
                        T.op(ACT, lambda: A.activation(out=eb[:], in_=pb[7][:, 0:256], func=AF.Exp), ["pb7"], ["eb"])
                        T.op(ACT, lambda: A.activation(out=enb[:], in_=pb[7][:, 0:256], func=AF.Exp, scale=-1.0),
                             ["pb7"], ["enb"])
                    T.op(ACT, lambda: A.activation(out=dec[:], in_=pb[6][:, 96:100], func=AF.Exp), ["pb6"], ["dec"])
                    T.op(DVE, lambda: V.tensor_tensor(out=kh16[:], in0=gk, in1=ec[:], op=ALU.mult),
                         [gkk, "ec"], ["kh16"])
                    if own:
                        T.op(DVE, lambda: V.scalar_tensor_tensor(out=qt16[:], in0=gq, scalar=0.125,
                                                                 in1=eb[:], op0=ALU.mult, op1=ALU.mult),
                             [gqk, "eb"], ["qt16"])
                        T.op(DVE, lambda: V.tensor_tensor(out=kt16[:], in0=gk, in1=enb[:], op=ALU.mult),
                             [gkk, "enb"], ["kt16"])
                        for p in range(2):
                            T.op(PE, lambda p=p: PEe.transpose(out=pbb[4][:, 512 + p * 128:512 + (p + 1) * 128],
                                                               in_=qt16[:, p * 128:(p + 1) * 128], identity=identb[:]),
                                 ["qt16", "identb"], ["pb4"], inc=False)
                        for p in range(2):
                            T.op(PE, lambda p=p: PEe.transpose(out=pbb[4][:, 768 + p * 128:768 + (p + 1) * 128],
                                                               in_=kt16[:, p * 128:(p + 1) * 128], identity=identb[:]),
                                 ["kt16", "identb"], ["pb4"], inc=(p == 1))
                        T.op(ACT, lambda: A.copy(out=qkT[:], in_=pbb[4][:, 512:1024]), ["pb4"], ["qkT"])

                def s_c2():
                    for hh in range(2):
                        bank = 0 if hh == 0 else 3
                        for p in range(2):
                            T.op(PE, lambda hh=hh, p=p, bank=bank:
                                 PEe.matmul(pb[bank][:, p * 128:(p + 1) * 128],
                                            lhsT=qkT[hh * 64:(hh + 1) * 64, 256 + p * 128:256 + (p + 1) * 128],
                                            rhs=qkT[hh * 64:(hh + 1) * 64, p * 128:(p + 1) * 128],
                                            start=True, stop=True),
                                 ["qkT"], ["pb%d" % bank], inc=(p == 1))
                    mA = bass.AP(maskA[:].tensor, maskA[:].offset, [list(maskA[:].ap[0]), [0, 2], [1, 128]])
                    for hh in range(2):
                        bank = 0 if hh == 0 else 3
                        outv = bass.AP(AT[:].tensor, AT[:, hh * 128:hh * 128 + 1].offset,
                                       [list(AT[:].ap[0]), [256, 2], [1, 128]])
                        T.op(DVE, lambda bank=bank, outv=outv: V.tensor_tensor(
                            out=outv, in0=pb[bank][:, 0:256].rearrange("p (a b) -> p a b", b=128), in1=mA, op=ALU.mult),
                            ["pb%d" % bank, "maskA"], ["AT"])

                def s_d(ch):
                    def f():
                        if own:
                            T.op(DVE, lambda: V.tensor_copy(out=Sb[ch][:], in_=S[:]), ["S"], ["Sb%d" % ch])
                        sbank, scol = psS[ch]
                        sk = "pb%d" % sbank
                        for p in range(2):
                            for hh in range(2):
                                h = 2 * p + hh
                                T.op(PE, lambda p=p, hh=hh, h=h:
                                     PEe.matmul(pb[sbank][hh * 64:(hh + 1) * 64, scol + p * 128:scol + (p + 1) * 128],
                                                lhsT=kh16[ch * 64:(ch + 1) * 64, h * 64:(h + 1) * 64],
                                                rhs=vb[ch * 64:(ch + 1) * 64, h * 128:(h + 1) * 128],
                                                start=True, stop=True),
                                     ["kh16", vbk], [sk], inc=(p == 1 and hh == 1))
                        for p in range(2):
                            T.op(DVE, lambda p=p:
                                 V.scalar_tensor_tensor(out=S[:, p * 128:(p + 1) * 128], in0=S[:, p * 128:(p + 1) * 128],
                                                        scalar=dec[:, 2 * p + ch:2 * p + ch + 1],
                                                        in1=pb[sbank][:, scol + p * 128:scol + (p + 1) * 128],
                                                        op0=ALU.mult, op1=ALU.add),
                                 ["S", "dec", sk], ["S"])
                    return f

                def s_o():
                    for hh in range(2):
                        bank = 0 if hh == 0 else 3
                        bk = "pb%d" % bank
                        for p in range(2):
                            h = 2 * p + hh
                            oc = 256 + p * 128
                            T.op(PE, lambda h=h, bank=bank, oc=oc:
                                 PEe.matmul(pb[bank][:, oc:oc + 128], lhsT=AT[:, h * 128:(h + 1) * 128],
                                            rhs=vb[:, h * 128:(h + 1) * 128], start=True, stop=False),
                                 ["AT", vbk], [bk], inc=False)
                            for ch in range(2):
                                T.op(PE, lambda h=h, hh=hh, p=p, ch=ch, bank=bank, oc=oc:
                                     PEe.matmul(pb[bank][ch * 64:(ch + 1) * 64, oc:oc + 128],
                                                lhsT=qkT[hh * 64:(hh + 1) * 64, p * 128 + ch * 64:p * 128 + (ch + 1) * 64],
                                                rhs=Sb[ch][hh * 64:(hh + 1) * 64, p * 128:(p + 1) * 128],
                                                start=False, stop=True),
                                     ["qkT", "Sb%d" % ch], [bk], inc=(ch == 1 and p == 1))
                    for hh in range(2):
                        bank = 0 if hh == 0 else 3
                        bk = "pb%d" % bank
                        T.op(ACT, lambda bank=bank: A.copy(out=kf[:, 0:256], in_=pb[bank][:, 256:512]), [bk], ["kf"])
                        T.op(DVE, lambda: V.tensor_tensor(out=sq[:, 0:256], in0=kf[:, 0:256], in1=kf[:, 0:256],
                                                          op=ALU.mult), ["kf"], ["sq"])
                        T.op(DVE, lambda: V.tensor_reduce(out=st[:, 16:18],
                                                          in_=sq[:, 0:256].rearrange("p (a b) -> p a b", b=128),
                                                          axis=AX.X, op=ALU.add), ["sq"], ["stk"])
                        rstd_from_ss(st[:, 16:18], st[:, 24:26], 128.0, ["stk"])
                        bc = bass.AP(st[:].tensor, st[:, 24:25].offset, [list(st[:].ap[0]), [1, 2], [0, 128]])
                        ggv = bass.AP(gg[:].tensor, gg[:, hh * 128:hh * 128 + 1].offset,
                                      [list(gg[:].ap[0]), [256, 2], [1, 128]])
                        oglv = bass.AP(ogl[:].tensor, ogl[:, hh * 128:hh * 128 + 1].offset,
                                       [list(ogl[:].ap[0]), [256, 2], [1, 128]])
                        T.op(DVE, lambda bc=bc: V.tensor_tensor(out=sq[:, 256:512].rearrange("p (a b) -> p a b", b=128),
                                                                in0=kf[:, 0:256].rearrange("p (a b) -> p a b", b=128),
                                                                in1=bc, op=ALU.mult), ["kf", "stk"], ["sq2"])
                        T.op(DVE, lambda ggv=ggv, oglv=oglv: V.tensor_tensor(
                            out=oglv, in0=sq[:, 256:512].rearrange("p (a b) -> p a b", b=128), in1=ggv, op=ALU.mult),
                            ["sq2", "gg"], ["ogl"])
                    T.dma(SP, mix_scr[i_own * 128:(i_own + 1) * 128, 512:1024], ogl[:], ["ogl"], ["mixg%d" % i_own], "mx")

                if own:
                    return [s_a, s_b, s_c, s_c2, s_d(0), s_d(1), s_o]
                return [s_a, s_b, s_c, s_d(0), s_d(1)]

            def seg_load(t):
                sl = t % 3
                T.dma(SP, xbs[sl][:], x_all[t * 128:(t + 1) * 128, :], [], ["xb%d" % sl], "x%d" % sl)

            def seg_stats(t):
                sl = t % 3
                T.op(DVE, lambda: V.scalar_tensor_tensor(out=junk[:], in0=xbs[sl][:], scalar=1.0, in1=xbs[sl][:],
                                                         op0=ALU.mult, op1=ALU.mult, accum_out=ssx[:, sl:sl + 1]),
                     ["xb%d" % sl], ["junk", "ssx%d" % sl])
                T.op(ACT, lambda: A.activation(out=rstd_all[:, t:t + 1], in_=ssx[:, sl:sl + 1], func=AF.Ln,
                                               scale=1.0 / D, bias=EPS), ["ssx%d" % sl], ["rs%d" % t])
                T.op(ACT, lambda: A.activation(out=rstd_all[:, t:t + 1], in_=rstd_all[:, t:t + 1], func=AF.Exp, scale=-0.5),
                     ["rs%d" % t], ["rs%d" % t])

            def seg_n(t):
                sl = t % 3
                ns = t % 2
                T.op(DVE, lambda: V.scalar_tensor_tensor(out=nbs[ns][:], in0=xbs[sl][:], scalar=rstd_all[:, t:t + 1],
                                                         in1=gat[:], op0=ALU.mult, op1=ALU.mult),
                     ["xb%d" % sl, "rs%d" % t, "gat"], ["nb%d" % ns])

            def seg_tr(t):
                ns = t % 2
                for c in range(8):
                    T.op(PE, lambda c=c: PEe.transpose(out=pbb[3][:, c * 128:(c + 1) * 128],
                                                       in_=nbs[ns][:, c * 128:(c + 1) * 128], identity=identb[:]),
                         ["nb%d" % ns, "identb"], ["pb3"], inc=(c == 7))
                T.op(ACT, lambda: A.copy(out=nTs[ns][:], in_=pbb[3][:, 0:1024]), ["pb3"], ["nT%d" % ns])

            def run_pipe(n_tiles, early_fn, main_fn, late_fn):
                seg_load(0)
                for it in range(-1, n_tiles + 1):
                    early = early_fn(it + 1) if 0 <= it + 1 < n_tiles else []
                    main = main_fn(it) if 0 <= it < n_tiles else []
                    late = late_fn(it - 1) if 0 <= it - 1 < n_tiles else []
                    for k in range(max(len(early), len(main), len(late))):
                        if k < len(main):
                            main[k]()
                        if k < len(late):
                            late[k]()
                        if k < len(early):
                            early[k]()

            def kv_store(t):
                def dst(src3):
                    T.op(ACT, lambda: A.copy(out=KT[:, :, t * 128:(t + 1) * 128], in_=src3), ["pb4"], ["KT"])
                return dst

            def v_store(t):
                T.op(DVE, lambda: V.tensor_copy(out=VE[:, t, :, 0:128],
                                                in_=pb[0][:, 256:512].rearrange("p (a b) -> p a b", b=128)),
                     ["pb0"], ["VE"])

            def p0_iter(it):
                ok = lambda t: 0 <= t < NT
                if ok(it + 2):
                    seg_stats(it + 2)
                if ok(it + 1):
                    seg_tr(it + 1)
                late = []
                if ok(it - 1):
                    t1 = it - 1
                    n1 = t1 % 2
                    late = gla_segs(False, None, gkf[n1][:], "gkf%d" % n1, None, None, vb16s[n1], "vb16_%d" % n1,
                                    glr16s[n1], "glr16_%d" % n1, ((5, 0), (5, 256)))
                ns = it % 2
                if ok(it):
                    project(nTs[ns], "nT%d" % ns, WB, "WB", [(0, 0, 0, 512)])
                    k_evac_a(0, 0, 4)
                    v_store(it)
                    k_evac_b(4)
                if ok(it - 1):
                    t1 = it - 1
                    k_tr(t1 % 2, 256, kv_store(t1))
                    if t1 % 16 % 4 == 0:
                        u = t1 // 16
                        r = (t1 % 16) // 4
                        if r == 0:
                            T.op(DVE, lambda: V.tensor_scalar(out=snap[:, u, :], in0=S[:],
                                                              scalar1=tabf[:, 256 + r:257 + r], scalar2=None,
                                                              op0=ALU.mult), ["S", "tabf"], ["snap"])
                        else:
                            T.op(DVE, lambda: V.scalar_tensor_tensor(out=snap[:, u, :], in0=S[:],
                                                                     scalar=tabf[:, 256 + r:257 + r],
                                                                     in1=snap[:, u, :], op0=ALU.mult, op1=ALU.add),
                                 ["S", "tabf", "snap"], ["snap"])
                    late[0]()
                if ok(it + 2):
                    seg_n(it + 2)
                if ok(it):
                    project(nTs[ns], "nT%d" % ns, WB, "WB", [(1, 0, 512, 272)])
                    T.op(DVE, lambda: V.tensor_copy(out=gkf[ns][:], in_=pb[1][:, 0:256]), ["pb1"], ["gkf%d" % ns])
                    T.op(DVE, lambda: V.tensor_copy(out=glr16s[ns][:], in_=pb[1][:, 256:272]), ["pb1"], ["glr16_%d" % ns])
                    k_evac_c(4, ns)
                if ok(it - 1):
                    late[1]()
                if ok(it):
                    project(nTs[ns], "nT%d" % ns, WB, "WB", [(2, 0, 784, 512)])
                    T.op(ACT, lambda: A.copy(out=vb16s[ns][:], in_=pb[2][:, :]), ["pb2"], ["vb16_%d" % ns])
                if ok(it - 1):
                    late[2]()
                    late[3]()
                    late[4]()
                if ok(it + 3):
                    seg_load(it + 3)

            for t in range(min(3, NT)):
                seg_load(t)
            for it in range(-2, NT + 1):
                p0_iter(it)

            T.dma(POOL, WB[:, :, :], w_ow.rearrange("(c p) n -> p c n", p=128), [], ["WB"], "w0")

            def q_store(i):
                def dst(src3):
                    T.op(ACT, lambda: A.copy(out=QT[:, :, i * 128:(i + 1) * 128], in_=src3), ["pb4"], ["QT"])
                return dst

            gnb = bass.AP(gn[:].tensor, gn[:, 128:129].offset, [list(gn[:].ap[0]), [0, 4], [1, 128]])

            def o_load(i):
                sl = i % 3
                T.dma(SP, xbs[sl][:], x_own[i * 128:(i + 1) * 128, :], [], ["xb%d" % sl], "x%d" % sl)

            def o_stats(i):
                sl = i % 3
                T.op(DVE, lambda: V.scalar_tensor_tensor(out=junk[:], in0=xbs[sl][:], scalar=1.0, in1=xbs[sl][:],
                                                         op0=ALU.mult, op1=ALU.mult, accum_out=ssx[:, sl:sl + 1]),
                     ["xb%d" % sl], ["junk", "ssx%d" % sl])
                rstd_from_ss(ssx[:, sl:sl + 1], ssx[:, 3:4] if False else st[:, 1 + (i % 2):2 + (i % 2)], float(D), ["ssx%d" % sl, "sto%d" % (i % 2)])

            def o_n(i):
                sl = i % 3
                ns = i % 2
                T.op(DVE, lambda: V.scalar_tensor_tensor(out=nbs[ns][:], in0=xbs[sl][:], scalar=st[:, 1 + ns:2 + ns],
                                                         in1=gat[:], op0=ALU.mult, op1=ALU.mult),
                     ["xb%d" % sl, "sto%d" % ns, "gat"], ["nb%d" % ns])

            def o_tr(i):
                ns = i % 2
                for c in range(8):
                    T.op(PE, lambda c=c: PEe.transpose(out=pbb[5][:, c * 128:(c + 1) * 128],
                                                       in_=nbs[ns][:, c * 128:(c + 1) * 128], identity=identb[:]),
                         ["nb%d" % ns, "identb"], ["pb5"], inc=(c == 7))
                T.op(ACT, lambda: A.copy(out=nTs[ns][:], in_=pbb[5][:, 0:1024]), ["pb5"], ["nT%d" % ns])

            for i in range(min(2, NO)):
                o_load(i)
            o_stats(0)
            o_n(0)
            o_tr(0)
            for i in range(NO):
                s = i % 2
                if i + 2 < NO:
                    o_load(i + 2)
                if i + 1 < NO:
                    o_stats(i + 1)
                if i % 4 == 0:
                    T.op(DVE, lambda i=i: V.tensor_copy(out=S[:], in_=snap[:, i // 4, :]), ["snap"], ["S"])
                project(nTs[s], "nT%d" % s, WB, "WB", [(0, 0, 0, 512), (3, 0, 1536, 512)])
                if i + 1 < NO:
                    o_n(i + 1)
                k_evac_a(0, 0, 8)
                T.op(ACT, lambda: A.activation(out=sgt[:], in_=pb[3][:, :], func=AF.Exp, scale=-1.0), ["pb3"], ["sgt"])
                T.op(ACT, lambda: A.activation(out=sgt[:], in_=sgt[:], func=AF.Ln, bias=1.0), ["sgt"], ["sgt"])
                T.op(ACT, lambda: A.activation(out=sgt[:], in_=sgt[:], func=AF.Exp, scale=-1.0), ["sgt"], ["sgt"])
                project(nTs[s], "nT%d" % s, WB, "WB", [(1, 0, 512, 512), (2, 0, 1024, 512), (6, 0, 2048, 16)])
                if i + 1 < NO:
                    o_tr(i + 1)
                k_evac_b(8)
                T.op(DVE, lambda: V.tensor_tensor(out=gg[:], in0=pb[3][:, :], in1=sgt[:], op=ALU.mult),
                     ["pb3", "sgt"], ["gg"])
                T.op(DVE, lambda: V.tensor_copy(out=glr16s[0][:], in_=pb[6][:, 0:16]), ["pb6"], ["glr16_0"])
                T.op(ACT, lambda: A.copy(out=vb16s[0][:], in_=pb[2][:, :]), ["pb2"], ["vb16_0"])
                k_evac_c(8, 0, gain=qg[:, 0:512])
                T.op(DVE, lambda: V.tensor_tensor(out=gg[:].rearrange("p (a b) -> p a b", b=128),
                                                  in0=gg[:].rearrange("p (a b) -> p a b", b=128), in1=gnb, op=ALU.mult),
                     ["gg", "gn"], ["gg"])
                segs = gla_segs(True, i, pb[1][:, 256:512], "pb1", pb[1][:, 0:256], "pb1", vb16s[0], "vb16_0",
                                glr16s[0], "glr16_0", ((2, 0), (2, 256)))
                segs[0]()
                k_tr(0, 512, q_store(i))
                for sg in segs[1:]:
                    sg()

            def acc_ap(a):
                bank = 5 + a // 3
                col = (a % 3) * 130
                return bank, col

            def attention(hp):
                def kb_lo(h, u):
                    n_h = int(np.ceil(1.0 + (UNDERFLOW + 2.0 * S_MAX) / (128.0 * SLOPES[h])))
                    return max(0, 16 * u - (n_h - 1))
                steps = [(u, hl, kb) for u in range(NU) for hl in range(2)
                         for kb in range(kb_lo(2 * hp + hl, u), 16 * u + 16)]
                LOOK = 2

                def emit_qk(idx):
                    u, hl, kb = steps[idx]
                    h = 2 * hp + hl
                    e = kb - 16 * u
                    ei = e + 48
                    for m in range(2):
                        sbank = 1 + (2 * idx + m) % 4
                        T.op(PE, lambda m=m, sbank=sbank:
                             PEe.matmul(pb[sbank][:, 0:512],
                                        lhsT=KT[m * 64:(m + 1) * 64, hl, kb * 128:(kb + 1) * 128],
                                        rhs=QT[m * 64:(m + 1) * 64, h, u * 512:(u + 1) * 512],
                                        start=True, stop=True),
                             ["KT", "QT"], ["pb%d" % sbank])
                    for m in range(2):
                        sbank = 1 + (2 * idx + m) % 4
                        slot = (2 * idx + m) % 6
                        T.op(ACT, lambda sbank=sbank, slot=slot:
                             A.activation(out=PTs[slot][:], in_=pb[sbank][:, 0:512], func=AF.Exp,
                                          bias=tabf[:, h * 64 + ei:h * 64 + ei + 1]),
                             ["pb%d" % sbank, "tabf"], ["PT%d" % slot])
                        if e >= 0:
                            T.op(POOL, lambda slot=slot:
                                 G.tensor_tensor(out=PTs[slot][:], in0=PTs[slot][:], in1=MK[:, e, :], op=ALU.mult),
                                 ["PT%d" % slot, "MK"], ["PT%d" % slot])

                def emit_pv(idx):
                    u, hl, kb = steps[idx]
                    h = 2 * hp + hl
                    nkb = 16 * u + 16
                    for m in range(2):
                        slot = (2 * idx + m) % 6
                        for c in range(4):
                            bank, col = acc_ap(c * 2 + m)
                            first = (kb == kb_lo(h, u) and (c * 2 + m) in (0, 4, 6))
                            T.op(PE, lambda c=c, bank=bank, col=col, slot=slot, first=first:
                                 PEe.matmul(pb[bank][:, col:col + 130],
                                            lhsT=PTs[slot][:, c * 128:(c + 1) * 128],
                                            rhs=VE[:, kb, hl, :], start=first, stop=(kb == nkb - 1),
                                            skip_group_check=True),
                                 ["PT%d" % slot, "VE"], ["pb%d" % bank], inc=(c == 3 and m == 1))
                    if kb != nkb - 1:
                        return
                    for bi in range(3):
                        T.op(DVE, lambda bi=bi: V.tensor_copy(out=accS[:, bi, :], in_=pb[5 + bi][:, 0:390]),
                             ["pb%d" % (5 + bi)], ["accS"])
                    for c in range(4):
                        a0 = c * 2
                        a1 = c * 2 + 1
                        A0 = accS[:, a0 // 3, (a0 % 3) * 130:(a0 % 3) * 130 + 130]
                        A1 = accS[:, a1 // 3, (a1 % 3) * 130:(a1 % 3) * 130 + 130]
                        T.op(DVE, lambda A0=A0: V.reciprocal(out=st[:, 10:11], in_=A0[:, 128:129]), ["accS"], ["sta"])
                        T.op(DVE, lambda A1=A1: V.reciprocal(out=st[:, 11:12], in_=A1[:, 128:129]), ["accS", "sta"], ["sta"])
                        T.op(DVE, lambda: V.tensor_tensor(out=st[:, 11:12], in0=st[:, 11:12], in1=neglam, op=ALU.mult),
                             ["sta", "stl"], ["sta"])
                        T.op(DVE, lambda A1=A1: V.tensor_scalar(out=t1[:], in0=A1[:, 0:128], scalar1=st[:, 11:12],
                                                                scalar2=None, op0=ALU.mult), ["accS", "sta"], ["t1"])
                        T.op(DVE, lambda A0=A0: V.scalar_tensor_tensor(out=of[:], in0=A0[:, 0:128], scalar=st[:, 10:11],
                                                                       in1=t1[:], op0=ALU.mult, op1=ALU.add),
                             ["accS", "sta", "t1"], ["of"])
                        T.op(DVE, lambda: V.scalar_tensor_tensor(out=junk[:, 0:128], in0=of[:], scalar=1.0, in1=of[:],
                                                                 op0=ALU.mult, op1=ALU.mult, accum_out=st[:, 12:13]),
                             ["of"], ["junk", "stb"])
                        rstd_from_ss(st[:, 12:13], st[:, 13:14], 128.0, ["stb"])
                        T.op(DVE, lambda c=c: V.scalar_tensor_tensor(out=og[:, c, :], in0=of[:], scalar=st[:, 13:14],
                                                                     in1=gn[:, 0:128], op0=ALU.mult, op1=ALU.mult),
                             ["of", "stb", "gn"], ["og"])
                    T.dma(SP, mix_scr[u * 512:(u + 1) * 512, h * 128:(h + 1) * 128].rearrange("(c p) n -> p c n", p=128),
                          og[:], ["og"], ["mixd%d_%d" % (u, h)], "mx2")

                for i in range(len(steps) + LOOK):
                    if i < len(steps):
                        emit_qk(i)
                    if i - LOOK >= 0:
                        emit_pv(i - LOOK)

            attention(0)

            T.dma(POOL, WB[:, :, 0:512], w_p1.rearrange("(c p) n -> p c n", p=128), [], ["WB"], "w0")

            def p1_iter(it):
                ok = lambda t: 0 <= t < NT
                if ok(it + 2):
                    seg_n(it + 2)
                if ok(it + 1):
                    seg_tr(it + 1)
                ns = it % 2
                if ok(it):
                    project(nTs[ns], "nT%d" % ns, WB, "WB", [(0, 0, 0, 512)])
                    k_evac_a(0, 0, 4)
                    v_store(it)
                    k_evac_b(4)
                if ok(it - 1):
                    k_tr((it - 1) % 2, 256, kv_store(it - 1))
                if ok(it):
                    k_evac_c(4, ns)
                if ok(it + 3):
                    seg_load(it + 3)

            for t in range(min(3, NT)):
                seg_load(t)
            for it in range(-2, NT + 1):
                p1_iter(it)
            T.dma(POOL, WG, w_g.rearrange("(c p) n -> p c n", p=128), [], ["WB", "xb0", "xb1", "xb2", "WG"], "w2")
            attention(1)
            T.barrier()
            mix_keys = ["mixg%d" % i for i in range(NO)] + ["mixd%d_%d" % (u, h) for u in range(NU) for h in range(4)]

        with contextlib.ExitStack() as fst:
            def sb(n, s, d):
                return fst.enter_context(nc.sbuf_tensor(n, s, d))

            WU = sb("WU", [128, 8, DFF], BF16)
            WD = sb("WD", [128, NFF, D], BF16)
            gff = sb("gff", [128, D], F32)
            hbs = [sb("hb%d" % i, [128, D], F32) for i in range(2)]
            nbs = [sb("fnb%d" % i, [128, D], BF16) for i in range(2)]
            obs = [sb("ob%d" % i, [128, D], F32) for i in range(2)]
            T.dma(SP, gff[:], g_ffn[:, :], [], ["gff"], "c2")
            with contextlib.ExitStack() as ost:
                def sbo(n, s, d):
                    return ost.enter_context(nc.sbuf_tensor(n, s, d))
                WO = sbo("WO", [128, 8, D], BF16)
                mxs = [sbo("mx%d" % i, [128, D], BF16) for i in range(2)]
                mxT = [sbo("mxT%d" % i, [128, D], BF16) for i in range(2)]
                xos = [sbo("xo%d" % i, [128, D], F32) for i in range(2)]
                T.dma(POOL, WO[:], w_o.rearrange("(c p) n -> p c n", p=128), [], ["WO"], "w1")
                T.dma(POOL, WU[:], w_u.rearrange("(c p) n -> p c n", p=128), [], ["WU"], "w3")
                T.dma(POOL, WD[:], w_d.rearrange("(c p) n -> p c n", p=128), [], ["WD"], "w4")
                for i in range(NO):
                    s = i % 2
                    T.dma(SP, mxs[s][:], mix_scr[i * 128:(i + 1) * 128, :], mix_keys if i < 2 else [], ["mx%d" % s], "m%d" % s)
                    T.dma(SP, xos[s][:], x_own[i * 128:(i + 1) * 128, :], [], ["xo%d" % s], "xo%d" % s)
                    for c in range(8):
                        T.op(PE, lambda c=c, s=s: PEe.transpose(out=pbb[0][:, c * 128:(c + 1) * 128],
                                                                in_=mxs[s][:, c * 128:(c + 1) * 128], identity=identb[:]),
                             ["mx%d" % s, "identb"], ["pb0"], inc=(c == 7))
                    T.op(ACT, lambda s=s: A.copy(out=mxT[s][:], in_=pbb[0][:, 0:1024]), ["pb0"], ["mxT%d" % s])
                    for nb_ in range(2):
                        bank = 1 + nb_
                        for c in range(8):
                            T.op(PE, lambda c=c, s=s, nb_=nb_, bank=bank:
                                 PEe.matmul(pb[bank][:, 0:512], lhsT=mxT[s][:, c * 128:(c + 1) * 128],
                                            rhs=WO[:, c, nb_ * 512:(nb_ + 1) * 512], start=(c == 0), stop=(c == 7)),
                                 ["mxT%d" % s, "WO"], ["pb%d" % bank], inc=(c == 7))
                        T.op(DVE, lambda s=s, nb_=nb_, bank=bank:
                             V.tensor_tensor(out=hbs[s][:, nb_ * 512:(nb_ + 1) * 512], in0=pb[bank][:, 0:512],
                                             in1=xos[s][:, nb_ * 512:(nb_ + 1) * 512], op=ALU.add),
                             ["pb%d" % bank, "xo%d" % s], ["hb%d" % s])
                    T.dma(SP, h_scr[i * 128:(i + 1) * 128, :], hbs[s][:], ["hb%d" % s], ["hscr%d" % i], "hs%d" % s)
                    if dbg:
                        T.dma(SP, dbg_outs["d_h"][i * 128:(i + 1) * 128, :], hbs[s][:], ["hb%d" % s], [], "dbg")
                        T.op(DVE, lambda s=s: V.tensor_copy(out=xos[s][:], in_=mxs[s][:]), ["mx%d" % s, "xo%d" % s], ["xo%d" % s])
                        T.dma(SP, dbg_outs["d_mix"][i * 128:(i + 1) * 128, :], xos[s][:], ["xo%d" % s], [], "dbg")

            T.barrier()
            mT = sb("mT", [128, 8, 512], BF16)
            actT = sb("actT", [128, NFF, 512], BF16)
            ee = sb("ee", [128, 512], F32)
            for gI in range(NU):
                for tt in range(4):
                    i = gI * 4 + tt
                    s = i % 2
                    T.dma(SP, hbs[s][:], h_scr[i * 128:(i + 1) * 128, :], ["hscr%d" % i], ["hb%d" % s], "hl%d" % s)
                    T.op(DVE, lambda s=s: V.scalar_tensor_tensor(out=junk[:], in0=hbs[s][:], scalar=1.0, in1=hbs[s][:],
                                                                 op0=ALU.mult, op1=ALU.mult, accum_out=st[:, 0:1]),
                         ["hb%d" % s], ["junk", "st0"])
                    rstd_from_ss(st[:, 0:1], st[:, 1:2], float(D), ["st0"])
                    T.op(DVE, lambda s=s: V.scalar_tensor_tensor(out=nbs[s][:], in0=hbs[s][:], scalar=st[:, 1:2],
                                                                 in1=gff[:], op0=ALU.mult, op1=ALU.mult),
                         ["hb%d" % s, "st0", "gff"], ["fnb%d" % s])
                    for c in range(8):
                        T.op(PE, lambda c=c, s=s: PEe.transpose(out=pbb[0][:, c * 128:(c + 1) * 128],
                                                                in_=nbs[s][:, c * 128:(c + 1) * 128], identity=identb[:]),
                             ["fnb%d" % s, "identb"], ["pb0"], inc=(c == 7))
                    T.op(ACT, lambda tt=tt: A.copy(out=mT[:, :, tt * 128:(tt + 1) * 128],
                                                   in_=pbb[0][:, 0:1024].rearrange("p (a b) -> p a b", b=128)),
                         ["pb0"], ["mT"])
                for f in range(NFF):
                    gb = 1 + (f % 2)
                    ub = 3 + (f % 2)
                    for c in range(8):
                        T.op(PE, lambda c=c, f=f, gb=gb: PEe.matmul(pb[gb][:, 0:512], lhsT=WG[:, c, f * 128:(f + 1) * 128],
                                                                    rhs=mT[:, c, :], start=(c == 0), stop=(c == 7)),
                             ["WG", "mT"], ["pb%d" % gb], inc=(c == 7))
                    for c in range(8):
                        T.op(PE, lambda c=c, f=f, ub=ub: PEe.matmul(pb[ub][:, 0:512], lhsT=WU[:, c, f * 128:(f + 1) * 128],
                                                                    rhs=mT[:, c, :], start=(c == 0), stop=(c == 7)),
                             ["WU", "mT"], ["pb%d" % ub], inc=(c == 7))
                    T.op(ACT, lambda gb=gb: A.activation(out=ee[:], in_=pb[gb][:, 0:512], func=AF.Silu),
                         ["pb%d" % gb], ["ee"])
                    T.op(DVE, lambda f=f, ub=ub: V.tensor_tensor(out=actT[:, f, :], in0=pb[ub][:, 0:512], in1=ee[:], op=ALU.mult),
                         ["pb%d" % ub, "ee"], ["actT"])
                for tt in range(4):
                    i = gI * 4 + tt
                    s = i % 2
                    T.dma(SP, hbs[s][:], h_scr[i * 128:(i + 1) * 128, :], ["hscr%d" % i], ["hb%d" % s], "hl%d" % s)
                    for nb_ in range(2):
                        bank = 5 + nb_
                        for f in range(NFF):
                            T.op(PE, lambda f=f, tt=tt, nb_=nb_, bank=bank:
                                 PEe.matmul(pb[bank][:, 0:512], lhsT=actT[:, f, tt * 128:(tt + 1) * 128],
                                            rhs=WD[:, f, nb_ * 512:(nb_ + 1) * 512], start=(f == 0), stop=(f == NFF - 1)),
                                 ["actT", "WD"], ["pb%d" % bank], inc=(f == NFF - 1))
                        T.op(DVE, lambda s=s, nb_=nb_, bank=bank:
                             V.tensor_tensor(out=obs[s][:, nb_ * 512:(nb_ + 1) * 512], in0=pb[bank][:, 0:512],
                                             in1=hbs[s][:, nb_ * 512:(nb_ + 1) * 512], op=ALU.add),
                             ["pb%d" % bank, "hb%d" % s], ["ob%d" % s])
                    T.dma(SP, out[i * 128:(i + 1) * 128, :], obs[s][:], ["ob%d" % s], [], "o%d" % s)

        for n in list(T.dsems):
            d = T.dsem(n)
            nc.sync.wait_ge(d.sem, d.count)
    return nc


def _consts():
    c = np.zeros((128, 514), np.float32)
    c[:, 0:128] = np.eye(128, dtype=np.float32)
    s = np.arange(128)[:, None]
    t = np.arange(128)[None, :]
    same = (s // 64) == (t // 64)
    c[:, 128:256] = np.where(same & (s <= t), -1.0 / 16.0, 0.0)
    c[:, 256:384] = np.where(same & (s > t), -1.0 / 16.0, 0.0)
    c[:, 384:512] = np.where(same & (s <= t), 1.0, 0.0)
    c[0:64, 512] = -1.0 / 16.0
    c[64:128, 513] = -1.0 / 16.0
    return c


def _tabs(j):
    tb = np.zeros((128, 260), np.float32)
    kl = np.arange(128, dtype=np.float64)
    for h in range(4):
        for ei in range(64):
            e = ei - 48
            if e <= 4 * j + 3:
                tb[:, h * 64 + ei] = SLOPES[h] * (128.0 * (e - 4 * j) + kl - 256.0)
            else:
                tb[:, h * 64 + ei] = NEG
    tb[:, 256 + j] = 1.0
    return tb


def _masks(j):
    m = np.zeros((128, 16, 4, 128), np.float32)
    k = np.arange(128)[:, None]
    q = np.arange(128)[None, :]
    tri = (k <= q).astype(np.float32)
    for e in range(16):
        for c in range(4):
            cs = 4 * j + c
            if e < cs:
                m[:, e, c, :] = 1.0
            elif e == cs:
                m[:, e, c, :] = tri
    return m.reshape(128, 16 * 512)


def _rep(v, n=128):
    return np.ascontiguousarray(np.broadcast_to(np.asarray(v, np.float32).reshape(1, -1), (n, v.size)))


def prep_inputs(inp, NU=4):
    T_ = NU * 2048
    f = lambda a: np.ascontiguousarray(np.asarray(a, dtype=np.float32))
    x = f(inp["x"])
    w_in = f(inp["w_in"])[0]
    cols = lambda a, b: list(range(a, b))
    dq, dk, dv = 0, 512, 1024
    gq, gk, gv, go, gl = 1536, 1792, 2048, 2560, 3072
    c_p0 = cols(dk, dk + 256) + cols(dv, dv + 256) + cols(gk, gk + 256) + cols(gl, gl + 16) + cols(gv, gv + 512)
    c_p1 = cols(dk + 256, dk + 512) + cols(dv + 256, dv + 512)
    c_ow = cols(dq, dq + 512) + cols(gq, gq + 256) + cols(gk, gk + 256) + cols(gv, gv + 512) + cols(go, go + 512) + cols(gl, gl + 16)
    shared = {
        "w_p0": np.ascontiguousarray(w_in[:, c_p0]),
        "w_p1": np.ascontiguousarray(w_in[:, c_p1]),
        "w_ow": np.ascontiguousarray(w_in[:, c_ow]),
        "w_up": f(inp["w_gla_gate_up"])[0],
        "w_o": f(inp["w_out"])[0],
        "w_g": f(inp["w_ffn_gate"])[0],
        "w_u": f(inp["w_ffn_up"])[0],
        "w_d": f(inp["w_ffn_down"])[0],
        "g_attn": _rep(f(inp["attn_norm_gain"])[0]),
        "g_ffn": _rep(f(inp["ffn_norm_gain"])[0]),
        "qk_rep": np.concatenate([_rep(np.tile(f(inp["q_norm_gain"])[0], 8)),
                                  _rep(np.tile(f(inp["k_norm_gain"])[0], 8))], axis=1),
        "b_rep": _rep(f(inp["b_gla_gate"])[0]),
        "gn_rep": np.concatenate([_rep(f(inp["diff_out_norm_gain"])[0]), _rep(f(inp["gla_out_norm_gain"])[0])], axis=1),
        "lamv": np.concatenate([_rep(f(inp[k])[0]) for k in ("lambda_q1", "lambda_k1", "lambda_q2", "lambda_k2")], axis=1),
        "cst": _consts(),
    }
    in_maps = []
    for core in range(8):
        b, j = core // 4, core % 4
        own_rows = np.concatenate([np.arange((4 * u + j) * 512, (4 * u + j + 1) * 512) for u in range(NU)])
        m = dict(shared)
        m["x_all"] = np.ascontiguousarray(x[b, :T_])
        m["x_own"] = np.ascontiguousarray(x[b, own_rows])
        m["tabs"] = _tabs(j)
        m["msk"] = _masks(j)
        in_maps.append(m)
    return in_maps


def assemble(results, NU=4, B=2, key="out"):
    T_ = NU * 2048
    outp = np.zeros((B, T_, D), np.float32)
    for core in range(8):
        b, j = core // 4, core % 4
        r = np.asarray(results[core][key])
        for u in range(NU):
            outp[b, (4 * u + j) * 512:(4 * u + j + 1) * 512] = r[u * 512:(u + 1) * 512]
    return outp


_NC_CACHE = {}


def kernel(**inputs):
    NU = 4
    if NU not in _NC_CACHE:
        _NC_CACHE[NU] = build(NU)
    nc = _NC_CACHE[NU]
    in_maps = prep_inputs(inputs, NU)
    res = run_bass_kernel_spmd(nc, in_maps, core_ids=list(range(8)))
    return assemble(res.results, NU)
```

```python
import contextlib
import numpy as np
import concourse.bass as bass
import concourse.mybir as mybir
from concourse.bass_utils import run_bass_kernel_spmd

F32 = mybir.dt.float32
BF16 = mybir.dt.bfloat16
AF = mybir.ActivationFunctionType
ALU = mybir.AluOpType
AX = mybir.AxisListType

D = 1024
DFF = 2816
NFF = DFF // 128
EPS = 1e-6
LAMBDA_INIT = 0.8 - 0.6 * 1.0
SLOPES = [2.0 ** (-8.0 * (h + 1.0) / 4.0) for h in range(4)]
NEG = -30000.0
S_MAX = 8.0 * 1.25
UNDERFLOW = 104.0
SAME_ENGINE_SYNC = True


class ES:
    def __init__(self, name, eng, sem):
        self.name = name
        self.eng = eng
        self.sem = sem
        self.count = 0
        self.seen = {}


class Trk:
    def __init__(self, nc, stack):
        self.nc = nc
        self.stack = stack
        self.lw = {}
        self.rd = {}
        self.dsems = {}
        mk = lambda n, e: ES(n, e, stack.enter_context(nc.semaphore("s_" + n)))
        self.pe = mk("pe", nc.tensor)
        self.act = mk("act", nc.scalar)
        self.dve = mk("dve", nc.vector)
        self.pool = mk("pool", nc.gpsimd)
        self.sp = mk("sp", nc.sync)

    def dsem(self, name):
        if name not in self.dsems:
            self.dsems[name] = ES("d_" + name, None,
                                  self.stack.enter_context(self.nc.semaphore("d_" + name)))
        return self.dsems[name]

    def _deps(self, reads, writes):
        raw = {}
        other = {}

        def add(d, e, v):
            if d.get(e, 0) < v:
                d[e] = v
        for k in reads:
            if k in self.lw:
                add(raw, *self.lw[k])
        for k in writes:
            if k in self.lw:
                add(other, *self.lw[k])
            for e, v in self.rd.get(k, {}).items():
                add(other, e, v)
        return raw, other

    def _wait(self, E, deps):
        raw, other = deps
        allv = dict(other)
        for e, v in raw.items():
            if allv.get(e, 0) < v:
                allv[e] = v
        for e, v in allv.items():
            if e is E:
                if E.name == "pe" or not SAME_ENGINE_SYNC:
                    continue
                v = raw.get(e, 0)
                if v == 0:
                    continue
            if E.seen.get(e, 0) >= v:
                continue
            assert v <= e.count, (E.name, e.name, v, e.count)
            E.eng.wait_ge(e.sem, v)
            E.seen[e] = v

    def _rec(self, E, val, reads, writes):
        for k in writes:
            self.lw[k] = (E, val)
            self.rd[k] = {}
        for k in reads:
            d = self.rd.setdefault(k, {})
            if d.get(E, 0) < val:
                d[E] = val

    def op(self, E, fn, reads=(), writes=(), inc=True):
        pr = [k for k in reads if k.startswith("pb")]
        if pr:
            reads = [k for k in reads if not k.startswith("pb")]
            writes = list(writes) + [k for k in pr if k not in writes]
        self._wait(E, self._deps(reads, writes))
        inst = fn()
        if inc:
            inst.then_inc(E.sem, 1)
            E.count += 1
            val = E.count
        else:
            val = E.count + 1
        self._rec(E, val, reads, writes)
        return inst

    def barrier(self):
        engs = [self.pe, self.act, self.dve, self.pool, self.sp]
        srcs = engs + list(self.dsems.values())
        for E in engs:
            for e in srcs:
                if e is E or e.count == 0 or E.seen.get(e, 0) >= e.count:
                    continue
                E.eng.wait_ge(e.sem, e.count)
                E.seen[e] = e.count

    def dma(self, Q, out, in_, reads, writes, sem):
        self._wait(Q, self._deps(reads, writes))
        Dm = self.dsem(sem)
        inst = Q.eng.dma_start(out=out, in_=in_)
        inst.then_inc(Dm.sem, 16)
        Dm.count += 16
        self._rec(Dm, Dm.count, reads, writes)
        return inst


def build(NU=4, dbg=False):
    T_ = NU * 2048
    NT = T_ // 128
    NO = NU * 4
    TO = NO * 128
    nc = bass.Bass("TRN2", target_bir_lowering=False)

    def din(name, shape, dt=F32):
        return nc.dram_tensor(name, shape, dt, kind="ExternalInput").ap()

    x_all = din("x_all", [T_, D])
    x_own = din("x_own", [TO, D])
    w_p0 = din("w_p0", [D, 1296])
    w_p1 = din("w_p1", [D, 512])
    w_ow = din("w_ow", [D, 2064])
    w_up = din("w_up", [16, 256])
    w_o = din("w_o", [D, D])
    w_g = din("w_g", [D, DFF])
    w_u = din("w_u", [D, DFF])
    w_d = din("w_d", [DFF, D])
    g_attn = din("g_attn", [128, D])
    g_ffn = din("g_ffn", [128, D])
    qk_rep = din("qk_rep", [128, 1024])
    b_rep = din("b_rep", [128, 256])
    gn_rep = din("gn_rep", [128, 256])
    lamv = din("lamv", [128, 256])
    cst = din("cst", [128, 514])
    tabs = din("tabs", [128, 260])
    msk = din("msk", [128, 16 * 512])
    out = nc.dram_tensor("out", [TO, D], F32, kind="ExternalOutput").ap()
    mix_scr = nc.dram_tensor("mix_scr", [TO, D], BF16, kind="Internal").ap()
    h_scr = nc.dram_tensor("h_scr", [TO, D], F32, kind="Internal").ap()
    dbg_outs = {}
    if dbg:
        dbg_outs["d_mix"] = nc.dram_tensor("d_mix", [TO, D], F32, kind="ExternalOutput").ap()
        dbg_outs["d_h"] = nc.dram_tensor("d_h", [TO, D], F32, kind="ExternalOutput").ap()

    with contextlib.ExitStack() as gst:
        T = Trk(nc, gst)
        PE, ACT, DVE, POOL, SP = T.pe, T.act, T.dve, T.pool, T.sp
        V = nc.vector
        A = nc.scalar
        G = nc.gpsimd
        PEe = nc.tensor

        def sbg(n, s, d):
            return gst.enter_context(nc.sbuf_tensor(n, s, d))

        pb = [gst.enter_context(nc.psum_tensor("pb%d" % i, [128, 512], F32)) for i in range(8)]
        pbb = [p[:].bitcast(BF16) for p in pb]

        cstf = sbg("cstf", [128, 514], F32)
        identb = sbg("identb", [128, 128], BF16)
        st = sbg("st", [128, 32], F32)
        junk = sbg("junk", [128, 1024], BF16)
        arena = sbg("arena", [128, 16512 + 3 * 2048], BF16)
        WB = arena[:, 0:16512].rearrange("p (c n) -> p c n", n=2064)
        xbs = [arena[:, 16512 + k * 2048:16512 + (k + 1) * 2048].bitcast(F32) for k in range(3)]
        WG = arena[:, 0:8 * DFF].rearrange("p (c n) -> p c n", n=DFF)
        T.dma(SP, cstf[:], cst[:, :], [], ["cstf"], "c0")
        T.op(DVE, lambda: V.tensor_copy(out=identb[:], in_=cstf[:, 0:128]), ["cstf"], ["identb"])
        triS = cstf[:, 128:256]
        upS = cstf[:, 256:384]
        chI = cstf[:, 512:514]

        def rstd_from_ss(ss_ap, out_ap, n, keys):
            T.op(ACT, lambda: A.activation(out=out_ap, in_=ss_ap, func=AF.Ln, scale=1.0 / n, bias=EPS),
                 keys, keys)
            T.op(ACT, lambda: A.activation(out=out_ap, in_=out_ap, func=AF.Exp, scale=-0.5),
                 keys, keys)

        def norm_transpose(src_ap, xb, xk, nb, nbk, nTt, nTk, grep, grepk, dq, dsem, bank=5):
            T.dma(dq, xb[:], src_ap, [], [xk], dsem)
            T.op(DVE, lambda: V.scalar_tensor_tensor(out=junk[:], in0=xb[:], scalar=1.0, in1=xb[:],
                                                     op0=ALU.mult, op1=ALU.mult, accum_out=st[:, 0:1]),
                 [xk], ["junk", "st0"])
            rstd_from_ss(st[:, 0:1], st[:, 1:2], float(D), ["st0"])
            T.op(DVE, lambda: V.scalar_tensor_tensor(out=nb[:], in0=xb[:], scalar=st[:, 1:2], in1=grep[:],
                                                     op0=ALU.mult, op1=ALU.mult),
                 [xk, "st0", grepk], [nbk])
            bk = "pb%d" % bank
            for c in range(8):
                T.op(PE, lambda c=c: PEe.transpose(out=pbb[bank][:, c * 128:(c + 1) * 128],
                                                   in_=nb[:, c * 128:(c + 1) * 128], identity=identb[:]),
                     [nbk, "identb"], [bk], inc=(c == 7))
            T.op(ACT, lambda: A.copy(out=nTt[:], in_=pbb[bank][:, 0:1024]), [bk], [nTk])

        def project(nTt, nTk, W, Wk, groups):
            for (bank, pc0, wc0, ncol) in groups:
                bk = "pb%d" % bank
                for c in range(8):
                    T.op(PE, lambda c=c, bank=bank, pc0=pc0, wc0=wc0, ncol=ncol:
                         PEe.matmul(pb[bank][:, pc0:pc0 + ncol], lhsT=nTt[:, c * 128:(c + 1) * 128],
                                    rhs=W[:, c, wc0:wc0 + ncol], start=(c == 0), stop=(c == 7)),
                         [nTk, Wk], [bk], inc=(c == 7))

        with contextlib.ExitStack() as mst:
            def sb(n, s, d):
                return mst.enter_context(nc.sbuf_tensor(n, s, d))

            KT = sb("KT", [128, 2, T_], BF16)
            VE = sb("VE", [128, NT, 2, 130], BF16)
            QT = sb("QT", [128, 4, TO], BF16)
            MK = sb("MK", [128, 16, 512], BF16)
            tabf = sb("tabf", [128, 260], F32)
            gat = sb("gat", [128, D], F32)
            qg = sb("qg", [128, 512], F32)
            brep = sb("brep", [128, 256], F32)
            gn = sb("gn", [128, 256], F32)
            wupb = sb("wupb", [16, 256], BF16)
            maskA = sb("maskA", [128, 128], BF16)
            ssx = sb("ssx", [128, 4], F32)
            rstd_all = sb("rstd_all", [128, NT], F32)
            nbs = [sb("nb%d" % i, [128, D], BF16) for i in range(2)]
            nTs = [sb("nT%d" % i, [128, D], BF16) for i in range(2)]
            kf = sb("kf", [128, 512], F32)
            sq = sb("sq", [128, 512], F32)
            k16s = [sb("k16_%d" % i, [128, 512], BF16) for i in range(2)]
            gkf = [sb("gkf%d" % i, [128, 256], F32) for i in range(2)]
            glr16s = [sb("glr16_%d" % i, [128, 16], BF16) for i in range(2)]
            glrT = sb("glrT", [16, 128], BF16)
            zb = sb("zb", [128, 256], F32)
            lg = sb("lg", [128, 256], F32)
            eb = sb("eb", [128, 256], F32)
            enb = sb("enb", [128, 256], F32)
            ec = sb("ec", [128, 256], F32)
            dec = sb("dec", [128, 4], F32)
            qt16 = sb("qt16", [128, 256], BF16)
            kt16 = sb("kt16", [128, 256], BF16)
            kh16 = sb("kh16", [128, 256], BF16)
            vb16s = [sb("vb16_%d" % i, [128, 512], BF16) for i in range(2)]
            qkT = sb("qkT", [128, 512], BF16)
            AT = sb("AT", [128, 512], BF16)
            S = sb("S", [128, 256], F32)
            Sb = [sb("Sb%d" % i, [128, 256], BF16) for i in range(2)]
            snap = sb("snap", [128, NU, 256], F32)
            sgt = sb("sgt", [128, 512], F32)
            gg = sb("gg", [128, 512], F32)
            ogl = sb("ogl", [128, 512], BF16)
            PTs = [sb("PT%d" % i, [128, 512], BF16) for i in range(6)]
            accS = sb("accS", [128, 3, 390], F32)
            t1 = sb("t1", [128, 128], F32)
            of = sb("of", [128, 128], F32)
            og = sb("og", [128, 4, 128], BF16)

            T.dma(SP, tabf[:], tabs[:, :], [], ["tabf"], "c1")
            T.dma(SP, gat[:], g_attn[:, :], [], ["gat"], "c2")
            T.dma(SP, kf[:], qk_rep[:, 0:512], [], ["kf"], "c3")
            T.dma(SP, sq[:], qk_rep[:, 512:1024], [], ["sq"], "c3b")
            T.dma(SP, brep[:], b_rep[:, :], [], ["brep"], "c4")
            T.dma(SP, gn[:], gn_rep[:, :], [], ["gn"], "c5")
            lam = zb
            T.dma(SP, lam[:], lamv[:, :], [], ["zb"], "c6")
            T.dma(POOL, wupb[:], w_up[:, :], [], ["wupb"], "c7")
            T.dma(POOL, maskA[:], cst[:, 384:512], [], ["maskA"], "c8")
            T.dma(POOL, WB[:, :, 0:1296], w_p0.rearrange("(c p) n -> p c n", p=128), [], ["WB"], "w0")
            for e4 in range(4):
                T.dma(POOL, MK[:, e4 * 4:(e4 + 1) * 4, :],
                      msk[:, e4 * 2048:(e4 + 1) * 2048].rearrange("p (a b) -> p a b", b=512),
                      [], ["MK"], "c9")
            T.op(DVE, lambda: V.scalar_tensor_tensor(out=qg[:, 0:512], in0=kf[:, 0:512], scalar=0.125,
                                                     in1=sq[:, 0:512], op0=ALU.mult, op1=ALU.mult),
                 ["kf", "sq"], ["qg"])
            T.op(DVE, lambda: V.tensor_scalar(out=gn[:, 0:128], in0=gn[:, 0:128], scalar1=1.0 - LAMBDA_INIT,
                                              scalar2=None, op0=ALU.mult), ["gn"], ["gn"])
            T.op(DVE, lambda: V.scalar_tensor_tensor(out=junk[:, 0:64], in0=lam[:, 0:64], scalar=1.0,
                                                     in1=lam[:, 64:128], op0=ALU.mult, op1=ALU.mult,
                                                     accum_out=st[:, 4:5]), ["zb"], ["junk", "stl"])
            T.op(DVE, lambda: V.scalar_tensor_tensor(out=junk[:, 0:64], in0=lam[:, 128:192], scalar=1.0,
                                                     in1=lam[:, 192:256], op0=ALU.mult, op1=ALU.mult,
                                                     accum_out=st[:, 5:6]), ["zb", "stl"], ["junk", "stl"])
            T.op(ACT, lambda: A.activation(out=st[:, 6:8], in_=st[:, 4:6], func=AF.Exp), ["stl"], ["stl"])
            T.op(DVE, lambda: V.tensor_tensor(out=st[:, 8:9], in0=st[:, 7:8], in1=st[:, 6:7], op=ALU.subtract),
                 ["stl"], ["stl"])
            T.op(DVE, lambda: V.tensor_scalar(out=st[:, 8:9], in0=st[:, 8:9], scalar1=-LAMBDA_INIT,
                                              scalar2=None, op0=ALU.add), ["stl"], ["stl"])
            neglam = st[:, 8:9]
            T.op(POOL, lambda: G.memset(VE[:, :, :, 128:130], 1.0), [], ["VE"])
            T.op(DVE, lambda: V.memset(S[:], 0.0), [], ["S"])

            def k_evac_a(bank, c0, ngrp):
                n = 64 * ngrp
                T.op(ACT, lambda: A.copy(out=kf[:, 0:n], in_=pb[bank][:, c0:c0 + n]), ["pb%d" % bank], ["kf"])

            def k_evac_b(ngrp):
                n = 64 * ngrp
                T.op(DVE, lambda: V.tensor_tensor(out=sq[:, 0:n], in0=kf[:, 0:n], in1=kf[:, 0:n], op=ALU.mult),
                     ["kf"], ["sq"])
                T.op(DVE, lambda: V.tensor_reduce(out=st[:, 16:16 + ngrp],
                                                  in_=sq[:, 0:n].rearrange("p (a b) -> p a b", b=64),
                                                  axis=AX.X, op=ALU.add), ["sq"], ["stk"])
                rstd_from_ss(st[:, 16:16 + ngrp], st[:, 24:24 + ngrp], 64.0, ["stk"])

            def k_evac_c(ngrp, ks, gain=None):
                n = 64 * ngrp
                kk = "k16_%d" % ks
                bc = bass.AP(st[:].tensor, st[:, 24:25].offset, [list(st[:].ap[0]), [1, ngrp], [0, 64]])
                if gain is None:
                    T.op(DVE, lambda: V.tensor_tensor(out=k16s[ks][:, 0:n].rearrange("p (a b) -> p a b", b=64),
                                                      in0=kf[:, 0:n].rearrange("p (a b) -> p a b", b=64),
                                                      in1=bc, op=ALU.mult), ["kf", "stk"], [kk])
                else:
                    T.op(DVE, lambda: V.tensor_tensor(out=sq[:, 0:n].rearrange("p (a b) -> p a b", b=64),
                                                      in0=kf[:, 0:n].rearrange("p (a b) -> p a b", b=64),
                                                      in1=bc, op=ALU.mult), ["kf", "stk", "sq"], ["sq"])
                    T.op(DVE, lambda: V.tensor_tensor(out=k16s[ks][:, 0:n], in0=sq[:, 0:n], in1=gain,
                                                      op=ALU.mult), ["sq", "qg"], [kk])

            def k_evac(bank, c0, ngrp, ks, gain=None):
                k_evac_a(bank, c0, ngrp)
                k_evac_b(ngrp)
                k_evac_c(ngrp, ks, gain)

            def k_tr(ks, n, dst_fn):
                nh = n // 128
                for hh in range(nh):
                    T.op(PE, lambda hh=hh: PEe.transpose(out=pbb[4][:, hh * 128:(hh + 1) * 128],
                                                         in_=k16s[ks][:, hh * 128:(hh + 1) * 128], identity=identb[:]),
                         ["k16_%d" % ks, "identb"], ["pb4"], inc=(hh == nh - 1))
                dst_fn(pbb[4][:, 0:n].rearrange("p (a b) -> p a b", b=128))

            def gla_segs(own, i_own, gk, gkk, gq, gqk, vb, vbk, g16, g16k, psS):
                def s_a():
                    T.op(PE, lambda: PEe.transpose(out=pbb[6][0:16, 64:192], in_=g16[:], identity=identb[:]),
                         [g16k, "identb"], ["pb6"])
                    T.op(ACT, lambda: A.copy(out=glrT[:], in_=pbb[6][0:16, 64:192]), ["pb6"], ["glrT"])

                def s_b():
                    T.op(PE, lambda: PEe.matmul(pb[6][:, 128:384], lhsT=glrT[:], rhs=wupb[:], start=True, stop=True),
                         ["glrT", "wupb"], ["pb6"])
                    T.op(DVE, lambda: V.tensor_tensor(out=zb[:], in0=pb[6][:, 128:384], in1=brep[:], op=ALU.add),
                         ["pb6", "brep"], ["zb"])
                    T.op(ACT, lambda: A.activation(out=lg[:], in_=zb[:], func=AF.Exp, scale=-1.0), ["zb"], ["lg"])
                    T.op(ACT, lambda: A.activation(out=lg[:], in_=lg[:], func=AF.Ln, bias=1.0), ["lg"], ["lg"])

                def s_c():
                    if own:
                        T.op(PE, lambda: PEe.matmul(pb[7][:, 0:256], lhsT=triS, rhs=lg[:], start=True, stop=True),
                             ["cstf", "lg"], ["pb7"], inc=False)
                    T.op(PE, lambda: PEe.matmul(pb[7][:, 256:512], lhsT=upS, rhs=lg[:], start=True, stop=True),
                         ["cstf", "lg"], ["pb7"])
                    for p in range(2):
                        T.op(PE, lambda p=p: PEe.matmul(pb[6][:, 96 + 2 * p:98 + 2 * p], lhsT=lg[:, p * 128:(p + 1) * 128],
                                                        rhs=chI, start=True, stop=True),
                             ["lg", "cstf"], ["pb6"], inc=(p == 1))
                    T.op(ACT, lambda: A.activation(out=ec[:], in_=pb[7][:, 256:512], func=AF.Exp), ["pb7"], ["ec"])
                    if own:
                        T.op(ACT, lambda: A.activation(out=eb[:], in_=pb[7][:, 0:256], func=AF.Exp), ["pb7"], ["eb"])
                        T.op(ACT, lambda: A.activation(out=enb[:], in_=pb[7][:, 0:256], func=AF.Exp, scale=-1.0),
                             ["pb7"], ["enb"])
                    T.op(ACT, lambda: A.activation(out=dec[:], in_=pb[6][:, 96:100], func=AF.Exp), ["pb6"], ["dec"])
                    T.op(DVE, lambda: V.tensor_tensor(out=kh16[:], in0=gk, in1=ec[:], op=ALU.mult),
                         [gkk, "ec"], ["kh16"])
                    if own:
                        T.op(DVE, lambda: V.scalar_tensor_tensor(out=qt16[:], in0=gq, scalar=0.125,
                                                                 in1=eb[:], op0=ALU.mult, op1=ALU.mult),
                             [gqk, "eb"], ["qt16"])
                        T.op(DVE, lambda: V.tensor_tensor(out=kt16[:], in0=gk, in1=enb[:], op=ALU.mult),
                             [gkk, "enb"], ["kt16"])
                        for p in range(2):
                            T.op(PE, lambda p=p: PEe.transpose(out=pbb[4][:, 512 + p * 128:512 + (p + 1) * 128],
                                                               in_=qt16[:, p * 128:(p + 1) * 128], identity=identb[:]),
                                 ["qt16", "identb"], ["pb4"], inc=False)
                        for p in range(2):
                            T.op(PE, lambda p=p: PEe.transpose(out=pbb[4][:, 768 + p * 128:768 + (p + 1) * 128],
                                                               in_=kt16[:, p * 128:(p + 1) * 128], identity=identb[:]),
                                 ["kt16", "identb"], ["pb4"], inc=(p == 1))
                        T.op(ACT, lambda: A.copy(out=qkT[:], in_=pbb[4][:, 512:1024]), ["pb4"], ["qkT"])

                def s_c2():
                    for hh in range(2):
                        bank = 0 if hh == 0 else 3
                        for p in range(2):
                            T.op(PE, lambda hh=hh, p=p, bank=bank:
                                 PEe.matmul(pb[bank][:, p * 128:(p + 1) * 128],
                                            lhsT=qkT[hh * 64:(hh + 1) * 64, 256 + p * 128:256 + (p + 1) * 128],
                                            rhs=qkT[hh * 64:(hh + 1) * 64, p * 128:(p + 1) * 128],
                                            start=True, stop=True),
                                 ["qkT"], ["pb%d" % bank], inc=(p == 1))
                    mA = bass.AP(maskA[:].tensor, maskA[:].offset, [list(maskA[:].ap[0]), [0, 2], [1, 128]])
                    for hh in range(2):
                        bank = 0 if hh == 0 else 3
                        outv = bass.AP(AT[:].tensor, AT[:, hh * 128:hh * 128 + 1].offset,
                                       [list(AT[:].ap[0]), [256, 2], [1, 128]])
                        T.op(DVE, lambda bank=bank, outv=outv: V.tensor_tensor(
                            out=outv, in0=pb[bank][:, 0:256].rearrange("p (a b) -> p a b", b=128), in1=mA, op=ALU.mult),
                            ["pb%d" % bank, "maskA"], ["AT"])

                def s_d(ch):
                    def f():
                        if own:
                            T.op(DVE, lambda: V.tensor_copy(out=Sb[ch][:], in_=S[:]), ["S"], ["Sb%d" % ch])
                        sbank, scol = psS[ch]
                        sk = "pb%d" % sbank
                        for p in range(2):
                            for hh in range(2):
                                h = 2 * p + hh
                                T.op(PE, lambda p=p, hh=hh, h=h:
                                     PEe.matmul(pb[sbank][hh * 64:(hh + 1) * 64, scol + p * 128:scol + (p + 1) * 128],
                                                lhsT=kh16[ch * 64:(ch + 1) * 64, h * 64:(h + 1) * 64],
                                                rhs=vb[ch * 64:(ch + 1) * 64, h * 128:(h + 1) * 128],
                                                start=True, stop=True),
                                     ["kh16", vbk], [sk], inc=(p == 1 and hh == 1))
                        for p in range(2):
                            T.op(DVE, lambda p=p:
                                 V.scalar_tensor_tensor(out=S[:, p * 128:(p + 1) * 128], in0=S[:, p * 128:(p + 1) * 128],
                                                        scalar=dec[:, 2 * p + ch:2 * p + ch + 1],
                                                        in1=pb[sbank][:, scol + p * 128:scol + (p + 1) * 128],
                                                        op0=ALU.mult, op1=ALU.add),
                                 ["S", "dec", sk], ["S"])
                    return f

                def s_o():
                    for hh in range(2):
                        bank = 0 if hh == 0 else 3
                        bk = "pb%d" % bank
                        for p in range(2):
                            h = 2 * p + hh
                            oc = 256 + p * 128
                            T.op(PE, lambda h=h, bank=bank, oc=oc:
                                 PEe.matmul(pb[bank][:, oc:oc + 128], lhsT=AT[:, h * 128:(h + 1) * 128],
                                            rhs=vb[:, h * 128:(h + 1) * 128], start=True, stop=False),
                                 ["AT", vbk], [bk], inc=False)
                            for ch in range(2):
                                T.op(PE, lambda h=h, hh=hh, p=p, ch=ch, bank=bank, oc=oc:
                                     PEe.matmul(pb[bank][ch * 64:(ch + 1) * 64, oc:oc + 128],
                                                lhsT=qkT[hh * 64:(hh + 1) * 64, p * 128 + ch * 64:p * 128 + (ch + 1) * 64],
                                                rhs=Sb[ch][hh * 64:(hh + 1) * 64, p * 128:(p + 1) * 128],
                                                start=False, stop=True),
                                     ["qkT", "Sb%d" % ch], [bk], inc=(ch == 1 and p == 1))
                    for hh in range(2):
                        bank = 0 if hh == 0 else 3
                        bk = "pb%d" % bank
                        T.op(ACT, lambda bank=bank: A.copy(out=kf[:, 0:256], in_=pb[bank][:, 256:512]), [bk], ["kf"])
                        T.op(DVE, lambda: V.tensor_tensor(out=sq[:, 0:256], in0=kf[:, 0:256], in1=kf[:, 0:256],
                                                          op=ALU.mult), ["kf"], ["sq"])
                        T.op(DVE, lambda: V.tensor_reduce(out=st[:, 16:18],
                                                          in_=sq[:, 0:256].rearrange("p (a b) -> p a b", b=128),
                                                          axis=AX.X, op=ALU.add), ["sq"], ["stk"])
                        rstd_from_ss(st[:, 16:18], st[:, 24:26], 128.0, ["stk"])
                        bc = bass.AP(st[:].tensor, st[:, 24:25].offset, [list(st[:].ap[0]), [1, 2], [0, 128]])
                        ggv = bass.AP(gg[:].tensor, gg[:, hh * 128:hh * 128 + 1].offset,
                                      [list(gg[:].ap[0]), [256, 2], [1, 128]])
                        oglv = bass.AP(ogl[:].tensor, ogl[:, hh * 128:hh * 128 + 1].offset,
                                       [list(ogl[:].ap[0]), [256, 2], [1, 128]])
                        T.op(DVE, lambda bc=bc: V.tensor_tensor(out=sq[:, 256:512].rearrange("p (a b) -> p a b", b=128),
                                                                in0=kf[:, 0:256].rearrange("p (a b) -> p a b", b=128),
                                                                in1=bc, op=ALU.mult), ["kf", "stk"], ["sq2"])
                        T.op(DVE, lambda ggv=ggv, oglv=oglv: V.tensor_tensor(
                            out=oglv, in0=sq[:, 256:512].rearrange("p (a b) -> p a b", b=128), in1=ggv, op=ALU.mult),
                            ["sq2", "gg"], ["ogl"])
                    T.dma(SP, mix_scr[i_own * 128:(i_own + 1) * 128, 512:1024], ogl[:], ["ogl"], ["mixg%d" % i_own], "mx")

                if own:
                    return [s_a, s_b, s_c, s_c2, s_d(0), s_d(1), s_o]
                return [s_a, s_b, s_c, s_d(0), s_d(1)]

            def seg_load(t):
                sl = t % 3
                T.dma(SP, xbs[sl][:], x_all[t * 128:(t + 1) * 128, :], [], ["xb%d" % sl], "x%d" % sl)

            def seg_stats(t):
                sl = t % 3
                T.op(DVE, lambda: V.scalar_tensor_tensor(out=junk[:], in0=xbs[sl][:], scalar=1.0, in1=xbs[sl][:],
                                                         op0=ALU.mult, op1=ALU.mult, accum_out=ssx[:, sl:sl + 1]),
                     ["xb%d" % sl], ["junk", "ssx%d" % sl])
                T.op(ACT, lambda: A.activation(out=rstd_all[:, t:t + 1], in_=ssx[:, sl:sl + 1], func=AF.Ln,
                                               scale=1.0 / D, bias=EPS), ["ssx%d" % sl], ["rs%d" % t])
                T.op(ACT, lambda: A.activation(out=rstd_all[:, t:t + 1], in_=rstd_all[:, t:t + 1], func=AF.Exp, scale=-0.5),
                     ["rs%d" % t], ["rs%d" % t])

            def seg_n(t):
                sl = t % 3
                ns = t % 2
                T.op(DVE, lambda: V.scalar_tensor_tensor(out=nbs[ns][:], in0=xbs[sl][:], scalar=rstd_all[:, t:t + 1],
                                                         in1=gat[:], op0=ALU.mult, op1=ALU.mult),
                     ["xb%d" % sl, "rs%d" % t, "gat"], ["nb%d" % ns])

            def seg_tr(t):
                ns = t % 2
                for c in range(8):
                    T.op(PE, lambda c=c: PEe.transpose(out=pbb[3][:, c * 128:(c + 1) * 128],
                                                       in_=nbs[ns][:, c * 128:(c + 1) * 128], identity=identb[:]),
                         ["nb%d" % ns, "identb"], ["pb3"], inc=(c == 7))
                T.op(ACT, lambda: A.copy(out=nTs[ns][:], in_=pbb[3][:, 0:1024]), ["pb3"], ["nT%d" % ns])

            def run_pipe(n_tiles, early_fn, main_fn, late_fn):
                seg_load(0)
                for it in range(-1, n_tiles + 1):
                    early = early_fn(it + 1) if 0 <= it + 1 < n_tiles else []
                    main = main_fn(it) if 0 <= it < n_tiles else []
                    late = late_fn(it - 1) if 0 <= it - 1 < n_tiles else []
                    for k in range(max(len(early), len(main), len(late))):
                        if k < len(main):
                            main[k]()
                        if k < len(late):
                            late[k]()
                        if k < len(early):
                            early[k]()

            def kv_store(t):
                def dst(src3):
                    T.op(ACT, lambda: A.copy(out=KT[:, :, t * 128:(t + 1) * 128], in_=src3), ["pb4"], ["KT"])
                return dst

            def v_store(t):
                T.op(DVE, lambda: V.tensor_copy(out=VE[:, t, :, 0:128],
                                                in_=pb[0][:, 256:512].rearrange("p (a b) -> p a b", b=128)),
                     ["pb0"], ["VE"])

            def p0_iter(it):
                ok = lambda t: 0 <= t < NT
                late = []
                t1 = it - 1
                if ok(t1):
                    n1 = t1 % 2
                    late = gla_segs(False, None, gkf[n1][:], "gkf%d" % n1, None, None, vb16s[n1], "vb16_%d" % n1,
                                    glr16s[n1], "glr16_%d" % n1, ((5, 0), (7, 0)))
                    late[0]()
                if ok(it + 2):
                    seg_stats(it + 2)
                if ok(it + 1):
                    seg_tr(it + 1)
                ns = it % 2
                if ok(it):
                    project(nTs[ns], "nT%d" % ns, WB, "WB", [(0, 0, 0, 512)])
                    k_evac_a(0, 0, 4)
                    v_store(it)
                    k_evac_b(4)
                if ok(t1):
                    late[1]()
                if ok(it + 2):
                    seg_n(it + 2)
                if ok(it):
                    project(nTs[ns], "nT%d" % ns, WB, "WB", [(1, 0, 512, 272)])
                    T.op(DVE, lambda: V.tensor_copy(out=gkf[ns][:], in_=pb[1][:, 0:256]), ["pb1"], ["gkf%d" % ns])
                    T.op(DVE, lambda: V.tensor_copy(out=glr16s[ns][:], in_=pb[1][:, 256:272]), ["pb1"], ["glr16_%d" % ns])
                    k_evac_c(4, ns)
                if ok(t1):
                    late[2]()
                    k_tr(t1 % 2, 256, kv_store(t1))
                if ok(it):
                    project(nTs[ns], "nT%d" % ns, WB, "WB", [(2, 0, 784, 512)])
                    T.op(ACT, lambda: A.copy(out=vb16s[ns][:], in_=pb[2][:, :]), ["pb2"], ["vb16_%d" % ns])
                if ok(t1):
                    if t1 % 16 % 4 == 0:
                        u = t1 // 16
                        r = (t1 % 16) // 4
                        if r == 0:
                            T.op(DVE, lambda: V.tensor_scalar(out=snap[:, u, :], in0=S[:],
                                                              scalar1=tabf[:, 256 + r:257 + r], scalar2=None,
                                                              op0=ALU.mult), ["S", "tabf"], ["snap"])
                        else:
                            T.op(DVE, lambda: V.scalar_tensor_tensor(out=snap[:, u, :], in0=S[:],
                                                                     scalar=tabf[:, 256 + r:257 + r],
                                                                     in1=snap[:, u, :], op0=ALU.mult, op1=ALU.add),
                                 ["S", "tabf", "snap"], ["snap"])
                    late[3]()
                    late[4]()
                if ok(it + 3):
                    seg_load(it + 3)

            for t in range(min(3, NT)):
                seg_load(t)
            for it in range(-2, NT + 1):
                p0_iter(it)

            T.dma(POOL, WB[:, :, :], w_ow.rearrange("(c p) n -> p c n", p=128), [], ["WB"], "w0")

            def q_store(i):
                def dst(src3):
                    T.op(ACT, lambda: A.copy(out=QT[:, :, i * 128:(i + 1) * 128], in_=src3), ["pb4"], ["QT"])
                return dst

            gnb = bass.AP(gn[:].tensor, gn[:, 128:129].offset, [list(gn[:].ap[0]), [0, 4], [1, 128]])

            def o_load(i):
                sl = i % 3
                T.dma(SP, xbs[sl][:], x_own[i * 128:(i + 1) * 128, :], [], ["xb%d" % sl], "x%d" % sl)

            def o_stats(i):
                sl = i % 3
                T.op(DVE, lambda: V.scalar_tensor_tensor(out=junk[:], in0=xbs[sl][:], scalar=1.0, in1=xbs[sl][:],
                                                         op0=ALU.mult, op1=ALU.mult, accum_out=ssx[:, sl:sl + 1]),
                     ["xb%d" % sl], ["junk", "ssx%d" % sl])
                rstd_from_ss(ssx[:, sl:sl + 1], ssx[:, 3:4] if False else st[:, 1 + (i % 2):2 + (i % 2)], float(D), ["ssx%d" % sl, "sto%d" % (i % 2)])

            def o_n(i):
                sl = i % 3
                ns = i % 2
                T.op(DVE, lambda: V.scalar_tensor_tensor(out=nbs[ns][:], in0=xbs[sl][:], scalar=st[:, 1 + ns:2 + ns],
                                                         in1=gat[:], op0=ALU.mult, op1=ALU.mult),
                     ["xb%d" % sl, "sto%d" % ns, "gat"], ["nb%d" % ns])

            def o_tr(i):
                ns = i % 2
                for c in range(8):
                    T.op(PE, lambda c=c: PEe.transpose(out=pbb[5][:, c * 128:(c + 1) * 128],
                                                       in_=nbs[ns][:, c * 128:(c + 1) * 128], identity=identb[:]),
                         ["nb%d" % ns, "identb"], ["pb5"], inc=(c == 7))
                T.op(ACT, lambda: A.copy(out=nTs[ns][:], in_=pbb[5][:, 0:1024]), ["pb5"], ["nT%d" % ns])

            for i in range(min(2, NO)):
                o_load(i)
            o_stats(0)
            o_n(0)
            o_tr(0)
            for i in range(NO):
                s = i % 2
                if i + 2 < NO:
                    o_load(i + 2)
                if i + 1 < NO:
                    o_stats(i + 1)
                if i % 4 == 0:
                    T.op(DVE, lambda i=i: V.tensor_copy(out=S[:], in_=snap[:, i // 4, :]), ["snap"], ["S"])
                project(nTs[s], "nT%d" % s, WB, "WB", [(0, 0, 0, 512), (3, 0, 1536, 512)])
                if i + 1 < NO:
                    o_n(i + 1)
                k_evac_a(0, 0, 8)
                T.op(ACT, lambda: A.activation(out=sgt[:], in_=pb[3][:, :], func=AF.Exp, scale=-1.0), ["pb3"], ["sgt"])
                T.op(ACT, lambda: A.activation(out=sgt[:], in_=sgt[:], func=AF.Ln, bias=1.0), ["sgt"], ["sgt"])
                T.op(ACT, lambda: A.activation(out=sgt[:], in_=sgt[:], func=AF.Exp, scale=-1.0), ["sgt"], ["sgt"])
                project(nTs[s], "nT%d" % s, WB, "WB", [(1, 0, 512, 512), (2, 0, 1024, 512), (6, 0, 2048, 16)])
                if i + 1 < NO:
                    o_tr(i + 1)
                k_evac_b(8)
                T.op(DVE, lambda: V.tensor_tensor(out=gg[:], in0=pb[3][:, :], in1=sgt[:], op=ALU.mult),
                     ["pb3", "sgt"], ["gg"])
                T.op(DVE, lambda: V.tensor_copy(out=glr16s[0][:], in_=pb[6][:, 0:16]), ["pb6"], ["glr16_0"])
                T.op(ACT, lambda: A.copy(out=vb16s[0][:], in_=pb[2][:, :]), ["pb2"], ["vb16_0"])
                k_evac_c(8, 0, gain=qg[:, 0:512])
                T.op(DVE, lambda: V.tensor_tensor(out=gg[:].rearrange("p (a b) -> p a b", b=128),
                                                  in0=gg[:].rearrange("p (a b) -> p a b", b=128), in1=gnb, op=ALU.mult),
                     ["gg", "gn"], ["gg"])
                segs = gla_segs(True, i, pb[1][:, 256:512], "pb1", pb[1][:, 0:256], "pb1", vb16s[0], "vb16_0",
                                glr16s[0], "glr16_0", ((2, 0), (2, 256)))
                segs[0]()
                k_tr(0, 512, q_store(i))
                for sg in segs[1:]:
                    sg()

            def acc_ap(a):
                bank = 5 + a // 3
                col = (a % 3) * 130
                return bank, col

            def attention(hp):
                def kb_lo(h, u):
                    n_h = int(np.ceil(1.0 + (UNDERFLOW + 2.0 * S_MAX) / (128.0 * SLOPES[h])))
                    return max(0, 16 * u - (n_h - 1))
                steps = [(u, hl, kb) for u in range(NU) for hl in range(2)
                         for kb in range(kb_lo(2 * hp + hl, u), 16 * u + 16)]
                LOOK = 2

                def emit_qk(idx):
                    u, hl, kb = steps[idx]
                    h = 2 * hp + hl
                    e = kb - 16 * u
                    ei = e + 48
                    for m in range(2):
                        sbank = 1 + (2 * idx + m) % 4
                        T.op(PE, lambda m=m, sbank=sbank:
                             PEe.matmul(pb[sbank][:, 0:512],
                                        lhsT=KT[m * 64:(m + 1) * 64, hl, kb * 128:(kb + 1) * 128],
                                        rhs=QT[m * 64:(m + 1) * 64, h, u * 512:(u + 1) * 512],
                                        start=True, stop=True),
                             ["KT", "QT"], ["pb%d" % sbank])
                    for m in range(2):
                        sbank = 1 + (2 * idx + m) % 4
                        slot = (2 * idx + m) % 6
                        T.op(ACT, lambda sbank=sbank, slot=slot:
                             A.activation(out=PTs[slot][:], in_=pb[sbank][:, 0:512], func=AF.Exp,
                                          bias=tabf[:, h * 64 + ei:h * 64 + ei + 1]),
                             ["pb%d" % sbank, "tabf"], ["PT%d" % slot])
                        if e >= 0:
                            T.op(DVE, lambda slot=slot:
                                 V.tensor_tensor(out=PTs[slot][:], in0=PTs[slot][:], in1=MK[:, e, :], op=ALU.mult),
                                 ["PT%d" % slot, "MK"], ["PT%d" % slot])

                def emit_pv(idx):
                    u, hl, kb = steps[idx]
                    h = 2 * hp + hl
                    nkb = 16 * u + 16
                    for m in range(2):
                        slot = (2 * idx + m) % 6
                        for c in range(4):
                            bank, col = acc_ap(c * 2 + m)
                            first = (kb == kb_lo(h, u) and (c * 2 + m) in (0, 4, 6))
                            T.op(PE, lambda c=c, bank=bank, col=col, slot=slot, first=first:
                                 PEe.matmul(pb[bank][:, col:col + 130],
                                            lhsT=PTs[slot][:, c * 128:(c + 1) * 128],
                                            rhs=VE[:, kb, hl, :], start=first, stop=(kb == nkb - 1),
                                            skip_group_check=True),
                                 ["PT%d" % slot, "VE"], ["pb%d" % bank], inc=(c == 3 and m == 1))
                    if kb != nkb - 1:
                        return
                    for bi in range(3):
                        T.op(DVE, lambda bi=bi: V.tensor_copy(out=accS[:, bi, :], in_=pb[5 + bi][:, 0:390]),
                             ["pb%d" % (5 + bi)], ["accS"])
                    for c in range(4):
                        a0 = c * 2
                        a1 = c * 2 + 1
                        A0 = accS[:, a0 // 3, (a0 % 3) * 130:(a0 % 3) * 130 + 130]
                        A1 = accS[:, a1 // 3, (a1 % 3) * 130:(a1 % 3) * 130 + 130]
                        T.op(DVE, lambda A0=A0: V.reciprocal(out=st[:, 10:11], in_=A0[:, 128:129]), ["accS"], ["sta"])
                        T.op(DVE, lambda A1=A1: V.reciprocal(out=st[:, 11:12], in_=A1[:, 128:129]), ["accS", "sta"], ["sta"])
                        T.op(DVE, lambda: V.tensor_tensor(out=st[:, 11:12], in0=st[:, 11:12], in1=neglam, op=ALU.mult),
                             ["sta", "stl"], ["sta"])
                        T.op(DVE, lambda A1=A1: V.tensor_scalar(out=t1[:], in0=A1[:, 0:128], scalar1=st[:, 11:12],
                                                                scalar2=None, op0=ALU.mult), ["accS", "sta"], ["t1"])
                        T.op(DVE, lambda A0=A0: V.scalar_tensor_tensor(out=of[:], in0=A0[:, 0:128], scalar=st[:, 10:11],
                                                                       in1=t1[:], op0=ALU.mult, op1=ALU.add),
                             ["accS", "sta", "t1"], ["of"])
                        T.op(DVE, lambda: V.scalar_tensor_tensor(out=junk[:, 0:128], in0=of[:], scalar=1.0, in1=of[:],
                                                                 op0=ALU.mult, op1=ALU.mult, accum_out=st[:, 12:13]),
                             ["of"], ["junk", "stb"])
                        rstd_from_ss(st[:, 12:13], st[:, 13:14], 128.0, ["stb"])
                        T.op(DVE, lambda c=c: V.scalar_tensor_tensor(out=og[:, c, :], in0=of[:], scalar=st[:, 13:14],
                                                                     in1=gn[:, 0:128], op0=ALU.mult, op1=ALU.mult),
                             ["of", "stb", "gn"], ["og"])
                    T.dma(SP, mix_scr[u * 512:(u + 1) * 512, h * 128:(h + 1) * 128].rearrange("(c p) n -> p c n", p=128),
                          og[:], ["og"], ["mixd%d_%d" % (u, h)], "mx2")

                for i in range(len(steps) + LOOK):
                    if i < len(steps):
                        emit_qk(i)
                    if i - LOOK >= 0:
                        emit_pv(i - LOOK)

            attention(0)

            T.dma(POOL, WB[:, :, 0:512], w_p1.rearrange("(c p) n -> p c n", p=128), [], ["WB"], "w0")

            def p1_iter(it):
                ok = lambda t: 0 <= t < NT
                if ok(it + 2):
                    seg_n(it + 2)
                if ok(it + 1):
                    seg_tr(it + 1)
                ns = it % 2
                if ok(it):
                    project(nTs[ns], "nT%d" % ns, WB, "WB", [(0, 0, 0, 512)])
                    k_evac_a(0, 0, 4)
                    v_store(it)
                    k_evac_b(4)
                if ok(it - 1):
                    k_tr((it - 1) % 2, 256, kv_store(it - 1))
                if ok(it):
                    k_evac_c(4, ns)
                if ok(it + 3):
                    seg_load(it + 3)

            for t in range(min(3, NT)):
                seg_load(t)
            for it in range(-2, NT + 1):
                p1_iter(it)
            T.dma(POOL, WG, w_g.rearrange("(c p) n -> p c n", p=128), [], ["WB", "xb0", "xb1", "xb2", "WG"], "w2")
            attention(1)
            T.barrier()
            mix_keys = ["mixg%d" % i for i in range(NO)] + ["mixd%d_%d" % (u, h) for u in range(NU) for h in range(4)]

        with contextlib.ExitStack() as fst:
            def sb(n, s, d):
                return fst.enter_context(nc.sbuf_tensor(n, s, d))

            WU = sb("WU", [128, 8, DFF], BF16)
            WD = sb("WD", [128, NFF, D], BF16)
            gff = sb("gff", [128, D], F32)
            hbs = [sb("hb%d" % i, [128, D], F32) for i in range(2)]
            nbs = [sb("fnb%d" % i, [128, D], BF16) for i in range(2)]
            obs = [sb("ob%d" % i, [128, D], F32) for i in range(2)]
            T.dma(SP, gff[:], g_ffn[:, :], [], ["gff"], "c2")
            with contextlib.ExitStack() as ost:
                def sbo(n, s, d):
                    return ost.enter_context(nc.sbuf_tensor(n, s, d))
                WO = sbo("WO", [128, 8, D], BF16)
                mxs = [sbo("mx%d" % i, [128, D], BF16) for i in range(2)]
                mxT = [sbo("mxT%d" % i, [128, D], BF16) for i in range(2)]
                xos = [sbo("xo%d" % i, [128, D], F32) for i in range(2)]
                T.dma(POOL, WO[:], w_o.rearrange("(c p) n -> p c n", p=128), [], ["WO"], "w1")
                T.dma(POOL, WU[:], w_u.rearrange("(c p) n -> p c n", p=128), [], ["WU"], "w3")
                T.dma(POOL, WD[:], w_d.rearrange("(c p) n -> p c n", p=128), [], ["WD"], "w4")
                for i in range(NO):
                    s = i % 2
                    T.dma(SP, mxs[s][:], mix_scr[i * 128:(i + 1) * 128, :], mix_keys if i < 2 else [], ["mx%d" % s], "m%d" % s)
                    T.dma(SP, xos[s][:], x_own[i * 128:(i + 1) * 128, :], [], ["xo%d" % s], "xo%d" % s)
                    for c in range(8):
                        T.op(PE, lambda c=c, s=s: PEe.transpose(out=pbb[0][:, c * 128:(c + 1) * 128],
                                                                in_=mxs[s][:, c * 128:(c + 1) * 128], identity=identb[:]),
                             ["mx%d" % s, "identb"], ["pb0"], inc=(c == 7))
                    T.op(ACT, lambda s=s: A.copy(out=mxT[s][:], in_=pbb[0][:, 0:1024]), ["pb0"], ["mxT%d" % s])
                    for nb_ in range(2):
                        bank = 1 + nb_
                        for c in range(8):
                            T.op(PE, lambda c=c, s=s, nb_=nb_, bank=bank:
                                 PEe.matmul(pb[bank][:, 0:512], lhsT=mxT[s][:, c * 128:(c + 1) * 128],
                                            rhs=WO[:, c, nb_ * 512:(nb_ + 1) * 512], start=(c == 0), stop=(c == 7)),
                                 ["mxT%d" % s, "WO"], ["pb%d" % bank], inc=(c == 7))
                        T.op(DVE, lambda s=s, nb_=nb_, bank=bank:
                             V.tensor_tensor(out=hbs[s][:, nb_ * 512:(nb_ + 1) * 512], in0=pb[bank][:, 0:512],
                                             in1=xos[s][:, nb_ * 512:(nb_ + 1) * 512], op=ALU.add),
                             ["pb%d" % bank, "xo%d" % s], ["hb%d" % s])
                    T.dma(SP, h_scr[i * 128:(i + 1) * 128, :], hbs[s][:], ["hb%d" % s], ["hscr%d" % i], "hs%d" % s)
                    if dbg:
                        T.dma(SP, dbg_outs["d_h"][i * 128:(i + 1) * 128, :], hbs[s][:], ["hb%d" % s], [], "dbg")
                        T.op(DVE, lambda s=s: V.tensor_copy(out=xos[s][:], in_=mxs[s][:]), ["mx%d" % s, "xo%d" % s], ["xo%d" % s])
                        T.dma(SP, dbg_outs["d_mix"][i * 128:(i + 1) * 128, :], xos[s][:], ["xo%d" % s], [], "dbg")

            T.barrier()
            mT = sb("mT", [128, 8, 512], BF16)
            actT = sb("actT", [128, NFF, 512], BF16)
            ee = sb("ee", [128, 512], F32)
            for gI in range(NU):
                for tt in range(4):
                    i = gI * 4 + tt
                    s = i % 2
                    T.dma(SP, hbs[s][:], h_scr[i * 128:(i + 1) * 128, :], ["hscr%d" % i], ["hb%d" % s], "hl%d" % s)
                    T.op(DVE, lambda s=s: V.scalar_tensor_tensor(out=junk[:], in0=hbs[s][:], scalar=1.0, in1=hbs[s][:],
                                                                 op0=ALU.mult, op1=ALU.mult, accum_out=st[:, 0:1]),
                         ["hb%d" % s], ["junk", "st0"])
                    rstd_from_ss(st[:, 0:1], st[:, 1:2], float(D), ["st0"])
                    T.op(DVE, lambda s=s: V.scalar_tensor_tensor(out=nbs[s][:], in0=hbs[s][:], scalar=st[:, 1:2],
                                                                 in1=gff[:], op0=ALU.mult, op1=ALU.mult),
                         ["hb%d" % s, "st0", "gff"], ["fnb%d" % s])
                    for c in range(8):
                        T.op(PE, lambda c=c, s=s: PEe.transpose(out=pbb[0][:, c * 128:(c + 1) * 128],
                                                                in_=nbs[s][:, c * 128:(c + 1) * 128], identity=identb[:]),
                             ["fnb%d" % s, "identb"], ["pb0"], inc=(c == 7))
                    T.op(ACT, lambda tt=tt: A.copy(out=mT[:, :, tt * 128:(tt + 1) * 128],
                                                   in_=pbb[0][:, 0:1024].rearrange("p (a b) -> p a b", b=128)),
                         ["pb0"], ["mT"])
                for f in range(NFF):
                    gb = 1 + (f % 2)
                    ub = 3 + (f % 2)
                    for c in range(8):
                        T.op(PE, lambda c=c, f=f, gb=gb: PEe.matmul(pb[gb][:, 0:512], lhsT=WG[:, c, f * 128:(f + 1) * 128],
                                                                    rhs=mT[:, c, :], start=(c == 0), stop=(c == 7)),
                             ["WG", "mT"], ["pb%d" % gb], inc=(c == 7))
                    for c in range(8):
                        T.op(PE, lambda c=c, f=f, ub=ub: PEe.matmul(pb[ub][:, 0:512], lhsT=WU[:, c, f * 128:(f + 1) * 128],
                                                                    rhs=mT[:, c, :], start=(c == 0), stop=(c == 7)),
                             ["WU", "mT"], ["pb%d" % ub], inc=(c == 7))
                    T.op(ACT, lambda gb=gb: A.activation(out=ee[:], in_=pb[gb][:, 0:512], func=AF.Silu),
                         ["pb%d" % gb], ["ee"])
                    T.op(DVE, lambda f=f, ub=ub: V.tensor_tensor(out=actT[:, f, :], in0=pb[ub][:, 0:512], in1=ee[:], op=ALU.mult),
                         ["pb%d" % ub, "ee"], ["actT"])
                for tt in range(4):
                    i = gI * 4 + tt
                    s = i % 2
                    T.dma(SP, hbs[s][:], h_scr[i * 128:(i + 1) * 128, :], ["hscr%d" % i], ["hb%d" % s], "hl%d" % s)
                    for nb_ in range(2):
                        bank = 5 + nb_
                        for f in range(NFF):
                            T.op(PE, lambda f=f, tt=tt, nb_=nb_, bank=bank:
                                 PEe.matmul(pb[bank][:, 0:512], lhsT=actT[:, f, tt * 128:(tt + 1) * 128],
                                            rhs=WD[:, f, nb_ * 512:(nb_ + 1) * 512], start=(f == 0), stop=(f == NFF - 1)),
                                 ["actT", "WD"], ["pb%d" % bank], inc=(f == NFF - 1))
                        T.op(DVE, lambda s=s, nb_=nb_, bank=bank:
                             V.tensor_tensor(out=obs[s][:, nb_ * 512:(nb_ + 1) * 512], in0=pb[bank][:, 0:512],
                                             in1=hbs[s][:, nb_ * 512:(nb_ + 1) * 512], op=ALU.add),
                             ["pb%d" % bank, "hb%d" % s], ["ob%d" % s])
                    T.dma(SP, out[i * 128:(i + 1) * 128, :], obs[s][:], ["ob%d" % s], [], "o%d" % s)

        for n in list(T.dsems):
            d = T.dsem(n)
            nc.sync.wait_ge(d.sem, d.count)
    return nc


def _consts():
    c = np.zeros((128, 514), np.float32)
    c[:, 0:128] = np.eye(128, dtype=np.float32)
    s = np.arange(128)[:, None]
    t = np.arange(128)[None, :]
    same = (s // 64) == (t // 64)
    c[:, 128:256] = np.where(same & (s <= t), -1.0 / 16.0, 0.0)
    c[:, 256:384] = np.where(same & (s > t), -1.0 / 16.0, 0.0)
    c[:, 384:512] = np.where(same & (s <= t), 1.0, 0.0)
    c[0:64, 512] = -1.0 / 16.0
    c[64:128, 513] = -1.0 / 16.0
    return c


def _tabs(j):
    tb = np.zeros((128, 260), np.float32)
    kl = np.arange(128, dtype=np.float64)
    for h in range(4):
        for ei in range(64):
            e = ei - 48
            if e <= 4 * j + 3:
                tb[:, h * 64 + ei] = SLOPES[h] * (128.0 * (e - 4 * j) + kl - 256.0)
            else:
                tb[:, h * 64 + ei] = NEG
    tb[:, 256 + j] = 1.0
    return tb


def _masks(j):
    m = np.zeros((128, 16, 4, 128), np.float32)
    k = np.arange(128)[:, None]
    q = np.arange(128)[None, :]
    tri = (k <= q).astype(np.float32)
    for e in range(16):
        for c in range(4):
            cs = 4 * j + c
            if e < cs:
                m[:, e, c, :] = 1.0
            elif e == cs:
                m[:, e, c, :] = tri
    return m.reshape(128, 16 * 512)


def _rep(v, n=128):
    return np.ascontiguousarray(np.broadcast_to(np.asarray(v, np.float32).reshape(1, -1), (n, v.size)))


def prep_inputs(inp, NU=4):
    T_ = NU * 2048
    f = lambda a: np.ascontiguousarray(np.asarray(a, dtype=np.float32))
    x = f(inp["x"])
    w_in = f(inp["w_in"])[0]
    cols = lambda a, b: list(range(a, b))
    dq, dk, dv = 0, 512, 1024
    gq, gk, gv, go, gl = 1536, 1792, 2048, 2560, 3072
    c_p0 = cols(dk, dk + 256) + cols(dv, dv + 256) + cols(gk, gk + 256) + cols(gl, gl + 16) + cols(gv, gv + 512)
    c_p1 = cols(dk + 256, dk + 512) + cols(dv + 256, dv + 512)
    c_ow = cols(dq, dq + 512) + cols(gq, gq + 256) + cols(gk, gk + 256) + cols(gv, gv + 512) + cols(go, go + 512) + cols(gl, gl + 16)
    shared = {
        "w_p0": np.ascontiguousarray(w_in[:, c_p0]),
        "w_p1": np.ascontiguousarray(w_in[:, c_p1]),
        "w_ow": np.ascontiguousarray(w_in[:, c_ow]),
        "w_up": f(inp["w_gla_gate_up"])[0],
        "w_o": f(inp["w_out"])[0],
        "w_g": f(inp["w_ffn_gate"])[0],
        "w_u": f(inp["w_ffn_up"])[0],
        "w_d": f(inp["w_ffn_down"])[0],
        "g_attn": _rep(f(inp["attn_norm_gain"])[0]),
        "g_ffn": _rep(f(inp["ffn_norm_gain"])[0]),
        "qk_rep": np.concatenate([_rep(np.tile(f(inp["q_norm_gain"])[0], 8)),
                                  _rep(np.tile(f(inp["k_norm_gain"])[0], 8))], axis=1),
        "b_rep": _rep(f(inp["b_gla_gate"])[0]),
        "gn_rep": np.concatenate([_rep(f(inp["diff_out_norm_gain"])[0]), _rep(f(inp["gla_out_norm_gain"])[0])], axis=1),
        "lamv": np.concatenate([_rep(f(inp[k])[0]) for k in ("lambda_q1", "lambda_k1", "lambda_q2", "lambda_k2")], axis=1),
        "cst": _consts(),
    }
    in_maps = []
    for core in range(8):
        b, j = core // 4, core % 4
        own_rows = np.concatenate([np.arange((4 * u + j) * 512, (4 * u + j + 1) * 512) for u in range(NU)])
        m = dict(shared)
        m["x_all"] = np.ascontiguousarray(x[b, :T_])
        m["x_own"] = np.ascontiguousarray(x[b, own_rows])
        m["tabs"] = _tabs(j)
        m["msk"] = _masks(j)
        in_maps.append(m)
    return in_maps


def assemble(results, NU=4, B=2, key="out"):
    T_ = NU * 2048
    outp = np.zeros((B, T_, D), np.float32)
    for core in range(8):
        b, j = core // 4, core % 4
        r = np.asarray(results[core][key])
        for u in range(NU):
            outp[b, (4 * u + j) * 512:(4 * u + j + 1) * 512] = r[u * 512:(u + 1) * 512]
    return outp


_NC_CACHE = {}


def kernel(**inputs):
    NU = 4
    if NU not in _NC_CACHE:
        _NC_CACHE[NU] = build(NU)
    nc = _NC_CACHE[NU]
    in_maps = prep_inputs(inputs, NU)
    res = run_bass_kernel_spmd(nc, in_maps, core_ids=list(range(8)))
    return assemble(res.results, NU)
```

```python
import contextlib
import numpy as np
import concourse.bass as bass
import concourse.mybir as mybir
from concourse.bass_utils import run_bass_kernel_spmd

F32 = mybir.dt.float32
BF16 = mybir.dt.bfloat16
AF = mybir.ActivationFunctionType
ALU = mybir.AluOpType
AX = mybir.AxisListType

D = 1024
DFF = 2816
NFF = DFF // 128
EPS = 1e-6
LAMBDA_INIT = 0.8 - 0.6 * 1.0
SLOPES = [2.0 ** (-8.0 * (h + 1.0) / 4.0) for h in range(4)]
NEG = -30000.0
S_MAX = 8.0 * 1.25
UNDERFLOW = 104.0
SAME_ENGINE_SYNC = True


class ES:
    def __init__(self, name, eng, sem):
        self.name = name
        self.eng = eng
        self.sem = sem
        self.count = 0
        self.seen = {}


class Trk:
    def __init__(self, nc, stack):
        self.nc = nc
        self.stack = stack
        self.lw = {}
        self.rd = {}
        self.dsems = {}
        mk = lambda n, e: ES(n, e, stack.enter_context(nc.semaphore("s_" + n)))
        self.pe = mk("pe", nc.tensor)
        self.act = mk("act", nc.scalar)
        self.dve = mk("dve", nc.vector)
        self.pool = mk("pool", nc.gpsimd)
        self.sp = mk("sp", nc.sync)

    def dsem(self, name):
        if name not in self.dsems:
            self.dsems[name] = ES("d_" + name, None,
                                  self.stack.enter_context(self.nc.semaphore("d_" + name)))
        return self.dsems[name]

    def _deps(self, reads, writes):
        raw = {}
        other = {}

        def add(d, e, v):
            if d.get(e, 0) < v:
                d[e] = v
        for k in reads:
            if k in self.lw:
                add(raw, *self.lw[k])
        for k in writes:
            if k in self.lw:
                add(other, *self.lw[k])
            for e, v in self.rd.get(k, {}).items():
                add(other, e, v)
        return raw, other

    def _wait(self, E, deps):
        raw, other = deps
        allv = dict(other)
        for e, v in raw.items():
            if allv.get(e, 0) < v:
                allv[e] = v
        for e, v in allv.items():
            if e is E:
                if E.name == "pe" or not SAME_ENGINE_SYNC:
                    continue
                v = raw.get(e, 0)
                if v == 0:
                    continue
            if E.seen.get(e, 0) >= v:
                continue
            assert v <= e.count, (E.name, e.name, v, e.count)
            E.eng.wait_ge(e.sem, v)
            E.seen[e] = v

    def _rec(self, E, val, reads, writes):
        for k in writes:
            self.lw[k] = (E, val)
            self.rd[k] = {}
        for k in reads:
            d = self.rd.setdefault(k, {})
            if d.get(E, 0) < val:
                d[E] = val

    def op(self, E, fn, reads=(), writes=(), inc=True):
        pr = [k for k in reads if k.startswith("pb")]
        if pr:
            reads = [k for k in reads if not k.startswith("pb")]
            writes = list(writes) + [k for k in pr if k not in writes]
        self._wait(E, self._deps(reads, writes))
        inst = fn()
        if inc:
            inst.then_inc(E.sem, 1)
            E.count += 1
            val = E.count
        else:
            val = E.count + 1
        self._rec(E, val, reads, writes)
        return inst

    def barrier(self):
        engs = [self.pe, self.act, self.dve, self.pool, self.sp]
        srcs = engs + list(self.dsems.values())
        for E in engs:
            for e in srcs:
                if e is E or e.count == 0 or E.seen.get(e, 0) >= e.count:
                    continue
                E.eng.wait_ge(e.sem, e.count)
                E.seen[e] = e.count

    def dma(self, Q, out, in_, reads, writes, sem):
        self._wait(Q, self._deps(reads, writes))
        Dm = self.dsem(sem)
        inst = Q.eng.dma_start(out=out, in_=in_)
        inst.then_inc(Dm.sem, 16)
        Dm.count += 16
        self._rec(Dm, Dm.count, reads, writes)
        return inst


def build(NU=4, dbg=False):
    T_ = NU * 2048
    NT = T_ // 128
    NO = NU * 4
    TO = NO * 128
    nc = bass.Bass("TRN2", target_bir_lowering=False)

    def din(name, shape, dt=F32):
        return nc.dram_tensor(name, shape, dt, kind="ExternalInput").ap()

    x_all = din("x_all", [T_, D])
    x_own = din("x_own", [TO, D])
    w_p0 = din("w_p0", [D, 1296])
    w_p1 = din("w_p1", [D, 512])
    w_ow = din("w_ow", [D, 2064])
    w_up = din("w_up", [16, 256])
    w_o = din("w_o", [D, D])
    w_g = din("w_g", [D, DFF])
    w_u = din("w_u", [D, DFF])
    w_d = din("w_d", [DFF, D])
    g_attn = din("g_attn", [128, D])
    g_ffn = din("g_ffn", [128, D])
    qk_rep = din("qk_rep", [128, 1024])
    b_rep = din("b_rep", [128, 256])
    gn_rep = din("gn_rep", [128, 256])
    lamv = din("lamv", [128, 256])
    cst = din("cst", [128, 514])
    tabs = din("tabs", [128, 260])
    msk = din("msk", [128, 16 * 512])
    out = nc.dram_tensor("out", [TO, D], F32, kind="ExternalOutput").ap()
    mix_scr = nc.dram_tensor("mix_scr", [TO, D], BF16, kind="Internal").ap()
    h_scr = nc.dram_tensor("h_scr", [TO, D], F32, kind="Internal").ap()
    dbg_outs = {}
    if dbg:
        dbg_outs["d_mix"] = nc.dram_tensor("d_mix", [TO, D], F32, kind="ExternalOutput").ap()
        dbg_outs["d_h"] = nc.dram_tensor("d_h", [TO, D], F32, kind="ExternalOutput").ap()

    with contextlib.ExitStack() as gst:
        T = Trk(nc, gst)
        PE, ACT, DVE, POOL, SP = T.pe, T.act, T.dve, T.pool, T.sp
        V = nc.vector
        A = nc.scalar
        G = nc.gpsimd
        PEe = nc.tensor

        def sbg(n, s, d):
            return gst.enter_context(nc.sbuf_tensor(n, s, d))

        pb = [gst.enter_context(nc.psum_tensor("pb%d" % i, [128, 512], F32)) for i in range(8)]
        pbb = [p[:].bitcast(BF16) for p in pb]

        cstf = sbg("cstf", [128, 514], F32)
        identb = sbg("identb", [128, 128], BF16)
        st = sbg("st", [128, 32], F32)
        junk = sbg("junk", [128, 1024], BF16)
        arena = sbg("arena", [128, 16512 + 3 * 2048], BF16)
        WB = arena[:, 0:16512].rearrange("p (c n) -> p c n", n=2064)
        xbs = [arena[:, 16512 + k * 2048:16512 + (k + 1) * 2048].bitcast(F32) for k in range(3)]
        WG = arena[:, 0:8 * DFF].rearrange("p (c n) -> p c n", n=DFF)
        T.dma(SP, cstf[:], cst[:, :], [], ["cstf"], "c0")
        T.op(DVE, lambda: V.tensor_copy(out=identb[:], in_=cstf[:, 0:128]), ["cstf"], ["identb"])
        triS = cstf[:, 128:256]
        upS = cstf[:, 256:384]
        chI = cstf[:, 512:514]

        def rstd_from_ss(ss_ap, out_ap, n, keys):
            T.op(ACT, lambda: A.activation(out=out_ap, in_=ss_ap, func=AF.Ln, scale=1.0 / n, bias=EPS),
                 keys, keys)
            T.op(ACT, lambda: A.activation(out=out_ap, in_=out_ap, func=AF.Exp, scale=-0.5),
                 keys, keys)

        def norm_transpose(src_ap, xb, xk, nb, nbk, nTt, nTk, grep, grepk, dq, dsem, bank=5):
            T.dma(dq, xb[:], src_ap, [], [xk], dsem)
            T.op(DVE, lambda: V.scalar_tensor_tensor(out=junk[:], in0=xb[:], scalar=1.0, in1=xb[:],
                                                     op0=ALU.mult, op1=ALU.mult, accum_out=st[:, 0:1]),
                 [xk], ["junk", "st0"])
            rstd_from_ss(st[:, 0:1], st[:, 1:2], float(D), ["st0"])
            T.op(DVE, lambda: V.scalar_tensor_tensor(out=nb[:], in0=xb[:], scalar=st[:, 1:2], in1=grep[:],
                                                     op0=ALU.mult, op1=ALU.mult),
                 [xk, "st0", grepk], [nbk])
            bk = "pb%d" % bank
            for c in range(8):
                T.op(PE, lambda c=c: PEe.transpose(out=pbb[bank][:, c * 128:(c + 1) * 128],
                                                   in_=nb[:, c * 128:(c + 1) * 128], identity=identb[:]),
                     [nbk, "identb"], [bk], inc=(c == 7))
            T.op(ACT, lambda: A.copy(out=nTt[:], in_=pbb[bank][:, 0:1024]), [bk], [nTk])

        def project(nTt, nTk, W, Wk, groups):
            for (bank, pc0, wc0, ncol) in groups:
                bk = "pb%d" % bank
                for c in range(8):
                    T.op(PE, lambda c=c, bank=bank, pc0=pc0, wc0=wc0, ncol=ncol:
                         PEe.matmul(pb[bank][:, pc0:pc0 + ncol], lhsT=nTt[:, c * 128:(c + 1) * 128],
                                    rhs=W[:, c, wc0:wc0 + ncol], start=(c == 0), stop=(c == 7)),
                         [nTk, Wk], [bk], inc=(c == 7))

        with contextlib.ExitStack() as mst:
            def sb(n, s, d):
                return mst.enter_context(nc.sbuf_tensor(n, s, d))

            KT = sb("KT", [128, 2, T_], BF16)
            VE = sb("VE", [128, NT, 2, 130], BF16)
            QT = sb("QT", [128, 4, TO], BF16)
            MK = sb("MK", [128, 16, 512], BF16)
            tabf = sb("tabf", [128, 260], F32)
            gat = sb("gat", [128, D], F32)
            qg = sb("qg", [128, 512], F32)
            brep = sb("brep", [128, 256], F32)
            gn = sb("gn", [128, 256], F32)
            wupb = sb("wupb", [16, 256], BF16)
            maskA = sb("maskA", [128, 128], BF16)
            ssx = sb("ssx", [128, 4], F32)
            rstd_all = sb("rstd_all", [128, NT], F32)
            nbs = [sb("nb%d" % i, [128, D], BF16) for i in range(2)]
            nTs = [sb("nT%d" % i, [128, D], BF16) for i in range(2)]
            kf = sb("kf", [128, 512], F32)
            sq = sb("sq", [128, 512], F32)
            k16s = [sb("k16_%d" % i, [128, 512], BF16) for i in range(2)]
            gkf = [sb("gkf%d" % i, [128, 256], F32) for i in range(2)]
            glr16s = [sb("glr16_%d" % i, [128, 16], BF16) for i in range(2)]
            glrT = sb("glrT", [16, 128], BF16)
            zb = sb("zb", [128, 256], F32)
            lg = sb("lg", [128, 256], F32)
            eb = sb("eb", [128, 256], F32)
            enb = sb("enb", [128, 256], F32)
            ec = sb("ec", [128, 256], F32)
            dec = sb("dec", [128, 4], F32)
            qt16 = sb("qt16", [128, 256], BF16)
            kt16 = sb("kt16", [128, 256], BF16)
            kh16 = sb("kh16", [128, 256], BF16)
            vb16s = [sb("vb16_%d" % i, [128, 512], BF16) for i in range(2)]
            qkT = sb("qkT", [128, 512], BF16)
            AT = sb("AT", [128, 512], BF16)
            S = sb("S", [128, 256], F32)
            Sb = [sb("Sb%d" % i, [128, 256], BF16) for i in range(2)]
            snap = sb("snap", [128, NU, 256], F32)
            sgt = sb("sgt", [128, 512], F32)
            gg = sb("gg", [128, 512], F32)
            ogl = sb("ogl", [128, 512], BF16)
            PTs = [sb("PT%d" % i, [128, 512], BF16) for i in range(6)]
            accS = sb("accS", [128, 3, 390], F32)
            t1 = sb("t1", [128, 128], F32)
            of = sb("of", [128, 128], F32)
            og = sb("og", [128, 4, 128], BF16)

            T.dma(SP, tabf[:], tabs[:, :], [], ["tabf"], "c1")
            T.dma(SP, gat[:], g_attn[:, :], [], ["gat"], "c2")
            T.dma(SP, kf[:], qk_rep[:, 0:512], [], ["kf"], "c3")
            T.dma(SP, sq[:], qk_rep[:, 512:1024], [], ["sq"], "c3b")
            T.dma(SP, brep[:], b_rep[:, :], [], ["brep"], "c4")
            T.dma(SP, gn[:], gn_rep[:, :], [], ["gn"], "c5")
            lam = zb
            T.dma(SP, lam[:], lamv[:, :], [], ["zb"], "c6")
            T.dma(POOL, wupb[:], w_up[:, :], [], ["wupb"], "c7")
            T.dma(POOL, maskA[:], cst[:, 384:512], [], ["maskA"], "c8")
            T.dma(POOL, WB[:, :, 0:1296], w_p0.rearrange("(c p) n -> p c n", p=128), [], ["WB"], "w0")
            for e4 in range(4):
                T.dma(POOL, MK[:, e4 * 4:(e4 + 1) * 4, :],
                      msk[:, e4 * 2048:(e4 + 1) * 2048].rearrange("p (a b) -> p a b", b=512),
                      [], ["MK"], "c9")
            T.op(DVE, lambda: V.scalar_tensor_tensor(out=qg[:, 0:512], in0=kf[:, 0:512], scalar=0.125,
                                                     in1=sq[:, 0:512], op0=ALU.mult, op1=ALU.mult),
                 ["kf", "sq"], ["qg"])
            T.op(DVE, lambda: V.tensor_scalar(out=gn[:, 0:128], in0=gn[:, 0:128], scalar1=1.0 - LAMBDA_INIT,
                                              scalar2=None, op0=ALU.mult), ["gn"], ["gn"])
            T.op(DVE, lambda: V.scalar_tensor_tensor(out=junk[:, 0:64], in0=lam[:, 0:64], scalar=1.0,
                                                     in1=lam[:, 64:128], op0=ALU.mult, op1=ALU.mult,
                                                     accum_out=st[:, 4:5]), ["zb"], ["junk", "stl"])
            T.op(DVE, lambda: V.scalar_tensor_tensor(out=junk[:, 0:64], in0=lam[:, 128:192], scalar=1.0,
                                                     in1=lam[:, 192:256], op0=ALU.mult, op1=ALU.mult,
                                                     accum_out=st[:, 5:6]), ["zb", "stl"], ["junk", "stl"])
            T.op(ACT, lambda: A.activation(out=st[:, 6:8], in_=st[:, 4:6], func=AF.Exp), ["stl"], ["stl"])
            T.op(DVE, lambda: V.tensor_tensor(out=st[:, 8:9], in0=st[:, 7:8], in1=st[:, 6:7], op=ALU.subtract),
                 ["stl"], ["stl"])
            T.op(DVE, lambda: V.tensor_scalar(out=st[:, 8:9], in0=st[:, 8:9], scalar1=-LAMBDA_INIT,
                                              scalar2=None, op0=ALU.add), ["stl"], ["stl"])
            neglam = st[:, 8:9]
            T.op(POOL, lambda: G.memset(VE[:, :, :, 128:130], 1.0), [], ["VE"])
            T.op(DVE, lambda: V.memset(S[:], 0.0), [], ["S"])

            def k_evac_a(bank, c0, ngrp):
                n = 64 * ngrp
                T.op(ACT, lambda: A.copy(out=kf[:, 0:n], in_=pb[bank][:, c0:c0 + n]), ["pb%d" % bank], ["kf"])

            def k_evac_b(ngrp):
                n = 64 * ngrp
                T.op(DVE, lambda: V.tensor_tensor(out=sq[:, 0:n], in0=kf[:, 0:n], in1=kf[:, 0:n], op=ALU.mult),
                     ["kf"], ["sq"])
                T.op(DVE, lambda: V.tensor_reduce(out=st[:, 16:16 + ngrp],
                                                  in_=sq[:, 0:n].rearrange("p (a b) -> p a b", b=64),
                                                  axis=AX.X, op=ALU.add), ["sq"], ["stk"])
                rstd_from_ss(st[:, 16:16 + ngrp], st[:, 24:24 + ngrp], 64.0, ["stk"])

            def k_evac_c(ngrp, ks, gain=None):
                n = 64 * ngrp
                kk = "k16_%d" % ks
                bc = bass.AP(st[:].tensor, st[:, 24:25].offset, [list(st[:].ap[0]), [1, ngrp], [0, 64]])
                if gain is None:
                    T.op(DVE, lambda: V.tensor_tensor(out=k16s[ks][:, 0:n].rearrange("p (a b) -> p a b", b=64),
                                                      in0=kf[:, 0:n].rearrange("p (a b) -> p a b", b=64),
                                                      in1=bc, op=ALU.mult), ["kf", "stk"], [kk])
                else:
                    T.op(DVE, lambda: V.tensor_tensor(out=sq[:, 0:n].rearrange("p (a b) -> p a b", b=64),
                                                      in0=kf[:, 0:n].rearrange("p (a b) -> p a b", b=64),
                                                      in1=bc, op=ALU.mult), ["kf", "stk", "sq"], ["sq"])
                    T.op(DVE, lambda: V.tensor_tensor(out=k16s[ks][:, 0:n], in0=sq[:, 0:n], in1=gain,
                                                      op=ALU.mult), ["sq", "qg"], [kk])

            def k_evac(bank, c0, ngrp, ks, gain=None):
                k_evac_a(bank, c0, ngrp)
                k_evac_b(ngrp)
                k_evac_c(ngrp, ks, gain)

            def k_tr(ks, n, dst_fn):
                nh = n // 128
                for hh in range(nh):
                    T.op(PE, lambda hh=hh: PEe.transpose(out=pbb[4][:, hh * 128:(hh + 1) * 128],
                                                         in_=k16s[ks][:, hh * 128:(hh + 1) * 128], identity=identb[:]),
                         ["k16_%d" % ks, "identb"], ["pb4"], inc=(hh == nh - 1))
                dst_fn(pbb[4][:, 0:n].rearrange("p (a b) -> p a b", b=128))

            def gla_segs(own, i_own, gk, gkk, gq, gqk, vb, vbk, g16, g16k, psS):
                def s_a():
                    T.op(PE, lambda: PEe.transpose(out=pbb[6][0:16, 64:192], in_=g16[:], identity=identb[:]),
                         [g16k, "identb"], ["pb6"])
                    T.op(ACT, lambda: A.copy(out=glrT[:], in_=pbb[6][0:16, 64:192]), ["pb6"], ["glrT"])

                def s_b():
                    T.op(PE, lambda: PEe.matmul(pb[6][:, 128:384], lhsT=glrT[:], rhs=wupb[:], start=True, stop=True),
                         ["glrT", "wupb"], ["pb6"])
                    T.op(DVE, lambda: V.tensor_tensor(out=zb[:], in0=pb[6][:, 128:384], in1=brep[:], op=ALU.add),
                         ["pb6", "brep"], ["zb"])
                    T.op(ACT, lambda: A.activation(out=lg[:], in_=zb[:], func=AF.Exp, scale=-1.0), ["zb"], ["lg"])
                    T.op(ACT, lambda: A.activation(out=lg[:], in_=lg[:], func=AF.Ln, bias=1.0), ["lg"], ["lg"])

                def s_c():
                    if own:
                        T.op(PE, lambda: PEe.matmul(pb[7][:, 0:256], lhsT=triS, rhs=lg[:], start=True, stop=True),
                             ["cstf", "lg"], ["pb7"], inc=False)
                    T.op(PE, lambda: PEe.matmul(pb[7][:, 256:512], lhsT=upS, rhs=lg[:], start=True, stop=True),
                         ["cstf", "lg"], ["pb7"])
                    for p in range(2):
                        T.op(PE, lambda p=p: PEe.matmul(pb[6][:, 96 + 2 * p:98 + 2 * p], lhsT=lg[:, p * 128:(p + 1) * 128],
                                                        rhs=chI, start=True, stop=True),
                             ["lg", "cstf"], ["pb6"], inc=(p == 1))
                    T.op(ACT, lambda: A.activation(out=ec[:], in_=pb[7][:, 256:512], func=AF.Exp), ["pb7"], ["ec"])
                    if own:
                        T.op(ACT, lambda: A.activation(out=eb[:], in_=pb[7][:, 0:256], func=AF.Exp), ["pb7"], ["eb"])
                        T.op(ACT, lambda: A.activation(out=enb[:], in_=pb[7][:, 0:256], func=AF.Exp, scale=-1.0),
                             ["pb7"], ["enb"])
                    T.op(ACT, lambda: A.activation(out=dec[:], in_=pb[6][:, 96:100], func=AF.Exp), ["pb6"], ["dec"])
                    T.op(DVE, lambda: V.tensor_tensor(out=kh16[:], in0=gk, in1=ec[:], op=ALU.mult),
                         [gkk, "ec"], ["kh16"])
                    if own:
                        T.op(DVE, lambda: V.scalar_tensor_tensor(out=qt16[:], in0=gq, scalar=0.125,
                                                                 in1=eb[:], op0=ALU.mult, op1=ALU.mult),
                             [gqk, "eb"], ["qt16"])
                        T.op(DVE, lambda: V.tensor_tensor(out=kt16[:], in0=gk, in1=enb[:], op=ALU.mult),
                             [gkk, "enb"], ["kt16"])
                        for p in range(2):
                            T.op(PE, lambda p=p: PEe.transpose(out=pbb[4][:, 512 + p * 128:512 + (p + 1) * 128],
                                                               in_=qt16[:, p * 128:(p + 1) * 128], identity=identb[:]),
                                 ["qt16", "identb"], ["pb4"], inc=False)
                        for p in range(2):
                            T.op(PE, lambda p=p: PEe.transpose(out=pbb[4][:, 768 + p * 128:768 + (p + 1) * 128],
                                                               in_=kt16[:, p * 128:(p + 1) * 128], identity=identb[:]),
                                 ["kt16", "identb"], ["pb4"], inc=(p == 1))
                        T.op(ACT, lambda: A.copy(out=qkT[:], in_=pbb[4][:, 512:1024]), ["pb4"], ["qkT"])

                def s_c2():
                    for hh in range(2):
                        bank = 0 if hh == 0 else 3
                        for p in range(2):
                            T.op(PE, lambda hh=hh, p=p, bank=bank:
                                 PEe.matmul(pb[bank][:, p * 128:(p + 1) * 128],
                                            lhsT=qkT[hh * 64:(hh + 1) * 64, 256 + p * 128:256 + (p + 1) * 128],
                                            rhs=qkT[hh * 64:(hh + 1) * 64, p * 128:(p + 1) * 128],
                                            start=True, stop=True),
                                 ["qkT"], ["pb%d" % bank], inc=(p == 1))
                    mA = bass.AP(maskA[:].tensor, maskA[:].offset, [list(maskA[:].ap[0]), [0, 2], [1, 128]])
                    for hh in range(2):
                        bank = 0 if hh == 0 else 3
                        outv = bass.AP(AT[:].tensor, AT[:, hh * 128:hh * 128 + 1].offset,
                                       [list(AT[:].ap[0]), [256, 2], [1, 128]])
                        T.op(DVE, lambda bank=bank, outv=outv: V.tensor_tensor(
                            out=outv, in0=pb[bank][:, 0:256].rearrange("p (a b) -> p a b", b=128), in1=mA, op=ALU.mult),
                            ["pb%d" % bank, "maskA"], ["AT"])

                def s_d(ch):
                    def f():
                        if own:
                            T.op(DVE, lambda: V.tensor_copy(out=Sb[ch][:], in_=S[:]), ["S"], ["Sb%d" % ch])
                        sbank, scol = psS[ch]
                        sk = "pb%d" % sbank
                        for p in range(2):
                            for hh in range(2):
                                h = 2 * p + hh
                                T.op(PE, lambda p=p, hh=hh, h=h:
                                     PEe.matmul(pb[sbank][hh * 64:(hh + 1) * 64, scol + p * 128:scol + (p + 1) * 128],
                                                lhsT=kh16[ch * 64:(ch + 1) * 64, h * 64:(h + 1) * 64],
                                                rhs=vb[ch * 64:(ch + 1) * 64, h * 128:(h + 1) * 128],
                                                start=True, stop=True),
                                     ["kh16", vbk], [sk], inc=(p == 1 and hh == 1))
                        for p in range(2):
                            T.op(DVE, lambda p=p:
                                 V.scalar_tensor_tensor(out=S[:, p * 128:(p + 1) * 128], in0=S[:, p * 128:(p + 1) * 128],
                                                        scalar=dec[:, 2 * p + ch:2 * p + ch + 1],
                                                        in1=pb[sbank][:, scol + p * 128:scol + (p + 1) * 128],
                                                        op0=ALU.mult, op1=ALU.add),
                                 ["S", "dec", sk], ["S"])
                    return f

                def s_o():
                    for hh in range(2):
                        bank = 0 if hh == 0 else 3
                        bk = "pb%d" % bank
                        for p in range(2):
                            h = 2 * p + hh
                            oc = 256 + p * 128
                            T.op(PE, lambda h=h, bank=bank, oc=oc:
                                 PEe.matmul(pb[bank][:, oc:oc + 128], lhsT=AT[:, h * 128:(h + 1) * 128],
                                            rhs=vb[:, h * 128:(h + 1) * 128], start=True, stop=False),
                                 ["AT", vbk], [bk], inc=False)
                            for ch in range(2):
                                T.op(PE, lambda h=h, hh=hh, p=p, ch=ch, bank=bank, oc=oc:
                                     PEe.matmul(pb[bank][ch * 64:(ch + 1) * 64, oc:oc + 128],
                                                lhsT=qkT[hh * 64:(hh + 1) * 64, p * 128 + ch * 64:p * 128 + (ch + 1) * 64],
                                                rhs=Sb[ch][hh * 64:(hh + 1) * 64, p * 128:(p + 1) * 128],
                                                start=False, stop=True),
                                     ["qkT", "Sb%d" % ch], [bk], inc=(ch == 1 and p == 1))
                    for hh in range(2):
                        bank = 0 if hh == 0 else 3
                        bk = "pb%d" % bank
                        T.op(ACT, lambda bank=bank: A.copy(out=kf[:, 0:256], in_=pb[bank][:, 256:512]), [bk], ["kf"])
                        T.op(DVE, lambda: V.tensor_tensor(out=sq[:, 0:256], in0=kf[:, 0:256], in1=kf[:, 0:256],
                                                          op=ALU.mult), ["kf"], ["sq"])
                        T.op(DVE, lambda: V.tensor_reduce(out=st[:, 16:18],
                                                          in_=sq[:, 0:256].rearrange("p (a b) -> p a b", b=128),
                                                          axis=AX.X, op=ALU.add), ["sq"], ["stk"])
                        rstd_from_ss(st[:, 16:18], st[:, 24:26], 128.0, ["stk"])
                        bc = bass.AP(st[:].tensor, st[:, 24:25].offset, [list(st[:].ap[0]), [1, 2], [0, 128]])
                        ggv = bass.AP(gg[:].tensor, gg[:, hh * 128:hh * 128 + 1].offset,
                                      [list(gg[:].ap[0]), [256, 2], [1, 128]])
                        oglv = bass.AP(ogl[:].tensor, ogl[:, hh * 128:hh * 128 + 1].offset,
                                       [list(ogl[:].ap[0]), [256, 2], [1, 128]])
                        T.op(DVE, lambda bc=bc: V.tensor_tensor(out=sq[:, 256:512].rearrange("p (a b) -> p a b", b=128),
                                                                in0=kf[:, 0:256].rearrange("p (a b) -> p a b", b=128),
                                                                in1=bc, op=ALU.mult), ["kf", "stk"], ["sq2"])
                        T.op(DVE, lambda ggv=ggv, oglv=oglv: V.tensor_tensor(
                            out=oglv, in0=sq[:, 256:512].rearrange("p (a b) -> p a b", b=128), in1=ggv, op=ALU.mult),
                            ["sq2", "gg"], ["ogl"])
                    T.dma(SP, mix_scr[i_own * 128:(i_own + 1) * 128, 512:1024], ogl[:], ["ogl"], ["mixg%d" % i_own], "mx")

                if own:
                    return [s_a, s_b, s_c, s_c2, s_d(0), s_d(1), s_o]
                return [s_a, s_b, s_c, s_d(0), s_d(1)]

            def seg_load(t):
                sl = t % 3
                T.dma(SP, xbs[sl][:], x_all[t * 128:(t + 1) * 128, :], [], ["xb%d" % sl], "x%d" % sl)

            def seg_stats(t):
                sl = t % 3
                T.op(DVE, lambda: V.scalar_tensor_tensor(out=junk[:], in0=xbs[sl][:], scalar=1.0, in1=xbs[sl][:],
                                                         op0=ALU.mult, op1=ALU.mult, accum_out=ssx[:, sl:sl + 1]),
                     ["xb%d" % sl], ["junk", "ssx%d" % sl])
                T.op(ACT, lambda: A.activation(out=rstd_all[:, t:t + 1], in_=ssx[:, sl:sl + 1], func=AF.Ln,
                                               scale=1.0 / D, bias=EPS), ["ssx%d" % sl], ["rs%d" % t])
                T.op(ACT, lambda: A.activation(out=rstd_all[:, t:t + 1], in_=rstd_all[:, t:t + 1], func=AF.Exp, scale=-0.5),
                     ["rs%d" % t], ["rs%d" % t])

            def seg_n(t):
                sl = t % 3
                ns = t % 2
                T.op(DVE, lambda: V.scalar_tensor_tensor(out=nbs[ns][:], in0=xbs[sl][:], scalar=rstd_all[:, t:t + 1],
                                                         in1=gat[:], op0=ALU.mult, op1=ALU.mult),
                     ["xb%d" % sl, "rs%d" % t, "gat"], ["nb%d" % ns])

            def seg_tr(t):
                ns = t % 2
                for c in range(8):
                    T.op(PE, lambda c=c: PEe.transpose(out=pbb[3][:, c * 128:(c + 1) * 128],
                                                       in_=nbs[ns][:, c * 128:(c + 1) * 128], identity=identb[:]),
                         ["nb%d" % ns, "identb"], ["pb3"], inc=(c == 7))
                T.op(ACT, lambda: A.copy(out=nTs[ns][:], in_=pbb[3][:, 0:1024]), ["pb3"], ["nT%d" % ns])

            def run_pipe(n_tiles, early_fn, main_fn, late_fn):
                seg_load(0)
                for it in range(-1, n_tiles + 1):
                    early = early_fn(it + 1) if 0 <= it + 1 < n_tiles else []
                    main = main_fn(it) if 0 <= it < n_tiles else []
                    late = late_fn(it - 1) if 0 <= it - 1 < n_tiles else []
                    for k in range(max(len(early), len(main), len(late))):
                        if k < len(main):
                            main[k]()
                        if k < len(late):
                            late[k]()
                        if k < len(early):
                            early[k]()

            def kv_store(t):
                def dst(src3):
                    T.op(ACT, lambda: A.copy(out=KT[:, :, t * 128:(t + 1) * 128], in_=src3), ["pb4"], ["KT"])
                return dst

            def v_store(t):
                T.op(DVE, lambda: V.tensor_copy(out=VE[:, t, :, 0:128],
                                                in_=pb[0][:, 256:512].rearrange("p (a b) -> p a b", b=128)),
                     ["pb0"], ["VE"])

            def p0_iter(it):
                ok = lambda t: 0 <= t < NT
                late = []
                t1 = it - 1
                if ok(t1):
                    n1 = t1 % 2
                    late = gla_segs(False, None, gkf[n1][:], "gkf%d" % n1, None, None, vb16s[n1], "vb16_%d" % n1,
                                    glr16s[n1], "glr16_%d" % n1, ((5, 0), (7, 0)))
                    late[0]()
                if ok(it + 1):
                    seg_tr(it + 1)
                ns = it % 2
                if ok(it):
                    project(nTs[ns], "nT%d" % ns, WB, "WB", [(0, 0, 0, 512)])
                    k_evac_a(0, 0, 4)
                    v_store(it)
                    k_evac_b(4)
                if ok(t1):
                    late[1]()
                if ok(it):
                    project(nTs[ns], "nT%d" % ns, WB, "WB", [(1, 0, 512, 272)])
                    T.op(DVE, lambda: V.tensor_copy(out=gkf[ns][:], in_=pb[1][:, 0:256]), ["pb1"], ["gkf%d" % ns])
                    T.op(DVE, lambda: V.tensor_copy(out=glr16s[ns][:], in_=pb[1][:, 256:272]), ["pb1"], ["glr16_%d" % ns])
                    k_evac_c(4, ns)
                if ok(it + 2):
                    seg_stats(it + 2)
                if ok(t1):
                    late[2]()
                    k_tr(t1 % 2, 256, kv_store(t1))
                if ok(it + 2):
                    seg_n(it + 2)
                if ok(it):
                    project(nTs[ns], "nT%d" % ns, WB, "WB", [(2, 0, 784, 512)])
                    T.op(ACT, lambda: A.copy(out=vb16s[ns][:], in_=pb[2][:, :]), ["pb2"], ["vb16_%d" % ns])
                if ok(t1):
                    if t1 % 16 % 4 == 0:
                        u = t1 // 16
                        r = (t1 % 16) // 4
                        if r == 0:
                            T.op(DVE, lambda: V.tensor_scalar(out=snap[:, u, :], in0=S[:],
                                                              scalar1=tabf[:, 256 + r:257 + r], scalar2=None,
                                                              op0=ALU.mult), ["S", "tabf"], ["snap"])
                        else:
                            T.op(DVE, lambda: V.scalar_tensor_tensor(out=snap[:, u, :], in0=S[:],
                                                                     scalar=tabf[:, 256 + r:257 + r],
                                                                     in1=snap[:, u, :], op0=ALU.mult, op1=ALU.add),
                                 ["S", "tabf", "snap"], ["snap"])
                    late[3]()
                    late[4]()
                if ok(it + 3):
                    seg_load(it + 3)

            for t in range(min(3, NT)):
                seg_load(t)
            for it in range(-2, NT + 1):
                p0_iter(it)

            T.dma(POOL, WB[:, :, :], w_ow.rearrange("(c p) n -> p c n", p=128), [], ["WB"], "w0")

            def q_store(i):
                def dst(src3):
                    T.op(ACT, lambda: A.copy(out=QT[:, :, i * 128:(i + 1) * 128], in_=src3), ["pb4"], ["QT"])
                return dst

            gnb = bass.AP(gn[:].tensor, gn[:, 128:129].offset, [list(gn[:].ap[0]), [0, 4], [1, 128]])

            def o_load(i):
                sl = i % 3
                T.dma(SP, xbs[sl][:], x_own[i * 128:(i + 1) * 128, :], [], ["xb%d" % sl], "x%d" % sl)

            def o_stats(i):
                sl = i % 3
                T.op(DVE, lambda: V.scalar_tensor_tensor(out=junk[:], in0=xbs[sl][:], scalar=1.0, in1=xbs[sl][:],
                                                         op0=ALU.mult, op1=ALU.mult, accum_out=ssx[:, sl:sl + 1]),
                     ["xb%d" % sl], ["junk", "ssx%d" % sl])
                rstd_from_ss(ssx[:, sl:sl + 1], ssx[:, 3:4] if False else st[:, 1 + (i % 2):2 + (i % 2)], float(D), ["ssx%d" % sl, "sto%d" % (i % 2)])

            def o_n(i):
                sl = i % 3
                ns = i % 2
                T.op(DVE, lambda: V.scalar_tensor_tensor(out=nbs[ns][:], in0=xbs[sl][:], scalar=st[:, 1 + ns:2 + ns],
                                                         in1=gat[:], op0=ALU.mult, op1=ALU.mult),
                     ["xb%d" % sl, "sto%d" % ns, "gat"], ["nb%d" % ns])

            def o_tr(i):
                ns = i % 2
                for c in range(8):
                    T.op(PE, lambda c=c: PEe.transpose(out=pbb[5][:, c * 128:(c + 1) * 128],
                                                       in_=nbs[ns][:, c * 128:(c + 1) * 128], identity=identb[:]),
                         ["nb%d" % ns, "identb"], ["pb5"], inc=(c == 7))
                T.op(ACT, lambda: A.copy(out=nTs[ns][:], in_=pbb[5][:, 0:1024]), ["pb5"], ["nT%d" % ns])

            for i in range(min(2, NO)):
                o_load(i)
            o_stats(0)
            o_n(0)
            o_tr(0)
            for i in range(NO):
                s = i % 2
                if i + 2 < NO:
                    o_load(i + 2)
                if i % 4 == 0:
                    T.op(DVE, lambda i=i: V.tensor_copy(out=S[:], in_=snap[:, i // 4, :]), ["snap"], ["S"])
                project(nTs[s], "nT%d" % s, WB, "WB", [(0, 0, 0, 512), (3, 0, 1536, 512)])
                k_evac_a(0, 0, 8)
                T.op(ACT, lambda: A.activation(out=sgt[:], in_=pb[3][:, :], func=AF.Exp, scale=-1.0), ["pb3"], ["sgt"])
                T.op(ACT, lambda: A.activation(out=sgt[:], in_=sgt[:], func=AF.Ln, bias=1.0), ["sgt"], ["sgt"])
                T.op(ACT, lambda: A.activation(out=sgt[:], in_=sgt[:], func=AF.Exp, scale=-1.0), ["sgt"], ["sgt"])
                project(nTs[s], "nT%d" % s, WB, "WB", [(1, 0, 512, 512), (2, 0, 1024, 512), (6, 0, 2048, 16)])
                k_evac_b(8)
                T.op(DVE, lambda: V.tensor_tensor(out=gg[:], in0=pb[3][:, :], in1=sgt[:], op=ALU.mult),
                     ["pb3", "sgt"], ["gg"])
                T.op(DVE, lambda: V.tensor_copy(out=glr16s[0][:], in_=pb[6][:, 0:16]), ["pb6"], ["glr16_0"])
                T.op(ACT, lambda: A.copy(out=vb16s[0][:], in_=pb[2][:, :]), ["pb2"], ["vb16_0"])
                k_evac_c(8, 0, gain=qg[:, 0:512])
                T.op(DVE, lambda: V.tensor_tensor(out=gg[:].rearrange("p (a b) -> p a b", b=128),
                                                  in0=gg[:].rearrange("p (a b) -> p a b", b=128), in1=gnb, op=ALU.mult),
                     ["gg", "gn"], ["gg"])
                segs = gla_segs(True, i, pb[1][:, 256:512], "pb1", pb[1][:, 0:256], "pb1", vb16s[0], "vb16_0",
                                glr16s[0], "glr16_0", ((2, 0), (2, 256)))
                segs[0]()
                if i + 1 < NO:
                    o_stats(i + 1)
                k_tr(0, 512, q_store(i))
                segs[1]()
                if i + 1 < NO:
                    o_n(i + 1)
                segs[2]()
                if i + 1 < NO:
                    o_tr(i + 1)
                for sg in segs[3:]:
                    sg()

            def acc_ap(a):
                bank = 5 + a // 3
                col = (a % 3) * 130
                return bank, col

            def attention(hp):
                def kb_lo(h, u):
                    n_h = int(np.ceil(1.0 + (UNDERFLOW + 2.0 * S_MAX) / (128.0 * SLOPES[h])))
                    return max(0, 16 * u - (n_h - 1))
                steps = [(u, hl, kb) for u in range(NU) for hl in range(2)
                         for kb in range(kb_lo(2 * hp + hl, u), 16 * u + 16)]
                LOOK = 2

                def emit_qk(idx):
                    u, hl, kb = steps[idx]
                    h = 2 * hp + hl
                    e = kb - 16 * u
                    ei = e + 48
                    for m in range(2):
                        sbank = 1 + (2 * idx + m) % 4
                        T.op(PE, lambda m=m, sbank=sbank:
                             PEe.matmul(pb[sbank][:, 0:512],
                                        lhsT=KT[m * 64:(m + 1) * 64, hl, kb * 128:(kb + 1) * 128],
                                        rhs=QT[m * 64:(m + 1) * 64, h, u * 512:(u + 1) * 512],
                                        start=True, stop=True),
                             ["KT", "QT"], ["pb%d" % sbank])
                    for m in range(2):
                        sbank = 1 + (2 * idx + m) % 4
                        slot = (2 * idx + m) % 6
                        T.op(ACT, lambda sbank=sbank, slot=slot:
                             A.activation(out=PTs[slot][:], in_=pb[sbank][:, 0:512], func=AF.Exp,
                                          bias=tabf[:, h * 64 + ei:h * 64 + ei + 1]),
                             ["pb%d" % sbank, "tabf"], ["PT%d" % slot])
                        if e >= 0:
                            T.op(DVE, lambda slot=slot:
                                 V.tensor_tensor(out=PTs[slot][:], in0=PTs[slot][:], in1=MK[:, e, :], op=ALU.mult),
                                 ["PT%d" % slot, "MK"], ["PT%d" % slot])

                def emit_pv(idx):
                    u, hl, kb = steps[idx]
                    h = 2 * hp + hl
                    nkb = 16 * u + 16
                    for m in range(2):
                        slot = (2 * idx + m) % 6
                        for c in range(4):
                            bank, col = acc_ap(c * 2 + m)
                            first = (kb == kb_lo(h, u) and (c * 2 + m) in (0, 4, 6))
                            T.op(PE, lambda c=c, bank=bank, col=col, slot=slot, first=first:
                                 PEe.matmul(pb[bank][:, col:col + 130],
                                            lhsT=PTs[slot][:, c * 128:(c + 1) * 128],
                                            rhs=VE[:, kb, hl, :], start=first, stop=(kb == nkb - 1),
                                            skip_group_check=True),
                                 ["PT%d" % slot, "VE"], ["pb%d" % bank], inc=(c == 3 and m == 1))
                    if kb != nkb - 1:
                        return
                    for bi in range(3):
                        T.op(DVE, lambda bi=bi: V.tensor_copy(out=accS[:, bi, :], in_=pb[5 + bi][:, 0:390]),
                             ["pb%d" % (5 + bi)], ["accS"])
                    for c in range(4):
                        a0 = c * 2
                        a1 = c * 2 + 1
                        A0 = accS[:, a0 // 3, (a0 % 3) * 130:(a0 % 3) * 130 + 130]
                        A1 = accS[:, a1 // 3, (a1 % 3) * 130:(a1 % 3) * 130 + 130]
                        T.op(DVE, lambda A0=A0: V.reciprocal(out=st[:, 10:11], in_=A0[:, 128:129]), ["accS"], ["sta"])
                        T.op(DVE, lambda A1=A1: V.reciprocal(out=st[:, 11:12], in_=A1[:, 128:129]), ["accS", "sta"], ["sta"])
                        T.op(DVE, lambda: V.tensor_tensor(out=st[:, 11:12], in0=st[:, 11:12], in1=neglam, op=ALU.mult),
                             ["sta", "stl"], ["sta"])
                        T.op(DVE, lambda A1=A1: V.tensor_scalar(out=t1[:], in0=A1[:, 0:128], scalar1=st[:, 11:12],
                                                                scalar2=None, op0=ALU.mult), ["accS", "sta"], ["t1"])
                        T.op(DVE, lambda A0=A0: V.scalar_tensor_tensor(out=of[:], in0=A0[:, 0:128], scalar=st[:, 10:11],
                                                                       in1=t1[:], op0=ALU.mult, op1=ALU.add),
                             ["accS", "sta", "t1"], ["of"])
                        T.op(DVE, lambda: V.scalar_tensor_tensor(out=junk[:, 0:128], in0=of[:], scalar=1.0, in1=of[:],
                                                                 op0=ALU.mult, op1=ALU.mult, accum_out=st[:, 12:13]),
                             ["of"], ["junk", "stb"])
                        rstd_from_ss(st[:, 12:13], st[:, 13:14], 128.0, ["stb"])
                        T.op(DVE, lambda c=c: V.scalar_tensor_tensor(out=og[:, c, :], in0=of[:], scalar=st[:, 13:14],
                                                                     in1=gn[:, 0:128], op0=ALU.mult, op1=ALU.mult),
                             ["of", "stb", "gn"], ["og"])
                    T.dma(SP, mix_scr[u * 512:(u + 1) * 512, h * 128:(h + 1) * 128].rearrange("(c p) n -> p c n", p=128),
                          og[:], ["og"], ["mixd%d_%d" % (u, h)], "mx2")

                for i in range(len(steps) + LOOK):
                    if i < len(steps):
                        emit_qk(i)
                    if i - LOOK >= 0:
                        emit_pv(i - LOOK)

            attention(0)

            T.dma(POOL, WB[:, :, 0:512], w_p1.rearrange("(c p) n -> p c n", p=128), [], ["WB"], "w0")

            def p1_iter(it):
                ok = lambda t: 0 <= t < NT
                if ok(it + 2):
                    seg_n(it + 2)
                if ok(it + 1):
                    seg_tr(it + 1)
                ns = it % 2
                if ok(it):
                    project(nTs[ns], "nT%d" % ns, WB, "WB", [(0, 0, 0, 512)])
                    k_evac_a(0, 0, 4)
                    v_store(it)
                    k_evac_b(4)
                if ok(it - 1):
                    k_tr((it - 1) % 2, 256, kv_store(it - 1))
                if ok(it):
                    k_evac_c(4, ns)
                if ok(it + 3):
                    seg_load(it + 3)

            for t in range(min(3, NT)):
                seg_load(t)
            for it in range(-2, NT + 1):
                p1_iter(it)
            T.dma(POOL, WG, w_g.rearrange("(c p) n -> p c n", p=128), [], ["WB", "xb0", "xb1", "xb2", "WG"], "w2")
            attention(1)
            T.barrier()
            mix_keys = ["mixg%d" % i for i in range(NO)] + ["mixd%d_%d" % (u, h) for u in range(NU) for h in range(4)]

        with contextlib.ExitStack() as fst:
            def sb(n, s, d):
                return fst.enter_context(nc.sbuf_tensor(n, s, d))

            WU = sb("WU", [128, 8, DFF], BF16)
            WD = sb("WD", [128, NFF, D], BF16)
            gff = sb("gff", [128, D], F32)
            hbs = [sb("hb%d" % i, [128, D], F32) for i in range(2)]
            nbs = [sb("fnb%d" % i, [128, D], BF16) for i in range(2)]
            obs = [sb("ob%d" % i, [128, D], F32) for i in range(2)]
            T.dma(SP, gff[:], g_ffn[:, :], [], ["gff"], "c2")
            with contextlib.ExitStack() as ost:
                def sbo(n, s, d):
                    return ost.enter_context(nc.sbuf_tensor(n, s, d))
                WO = sbo("WO", [128, 8, D], BF16)
                mxs = [sbo("mx%d" % i, [128, D], BF16) for i in range(2)]
                mxT = [sbo("mxT%d" % i, [128, D], BF16) for i in range(2)]
                xos = [sbo("xo%d" % i, [128, D], F32) for i in range(2)]
                T.dma(POOL, WO[:], w_o.rearrange("(c p) n -> p c n", p=128), [], ["WO"], "w1")
                T.dma(POOL, WU[:], w_u.rearrange("(c p) n -> p c n", p=128), [], ["WU"], "w3")
                T.dma(POOL, WD[:], w_d.rearrange("(c p) n -> p c n", p=128), [], ["WD"], "w4")
                for i in range(NO):
                    s = i % 2
                    T.dma(SP, mxs[s][:], mix_scr[i * 128:(i + 1) * 128, :], mix_keys if i < 2 else [], ["mx%d" % s], "m%d" % s)
                    T.dma(SP, xos[s][:], x_own[i * 128:(i + 1) * 128, :], [], ["xo%d" % s], "xo%d" % s)
                    for c in range(8):
                        T.op(PE, lambda c=c, s=s: PEe.transpose(out=pbb[0][:, c * 128:(c + 1) * 128],
                                                                in_=mxs[s][:, c * 128:(c + 1) * 128], identity=identb[:]),
                             ["mx%d" % s, "identb"], ["pb0"], inc=(c == 7))
                    T.op(ACT, lambda s=s: A.copy(out=mxT[s][:], in_=pbb[0][:, 0:1024]), ["pb0"], ["mxT%d" % s])
                    for nb_ in range(2):
                        bank = 1 + nb_
                        for c in range(8):
                            T.op(PE, lambda c=c, s=s, nb_=nb_, bank=bank:
                                 PEe.matmul(pb[bank][:, 0:512], lhsT=mxT[s][:, c * 128:(c + 1) * 128],
                                            rhs=WO[:, c, nb_ * 512:(nb_ + 1) * 512], start=(c == 0), stop=(c == 7)),
                                 ["mxT%d" % s, "WO"], ["pb%d" % bank], inc=(c == 7))
                        T.op(DVE, lambda s=s, nb_=nb_, bank=bank:
                             V.tensor_tensor(out=hbs[s][:, nb_ * 512:(nb_ + 1) * 512], in0=pb[bank][:, 0:512],
                                             in1=xos[s][:, nb_ * 512:(nb_ + 1) * 512], op=ALU.add),
                             ["pb%d" % bank, "xo%d" % s], ["hb%d" % s])
                    T.dma(SP, h_scr[i * 128:(i + 1) * 128, :], hbs[s][:], ["hb%d" % s], ["hscr%d" % i], "hs%d" % s)
                    if dbg:
                        T.dma(SP, dbg_outs["d_h"][i * 128:(i + 1) * 128, :], hbs[s][:], ["hb%d" % s], [], "dbg")
                        T.op(DVE, lambda s=s: V.tensor_copy(out=xos[s][:], in_=mxs[s][:]), ["mx%d" % s, "xo%d" % s], ["xo%d" % s])
                        T.dma(SP, dbg_outs["d_mix"][i * 128:(i + 1) * 128, :], xos[s][:], ["xo%d" % s], [], "dbg")

            T.barrier()
            mT = sb("mT", [128, 8, 512], BF16)
            actT = sb("actT", [128, NFF, 512], BF16)
            ee = sb("ee", [128, 512], F32)
            for gI in range(NU):
                for tt in range(4):
                    i = gI * 4 + tt
                    s = i % 2
                    T.dma(SP, hbs[s][:], h_scr[i * 128:(i + 1) * 128, :], ["hscr%d" % i], ["hb%d" % s], "hl%d" % s)
                    T.op(DVE, lambda s=s: V.scalar_tensor_tensor(out=junk[:], in0=hbs[s][:], scalar=1.0, in1=hbs[s][:],
                                                                 op0=ALU.mult, op1=ALU.mult, accum_out=st[:, 0:1]),
                         ["hb%d" % s], ["junk", "st0"])
                    rstd_from_ss(st[:, 0:1], st[:, 1:2], float(D), ["st0"])
                    T.op(DVE, lambda s=s: V.scalar_tensor_tensor(out=nbs[s][:], in0=hbs[s][:], scalar=st[:, 1:2],
                                                                 in1=gff[:], op0=ALU.mult, op1=ALU.mult),
                         ["hb%d" % s, "st0", "gff"], ["fnb%d" % s])
                    for c in range(8):
                        T.op(PE, lambda c=c, s=s: PEe.transpose(out=pbb[0][:, c * 128:(c + 1) * 128],
                                                                in_=nbs[s][:, c * 128:(c + 1) * 128], identity=identb[:]),
                             ["fnb%d" % s, "identb"], ["pb0"], inc=(c == 7))
                    T.op(ACT, lambda tt=tt: A.copy(out=mT[:, :, tt * 128:(tt + 1) * 128],
                                                   in_=pbb[0][:, 0:1024].rearrange("p (a b) -> p a b", b=128)),
                         ["pb0"], ["mT"])
                for f in range(NFF):
                    gb = 1 + (f % 2)
                    ub = 3 + (f % 2)
                    for c in range(8):
                        T.op(PE, lambda c=c, f=f, gb=gb: PEe.matmul(pb[gb][:, 0:512], lhsT=WG[:, c, f * 128:(f + 1) * 128],
                                                                    rhs=mT[:, c, :], start=(c == 0), stop=(c == 7)),
                             ["WG", "mT"], ["pb%d" % gb], inc=(c == 7))
                    for c in range(8):
                        T.op(PE, lambda c=c, f=f, ub=ub: PEe.matmul(pb[ub][:, 0:512], lhsT=WU[:, c, f * 128:(f + 1) * 128],
                                                                    rhs=mT[:, c, :], start=(c == 0), stop=(c == 7)),
                             ["WU", "mT"], ["pb%d" % ub], inc=(c == 7))
                    T.op(ACT, lambda gb=gb: A.activation(out=ee[:], in_=pb[gb][:, 0:512], func=AF.Silu),
                         ["pb%d" % gb], ["ee"])
                    T.op(DVE, lambda f=f, ub=ub: V.tensor_tensor(out=actT[:, f, :], in0=pb[ub][:, 0:512], in1=ee[:], op=ALU.mult),
                         ["pb%d" % ub, "ee"], ["actT"])
                for tt in range(4):
                    i = gI * 4 + tt
                    s = i % 2
                    T.dma(SP, hbs[s][:], h_scr[i * 128:(i + 1) * 128, :], ["hscr%d" % i], ["hb%d" % s], "hl%d" % s)
                    for nb_ in range(2):
                        bank = 5 + nb_
                        for f in range(NFF):
                            T.op(PE, lambda f=f, tt=tt, nb_=nb_, bank=bank:
                                 PEe.matmul(pb[bank][:, 0:512], lhsT=actT[:, f, tt * 128:(tt + 1) * 128],
                                            rhs=WD[:, f, nb_ * 512:(nb_ + 1) * 512], start=(f == 0), stop=(f == NFF - 1)),
                                 ["actT", "WD"], ["pb%d" % bank], inc=(f == NFF - 1))
                        T.op(DVE, lambda s=s, nb_=nb_, bank=bank:
                             V.tensor_tensor(out=obs[s][:, nb_ * 512:(nb_ + 1) * 512], in0=pb[bank][:, 0:512],
                                             in1=hbs[s][:, nb_ * 512:(nb_ + 1) * 512], op=ALU.add),
                             ["pb%d" % bank, "hb%d" % s], ["ob%d" % s])
                    T.dma(SP, out[i * 128:(i + 1) * 128, :], obs[s][:], ["ob%d" % s], [], "o%d" % s)

        for n in list(T.dsems):
            d = T.dsem(n)
            nc.sync.wait_ge(d.sem, d.count)
    return nc


def _consts():
    c = np.zeros((128, 514), np.float32)
    c[:, 0:128] = np.eye(128, dtype=np.float32)
    s = np.arange(128)[:, None]
    t = np.arange(128)[None, :]
    same = (s // 64) == (t // 64)
    c[:, 128:256] = np.where(same & (s <= t), -1.0 / 16.0, 0.0)
    c[:, 256:384] = np.where(same & (s > t), -1.0 / 16.0, 0.0)
    c[:, 384:512] = np.where(same & (s <= t), 1.0, 0.0)
    c[0:64, 512] = -1.0 / 16.0
    c[64:128, 513] = -1.0 / 16.0
    return c


def _tabs(j):
    tb = np.zeros((128, 260), np.float32)
    kl = np.arange(128, dtype=np.float64)
    for h in range(4):
        for ei in range(64):
            e = ei - 48
            if e <= 4 * j + 3:
                tb[:, h * 64 + ei] = SLOPES[h] * (128.0 * (e - 4 * j) + kl - 256.0)
            else:
                tb[:, h * 64 + ei] = NEG
    tb[:, 256 + j] = 1.0
    return tb


def _masks(j):
    m = np.zeros((128, 16, 4, 128), np.float32)
    k = np.arange(128)[:, None]
    q = np.arange(128)[None, :]
    tri = (k <= q).astype(np.float32)
    for e in range(16):
        for c in range(4):
            cs = 4 * j + c
            if e < cs:
                m[:, e, c, :] = 1.0
            elif e == cs:
                m[:, e, c, :] = tri
    return m.reshape(128, 16 * 512)


def _rep(v, n=128):
    return np.ascontiguousarray(np.broadcast_to(np.asarray(v, np.float32).reshape(1, -1), (n, v.size)))


def prep_inputs(inp, NU=4):
    T_ = NU * 2048
    f = lambda a: np.ascontiguousarray(np.asarray(a, dtype=np.float32))
    x = f(inp["x"])
    w_in = f(inp["w_in"])[0]
    cols = lambda a, b: list(range(a, b))
    dq, dk, dv = 0, 512, 1024
    gq, gk, gv, go, gl = 1536, 1792, 2048, 2560, 3072
    c_p0 = cols(dk, dk + 256) + cols(dv, dv + 256) + cols(gk, gk + 256) + cols(gl, gl + 16) + cols(gv, gv + 512)
    c_p1 = cols(dk + 256, dk + 512) + cols(dv + 256, dv + 512)
    c_ow = cols(dq, dq + 512) + cols(gq, gq + 256) + cols(gk, gk + 256) + cols(gv, gv + 512) + cols(go, go + 512) + cols(gl, gl + 16)
    shared = {
        "w_p0": np.ascontiguousarray(w_in[:, c_p0]),
        "w_p1": np.ascontiguousarray(w_in[:, c_p1]),
        "w_ow": np.ascontiguousarray(w_in[:, c_ow]),
        "w_up": f(inp["w_gla_gate_up"])[0],
        "w_o": f(inp["w_out"])[0],
        "w_g": f(inp["w_ffn_gate"])[0],
        "w_u": f(inp["w_ffn_up"])[0],
        "w_d": f(inp["w_ffn_down"])[0],
        "g_attn": _rep(f(inp["attn_norm_gain"])[0]),
        "g_ffn": _rep(f(inp["ffn_norm_gain"])[0]),
        "qk_rep": np.concatenate([_rep(np.tile(f(inp["q_norm_gain"])[0], 8)),
                                  _rep(np.tile(f(inp["k_norm_gain"])[0], 8))], axis=1),
        "b_rep": _rep(f(inp["b_gla_gate"])[0]),
        "gn_rep": np.concatenate([_rep(f(inp["diff_out_norm_gain"])[0]), _rep(f(inp["gla_out_norm_gain"])[0])], axis=1),
        "lamv": np.concatenate([_rep(f(inp[k])[0]) for k in ("lambda_q1", "lambda_k1", "lambda_q2", "lambda_k2")], axis=1),
        "cst": _consts(),
    }
    in_maps = []
    for core in range(8):
        b, j = core // 4, core % 4
        own_rows = np.concatenate([np.arange((4 * u + j) * 512, (4 * u + j + 1) * 512) for u in range(NU)])
        m = dict(shared)
        m["x_all"] = np.ascontiguousarray(x[b, :T_])
        m["x_own"] = np.ascontiguousarray(x[b, own_rows])
        m["tabs"] = _tabs(j)
        m["msk"] = _masks(j)
        in_maps.append(m)
    return in_maps


def assemble(results, NU=4, B=2, key="out"):
    T_ = NU * 2048
    outp = np.zeros((B, T_, D), np.float32)
    for core in range(8):
        b, j = core // 4, core % 4
        r = np.asarray(results[core][key])
        for u in range(NU):
            outp[b, (4 * u + j) * 512:(4 * u + j + 1) * 512] = r[u * 512:(u + 1) * 512]
    return outp


_NC_CACHE = {}


def kernel(**inputs):
    NU = 4
    if NU not in _NC_CACHE:
        _NC_CACHE[NU] = build(NU)
    nc = _NC_CACHE[NU]
    in_maps = prep_inputs(inputs, NU)
    res = run_bass_kernel_spmd(nc, in_maps, core_ids=list(range(8)))
    return assemble(res.results, NU)
```

```python
import contextlib
import numpy as np
import concourse.bass as bass
import concourse.mybir as mybir
from concourse.bass_utils import run_bass_kernel_spmd

F32 = mybir.dt.float32
BF16 = mybir.dt.bfloat16
AF = mybir.ActivationFunctionType
ALU = mybir.AluOpType
AX = mybir.AxisListType

D = 1024
DFF = 2816
NFF = DFF // 128
EPS = 1e-6
LAMBDA_INIT = 0.8 - 0.6 * 1.0
SLOPES = [2.0 ** (-8.0 * (h + 1.0) / 4.0) for h in range(4)]
NEG = -30000.0
S_MAX = 8.0 * 1.25
UNDERFLOW = 104.0
SAME_ENGINE_SYNC = True
PREFETCH_W = False


class ES:
    def __init__(self, name, eng, sem):
        self.name = name
        self.eng = eng
        self.sem = sem
        self.count = 0
        self.seen = {}


class Trk:
    def __init__(self, nc, stack):
        self.nc = nc
        self.stack = stack
        self.lw = {}
        self.rd = {}
        self.dsems = {}
        mk = lambda n, e: ES(n, e, stack.enter_context(nc.semaphore("s_" + n)))
        self.pe = mk("pe", nc.tensor)
        self.act = mk("act", nc.scalar)
        self.dve = mk("dve", nc.vector)
        self.pool = mk("pool", nc.gpsimd)
        self.sp = mk("sp", nc.sync)

    def dsem(self, name):
        if name not in self.dsems:
            self.dsems[name] = ES("d_" + name, None,
                                  self.stack.enter_context(self.nc.semaphore("d_" + name)))
        return self.dsems[name]

    def _deps(self, reads, writes):
        raw = {}
        other = {}

        def add(d, e, v):
            if d.get(e, 0) < v:
                d[e] = v
        for k in reads:
            if k in self.lw:
                add(raw, *self.lw[k])
        for k in writes:
            if k in self.lw:
                add(other, *self.lw[k])
            for e, v in self.rd.get(k, {}).items():
                add(other, e, v)
        return raw, other

    def _wait(self, E, deps):
        raw, other = deps
        allv = dict(other)
        for e, v in raw.items():
            if allv.get(e, 0) < v:
                allv[e] = v
        for e, v in allv.items():
            if e is E:
                if E.name == "pe" or not SAME_ENGINE_SYNC:
                    continue
                v = raw.get(e, 0)
                if v == 0:
                    continue
            if E.seen.get(e, 0) >= v:
                continue
            assert v <= e.count, (E.name, e.name, v, e.count)
            E.eng.wait_ge(e.sem, v)
            E.seen[e] = v

    def _rec(self, E, val, reads, writes):
        for k in writes:
            self.lw[k] = (E, val)
            self.rd[k] = {}
        for k in reads:
            d = self.rd.setdefault(k, {})
            if d.get(E, 0) < val:
                d[E] = val

    def op(self, E, fn, reads=(), writes=(), inc=True):
        pr = [k for k in reads if k.startswith("pb")]
        if pr:
            reads = [k for k in reads if not k.startswith("pb")]
            writes = list(writes) + [k for k in pr if k not in writes]
        self._wait(E, self._deps(reads, writes))
        inst = fn()
        if inc:
            inst.then_inc(E.sem, 1)
            E.count += 1
            val = E.count
        else:
            val = E.count + 1
        self._rec(E, val, reads, writes)
        return inst

    def barrier(self):
        engs = [self.pe, self.act, self.dve, self.pool, self.sp]
        srcs = engs + list(self.dsems.values())
        for E in engs:
            for e in srcs:
                if e is E or e.count == 0 or E.seen.get(e, 0) >= e.count:
                    continue
                E.eng.wait_ge(e.sem, e.count)
                E.seen[e] = e.count

    def dma(self, Q, out, in_, reads, writes, sem):
        self._wait(Q, self._deps(reads, writes))
        Dm = self.dsem(sem)
        inst = Q.eng.dma_start(out=out, in_=in_)
        inst.then_inc(Dm.sem, 16)
        Dm.count += 16
        self._rec(Dm, Dm.count, reads, writes)
        return inst


def build(NU=4, dbg=False):
    T_ = NU * 2048
    NT = T_ // 128
    NO = NU * 4
    TO = NO * 128
    nc = bass.Bass("TRN2", target_bir_lowering=False)

    def din(name, shape, dt=F32):
        return nc.dram_tensor(name, shape, dt, kind="ExternalInput").ap()

    x_all = din("x_all", [T_, D])
    x_own = din("x_own", [TO, D])
    w_p0 = din("w_p0", [D, 1296])
    w_p1 = din("w_p1", [D, 512])
    w_ow = din("w_ow", [D, 2064])
    w_up = din("w_up", [16, 256])
    w_o = din("w_o", [D, D])
    w_g = din("w_g", [D, DFF])
    w_u = din("w_u", [D, DFF])
    w_d = din("w_d", [DFF, D])
    g_attn = din("g_attn", [128, D])
    g_ffn = din("g_ffn", [128, D])
    qk_rep = din("qk_rep", [128, 1024])
    b_rep = din("b_rep", [128, 256])
    gn_rep = din("gn_rep", [128, 256])
    lamv = din("lamv", [128, 256])
    cst = din("cst", [128, 514])
    tabs = din("tabs", [128, 260])
    msk = din("msk", [128, 16 * 512])
    out = nc.dram_tensor("out", [TO, D], F32, kind="ExternalOutput").ap()
    mix_scr = nc.dram_tensor("mix_scr", [TO, D], BF16, kind="Internal").ap()
    h_scr = nc.dram_tensor("h_scr", [TO, D], F32, kind="Internal").ap()
    dbg_outs = {}
    if dbg:
        dbg_outs["d_mix"] = nc.dram_tensor("d_mix", [TO, D], F32, kind="ExternalOutput").ap()
        dbg_outs["d_h"] = nc.dram_tensor("d_h", [TO, D], F32, kind="ExternalOutput").ap()

    with contextlib.ExitStack() as gst:
        T = Trk(nc, gst)
        PE, ACT, DVE, POOL, SP = T.pe, T.act, T.dve, T.pool, T.sp
        V = nc.vector
        A = nc.scalar
        G = nc.gpsimd
        PEe = nc.tensor

        def sbg(n, s, d):
            return gst.enter_context(nc.sbuf_tensor(n, s, d))

        pb = [gst.enter_context(nc.psum_tensor("pb%d" % i, [128, 512], F32)) for i in range(8)]
        pbb = [p[:].bitcast(BF16) for p in pb]

        cstf = sbg("cstf", [128, 514], F32)
        identb = sbg("identb", [128, 128], BF16)
        st = sbg("st", [128, 32], F32)
        junk = sbg("junk", [128, 1024], BF16)
        arena = sbg("arena", [128, 16512 + 3 * 2048], BF16)
        WB = arena[:, 0:16512].rearrange("p (c n) -> p c n", n=2064)
        xbs = [arena[:, 16512 + k * 2048:16512 + (k + 1) * 2048].bitcast(F32) for k in range(3)]
        WG = arena[:, 0:8 * DFF].rearrange("p (c n) -> p c n", n=DFF)
        arena2 = sbg("arena2", [128, 8192], BF16)
        nbs = [arena2[:, 0:1024], arena2[:, 1024:2048]]
        nTs = [arena2[:, 2048:3072], arena2[:, 3072:4096]]
        kf = arena2[:, 4096:5120].bitcast(F32)
        sq = arena2[:, 5120:6144].bitcast(F32)
        k16s = [arena2[:, 6144:7168], arena2[:, 7168:8192]]
        WO = arena2[:, 0:8192].rearrange("p (c n) -> p c n", n=D)
        T.dma(SP, cstf[:], cst[:, :], [], ["cstf"], "c0")
        T.op(DVE, lambda: V.tensor_copy(out=identb[:], in_=cstf[:, 0:128]), ["cstf"], ["identb"])
        triS = cstf[:, 128:256]
        upS = cstf[:, 256:384]
        chI = cstf[:, 512:514]

        def rstd_from_ss(ss_ap, out_ap, n, keys):
            T.op(ACT, lambda: A.activation(out=out_ap, in_=ss_ap, func=AF.Ln, scale=1.0 / n, bias=EPS),
                 keys, keys)
            T.op(ACT, lambda: A.activation(out=out_ap, in_=out_ap, func=AF.Exp, scale=-0.5),
                 keys, keys)

        def norm_transpose(src_ap, xb, xk, nb, nbk, nTt, nTk, grep, grepk, dq, dsem, bank=5):
            T.dma(dq, xb[:], src_ap, [], [xk], dsem)
            T.op(DVE, lambda: V.scalar_tensor_tensor(out=junk[:], in0=xb[:], scalar=1.0, in1=xb[:],
                                                     op0=ALU.mult, op1=ALU.mult, accum_out=st[:, 0:1]),
                 [xk], ["junk", "st0"])
            rstd_from_ss(st[:, 0:1], st[:, 1:2], float(D), ["st0"])
            T.op(DVE, lambda: V.scalar_tensor_tensor(out=nb[:], in0=xb[:], scalar=st[:, 1:2], in1=grep[:],
                                                     op0=ALU.mult, op1=ALU.mult),
                 [xk, "st0", grepk], [nbk])
            bk = "pb%d" % bank
            for c in range(8):
                T.op(PE, lambda c=c: PEe.transpose(out=pbb[bank][:, c * 128:(c + 1) * 128],
                                                   in_=nb[:, c * 128:(c + 1) * 128], identity=identb[:]),
                     [nbk, "identb"], [bk], inc=(c == 7))
            T.op(ACT, lambda: A.copy(out=nTt[:], in_=pbb[bank][:, 0:1024]), [bk], [nTk])

        def project(nTt, nTk, W, Wk, groups):
            for (bank, pc0, wc0, ncol) in groups:
                bk = "pb%d" % bank
                for c in range(8):
                    T.op(PE, lambda c=c, bank=bank, pc0=pc0, wc0=wc0, ncol=ncol:
                         PEe.matmul(pb[bank][:, pc0:pc0 + ncol], lhsT=nTt[:, c * 128:(c + 1) * 128],
                                    rhs=W[:, c, wc0:wc0 + ncol], start=(c == 0), stop=(c == 7)),
                         [nTk, Wk], [bk], inc=(c == 7))

        with contextlib.ExitStack() as mst:
            def sb(n, s, d):
                return mst.enter_context(nc.sbuf_tensor(n, s, d))

            KT = sb("KT", [128, 2, T_], BF16)
            VE = sb("VE", [128, NT, 2, 130], BF16)
            QT = sb("QT", [128, 4, TO], BF16)
            MK = sb("MK", [128, 16, 512], BF16)
            tabf = sb("tabf", [128, 260], F32)
            gat = sb("gat", [128, D], F32)
            qg = sb("qg", [128, 512], F32)
            brep = sb("brep", [128, 256], F32)
            gn = sb("gn", [128, 256], F32)
            wupb = sb("wupb", [16, 256], BF16)
            maskA = sb("maskA", [128, 128], BF16)
            ssx = sb("ssx", [128, 4], F32)
            rstd_all = sb("rstd_all", [128, NT], F32)
            gkf = [sb("gkf%d" % i, [128, 256], F32) for i in range(2)]
            glr16s = [sb("glr16_%d" % i, [128, 16], BF16) for i in range(2)]
            glrT = sb("glrT", [16, 128], BF16)
            zb = sb("zb", [128, 256], F32)
            lg = sb("lg", [128, 256], F32)
            eb = sb("eb", [128, 256], F32)
            enb = sb("enb", [128, 256], F32)
            ec = sb("ec", [128, 256], F32)
            dec = sb("dec", [128, 4], F32)
            qt16 = sb("qt16", [128, 256], BF16)
            kt16 = sb("kt16", [128, 256], BF16)
            kh16 = sb("kh16", [128, 256], BF16)
            vb16s = [sb("vb16_%d" % i, [128, 512], BF16) for i in range(2)]
            qkT = sb("qkT", [128, 512], BF16)
            AT = sb("AT", [128, 512], BF16)
            S = sb("S", [128, 256], F32)
            Sb = [sb("Sb%d" % i, [128, 256], BF16) for i in range(2)]
            snap = sb("snap", [128, NU, 256], F32)
            sgt = sb("sgt", [128, 512], F32)
            gg = sb("gg", [128, 512], F32)
            ogl = sb("ogl", [128, 512], BF16)
            PTs = [sb("PT%d" % i, [128, 512], BF16) for i in range(6)]
            accS = sb("accS", [128, 3, 390], F32)
            t1 = sb("t1", [128, 128], F32)
            of = sb("of", [128, 128], F32)
            og = sb("og", [128, 4, 128], BF16)

            T.dma(SP, tabf[:], tabs[:, :], [], ["tabf"], "c1")
            T.dma(SP, gat[:], g_attn[:, :], [], ["gat"], "c2")
            T.dma(SP, kf[:], qk_rep[:, 0:512], [], ["kf"], "c3")
            T.dma(SP, sq[:], qk_rep[:, 512:1024], [], ["sq"], "c3b")
            T.dma(SP, brep[:], b_rep[:, :], [], ["brep"], "c4")
            T.dma(SP, gn[:], gn_rep[:, :], [], ["gn"], "c5")
            lam = zb
            T.dma(SP, lam[:], lamv[:, :], [], ["zb"], "c6")
            T.dma(POOL, wupb[:], w_up[:, :], [], ["wupb"], "c7")
            T.dma(POOL, maskA[:], cst[:, 384:512], [], ["maskA"], "c8")
            T.dma(POOL, WB[:, :, 0:1296], w_p0.rearrange("(c p) n -> p c n", p=128), [], ["WB"], "w0")
            for e4 in range(4):
                T.dma(POOL, MK[:, e4 * 4:(e4 + 1) * 4, :],
                      msk[:, e4 * 2048:(e4 + 1) * 2048].rearrange("p (a b) -> p a b", b=512),
                      [], ["MK"], "c9")
            T.op(DVE, lambda: V.scalar_tensor_tensor(out=qg[:, 0:512], in0=kf[:, 0:512], scalar=0.125,
                                                     in1=sq[:, 0:512], op0=ALU.mult, op1=ALU.mult),
                 ["kf", "sq"], ["qg"])
            T.op(DVE, lambda: V.tensor_scalar(out=gn[:, 0:128], in0=gn[:, 0:128], scalar1=1.0 - LAMBDA_INIT,
                                              scalar2=None, op0=ALU.mult), ["gn"], ["gn"])
            T.op(DVE, lambda: V.scalar_tensor_tensor(out=junk[:, 0:64], in0=lam[:, 0:64], scalar=1.0,
                                                     in1=lam[:, 64:128], op0=ALU.mult, op1=ALU.mult,
                                                     accum_out=st[:, 4:5]), ["zb"], ["junk", "stl"])
            T.op(DVE, lambda: V.scalar_tensor_tensor(out=junk[:, 0:64], in0=lam[:, 128:192], scalar=1.0,
                                                     in1=lam[:, 192:256], op0=ALU.mult, op1=ALU.mult,
                                                     accum_out=st[:, 5:6]), ["zb", "stl"], ["junk", "stl"])
            T.op(ACT, lambda: A.activation(out=st[:, 6:8], in_=st[:, 4:6], func=AF.Exp), ["stl"], ["stl"])
            T.op(DVE, lambda: V.tensor_tensor(out=st[:, 8:9], in0=st[:, 7:8], in1=st[:, 6:7], op=ALU.subtract),
                 ["stl"], ["stl"])
            T.op(DVE, lambda: V.tensor_scalar(out=st[:, 8:9], in0=st[:, 8:9], scalar1=-LAMBDA_INIT,
                                              scalar2=None, op0=ALU.add), ["stl"], ["stl"])
            neglam = st[:, 8:9]
            T.op(POOL, lambda: G.memset(VE[:, :, :, 128:130], 1.0), [], ["VE"])
            T.op(DVE, lambda: V.memset(S[:], 0.0), [], ["S"])

            def k_evac_a(bank, c0, ngrp):
                n = 64 * ngrp
                T.op(ACT, lambda: A.copy(out=kf[:, 0:n], in_=pb[bank][:, c0:c0 + n]), ["pb%d" % bank], ["kf"])

            def k_evac_b(ngrp):
                n = 64 * ngrp
                T.op(DVE, lambda: V.tensor_tensor(out=sq[:, 0:n], in0=kf[:, 0:n], in1=kf[:, 0:n], op=ALU.mult),
                     ["kf"], ["sq"])
                T.op(DVE, lambda: V.tensor_reduce(out=st[:, 16:16 + ngrp],
                                                  in_=sq[:, 0:n].rearrange("p (a b) -> p a b", b=64),
                                                  axis=AX.X, op=ALU.add), ["sq"], ["stk"])
                rstd_from_ss(st[:, 16:16 + ngrp], st[:, 24:24 + ngrp], 64.0, ["stk"])

            def k_evac_c(ngrp, ks, gain=None):
                n = 64 * ngrp
                kk = "k16_%d" % ks
                bc = bass.AP(st[:].tensor, st[:, 24:25].offset, [list(st[:].ap[0]), [1, ngrp], [0, 64]])
                if gain is None:
                    T.op(DVE, lambda: V.tensor_tensor(out=k16s[ks][:, 0:n].rearrange("p (a b) -> p a b", b=64),
                                                      in0=kf[:, 0:n].rearrange("p (a b) -> p a b", b=64),
                                                      in1=bc, op=ALU.mult), ["kf", "stk"], [kk])
                else:
                    T.op(DVE, lambda: V.tensor_tensor(out=sq[:, 0:n].rearrange("p (a b) -> p a b", b=64),
                                                      in0=kf[:, 0:n].rearrange("p (a b) -> p a b", b=64),
                                                      in1=bc, op=ALU.mult), ["kf", "stk", "sq"], ["sq"])
                    T.op(DVE, lambda: V.tensor_tensor(out=k16s[ks][:, 0:n], in0=sq[:, 0:n], in1=gain,
                                                      op=ALU.mult), ["sq", "qg"], [kk])

            def k_evac(bank, c0, ngrp, ks, gain=None):
                k_evac_a(bank, c0, ngrp)
                k_evac_b(ngrp)
                k_evac_c(ngrp, ks, gain)

            def k_tr(ks, n, dst_fn):
                nh = n // 128
                for hh in range(nh):
                    T.op(PE, lambda hh=hh: PEe.transpose(out=pbb[4][:, hh * 128:(hh + 1) * 128],
                                                         in_=k16s[ks][:, hh * 128:(hh + 1) * 128], identity=identb[:]),
                         ["k16_%d" % ks, "identb"], ["pb4"], inc=(hh == nh - 1))
                dst_fn(pbb[4][:, 0:n].rearrange("p (a b) -> p a b", b=128))

            def gla_segs(own, i_own, gk, gkk, gq, gqk, vb, vbk, g16, g16k, psS):
                def s_a():
                    T.op(PE, lambda: PEe.transpose(out=pbb[6][0:16, 64:192], in_=g16[:], identity=identb[:]),
                         [g16k, "identb"], ["pb6"])
                    T.op(ACT, lambda: A.copy(out=glrT[:], in_=pbb[6][0:16, 64:192]), ["pb6"], ["glrT"])

                def s_b():
                    T.op(PE, lambda: PEe.matmul(pb[6][:, 128:384], lhsT=glrT[:], rhs=wupb[:], start=True, stop=True),
                         ["glrT", "wupb"], ["pb6"])
                    T.op(DVE, lambda: V.tensor_tensor(out=zb[:], in0=pb[6][:, 128:384], in1=brep[:], op=ALU.add),
                         ["pb6", "brep"], ["zb"])
                    T.op(ACT, lambda: A.activation(out=lg[:], in_=zb[:], func=AF.Exp, scale=-1.0), ["zb"], ["lg"])
                    T.op(ACT, lambda: A.activation(out=lg[:], in_=lg[:], func=AF.Ln, bias=1.0), ["lg"], ["lg"])

                def s_c():
                    if own:
                        T.op(PE, lambda: PEe.matmul(pb[7][:, 0:256], lhsT=triS, rhs=lg[:], start=True, stop=True),
                             ["cstf", "lg"], ["pb7"], inc=False)
                    T.op(PE, lambda: PEe.matmul(pb[7][:, 256:512], lhsT=upS, rhs=lg[:], start=True, stop=True),
                         ["cstf", "lg"], ["pb7"])
                    for p in range(2):
                        T.op(PE, lambda p=p: PEe.matmul(pb[6][:, 96 + 2 * p:98 + 2 * p], lhsT=lg[:, p * 128:(p + 1) * 128],
                                                        rhs=chI, start=True, stop=True),
                             ["lg", "cstf"], ["pb6"], inc=(p == 1))
                    T.op(ACT, lambda: A.activation(out=ec[:], in_=pb[7][:, 256:512], func=AF.Exp), ["pb7"], ["ec"])
                    if own:
                        T.op(ACT, lambda: A.activation(out=eb[:], in_=pb[7][:, 0:256], func=AF.Exp), ["pb7"], ["eb"])
                        T.op(ACT, lambda: A.activation(out=enb[:], in_=pb[7][:, 0:256], func=AF.Exp, scale=-1.0),
                             ["pb7"], ["enb"])
                    T.op(ACT, lambda: A.activation(out=dec[:], in_=pb[6][:, 96:100], func=AF.Exp), ["pb6"], ["dec"])
                    T.op(DVE, lambda: V.tensor_tensor(out=kh16[:], in0=gk, in1=ec[:], op=ALU.mult),
                         [gkk, "ec"], ["kh16"])
                    if own:
                        T.op(DVE, lambda: V.scalar_tensor_tensor(out=qt16[:], in0=gq, scalar=0.125,
                                                                 in1=eb[:], op0=ALU.mult, op1=ALU.mult),
                             [gqk, "eb"], ["qt16"])
                        T.op(DVE, lambda: V.tensor_tensor(out=kt16[:], in0=gk, in1=enb[:], op=ALU.mult),
                             [gkk, "enb"], ["kt16"])
                        for p in range(2):
                            T.op(PE, lambda p=p: PEe.transpose(out=pbb[4][:, 512 + p * 128:512 + (p + 1) * 128],
                                                               in_=qt16[:, p * 128:(p + 1) * 128], identity=identb[:]),
                                 ["qt16", "identb"], ["pb4"], inc=False)
                        for p in range(2):
                            T.op(PE, lambda p=p: PEe.transpose(out=pbb[4][:, 768 + p * 128:768 + (p + 1) * 128],
                                                               in_=kt16[:, p * 128:(p + 1) * 128], identity=identb[:]),
                                 ["kt16", "identb"], ["pb4"], inc=(p == 1))
                        T.op(ACT, lambda: A.copy(out=qkT[:], in_=pbb[4][:, 512:1024]), ["pb4"], ["qkT"])

                def s_c2():
                    for hh in range(2):
                        bank = 0 if hh == 0 else 3
                        for p in range(2):
                            T.op(PE, lambda hh=hh, p=p, bank=bank:
                                 PEe.matmul(pb[bank][:, p * 128:(p + 1) * 128],
                                            lhsT=qkT[hh * 64:(hh + 1) * 64, 256 + p * 128:256 + (p + 1) * 128],
                                            rhs=qkT[hh * 64:(hh + 1) * 64, p * 128:(p + 1) * 128],
                                            start=True, stop=True),
                                 ["qkT"], ["pb%d" % bank], inc=(p == 1))
                    mA = bass.AP(maskA[:].tensor, maskA[:].offset, [list(maskA[:].ap[0]), [0, 2], [1, 128]])
                    for hh in range(2):
                        bank = 0 if hh == 0 else 3
                        outv = bass.AP(AT[:].tensor, AT[:, hh * 128:hh * 128 + 1].offset,
                                       [list(AT[:].ap[0]), [256, 2], [1, 128]])
                        T.op(DVE, lambda bank=bank, outv=outv: V.tensor_tensor(
                            out=outv, in0=pb[bank][:, 0:256].rearrange("p (a b) -> p a b", b=128), in1=mA, op=ALU.mult),
                            ["pb%d" % bank, "maskA"], ["AT"])

                def s_d(ch):
                    def f():
                        if own:
                            T.op(DVE, lambda: V.tensor_copy(out=Sb[ch][:], in_=S[:]), ["S"], ["Sb%d" % ch])
                        sbank, scol = psS[ch]
                        sk = "pb%d" % sbank
                        for p in range(2):
                            for hh in range(2):
                                h = 2 * p + hh
                                T.op(PE, lambda p=p, hh=hh, h=h:
                                     PEe.matmul(pb[sbank][hh * 64:(hh + 1) * 64, scol + p * 128:scol + (p + 1) * 128],
                                                lhsT=kh16[ch * 64:(ch + 1) * 64, h * 64:(h + 1) * 64],
                                                rhs=vb[ch * 64:(ch + 1) * 64, h * 128:(h + 1) * 128],
                                                start=True, stop=True),
                                     ["kh16", vbk], [sk], inc=(p == 1 and hh == 1))
                        for p in range(2):
                            T.op(DVE, lambda p=p:
                                 V.scalar_tensor_tensor(out=S[:, p * 128:(p + 1) * 128], in0=S[:, p * 128:(p + 1) * 128],
                                                        scalar=dec[:, 2 * p + ch:2 * p + ch + 1],
                                                        in1=pb[sbank][:, scol + p * 128:scol + (p + 1) * 128],
                                                        op0=ALU.mult, op1=ALU.add),
                                 ["S", "dec", sk], ["S"])
                    return f

                def s_o():
                    for hh in range(2):
                        bank = 0 if hh == 0 else 3
                        bk = "pb%d" % bank
                        for p in range(2):
                            h = 2 * p + hh
                            oc = 256 + p * 128
                            T.op(PE, lambda h=h, bank=bank, oc=oc:
                                 PEe.matmul(pb[bank][:, oc:oc + 128], lhsT=AT[:, h * 128:(h + 1) * 128],
                                            rhs=vb[:, h * 128:(h + 1) * 128], start=True, stop=False),
                                 ["AT", vbk], [bk], inc=False)
                            for ch in range(2):
                                T.op(PE, lambda h=h, hh=hh, p=p, ch=ch, bank=bank, oc=oc:
                                     PEe.matmul(pb[bank][ch * 64:(ch + 1) * 64, oc:oc + 128],
                                                lhsT=qkT[hh * 64:(hh + 1) * 64, p * 128 + ch * 64:p * 128 + (ch + 1) * 64],
                                                rhs=Sb[ch][hh * 64:(hh + 1) * 64, p * 128:(p + 1) * 128],
                                                start=False, stop=True),
                                     ["qkT", "Sb%d" % ch], [bk], inc=(ch == 1 and p == 1))
                    for hh in range(2):
                        bank = 0 if hh == 0 else 3
                        bk = "pb%d" % bank
                        T.op(ACT, lambda bank=bank: A.copy(out=kf[:, 0:256], in_=pb[bank][:, 256:512]), [bk], ["kf"])
                        T.op(DVE, lambda: V.tensor_tensor(out=sq[:, 0:256], in0=kf[:, 0:256], in1=kf[:, 0:256],
                                                          op=ALU.mult), ["kf"], ["sq"])
                        T.op(DVE, lambda: V.tensor_reduce(out=st[:, 16:18],
                                                          in_=sq[:, 0:256].rearrange("p (a b) -> p a b", b=128),
                                                          axis=AX.X, op=ALU.add), ["sq"], ["stk"])
                        rstd_from_ss(st[:, 16:18], st[:, 24:26], 128.0, ["stk"])
                        bc = bass.AP(st[:].tensor, st[:, 24:25].offset, [list(st[:].ap[0]), [1, 2], [0, 128]])
                        ggv = bass.AP(gg[:].tensor, gg[:, hh * 128:hh * 128 + 1].offset,
                                      [list(gg[:].ap[0]), [256, 2], [1, 128]])
                        oglv = bass.AP(ogl[:].tensor, ogl[:, hh * 128:hh * 128 + 1].offset,
                                       [list(ogl[:].ap[0]), [256, 2], [1, 128]])
                        T.op(DVE, lambda bc=bc: V.tensor_tensor(out=sq[:, 256:512].rearrange("p (a b) -> p a b", b=128),
                                                                in0=kf[:, 0:256].rearrange("p (a b) -> p a b", b=128),
                                                                in1=bc, op=ALU.mult), ["kf", "stk"], ["sq2"])
                        T.op(DVE, lambda ggv=ggv, oglv=oglv: V.tensor_tensor(
                            out=oglv, in0=sq[:, 256:512].rearrange("p (a b) -> p a b", b=128), in1=ggv, op=ALU.mult),
                            ["sq2", "gg"], ["ogl"])
                    T.dma(SP, mix_scr[i_own * 128:(i_own + 1) * 128, 512:1024], ogl[:], ["ogl"], ["mixg%d" % i_own], "mx")

                if own:
                    return [s_a, s_b, s_c, s_c2, s_d(0), s_d(1), s_o]
                return [s_a, s_b, s_c, s_d(0), s_d(1)]

            def seg_load(t):
                sl = t % 3
                T.dma(SP, xbs[sl][:], x_all[t * 128:(t + 1) * 128, :], [], ["xb%d" % sl], "x%d" % sl)

            def seg_stats(t):
                sl = t % 3
                T.op(DVE, lambda: V.scalar_tensor_tensor(out=junk[:], in0=xbs[sl][:], scalar=1.0, in1=xbs[sl][:],
                                                         op0=ALU.mult, op1=ALU.mult, accum_out=ssx[:, sl:sl + 1]),
                     ["xb%d" % sl], ["junk", "ssx%d" % sl])
                T.op(ACT, lambda: A.activation(out=rstd_all[:, t:t + 1], in_=ssx[:, sl:sl + 1], func=AF.Ln,
                                               scale=1.0 / D, bias=EPS), ["ssx%d" % sl], ["rs%d" % t])
                T.op(ACT, lambda: A.activation(out=rstd_all[:, t:t + 1], in_=rstd_all[:, t:t + 1], func=AF.Exp, scale=-0.5),
                     ["rs%d" % t], ["rs%d" % t])

            def seg_n(t):
                sl = t % 3
                ns = t % 2
                T.op(DVE, lambda: V.scalar_tensor_tensor(out=nbs[ns][:], in0=xbs[sl][:], scalar=rstd_all[:, t:t + 1],
                                                         in1=gat[:], op0=ALU.mult, op1=ALU.mult),
                     ["xb%d" % sl, "rs%d" % t, "gat"], ["nb%d" % ns])

            def seg_tr(t):
                ns = t % 2
                for c in range(8):
                    T.op(PE, lambda c=c: PEe.transpose(out=pbb[3][:, c * 128:(c + 1) * 128],
                                                       in_=nbs[ns][:, c * 128:(c + 1) * 128], identity=identb[:]),
                         ["nb%d" % ns, "identb"], ["pb3"], inc=(c == 7))
                T.op(ACT, lambda: A.copy(out=nTs[ns][:], in_=pbb[3][:, 0:1024]), ["pb3"], ["nT%d" % ns])

            def run_pipe(n_tiles, early_fn, main_fn, late_fn):
                seg_load(0)
                for it in range(-1, n_tiles + 1):
                    early = early_fn(it + 1) if 0 <= it + 1 < n_tiles else []
                    main = main_fn(it) if 0 <= it < n_tiles else []
                    late = late_fn(it - 1) if 0 <= it - 1 < n_tiles else []
                    for k in range(max(len(early), len(main), len(late))):
                        if k < len(main):
                            main[k]()
                        if k < len(late):
                            late[k]()
                        if k < len(early):
                            early[k]()

            def kv_store(t):
                def dst(src3):
                    T.op(ACT, lambda: A.copy(out=KT[:, :, t * 128:(t + 1) * 128], in_=src3), ["pb4"], ["KT"])
                return dst

            def v_store(t):
                T.op(DVE, lambda: V.tensor_copy(out=VE[:, t, :, 0:128],
                                                in_=pb[0][:, 256:512].rearrange("p (a b) -> p a b", b=128)),
                     ["pb0"], ["VE"])

            def p0_iter(it):
                ok = lambda t: 0 <= t < NT
                late = []
                t1 = it - 1
                if ok(t1):
                    n1 = t1 % 2
                    late = gla_segs(False, None, gkf[n1][:], "gkf%d" % n1, None, None, vb16s[n1], "vb16_%d" % n1,
                                    glr16s[n1], "glr16_%d" % n1, ((5, 0), (7, 0)))
                    late[0]()
                if ok(it + 1):
                    seg_tr(it + 1)
                ns = it % 2
                if ok(it):
                    project(nTs[ns], "nT%d" % ns, WB, "WB", [(0, 0, 0, 512)])
                    k_evac_a(0, 0, 4)
                    v_store(it)
                    k_evac_b(4)
                if ok(t1):
                    late[1]()
                if ok(it):
                    project(nTs[ns], "nT%d" % ns, WB, "WB", [(1, 0, 512, 272)])
                    T.op(DVE, lambda: V.tensor_copy(out=gkf[ns][:], in_=pb[1][:, 0:256]), ["pb1"], ["gkf%d" % ns])
                    T.op(DVE, lambda: V.tensor_copy(out=glr16s[ns][:], in_=pb[1][:, 256:272]), ["pb1"], ["glr16_%d" % ns])
                    k_evac_c(4, ns)
                if ok(it + 2):
                    seg_stats(it + 2)
                if ok(t1):
                    late[2]()
                    k_tr(t1 % 2, 256, kv_store(t1))
                if ok(it + 2):
                    seg_n(it + 2)
                if ok(it):
                    project(nTs[ns], "nT%d" % ns, WB, "WB", [(2, 0, 784, 512)])
                    T.op(ACT, lambda: A.copy(out=vb16s[ns][:], in_=pb[2][:, :]), ["pb2"], ["vb16_%d" % ns])
                if ok(t1):
                    if t1 % 16 % 4 == 0:
                        u = t1 // 16
                        r = (t1 % 16) // 4
                        if r == 0:
                            T.op(DVE, lambda: V.tensor_scalar(out=snap[:, u, :], in0=S[:],
                                                              scalar1=tabf[:, 256 + r:257 + r], scalar2=None,
                                                              op0=ALU.mult), ["S", "tabf"], ["snap"])
                        else:
                            T.op(DVE, lambda: V.scalar_tensor_tensor(out=snap[:, u, :], in0=S[:],
                                                                     scalar=tabf[:, 256 + r:257 + r],
                                                                     in1=snap[:, u, :], op0=ALU.mult, op1=ALU.add),
                                 ["S", "tabf", "snap"], ["snap"])
                    late[3]()
                    late[4]()
                if ok(it + 3):
                    seg_load(it + 3)

            for t in range(min(3, NT)):
                seg_load(t)
            for it in range(-2, NT + 1):
                p0_iter(it)

            T.dma(POOL, WB[:, :, :], w_ow.rearrange("(c p) n -> p c n", p=128), [], ["WB"], "w0")

            def q_store(i):
                def dst(src3):
                    T.op(ACT, lambda: A.copy(out=QT[:, :, i * 128:(i + 1) * 128], in_=src3), ["pb4"], ["QT"])
                return dst

            gnb = bass.AP(gn[:].tensor, gn[:, 128:129].offset, [list(gn[:].ap[0]), [0, 4], [1, 128]])

            def o_load(i):
                sl = i % 3
                T.dma(SP, xbs[sl][:], x_own[i * 128:(i + 1) * 128, :], [], ["xb%d" % sl], "x%d" % sl)

            def o_stats(i):
                sl = i % 3
                T.op(DVE, lambda: V.scalar_tensor_tensor(out=junk[:], in0=xbs[sl][:], scalar=1.0, in1=xbs[sl][:],
                                                         op0=ALU.mult, op1=ALU.mult, accum_out=ssx[:, sl:sl + 1]),
                     ["xb%d" % sl], ["junk", "ssx%d" % sl])
                rstd_from_ss(ssx[:, sl:sl + 1], ssx[:, 3:4] if False else st[:, 1 + (i % 2):2 + (i % 2)], float(D), ["ssx%d" % sl, "sto%d" % (i % 2)])

            def o_n(i):
                sl = i % 3
                ns = i % 2
                T.op(DVE, lambda: V.scalar_tensor_tensor(out=nbs[ns][:], in0=xbs[sl][:], scalar=st[:, 1 + ns:2 + ns],
                                                         in1=gat[:], op0=ALU.mult, op1=ALU.mult),
                     ["xb%d" % sl, "sto%d" % ns, "gat"], ["nb%d" % ns])

            def o_tr(i):
                ns = i % 2
                for c in range(8):
                    T.op(PE, lambda c=c: PEe.transpose(out=pbb[5][:, c * 128:(c + 1) * 128],
                                                       in_=nbs[ns][:, c * 128:(c + 1) * 128], identity=identb[:]),
                         ["nb%d" % ns, "identb"], ["pb5"], inc=(c == 7))
                T.op(ACT, lambda: A.copy(out=nTs[ns][:], in_=pbb[5][:, 0:1024]), ["pb5"], ["nT%d" % ns])

            for i in range(min(2, NO)):
                o_load(i)
            o_stats(0)
            o_n(0)
            o_tr(0)
            for i in range(NO):
                s = i % 2
                if i + 2 < NO:
                    o_load(i + 2)
                if i % 4 == 0:
                    T.op(DVE, lambda i=i: V.tensor_copy(out=S[:], in_=snap[:, i // 4, :]), ["snap"], ["S"])
                project(nTs[s], "nT%d" % s, WB, "WB", [(0, 0, 0, 512), (3, 0, 1536, 512)])
                k_evac_a(0, 0, 8)
                T.op(ACT, lambda: A.activation(out=sgt[:], in_=pb[3][:, :], func=AF.Exp, scale=-1.0), ["pb3"], ["sgt"])
                T.op(ACT, lambda: A.activation(out=sgt[:], in_=sgt[:], func=AF.Ln, bias=1.0), ["sgt"], ["sgt"])
                T.op(ACT, lambda: A.activation(out=sgt[:], in_=sgt[:], func=AF.Exp, scale=-1.0), ["sgt"], ["sgt"])
                project(nTs[s], "nT%d" % s, WB, "WB", [(1, 0, 512, 512), (2, 0, 1024, 512), (6, 0, 2048, 16)])
                k_evac_b(8)
                T.op(DVE, lambda: V.tensor_tensor(out=gg[:], in0=pb[3][:, :], in1=sgt[:], op=ALU.mult),
                     ["pb3", "sgt"], ["gg"])
                T.op(DVE, lambda: V.tensor_copy(out=glr16s[0][:], in_=pb[6][:, 0:16]), ["pb6"], ["glr16_0"])
                T.op(ACT, lambda: A.copy(out=vb16s[0][:], in_=pb[2][:, :]), ["pb2"], ["vb16_0"])
                k_evac_c(8, 0, gain=qg[:, 0:512])
                T.op(DVE, lambda: V.tensor_tensor(out=gg[:].rearrange("p (a b) -> p a b", b=128),
                                                  in0=gg[:].rearrange("p (a b) -> p a b", b=128), in1=gnb, op=ALU.mult),
                     ["gg", "gn"], ["gg"])
                segs = gla_segs(True, i, pb[1][:, 256:512], "pb1", pb[1][:, 0:256], "pb1", vb16s[0], "vb16_0",
                                glr16s[0], "glr16_0", ((2, 0), (2, 256)))
                segs[0]()
                if i + 1 < NO:
                    o_stats(i + 1)
                k_tr(0, 512, q_store(i))
                segs[1]()
                if i + 1 < NO:
                    o_n(i + 1)
                segs[2]()
                if i + 1 < NO:
                    o_tr(i + 1)
                for sg in segs[3:]:
                    sg()

            def acc_ap(a):
                bank = 5 + a // 3
                col = (a % 3) * 130
                return bank, col

            def attention(hp):
                def kb_lo(h, u):
                    n_h = int(np.ceil(1.0 + (UNDERFLOW + 2.0 * S_MAX) / (128.0 * SLOPES[h])))
                    return max(0, 16 * u - (n_h - 1))
                steps = [(u, hl, kb) for u in range(NU) for hl in range(2)
                         for kb in range(kb_lo(2 * hp + hl, u), 16 * u + 16)]
                LOOK = 2

                def emit_qk(idx):
                    u, hl, kb = steps[idx]
                    h = 2 * hp + hl
                    e = kb - 16 * u
                    ei = e + 48
                    for m in range(2):
                        sbank = 1 + (2 * idx + m) % 4
                        T.op(PE, lambda m=m, sbank=sbank:
                             PEe.matmul(pb[sbank][:, 0:512],
                                        lhsT=KT[m * 64:(m + 1) * 64, hl, kb * 128:(kb + 1) * 128],
                                        rhs=QT[m * 64:(m + 1) * 64, h, u * 512:(u + 1) * 512],
                                        start=True, stop=True),
                             ["KT", "QT"], ["pb%d" % sbank])
                    for m in range(2):
                        sbank = 1 + (2 * idx + m) % 4
                        slot = (2 * idx + m) % 6
                        T.op(ACT, lambda sbank=sbank, slot=slot:
                             A.activation(out=PTs[slot][:], in_=pb[sbank][:, 0:512], func=AF.Exp,
                                          bias=tabf[:, h * 64 + ei:h * 64 + ei + 1]),
                             ["pb%d" % sbank, "tabf"], ["PT%d" % slot])
                        if e >= 0:
                            T.op(DVE, lambda slot=slot:
                                 V.tensor_tensor(out=PTs[slot][:], in0=PTs[slot][:], in1=MK[:, e, :], op=ALU.mult),
                                 ["PT%d" % slot, "MK"], ["PT%d" % slot])

                def emit_pv(idx):
                    u, hl, kb = steps[idx]
                    h = 2 * hp + hl
                    nkb = 16 * u + 16
                    for m in range(2):
                        slot = (2 * idx + m) % 6
                        for c in range(4):
                            bank, col = acc_ap(c * 2 + m)
                            first = (kb == kb_lo(h, u) and (c * 2 + m) in (0, 4, 6))
                            T.op(PE, lambda c=c, bank=bank, col=col, slot=slot, first=first:
                                 PEe.matmul(pb[bank][:, col:col + 130],
                                            lhsT=PTs[slot][:, c * 128:(c + 1) * 128],
                                            rhs=VE[:, kb, hl, :], start=first, stop=(kb == nkb - 1),
                                            skip_group_check=True),
                                 ["PT%d" % slot, "VE"], ["pb%d" % bank], inc=(c == 3 and m == 1))
                    if kb != nkb - 1:
                        return
                    for bi in range(3):
                        T.op(DVE, lambda bi=bi: V.tensor_copy(out=accS[:, bi, :], in_=pb[5 + bi][:, 0:390]),
                             ["pb%d" % (5 + bi)], ["accS"])
                    for c in range(4):
                        a0 = c * 2
                        a1 = c * 2 + 1
                        A0 = accS[:, a0 // 3, (a0 % 3) * 130:(a0 % 3) * 130 + 130]
                        A1 = accS[:, a1 // 3, (a1 % 3) * 130:(a1 % 3) * 130 + 130]
                        T.op(DVE, lambda A0=A0: V.reciprocal(out=st[:, 10:11], in_=A0[:, 128:129]), ["accS"], ["sta"])
                        T.op(DVE, lambda A1=A1: V.reciprocal(out=st[:, 11:12], in_=A1[:, 128:129]), ["accS", "sta"], ["sta"])
                        T.op(DVE, lambda: V.tensor_tensor(out=st[:, 11:12], in0=st[:, 11:12], in1=neglam, op=ALU.mult),
                             ["sta", "stl"], ["sta"])
                        T.op(DVE, lambda A1=A1: V.tensor_scalar(out=t1[:], in0=A1[:, 0:128], scalar1=st[:, 11:12],
                                                                scalar2=None, op0=ALU.mult), ["accS", "sta"], ["t1"])
                        T.op(DVE, lambda A0=A0: V.scalar_tensor_tensor(out=of[:], in0=A0[:, 0:128], scalar=st[:, 10:11],
                                                                       in1=t1[:], op0=ALU.mult, op1=ALU.add),
                             ["accS", "sta", "t1"], ["of"])
                        T.op(DVE, lambda: V.scalar_tensor_tensor(out=junk[:, 0:128], in0=of[:], scalar=1.0, in1=of[:],
                                                                 op0=ALU.mult, op1=ALU.mult, accum_out=st[:, 12:13]),
                             ["of"], ["junk", "stb"])
                        rstd_from_ss(st[:, 12:13], st[:, 13:14], 128.0, ["stb"])
                        T.op(DVE, lambda c=c: V.scalar_tensor_tensor(out=og[:, c, :], in0=of[:], scalar=st[:, 13:14],
                                                                     in1=gn[:, 0:128], op0=ALU.mult, op1=ALU.mult),
                             ["of", "stb", "gn"], ["og"])
                    T.dma(SP, mix_scr[u * 512:(u + 1) * 512, h * 128:(h + 1) * 128].rearrange("(c p) n -> p c n", p=128),
                          og[:], ["og"], ["mixd%d_%d" % (u, h)], "mx2")

                for i in range(len(steps) + LOOK):
                    if i < len(steps):
                        emit_qk(i)
                    if i - LOOK >= 0:
                        emit_pv(i - LOOK)

            attention(0)

            T.dma(POOL, WB[:, :, 0:512], w_p1.rearrange("(c p) n -> p c n", p=128), [], ["WB"], "w0")

            def p1_iter(it):
                ok = lambda t: 0 <= t < NT
                if ok(it + 2):
                    seg_n(it + 2)
                if ok(it + 1):
                    seg_tr(it + 1)
                ns = it % 2
                if ok(it):
                    project(nTs[ns], "nT%d" % ns, WB, "WB", [(0, 0, 0, 512)])
                    k_evac_a(0, 0, 4)
                    v_store(it)
                    k_evac_b(4)
                if ok(it - 1):
                    k_tr((it - 1) % 2, 256, kv_store(it - 1))
                if ok(it):
                    k_evac_c(4, ns)
                if ok(it + 3):
                    seg_load(it + 3)

            for t in range(min(3, NT)):
                seg_load(t)
            for it in range(-2, NT + 1):
                p1_iter(it)
            if PREFETCH_W:
                T.dma(POOL, WO, w_o.rearrange("(c p) n -> p c n", p=128), [],
                      ["nb0", "nb1", "nT0", "nT1", "kf", "sq", "k16_0", "k16_1", "WO"], "w1")
                T.dma(POOL, WG, w_g.rearrange("(c p) n -> p c n", p=128), [], ["WB", "xb0", "xb1", "xb2", "WG"], "w2")
            attention(1)
            T.barrier()
            mix_keys = ["mixg%d" % i for i in range(NO)] + ["mixd%d_%d" % (u, h) for u in range(NU) for h in range(4)]

        with contextlib.ExitStack() as fst:
            def sb(n, s, d):
                return fst.enter_context(nc.sbuf_tensor(n, s, d))

            WU = sb("WU", [128, 8, DFF], BF16)
            WD = sb("WD", [128, NFF, D], BF16)
            gff = sb("gff", [128, D], F32)
            hbs = [sb("hb%d" % i, [128, D], F32) for i in range(2)]
            fnbs = [sb("fnb%d" % i, [128, D], BF16) for i in range(2)]
            obs = [sb("ob0", [128, D], F32)] * 2
            T.dma(SP, gff[:], g_ffn[:, :], [], ["gff"], "c2")
            with contextlib.ExitStack() as ost:
                def sbo(n, s, d):
                    return ost.enter_context(nc.sbuf_tensor(n, s, d))
                mxs = [sbo("mx%d" % i, [128, D], BF16) for i in range(3)]
                mxT = [sbo("mxT%d" % i, [128, D], BF16) for i in range(2)]
                xos = [sbo("xo%d" % i, [128, D], F32) for i in range(3)]
                if not PREFETCH_W:
                    T.dma(POOL, WO, w_o.rearrange("(c p) n -> p c n", p=128), [], ["WO"], "w1")
                    T.dma(POOL, WG, w_g.rearrange("(c p) n -> p c n", p=128), [], ["WG"], "w2")
                T.dma(POOL, WU[:], w_u.rearrange("(c p) n -> p c n", p=128), [], ["WU"], "w3")
                T.dma(POOL, WD[:], w_d.rearrange("(c p) n -> p c n", p=128), [], ["WD"], "w4")
                def op_load(i):
                    s3 = i % 3
                    T.dma(SP, mxs[s3][:], mix_scr[i * 128:(i + 1) * 128, :], mix_keys if i < 3 else [], ["mx%d" % s3], "m%d" % s3)
                    T.dma(SP, xos[s3][:], x_own[i * 128:(i + 1) * 128, :], [], ["xo%d" % s3], "xo%d" % s3)

                def op_tr(i):
                    s3 = i % 3
                    s = i % 2
                    for c in range(8):
                        T.op(PE, lambda c=c: PEe.transpose(out=pbb[0][:, c * 128:(c + 1) * 128],
                                                           in_=mxs[s3][:, c * 128:(c + 1) * 128], identity=identb[:]),
                             ["mx%d" % s3, "identb"], ["pb0"], inc=(c == 7))
                    T.op(ACT, lambda: A.copy(out=mxT[s][:], in_=pbb[0][:, 0:1024]), ["pb0"], ["mxT%d" % s])

                for i in range(min(2, NO)):
                    op_load(i)
                op_tr(0)
                for i in range(NO):
                    s = i % 2
                    s3 = i % 3
                    if i + 2 < NO:
                        op_load(i + 2)
                    if i + 1 < NO:
                        op_tr(i + 1)
                    for nb_ in range(2):
                        bank = 1 + nb_
                        for c in range(8):
                            T.op(PE, lambda c=c, s=s, nb_=nb_, bank=bank:
                                 PEe.matmul(pb[bank][:, 0:512], lhsT=mxT[s][:, c * 128:(c + 1) * 128],
                                            rhs=WO[:, c, nb_ * 512:(nb_ + 1) * 512], start=(c == 0), stop=(c == 7)),
                                 ["mxT%d" % s, "WO"], ["pb%d" % bank], inc=(c == 7))
                        T.op(DVE, lambda s=s, s3=s3, nb_=nb_, bank=bank:
                             V.tensor_tensor(out=hbs[s][:, nb_ * 512:(nb_ + 1) * 512], in0=pb[bank][:, 0:512],
                                             in1=xos[s3][:, nb_ * 512:(nb_ + 1) * 512], op=ALU.add),
                             ["pb%d" % bank, "xo%d" % s3], ["hb%d" % s])
                    T.dma(SP, h_scr[i * 128:(i + 1) * 128, :], hbs[s][:], ["hb%d" % s], ["hscr%d" % i], "hs%d" % s)
                    if dbg:
                        T.dma(SP, dbg_outs["d_h"][i * 128:(i + 1) * 128, :], hbs[s][:], ["hb%d" % s], [], "dbg")
                        T.op(DVE, lambda s3=s3: V.tensor_copy(out=xos[s3][:], in_=mxs[s3][:]), ["mx%d" % s3, "xo%d" % s3], ["xo%d" % s3])
                        T.dma(SP, dbg_outs["d_mix"][i * 128:(i + 1) * 128, :], xos[s3][:], ["xo%d" % s3], [], "dbg")

            T.barrier()
            mT = sb("mT", [128, 8, 512], BF16)
            actT = sb("actT", [128, NFF, 512], BF16)
            ee = sb("ee", [128, 512], F32)
            for gI in range(NU):
                for tt in range(4):
                    i = gI * 4 + tt
                    s = i % 2
                    T.dma(SP, hbs[s][:], h_scr[i * 128:(i + 1) * 128, :], ["hscr%d" % i], ["hb%d" % s], "hl%d" % s)
                    T.op(DVE, lambda s=s: V.scalar_tensor_tensor(out=junk[:], in0=hbs[s][:], scalar=1.0, in1=hbs[s][:],
                                                                 op0=ALU.mult, op1=ALU.mult, accum_out=st[:, 0:1]),
                         ["hb%d" % s], ["junk", "st0"])
                    rstd_from_ss(st[:, 0:1], st[:, 1:2], float(D), ["st0"])
                    T.op(DVE, lambda s=s: V.scalar_tensor_tensor(out=fnbs[s][:], in0=hbs[s][:], scalar=st[:, 1:2],
                                                                 in1=gff[:], op0=ALU.mult, op1=ALU.mult),
                         ["hb%d" % s, "st0", "gff"], ["fnb%d" % s])
                    for c in range(8):
                        T.op(PE, lambda c=c, s=s: PEe.transpose(out=pbb[0][:, c * 128:(c + 1) * 128],
                                                                in_=fnbs[s][:, c * 128:(c + 1) * 128], identity=identb[:]),
                             ["fnb%d" % s, "identb"], ["pb0"], inc=(c == 7))
                    T.op(ACT, lambda tt=tt: A.copy(out=mT[:, :, tt * 128:(tt + 1) * 128],
                                                   in_=pbb[0][:, 0:1024].rearrange("p (a b) -> p a b", b=128)),
                         ["pb0"], ["mT"])
                for f in range(NFF):
                    gb = 1 + (f % 2)
                    ub = 3 + (f % 2)
                    for c in range(8):
                        T.op(PE, lambda c=c, f=f, gb=gb: PEe.matmul(pb[gb][:, 0:512], lhsT=WG[:, c, f * 128:(f + 1) * 128],
                                                                    rhs=mT[:, c, :], start=(c == 0), stop=(c == 7)),
                             ["WG", "mT"], ["pb%d" % gb], inc=(c == 7))
                    for c in range(8):
                        T.op(PE, lambda c=c, f=f, ub=ub: PEe.matmul(pb[ub][:, 0:512], lhsT=WU[:, c, f * 128:(f + 1) * 128],
                                                                    rhs=mT[:, c, :], start=(c == 0), stop=(c == 7)),
                             ["WU", "mT"], ["pb%d" % ub], inc=(c == 7))
                    T.op(ACT, lambda gb=gb: A.activation(out=ee[:], in_=pb[gb][:, 0:512], func=AF.Silu),
                         ["pb%d" % gb], ["ee"])
                    T.op(DVE, lambda f=f, ub=ub: V.tensor_tensor(out=actT[:, f, :], in0=pb[ub][:, 0:512], in1=ee[:], op=ALU.mult),
                         ["pb%d" % ub, "ee"], ["actT"])
                for tt in range(4):
                    i = gI * 4 + tt
                    s = i % 2
                    T.dma(SP, hbs[s][:], h_scr[i * 128:(i + 1) * 128, :], ["hscr%d" % i], ["hb%d" % s], "hl%d" % s)
                    for nb_ in range(2):
                        bank = 5 + nb_
                        for f in range(NFF):
                            T.op(PE, lambda f=f, tt=tt, nb_=nb_, bank=bank:
                                 PEe.matmul(pb[bank][:, 0:512], lhsT=actT[:, f, tt * 128:(tt + 1) * 128],
                                            rhs=WD[:, f, nb_ * 512:(nb_ + 1) * 512], start=(f == 0), stop=(f == NFF - 1)),
                                 ["actT", "WD"], ["pb%d" % bank], inc=(f == NFF - 1))
                        T.op(DVE, lambda s=s, nb_=nb_, bank=bank:
                             V.tensor_tensor(out=obs[s][:, nb_ * 512:(nb_ + 1) * 512], in0=pb[bank][:, 0:512],
                                             in1=hbs[s][:, nb_ * 512:(nb_ + 1) * 512], op=ALU.add),
                             ["pb%d" % bank, "hb%d" % s], ["ob0"])
                    T.dma(SP, out[i * 128:(i + 1) * 128, :], obs[s][:], ["ob0"], [], "o0")

        for n in list(T.dsems):
            d = T.dsem(n)
            nc.sync.wait_ge(d.sem, d.count)
    return nc


def _consts():
    c = np.zeros((128, 514), np.float32)
    c[:, 0:128] = np.eye(128, dtype=np.float32)
    s = np.arange(128)[:, None]
    t = np.arange(128)[None, :]
    same = (s // 64) == (t // 64)
    c[:, 128:256] = np.where(same & (s <= t), -1.0 / 16.0, 0.0)
    c[:, 256:384] = np.where(same & (s > t), -1.0 / 16.0, 0.0)
    c[:, 384:512] = np.where(same & (s <= t), 1.0, 0.0)
    c[0:64, 512] = -1.0 / 16.0
    c[64:128, 513] = -1.0 / 16.0
    return c


def _tabs(j):
    tb = np.zeros((128, 260), np.float32)
    kl = np.arange(128, dtype=np.float64)
    for h in range(4):
        for ei in range(64):
            e = ei - 48
            if e <= 4 * j + 3:
                tb[:, h * 64 + ei] = SLOPES[h] * (128.0 * (e - 4 * j) + kl - 256.0)
            else:
                tb[:, h * 64 + ei] = NEG
    tb[:, 256 + j] = 1.0
    return tb


def _masks(j):
    m = np.zeros((128, 16, 4, 128), np.float32)
    k = np.arange(128)[:, None]
    q = np.arange(128)[None, :]
    tri = (k <= q).astype(np.float32)
    for e in range(16):
        for c in range(4):
            cs = 4 * j + c
            if e < cs:
                m[:, e, c, :] = 1.0
            elif e == cs:
                m[:, e, c, :] = tri
    return m.reshape(128, 16 * 512)


def _rep(v, n=128):
    return np.ascontiguousarray(np.broadcast_to(np.asarray(v, np.float32).reshape(1, -1), (n, v.size)))


def prep_inputs(inp, NU=4):
    T_ = NU * 2048
    f = lambda a: np.ascontiguousarray(np.asarray(a, dtype=np.float32))
    x = f(inp["x"])
    w_in = f(inp["w_in"])[0]
    cols = lambda a, b: list(range(a, b))
    dq, dk, dv = 0, 512, 1024
    gq, gk, gv, go, gl = 1536, 1792, 2048, 2560, 3072
    c_p0 = cols(dk, dk + 256) + cols(dv, dv + 256) + cols(gk, gk + 256) + cols(gl, gl + 16) + cols(gv, gv + 512)
    c_p1 = cols(dk + 256, dk + 512) + cols(dv + 256, dv + 512)
    c_ow = cols(dq, dq + 512) + cols(gq, gq + 256) + cols(gk, gk + 256) + cols(gv, gv + 512) + cols(go, go + 512) + cols(gl, gl + 16)
    shared = {
        "w_p0": np.ascontiguousarray(w_in[:, c_p0]),
        "w_p1": np.ascontiguousarray(w_in[:, c_p1]),
        "w_ow": np.ascontiguousarray(w_in[:, c_ow]),
        "w_up": f(inp["w_gla_gate_up"])[0],
        "w_o": f(inp["w_out"])[0],
        "w_g": f(inp["w_ffn_gate"])[0],
        "w_u": f(inp["w_ffn_up"])[0],
        "w_d": f(inp["w_ffn_down"])[0],
        "g_attn": _rep(f(inp["attn_norm_gain"])[0]),
        "g_ffn": _rep(f(inp["ffn_norm_gain"])[0]),
        "qk_rep": np.concatenate([_rep(np.tile(f(inp["q_norm_gain"])[0], 8)),
                                  _rep(np.tile(f(inp["k_norm_gain"])[0], 8))], axis=1),
        "b_rep": _rep(f(inp["b_gla_gate"])[0]),
        "gn_rep": np.concatenate([_rep(f(inp["diff_out_norm_gain"])[0]), _rep(f(inp["gla_out_norm_gain"])[0])], axis=1),
        "lamv": np.concatenate([_rep(f(inp[k])[0]) for k in ("lambda_q1", "lambda_k1", "lambda_q2", "lambda_k2")], axis=1),
        "cst": _consts(),
    }
    in_maps = []
    for core in range(8):
        b, j = core // 4, core % 4
        own_rows = np.concatenate([np.arange((4 * u + j) * 512, (4 * u + j + 1) * 512) for u in range(NU)])
        m = dict(shared)
        m["x_all"] = np.ascontiguousarray(x[b, :T_])
        m["x_own"] = np.ascontiguousarray(x[b, own_rows])
        m["tabs"] = _tabs(j)
        m["msk"] = _masks(j)
        in_maps.append(m)
    return in_maps


def assemble(results, NU=4, B=2, key="out"):
    T_ = NU * 2048
    outp = np.zeros((B, T_, D), np.float32)
    for core in range(8):
        b, j = core // 4, core % 4
        r = np.asarray(results[core][key])
        for u in range(NU):
            outp[b, (4 * u + j) * 512:(4 * u + j + 1) * 512] = r[u * 512:(u + 1) * 512]
    return outp


_NC_CACHE = {}


def kernel(**inputs):
    NU = 4
    if NU not in _NC_CACHE:
        _NC_CACHE[NU] = build(NU)
    nc = _NC_CACHE[NU]
    in_maps = prep_inputs(inputs, NU)
    res = run_bass_kernel_spmd(nc, in_maps, core_ids=list(range(8)))
    return assemble(res.results, NU)
```

```python
import contextlib
import numpy as np
import concourse.bass as bass
import concourse.mybir as mybir
from concourse.bass_utils import run_bass_kernel_spmd

F32 = mybir.dt.float32
BF16 = mybir.dt.bfloat16
AF = mybir.ActivationFunctionType
ALU = mybir.AluOpType
AX = mybir.AxisListType

D = 1024
DFF = 2816
NFF = DFF // 128
EPS = 1e-6
LAMBDA_INIT = 0.8 - 0.6 * 1.0
SLOPES = [2.0 ** (-8.0 * (h + 1.0) / 4.0) for h in range(4)]
NEG = -30000.0
S_MAX = 8.0 * 1.25
UNDERFLOW = 104.0
SAME_ENGINE_SYNC = True
PREFETCH_W = False


class ES:
    def __init__(self, name, eng, sem):
        self.name = name
        self.eng = eng
        self.sem = sem
        self.count = 0
        self.seen = {}


class Trk:
    def __init__(self, nc, stack):
        self.nc = nc
        self.stack = stack
        self.lw = {}
        self.rd = {}
        self.dsems = {}
        mk = lambda n, e: ES(n, e, stack.enter_context(nc.semaphore("s_" + n)))
        self.pe = mk("pe", nc.tensor)
        self.act = mk("act", nc.scalar)
        self.dve = mk("dve", nc.vector)
        self.pool = mk("pool", nc.gpsimd)
        self.sp = mk("sp", nc.sync)

    def dsem(self, name):
        if name not in self.dsems:
            self.dsems[name] = ES("d_" + name, None,
                                  self.stack.enter_context(self.nc.semaphore("d_" + name)))
        return self.dsems[name]

    def _deps(self, reads, writes):
        raw = {}
        other = {}

        def add(d, e, v):
            if d.get(e, 0) < v:
                d[e] = v
        for k in reads:
            if k in self.lw:
                add(raw, *self.lw[k])
        for k in writes:
            if k in self.lw:
                add(other, *self.lw[k])
            for e, v in self.rd.get(k, {}).items():
                add(other, e, v)
        return raw, other

    def _wait(self, E, deps):
        raw, other = deps
        allv = dict(other)
        for e, v in raw.items():
            if allv.get(e, 0) < v:
                allv[e] = v
        for e, v in allv.items():
            if e is E:
                if E.name == "pe" or not SAME_ENGINE_SYNC:
                    continue
                v = raw.get(e, 0)
                if v == 0:
                    continue
            if E.seen.get(e, 0) >= v:
                continue
            assert v <= e.count, (E.name, e.name, v, e.count)
            E.eng.wait_ge(e.sem, v)
            E.seen[e] = v

    def _rec(self, E, val, reads, writes):
        for k in writes:
            self.lw[k] = (E, val)
            self.rd[k] = {}
        for k in reads:
            d = self.rd.setdefault(k, {})
            if d.get(E, 0) < val:
                d[E] = val

    def op(self, E, fn, reads=(), writes=(), inc=True):
        pr = [k for k in reads if k.startswith("pb")]
        if pr:
            reads = [k for k in reads if not k.startswith("pb")]
            writes = list(writes) + [k for k in pr if k not in writes]
        self._wait(E, self._deps(reads, writes))
        inst = fn()
        if inc:
            inst.then_inc(E.sem, 1)
            E.count += 1
            val = E.count
        else:
            val = E.count + 1
        self._rec(E, val, reads, writes)
        return inst

    def barrier(self):
        engs = [self.pe, self.act, self.dve, self.pool, self.sp]
        srcs = engs + list(self.dsems.values())
        for E in engs:
            for e in srcs:
                if e is E or e.count == 0 or E.seen.get(e, 0) >= e.count:
                    continue
                E.eng.wait_ge(e.sem, e.count)
                E.seen[e] = e.count

    def dma(self, Q, out, in_, reads, writes, sem):
        self._wait(Q, self._deps(reads, writes))
        Dm = self.dsem(sem)
        inst = Q.eng.dma_start(out=out, in_=in_)
        inst.then_inc(Dm.sem, 16)
        Dm.count += 16
        self._rec(Dm, Dm.count, reads, writes)
        return inst


def build(NU=4, dbg=False):
    T_ = NU * 2048
    NT = T_ // 128
    NO = NU * 4
    TO = NO * 128
    nc = bass.Bass("TRN2", target_bir_lowering=False)

    def din(name, shape, dt=F32):
        return nc.dram_tensor(name, shape, dt, kind="ExternalInput").ap()

    x_all = din("x_all", [T_, D])
    x_own = din("x_own", [TO, D])
    w_p0 = din("w_p0", [D, 1296])
    w_p1 = din("w_p1", [D, 512])
    w_ow = din("w_ow", [D, 2064])
    w_up = din("w_up", [16, 256])
    w_o = din("w_o", [D, D])
    w_g = din("w_g", [D, DFF])
    w_u = din("w_u", [D, DFF])
    w_d = din("w_d", [DFF, D])
    g_attn = din("g_attn", [128, D])
    g_ffn = din("g_ffn", [128, D])
    qk_rep = din("qk_rep", [128, 1024])
    b_rep = din("b_rep", [128, 256])
    gn_rep = din("gn_rep", [128, 256])
    lamv = din("lamv", [128, 256])
    cst = din("cst", [128, 514])
    tabs = din("tabs", [128, 260])
    msk = din("msk", [128, 16 * 512])
    out = nc.dram_tensor("out", [TO, D], F32, kind="ExternalOutput").ap()
    mix_scr = nc.dram_tensor("mix_scr", [TO, D], BF16, kind="Internal").ap()
    h_scr = nc.dram_tensor("h_scr", [TO, D], F32, kind="Internal").ap()
    dbg_outs = {}
    if dbg:
        dbg_outs["d_mix"] = nc.dram_tensor("d_mix", [TO, D], F32, kind="ExternalOutput").ap()
        dbg_outs["d_h"] = nc.dram_tensor("d_h", [TO, D], F32, kind="ExternalOutput").ap()

    with contextlib.ExitStack() as gst:
        T = Trk(nc, gst)
        PE, ACT, DVE, POOL, SP = T.pe, T.act, T.dve, T.pool, T.sp
        V = nc.vector
        A = nc.scalar
        G = nc.gpsimd
        PEe = nc.tensor

        def sbg(n, s, d):
            return gst.enter_context(nc.sbuf_tensor(n, s, d))

        pb = [gst.enter_context(nc.psum_tensor("pb%d" % i, [128, 512], F32)) for i in range(8)]
        pbb = [p[:].bitcast(BF16) for p in pb]

        cstf = sbg("cstf", [128, 514], F32)
        identb = sbg("identb", [128, 128], BF16)
        st = sbg("st", [128, 32], F32)
        junk = sbg("junk", [128, 1024], BF16)
        arena = sbg("arena", [128, 16512 + 3 * 2048], BF16)
        WB = arena[:, 0:16512].rearrange("p (c n) -> p c n", n=2064)
        xbs = [arena[:, 16512 + k * 2048:16512 + (k + 1) * 2048].bitcast(F32) for k in range(3)]
        WG = arena[:, 0:8 * DFF].rearrange("p (c n) -> p c n", n=DFF)
        arena2 = sbg("arena2", [128, 8192], BF16)
        nbs = [arena2[:, 0:1024], arena2[:, 1024:2048]]
        nTs = [arena2[:, 2048:3072], arena2[:, 3072:4096]]
        kf = arena2[:, 4096:5120].bitcast(F32)
        sq = arena2[:, 5120:6144].bitcast(F32)
        k16s = [arena2[:, 6144:7168], arena2[:, 7168:8192]]
        WO = arena2[:, 0:8192].rearrange("p (c n) -> p c n", n=D)
        T.dma(SP, cstf[:], cst[:, :], [], ["cstf"], "c0")
        T.op(DVE, lambda: V.tensor_copy(out=identb[:], in_=cstf[:, 0:128]), ["cstf"], ["identb"])
        triS = cstf[:, 128:256]
        upS = cstf[:, 256:384]
        chI = cstf[:, 512:514]

        def rstd_from_ss(ss_ap, out_ap, n, keys):
            T.op(ACT, lambda: A.activation(out=out_ap, in_=ss_ap, func=AF.Ln, scale=1.0 / n, bias=EPS),
                 keys, keys)
            T.op(ACT, lambda: A.activation(out=out_ap, in_=out_ap, func=AF.Exp, scale=-0.5),
                 keys, keys)

        def norm_transpose(src_ap, xb, xk, nb, nbk, nTt, nTk, grep, grepk, dq, dsem, bank=5):
            T.dma(dq, xb[:], src_ap, [], [xk], dsem)
            T.op(DVE, lambda: V.scalar_tensor_tensor(out=junk[:], in0=xb[:], scalar=1.0, in1=xb[:],
                                                     op0=ALU.mult, op1=ALU.mult, accum_out=st[:, 0:1]),
                 [xk], ["junk", "st0"])
            rstd_from_ss(st[:, 0:1], st[:, 1:2], float(D), ["st0"])
            T.op(DVE, lambda: V.scalar_tensor_tensor(out=nb[:], in0=xb[:], scalar=st[:, 1:2], in1=grep[:],
                                                     op0=ALU.mult, op1=ALU.mult),
                 [xk, "st0", grepk], [nbk])
            bk = "pb%d" % bank
            for c in range(8):
                T.op(PE, lambda c=c: PEe.transpose(out=pbb[bank][:, c * 128:(c + 1) * 128],
                                                   in_=nb[:, c * 128:(c + 1) * 128], identity=identb[:]),
                     [nbk, "identb"], [bk], inc=(c == 7))
            T.op(ACT, lambda: A.copy(out=nTt[:], in_=pbb[bank][:, 0:1024]), [bk], [nTk])

        def project(nTt, nTk, W, Wk, groups):
            for (bank, pc0, wc0, ncol) in groups:
                bk = "pb%d" % bank
                for c in range(8):
                    T.op(PE, lambda c=c, bank=bank, pc0=pc0, wc0=wc0, ncol=ncol:
                         PEe.matmul(pb[bank][:, pc0:pc0 + ncol], lhsT=nTt[:, c * 128:(c + 1) * 128],
                                    rhs=W[:, c, wc0:wc0 + ncol], start=(c == 0), stop=(c == 7)),
                         [nTk, Wk], [bk], inc=(c == 7))

        with contextlib.ExitStack() as mst:
            def sb(n, s, d):
                return mst.enter_context(nc.sbuf_tensor(n, s, d))

            KT = sb("KT", [128, 2, T_], BF16)
            VE = sb("VE", [128, NT, 2, 130], BF16)
            QT = sb("QT", [128, 4, TO], BF16)
            MK = sb("MK", [128, 16, 512], BF16)
            tabf = sb("tabf", [128, 260], F32)
            gat = sb("gat", [128, D], F32)
            qg = sb("qg", [128, 512], F32)
            brep = sb("brep", [128, 256], F32)
            gn = sb("gn", [128, 256], F32)
            wupb = sb("wupb", [16, 256], BF16)
            maskA = sb("maskA", [128, 128], BF16)
            ssx = sb("ssx", [128, 4], F32)
            rstd_all = sb("rstd_all", [128, NT], F32)
            gkf = [sb("gkf%d" % i, [128, 256], F32) for i in range(2)]
            glr16s = [sb("glr16_%d" % i, [128, 16], BF16) for i in range(2)]
            glrT = sb("glrT", [16, 128], BF16)
            zb = sb("zb", [128, 256], F32)
            lg = sb("lg", [128, 256], F32)
            eb = sb("eb", [128, 256], F32)
            enb = sb("enb", [128, 256], F32)
            ec = sb("ec", [128, 256], F32)
            dec = sb("dec", [128, 4], F32)
            qt16 = sb("qt16", [128, 256], BF16)
            kt16 = sb("kt16", [128, 256], BF16)
            kh16 = sb("kh16", [128, 256], BF16)
            vb16s = [sb("vb16_%d" % i, [128, 512], BF16) for i in range(2)]
            qkT = sb("qkT", [128, 512], BF16)
            AT = sb("AT", [128, 512], BF16)
            S = sb("S", [128, 256], F32)
            Sb = [sb("Sb%d" % i, [128, 256], BF16) for i in range(2)]
            snap = sb("snap", [128, NU, 256], F32)
            sgt = sb("sgt", [128, 512], F32)
            gg = sb("gg", [128, 512], F32)
            ogl = sb("ogl", [128, 512], BF16)
            PTs = [sb("PT%d" % i, [128, 512], BF16) for i in range(6)]
            accS = sb("accS", [128, 3, 390], F32)
            t1 = sb("t1", [128, 128], F32)
            of = sb("of", [128, 128], F32)
            og = sb("og", [128, 4, 128], BF16)

            T.dma(SP, tabf[:], tabs[:, :], [], ["tabf"], "c1")
            T.dma(SP, gat[:], g_attn[:, :], [], ["gat"], "c2")
            T.dma(SP, kf[:], qk_rep[:, 0:512], [], ["kf"], "c3")
            T.dma(SP, sq[:], qk_rep[:, 512:1024], [], ["sq"], "c3b")
            T.dma(SP, brep[:], b_rep[:, :], [], ["brep"], "c4")
            T.dma(SP, gn[:], gn_rep[:, :], [], ["gn"], "c5")
            lam = zb
            T.dma(SP, lam[:], lamv[:, :], [], ["zb"], "c6")
            T.dma(POOL, wupb[:], w_up[:, :], [], ["wupb"], "c7")
            T.dma(POOL, maskA[:], cst[:, 384:512], [], ["maskA"], "c8")
            T.dma(POOL, WB[:, :, 0:1296], w_p0.rearrange("(c p) n -> p c n", p=128), [], ["WB"], "w0")
            for e4 in range(4):
                T.dma(POOL, MK[:, e4 * 4:(e4 + 1) * 4, :],
                      msk[:, e4 * 2048:(e4 + 1) * 2048].rearrange("p (a b) -> p a b", b=512),
                      [], ["MK"], "c9")
            T.op(DVE, lambda: V.scalar_tensor_tensor(out=qg[:, 0:512], in0=kf[:, 0:512], scalar=0.125,
                                                     in1=sq[:, 0:512], op0=ALU.mult, op1=ALU.mult),
                 ["kf", "sq"], ["qg"])
            T.op(DVE, lambda: V.tensor_scalar(out=gn[:, 0:128], in0=gn[:, 0:128], scalar1=1.0 - LAMBDA_INIT,
                                              scalar2=None, op0=ALU.mult), ["gn"], ["gn"])
            T.op(DVE, lambda: V.scalar_tensor_tensor(out=junk[:, 0:64], in0=lam[:, 0:64], scalar=1.0,
                                                     in1=lam[:, 64:128], op0=ALU.mult, op1=ALU.mult,
                                                     accum_out=st[:, 4:5]), ["zb"], ["junk", "stl"])
            T.op(DVE, lambda: V.scalar_tensor_tensor(out=junk[:, 0:64], in0=lam[:, 128:192], scalar=1.0,
                                                     in1=lam[:, 192:256], op0=ALU.mult, op1=ALU.mult,
                                                     accum_out=st[:, 5:6]), ["zb", "stl"], ["junk", "stl"])
            T.op(ACT, lambda: A.activation(out=st[:, 6:8], in_=st[:, 4:6], func=AF.Exp), ["stl"], ["stl"])
            T.op(DVE, lambda: V.tensor_tensor(out=st[:, 8:9], in0=st[:, 7:8], in1=st[:, 6:7], op=ALU.subtract),
                 ["stl"], ["stl"])
            T.op(DVE, lambda: V.tensor_scalar(out=st[:, 8:9], in0=st[:, 8:9], scalar1=-LAMBDA_INIT,
                                              scalar2=None, op0=ALU.add), ["stl"], ["stl"])
            neglam = st[:, 8:9]
            T.op(POOL, lambda: G.memset(VE[:, :, :, 128:130], 1.0), [], ["VE"])
            T.op(DVE, lambda: V.memset(S[:], 0.0), [], ["S"])

            def k_evac_a(bank, c0, ngrp):
                n = 64 * ngrp
                T.op(ACT, lambda: A.copy(out=kf[:, 0:n], in_=pb[bank][:, c0:c0 + n]), ["pb%d" % bank], ["kf"])

            def k_evac_b(ngrp):
                n = 64 * ngrp
                T.op(DVE, lambda: V.tensor_tensor(out=sq[:, 0:n], in0=kf[:, 0:n], in1=kf[:, 0:n], op=ALU.mult),
                     ["kf"], ["sq"])
                T.op(DVE, lambda: V.tensor_reduce(out=st[:, 16:16 + ngrp],
                                                  in_=sq[:, 0:n].rearrange("p (a b) -> p a b", b=64),
                                                  axis=AX.X, op=ALU.add), ["sq"], ["stk"])
                rstd_from_ss(st[:, 16:16 + ngrp], st[:, 24:24 + ngrp], 64.0, ["stk"])

            def k_evac_c(ngrp, ks, gain=None):
                n = 64 * ngrp
                kk = "k16_%d" % ks
                bc = bass.AP(st[:].tensor, st[:, 24:25].offset, [list(st[:].ap[0]), [1, ngrp], [0, 64]])
                if gain is None:
                    T.op(DVE, lambda: V.tensor_tensor(out=k16s[ks][:, 0:n].rearrange("p (a b) -> p a b", b=64),
                                                      in0=kf[:, 0:n].rearrange("p (a b) -> p a b", b=64),
                                                      in1=bc, op=ALU.mult), ["kf", "stk"], [kk])
                else:
                    T.op(DVE, lambda: V.tensor_tensor(out=sq[:, 0:n].rearrange("p (a b) -> p a b", b=64),
                                                      in0=kf[:, 0:n].rearrange("p (a b) -> p a b", b=64),
                                                      in1=bc, op=ALU.mult), ["kf", "stk", "sq"], ["sq"])
                    T.op(DVE, lambda: V.tensor_tensor(out=k16s[ks][:, 0:n], in0=sq[:, 0:n], in1=gain,
                                                      op=ALU.mult), ["sq", "qg"], [kk])

            def k_evac(bank, c0, ngrp, ks, gain=None):
                k_evac_a(bank, c0, ngrp)
                k_evac_b(ngrp)
                k_evac_c(ngrp, ks, gain)

            def k_tr(ks, n, dst_fn):
                nh = n // 128
                for hh in range(nh):
                    T.op(PE, lambda hh=hh: PEe.transpose(out=pbb[4][:, hh * 128:(hh + 1) * 128],
                                                         in_=k16s[ks][:, hh * 128:(hh + 1) * 128], identity=identb[:]),
                         ["k16_%d" % ks, "identb"], ["pb4"], inc=(hh == nh - 1))
                dst_fn(pbb[4][:, 0:n].rearrange("p (a b) -> p a b", b=128))

            def gla_segs(own, i_own, gk, gkk, gq, gqk, vb, vbk, g16, g16k, psS):
                def s_a():
                    T.op(PE, lambda: PEe.transpose(out=pbb[6][0:16, 64:192], in_=g16[:], identity=identb[:]),
                         [g16k, "identb"], ["pb6"])
                    T.op(ACT, lambda: A.copy(out=glrT[:], in_=pbb[6][0:16, 64:192]), ["pb6"], ["glrT"])

                def s_b():
                    T.op(PE, lambda: PEe.matmul(pb[6][:, 128:384], lhsT=glrT[:], rhs=wupb[:], start=True, stop=True),
                         ["glrT", "wupb"], ["pb6"])
                    T.op(DVE, lambda: V.tensor_tensor(out=zb[:], in0=pb[6][:, 128:384], in1=brep[:], op=ALU.add),
                         ["pb6", "brep"], ["zb"])
                    T.op(ACT, lambda: A.activation(out=lg[:], in_=zb[:], func=AF.Exp, scale=-1.0), ["zb"], ["lg"])
                    T.op(ACT, lambda: A.activation(out=lg[:], in_=lg[:], func=AF.Ln, bias=1.0), ["lg"], ["lg"])

                def s_c():
                    if own:
                        T.op(PE, lambda: PEe.matmul(pb[7][:, 0:256], lhsT=triS, rhs=lg[:], start=True, stop=True),
                             ["cstf", "lg"], ["pb7"], inc=False)
                    T.op(PE, lambda: PEe.matmul(pb[7][:, 256:512], lhsT=upS, rhs=lg[:], start=True, stop=True),
                         ["cstf", "lg"], ["pb7"])
                    for p in range(2):
                        T.op(PE, lambda p=p: PEe.matmul(pb[6][:, 96 + 2 * p:98 + 2 * p], lhsT=lg[:, p * 128:(p + 1) * 128],
                                                        rhs=chI, start=True, stop=True),
                             ["lg", "cstf"], ["pb6"], inc=(p == 1))
                    T.op(ACT, lambda: A.activation(out=ec[:], in_=pb[7][:, 256:512], func=AF.Exp), ["pb7"], ["ec"])
                    if own:
                        T.op(ACT, lambda: A.activation(out=eb[:], in_=pb[7][:, 0:256], func=AF.Exp), ["pb7"], ["eb"])
                        T.op(ACT, lambda: A.activation(out=enb[:], in_=pb[7][:, 0:256], func=AF.Exp, scale=-1.0),
                             ["pb7"], ["enb"])
                    T.op(ACT, lambda: A.activation(out=dec[:], in_=pb[6][:, 96:100], func=AF.Exp), ["pb6"], ["dec"])
                    T.op(DVE, lambda: V.tensor_tensor(out=kh16[:], in0=gk, in1=ec[:], op=ALU.mult),
                         [gkk, "ec"], ["kh16"])
                    if own:
                        T.op(DVE, lambda: V.scalar_tensor_tensor(out=qt16[:], in0=gq, scalar=0.125,
                                                                 in1=eb[:], op0=ALU.mult, op1=ALU.mult),
                             [gqk, "eb"], ["qt16"])
                        T.op(DVE, lambda: V.tensor_tensor(out=kt16[:], in0=gk, in1=enb[:], op=ALU.mult),
                             [gkk, "enb"], ["kt16"])
                        for p in range(2):
                            T.op(PE, lambda p=p: PEe.transpose(out=pbb[4][:, 512 + p * 128:512 + (p + 1) * 128],
                                                               in_=qt16[:, p * 128:(p + 1) * 128], identity=identb[:]),
                                 ["qt16", "identb"], ["pb4"], inc=False)
                        for p in range(2):
                            T.op(PE, lambda p=p: PEe.transpose(out=pbb[4][:, 768 + p * 128:768 + (p + 1) * 128],
                                                               in_=kt16[:, p * 128:(p + 1) * 128], identity=identb[:]),
                                 ["kt16", "identb"], ["pb4"], inc=(p == 1))
                        T.op(ACT, lambda: A.copy(out=qkT[:], in_=pbb[4][:, 512:1024]), ["pb4"], ["qkT"])

                def s_c2():
                    for hh in range(2):
                        bank = 0 if hh == 0 else 3
                        for p in range(2):
                            T.op(PE, lambda hh=hh, p=p, bank=bank:
                                 PEe.matmul(pb[bank][:, p * 128:(p + 1) * 128],
                                            lhsT=qkT[hh * 64:(hh + 1) * 64, 256 + p * 128:256 + (p + 1) * 128],
                                            rhs=qkT[hh * 64:(hh + 1) * 64, p * 128:(p + 1) * 128],
                                            start=True, stop=True),
                                 ["qkT"], ["pb%d" % bank], inc=(p == 1))
                    mA = bass.AP(maskA[:].tensor, maskA[:].offset, [list(maskA[:].ap[0]), [0, 2], [1, 128]])
                    for hh in range(2):
                        bank = 0 if hh == 0 else 3
                        outv = bass.AP(AT[:].tensor, AT[:, hh * 128:hh * 128 + 1].offset,
                                       [list(AT[:].ap[0]), [256, 2], [1, 128]])
                        T.op(DVE, lambda bank=bank, outv=outv: V.tensor_tensor(
                            out=outv, in0=pb[bank][:, 0:256].rearrange("p (a b) -> p a b", b=128), in1=mA, op=ALU.mult),
                            ["pb%d" % bank, "maskA"], ["AT"])

                def s_d(ch):
                    def f():
                        if own:
                            T.op(DVE, lambda: V.tensor_copy(out=Sb[ch][:], in_=S[:]), ["S"], ["Sb%d" % ch])
                        sbank, scol = psS[ch]
                        sk = "pb%d" % sbank
                        for p in range(2):
                            for hh in range(2):
                                h = 2 * p + hh
                                T.op(PE, lambda p=p, hh=hh, h=h:
                                     PEe.matmul(pb[sbank][hh * 64:(hh + 1) * 64, scol + p * 128:scol + (p + 1) * 128],
                                                lhsT=kh16[ch * 64:(ch + 1) * 64, h * 64:(h + 1) * 64],
                                                rhs=vb[ch * 64:(ch + 1) * 64, h * 128:(h + 1) * 128],
                                                start=True, stop=True),
                                     ["kh16", vbk], [sk], inc=(p == 1 and hh == 1))
                        for p in range(2):
                            T.op(DVE, lambda p=p:
                                 V.scalar_tensor_tensor(out=S[:, p * 128:(p + 1) * 128], in0=S[:, p * 128:(p + 1) * 128],
                                                        scalar=dec[:, 2 * p + ch:2 * p + ch + 1],
                                                        in1=pb[sbank][:, scol + p * 128:scol + (p + 1) * 128],
                                                        op0=ALU.mult, op1=ALU.add),
                                 ["S", "dec", sk], ["S"])
                    return f

                def s_o():
                    for hh in range(2):
                        bank = 0 if hh == 0 else 3
                        bk = "pb%d" % bank
                        for p in range(2):
                            h = 2 * p + hh
                            oc = 256 + p * 128
                            T.op(PE, lambda h=h, bank=bank, oc=oc:
                                 PEe.matmul(pb[bank][:, oc:oc + 128], lhsT=AT[:, h * 128:(h + 1) * 128],
                                            rhs=vb[:, h * 128:(h + 1) * 128], start=True, stop=False),
                                 ["AT", vbk], [bk], inc=False)
                            for ch in range(2):
                                T.op(PE, lambda h=h, hh=hh, p=p, ch=ch, bank=bank, oc=oc:
                                     PEe.matmul(pb[bank][ch * 64:(ch + 1) * 64, oc:oc + 128],
                                                lhsT=qkT[hh * 64:(hh + 1) * 64, p * 128 + ch * 64:p * 128 + (ch + 1) * 64],
                                                rhs=Sb[ch][hh * 64:(hh + 1) * 64, p * 128:(p + 1) * 128],
                                                start=False, stop=True),
                                     ["qkT", "Sb%d" % ch], [bk], inc=(ch == 1 and p == 1))
                    for hh in range(2):
                        bank = 0 if hh == 0 else 3
                        T.op(ACT, lambda bank=bank, hh=hh: A.copy(out=kf[:, hh * 256:(hh + 1) * 256], in_=pb[bank][:, 256:512]),
                             ["pb%d" % bank], ["kf"])
                    T.op(DVE, lambda: V.tensor_tensor(out=sq[:, 0:512], in0=kf[:, 0:512], in1=kf[:, 0:512],
                                                      op=ALU.mult), ["kf"], ["sq"])
                    T.op(DVE, lambda: V.tensor_reduce(out=st[:, 16:20],
                                                      in_=sq[:, 0:512].rearrange("p (a b) -> p a b", b=128),
                                                      axis=AX.X, op=ALU.add), ["sq"], ["stk"])
                    rstd_from_ss(st[:, 16:20], st[:, 24:28], 128.0, ["stk"])
                    bc = bass.AP(st[:].tensor, st[:, 24:25].offset, [list(st[:].ap[0]), [1, 4], [0, 128]])
                    T.op(DVE, lambda: V.tensor_tensor(out=sq[:, 0:512].rearrange("p (a b) -> p a b", b=128),
                                                      in0=kf[:, 0:512].rearrange("p (a b) -> p a b", b=128),
                                                      in1=bc, op=ALU.mult), ["kf", "stk", "sq"], ["sq"])
                    ggv = bass.AP(gg[:].tensor, gg[:].offset, [list(gg[:].ap[0]), [128, 2], [256, 2], [1, 128]])
                    oglv = bass.AP(ogl[:].tensor, ogl[:].offset, [list(ogl[:].ap[0]), [128, 2], [256, 2], [1, 128]])
                    T.op(DVE, lambda: V.tensor_tensor(out=oglv, in0=sq[:, 0:512].rearrange("p (a b c) -> p a b c", b=2, c=128),
                                                      in1=ggv, op=ALU.mult), ["sq", "gg"], ["ogl"])
                    T.dma(SP, mix_scr[i_own * 128:(i_own + 1) * 128, 512:1024], ogl[:], ["ogl"], ["mixg%d" % i_own], "mx")

                if own:
                    return [s_a, s_b, s_c, s_c2, s_d(0), s_d(1), s_o]
                return [s_a, s_b, s_c, s_d(0), s_d(1)]

            def seg_load(t):
                sl = t % 3
                T.dma(SP, xbs[sl][:], x_all[t * 128:(t + 1) * 128, :], [], ["xb%d" % sl], "x%d" % sl)

            def seg_stats(t):
                sl = t % 3
                T.op(DVE, lambda: V.scalar_tensor_tensor(out=junk[:], in0=xbs[sl][:], scalar=1.0, in1=xbs[sl][:],
                                                         op0=ALU.mult, op1=ALU.mult, accum_out=ssx[:, sl:sl + 1]),
                     ["xb%d" % sl], ["junk", "ssx%d" % sl])
                T.op(ACT, lambda: A.activation(out=rstd_all[:, t:t + 1], in_=ssx[:, sl:sl + 1], func=AF.Ln,
                                               scale=1.0 / D, bias=EPS), ["ssx%d" % sl], ["rs%d" % t])
                T.op(ACT, lambda: A.activation(out=rstd_all[:, t:t + 1], in_=rstd_all[:, t:t + 1], func=AF.Exp, scale=-0.5),
                     ["rs%d" % t], ["rs%d" % t])

            def seg_n(t):
                sl = t % 3
                ns = t % 2
                T.op(DVE, lambda: V.scalar_tensor_tensor(out=nbs[ns][:], in0=xbs[sl][:], scalar=rstd_all[:, t:t + 1],
                                                         in1=gat[:], op0=ALU.mult, op1=ALU.mult),
                     ["xb%d" % sl, "rs%d" % t, "gat"], ["nb%d" % ns])

            def seg_tr(t):
                ns = t % 2
                for c in range(8):
                    T.op(PE, lambda c=c: PEe.transpose(out=pbb[3][:, c * 128:(c + 1) * 128],
                                                       in_=nbs[ns][:, c * 128:(c + 1) * 128], identity=identb[:]),
                         ["nb%d" % ns, "identb"], ["pb3"], inc=(c == 7))
                T.op(ACT, lambda: A.copy(out=nTs[ns][:], in_=pbb[3][:, 0:1024]), ["pb3"], ["nT%d" % ns])

            def run_pipe(n_tiles, early_fn, main_fn, late_fn):
                seg_load(0)
                for it in range(-1, n_tiles + 1):
                    early = early_fn(it + 1) if 0 <= it + 1 < n_tiles else []
                    main = main_fn(it) if 0 <= it < n_tiles else []
                    late = late_fn(it - 1) if 0 <= it - 1 < n_tiles else []
                    for k in range(max(len(early), len(main), len(late))):
                        if k < len(main):
                            main[k]()
                        if k < len(late):
                            late[k]()
                        if k < len(early):
                            early[k]()

            def kv_store(t):
                def dst(src3):
                    T.op(ACT, lambda: A.copy(out=KT[:, :, t * 128:(t + 1) * 128], in_=src3), ["pb4"], ["KT"])
                return dst

            def v_store(t):
                T.op(DVE, lambda: V.tensor_copy(out=VE[:, t, :, 0:128],
                                                in_=pb[0][:, 256:512].rearrange("p (a b) -> p a b", b=128)),
                     ["pb0"], ["VE"])

            def p0_iter(it):
                ok = lambda t: 0 <= t < NT
                late = []
                t1 = it - 1
                if ok(t1):
                    n1 = t1 % 2
                    late = gla_segs(False, None, gkf[n1][:], "gkf%d" % n1, None, None, vb16s[n1], "vb16_%d" % n1,
                                    glr16s[n1], "glr16_%d" % n1, ((5, 0), (7, 0)))
                    late[0]()
                if ok(it + 1):
                    seg_tr(it + 1)
                ns = it % 2
                if ok(it):
                    project(nTs[ns], "nT%d" % ns, WB, "WB", [(0, 0, 0, 512)])
                    k_evac_a(0, 0, 4)
                    v_store(it)
                    k_evac_b(4)
                if ok(t1):
                    late[1]()
                if ok(it):
                    project(nTs[ns], "nT%d" % ns, WB, "WB", [(1, 0, 512, 272)])
                    T.op(DVE, lambda: V.tensor_copy(out=gkf[ns][:], in_=pb[1][:, 0:256]), ["pb1"], ["gkf%d" % ns])
                    T.op(DVE, lambda: V.tensor_copy(out=glr16s[ns][:], in_=pb[1][:, 256:272]), ["pb1"], ["glr16_%d" % ns])
                    k_evac_c(4, ns)
                if ok(it + 2):
                    seg_stats(it + 2)
                if ok(t1):
                    late[2]()
                    k_tr(t1 % 2, 256, kv_store(t1))
                if ok(it + 2):
                    seg_n(it + 2)
                if ok(it):
                    project(nTs[ns], "nT%d" % ns, WB, "WB", [(2, 0, 784, 512)])
                    T.op(ACT, lambda: A.copy(out=vb16s[ns][:], in_=pb[2][:, :]), ["pb2"], ["vb16_%d" % ns])
                if ok(t1):
                    if t1 % 16 % 4 == 0:
                        u = t1 // 16
                        r = (t1 % 16) // 4
                        if r == 0:
                            T.op(DVE, lambda: V.tensor_scalar(out=snap[:, u, :], in0=S[:],
                                                              scalar1=tabf[:, 256 + r:257 + r], scalar2=None,
                                                              op0=ALU.mult), ["S", "tabf"], ["snap"])
                        else:
                            T.op(DVE, lambda: V.scalar_tensor_tensor(out=snap[:, u, :], in0=S[:],
                                                                     scalar=tabf[:, 256 + r:257 + r],
                                                                     in1=snap[:, u, :], op0=ALU.mult, op1=ALU.add),
                                 ["S", "tabf", "snap"], ["snap"])
                    late[3]()
                    late[4]()
                if ok(it + 3):
                    seg_load(it + 3)

            for t in range(min(3, NT)):
                seg_load(t)
            for it in range(-2, NT + 1):
                p0_iter(it)

            T.dma(POOL, WB[:, :, :], w_ow.rearrange("(c p) n -> p c n", p=128), [], ["WB"], "w0")

            def q_store(i):
                def dst(src3):
                    T.op(ACT, lambda: A.copy(out=QT[:, :, i * 128:(i + 1) * 128], in_=src3), ["pb4"], ["QT"])
                return dst

            gnb = bass.AP(gn[:].tensor, gn[:, 128:129].offset, [list(gn[:].ap[0]), [0, 4], [1, 128]])

            def o_load(i):
                sl = i % 3
                T.dma(SP, xbs[sl][:], x_own[i * 128:(i + 1) * 128, :], [], ["xb%d" % sl], "x%d" % sl)

            def o_stats(i):
                sl = i % 3
                T.op(DVE, lambda: V.scalar_tensor_tensor(out=junk[:], in0=xbs[sl][:], scalar=1.0, in1=xbs[sl][:],
                                                         op0=ALU.mult, op1=ALU.mult, accum_out=ssx[:, sl:sl + 1]),
                     ["xb%d" % sl], ["junk", "ssx%d" % sl])
                rstd_from_ss(ssx[:, sl:sl + 1], ssx[:, 3:4] if False else st[:, 1 + (i % 2):2 + (i % 2)], float(D), ["ssx%d" % sl, "sto%d" % (i % 2)])

            def o_n(i):
                sl = i % 3
                ns = i % 2
                T.op(DVE, lambda: V.scalar_tensor_tensor(out=nbs[ns][:], in0=xbs[sl][:], scalar=st[:, 1 + ns:2 + ns],
                                                         in1=gat[:], op0=ALU.mult, op1=ALU.mult),
                     ["xb%d" % sl, "sto%d" % ns, "gat"], ["nb%d" % ns])

            def o_tr(i):
                ns = i % 2
                for c in range(8):
                    T.op(PE, lambda c=c: PEe.transpose(out=pbb[5][:, c * 128:(c + 1) * 128],
                                                       in_=nbs[ns][:, c * 128:(c + 1) * 128], identity=identb[:]),
                         ["nb%d" % ns, "identb"], ["pb5"], inc=(c == 7))
                T.op(ACT, lambda: A.copy(out=nTs[ns][:], in_=pbb[5][:, 0:1024]), ["pb5"], ["nT%d" % ns])

            for i in range(min(2, NO)):
                o_load(i)
            o_stats(0)
            o_n(0)
            o_tr(0)
            for i in range(NO):
                s = i % 2
                if i + 2 < NO:
                    o_load(i + 2)
                if i % 4 == 0:
                    T.op(DVE, lambda i=i: V.tensor_copy(out=S[:], in_=snap[:, i // 4, :]), ["snap"], ["S"])
                project(nTs[s], "nT%d" % s, WB, "WB", [(0, 0, 0, 512), (3, 0, 1536, 512)])
                k_evac_a(0, 0, 8)
                T.op(ACT, lambda: A.activation(out=sgt[:], in_=pb[3][:, :], func=AF.Exp, scale=-1.0), ["pb3"], ["sgt"])
                T.op(ACT, lambda: A.activation(out=sgt[:], in_=sgt[:], func=AF.Ln, bias=1.0), ["sgt"], ["sgt"])
                T.op(ACT, lambda: A.activation(out=sgt[:], in_=sgt[:], func=AF.Exp, scale=-1.0), ["sgt"], ["sgt"])
                project(nTs[s], "nT%d" % s, WB, "WB", [(1, 0, 512, 512), (2, 0, 1024, 512), (6, 0, 2048, 16)])
                k_evac_b(8)
                T.op(DVE, lambda: V.tensor_tensor(out=gg[:], in0=pb[3][:, :], in1=sgt[:], op=ALU.mult),
                     ["pb3", "sgt"], ["gg"])
                T.op(DVE, lambda: V.tensor_copy(out=glr16s[0][:], in_=pb[6][:, 0:16]), ["pb6"], ["glr16_0"])
                T.op(ACT, lambda: A.copy(out=vb16s[0][:], in_=pb[2][:, :]), ["pb2"], ["vb16_0"])
                k_evac_c(8, 0, gain=qg[:, 0:512])
                T.op(DVE, lambda: V.tensor_tensor(out=gg[:].rearrange("p (a b) -> p a b", b=128),
                                                  in0=gg[:].rearrange("p (a b) -> p a b", b=128), in1=gnb, op=ALU.mult),
                     ["gg", "gn"], ["gg"])
                segs = gla_segs(True, i, pb[1][:, 256:512], "pb1", pb[1][:, 0:256], "pb1", vb16s[0], "vb16_0",
                                glr16s[0], "glr16_0", ((2, 0), (2, 256)))
                segs[0]()
                if i + 1 < NO:
                    o_stats(i + 1)
                k_tr(0, 512, q_store(i))
                segs[1]()
                if i + 1 < NO:
                    o_n(i + 1)
                segs[2]()
                if i + 1 < NO:
                    o_tr(i + 1)
                for sg in segs[3:]:
                    sg()

            def acc_ap(a):
                bank = 5 + a // 3
                col = (a % 3) * 130
                return bank, col

            def attention(hp):
                def kb_lo(h, u):
                    n_h = int(np.ceil(1.0 + (UNDERFLOW + 2.0 * S_MAX) / (128.0 * SLOPES[h])))
                    return max(0, 16 * u - (n_h - 1))
                steps = [(u, hl, kb) for u in range(NU) for hl in range(2)
                         for kb in range(kb_lo(2 * hp + hl, u), 16 * u + 16)]
                LOOK = 2

                def emit_qk(idx):
                    u, hl, kb = steps[idx]
                    h = 2 * hp + hl
                    e = kb - 16 * u
                    ei = e + 48
                    for m in range(2):
                        sbank = 1 + (2 * idx + m) % 4
                        T.op(PE, lambda m=m, sbank=sbank:
                             PEe.matmul(pb[sbank][:, 0:512],
                                        lhsT=KT[m * 64:(m + 1) * 64, hl, kb * 128:(kb + 1) * 128],
                                        rhs=QT[m * 64:(m + 1) * 64, h, u * 512:(u + 1) * 512],
                                        start=True, stop=True),
                             ["KT", "QT"], ["pb%d" % sbank])
                    for m in range(2):
                        sbank = 1 + (2 * idx + m) % 4
                        slot = (2 * idx + m) % 6
                        T.op(ACT, lambda sbank=sbank, slot=slot:
                             A.activation(out=PTs[slot][:], in_=pb[sbank][:, 0:512], func=AF.Exp,
                                          bias=tabf[:, h * 64 + ei:h * 64 + ei + 1]),
                             ["pb%d" % sbank, "tabf"], ["PT%d" % slot])
                        if e >= 0:
                            T.op(DVE, lambda slot=slot:
                                 V.tensor_tensor(out=PTs[slot][:], in0=PTs[slot][:], in1=MK[:, e, :], op=ALU.mult),
                                 ["PT%d" % slot, "MK"], ["PT%d" % slot])

                def emit_pv(idx):
                    u, hl, kb = steps[idx]
                    h = 2 * hp + hl
                    nkb = 16 * u + 16
                    for m in range(2):
                        slot = (2 * idx + m) % 6
                        for c in range(4):
                            bank, col = acc_ap(c * 2 + m)
                            first = (kb == kb_lo(h, u) and (c * 2 + m) in (0, 4, 6))
                            T.op(PE, lambda c=c, bank=bank, col=col, slot=slot, first=first:
                                 PEe.matmul(pb[bank][:, col:col + 130],
                                            lhsT=PTs[slot][:, c * 128:(c + 1) * 128],
                                            rhs=VE[:, kb, hl, :], start=first, stop=(kb == nkb - 1),
                                            skip_group_check=True),
                                 ["PT%d" % slot, "VE"], ["pb%d" % bank], inc=(c == 3 and m == 1))
                    if kb != nkb - 1:
                        return
                    for bi in range(3):
                        T.op(DVE, lambda bi=bi: V.tensor_copy(out=accS[:, bi, :], in_=pb[5 + bi][:, 0:390]),
                             ["pb%d" % (5 + bi)], ["accS"])
                    for c in range(4):
                        a0 = c * 2
                        a1 = c * 2 + 1
                        A0 = accS[:, a0 // 3, (a0 % 3) * 130:(a0 % 3) * 130 + 130]
                        A1 = accS[:, a1 // 3, (a1 % 3) * 130:(a1 % 3) * 130 + 130]
                        T.op(DVE, lambda A0=A0: V.reciprocal(out=st[:, 10:11], in_=A0[:, 128:129]), ["accS"], ["sta"])
                        T.op(DVE, lambda A1=A1: V.reciprocal(out=st[:, 11:12], in_=A1[:, 128:129]), ["accS", "sta"], ["sta"])
                        T.op(DVE, lambda: V.tensor_tensor(out=st[:, 11:12], in0=st[:, 11:12], in1=neglam, op=ALU.mult),
                             ["sta", "stl"], ["sta"])
                        T.op(DVE, lambda A1=A1: V.tensor_scalar(out=t1[:], in0=A1[:, 0:128], scalar1=st[:, 11:12],
                                                                scalar2=None, op0=ALU.mult), ["accS", "sta"], ["t1"])
                        T.op(DVE, lambda A0=A0: V.scalar_tensor_tensor(out=of[:], in0=A0[:, 0:128], scalar=st[:, 10:11],
                                                                       in1=t1[:], op0=ALU.mult, op1=ALU.add),
                             ["accS", "sta", "t1"], ["of"])
                        T.op(DVE, lambda: V.scalar_tensor_tensor(out=junk[:, 0:128], in0=of[:], scalar=1.0, in1=of[:],
                                                                 op0=ALU.mult, op1=ALU.mult, accum_out=st[:, 12:13]),
                             ["of"], ["junk", "stb"])
                        rstd_from_ss(st[:, 12:13], st[:, 13:14], 128.0, ["stb"])
                        T.op(DVE, lambda c=c: V.scalar_tensor_tensor(out=og[:, c, :], in0=of[:], scalar=st[:, 13:14],
                                                                     in1=gn[:, 0:128], op0=ALU.mult, op1=ALU.mult),
                             ["of", "stb", "gn"], ["og"])
                    T.dma(SP, mix_scr[u * 512:(u + 1) * 512, h * 128:(h + 1) * 128].rearrange("(c p) n -> p c n", p=128),
                          og[:], ["og"], ["mixd%d_%d" % (u, h)], "mx2")

                for i in range(len(steps) + LOOK):
                    if i < len(steps):
                        emit_qk(i)
                    if i - LOOK >= 0:
                        emit_pv(i - LOOK)

            attention(0)

            T.dma(POOL, WB[:, :, 0:512], w_p1.rearrange("(c p) n -> p c n", p=128), [], ["WB"], "w0")

            def p1_iter(it):
                ok = lambda t: 0 <= t < NT
                if ok(it + 2):
                    seg_n(it + 2)
                if ok(it + 1):
                    seg_tr(it + 1)
                ns = it % 2
                if ok(it):
                    project(nTs[ns], "nT%d" % ns, WB, "WB", [(0, 0, 0, 512)])
                    k_evac_a(0, 0, 4)
                    v_store(it)
                    k_evac_b(4)
                if ok(it - 1):
                    k_tr((it - 1) % 2, 256, kv_store(it - 1))
                if ok(it):
                    k_evac_c(4, ns)
                if ok(it + 3):
                    seg_load(it + 3)

            for t in range(min(3, NT)):
                seg_load(t)
            for it in range(-2, NT + 1):
                p1_iter(it)
            if PREFETCH_W:
                T.dma(POOL, WO, w_o.rearrange("(c p) n -> p c n", p=128), [],
                      ["nb0", "nb1", "nT0", "nT1", "kf", "sq", "k16_0", "k16_1", "WO"], "w1")
                T.dma(POOL, WG, w_g.rearrange("(c p) n -> p c n", p=128), [], ["WB", "xb0", "xb1", "xb2", "WG"], "w2")
            attention(1)
            T.barrier()
            mix_keys = ["mixg%d" % i for i in range(NO)] + ["mixd%d_%d" % (u, h) for u in range(NU) for h in range(4)]

        with contextlib.ExitStack() as fst:
            def sb(n, s, d):
                return fst.enter_context(nc.sbuf_tensor(n, s, d))

            WU = sb("WU", [128, 8, DFF], BF16)
            WD = sb("WD", [128, NFF, D], BF16)
            gff = sb("gff", [128, D], F32)
            hbs = [sb("hb%d" % i, [128, D], F32) for i in range(2)]
            fnbs = [sb("fnb%d" % i, [128, D], BF16) for i in range(2)]
            obs = [sb("ob0", [128, D], F32)] * 2
            T.dma(SP, gff[:], g_ffn[:, :], [], ["gff"], "c2")
            with contextlib.ExitStack() as ost:
                def sbo(n, s, d):
                    return ost.enter_context(nc.sbuf_tensor(n, s, d))
                mxs = [sbo("mx%d" % i, [128, D], BF16) for i in range(3)]
                mxT = [sbo("mxT%d" % i, [128, D], BF16) for i in range(2)]
                xos = [sbo("xo%d" % i, [128, D], F32) for i in range(3)]
                if not PREFETCH_W:
                    T.dma(POOL, WO, w_o.rearrange("(c p) n -> p c n", p=128), [], ["WO"], "w1")
                    T.dma(POOL, WG, w_g.rearrange("(c p) n -> p c n", p=128), [], ["WG"], "w2")
                T.dma(POOL, WU[:], w_u.rearrange("(c p) n -> p c n", p=128), [], ["WU"], "w3")
                T.dma(POOL, WD[:], w_d.rearrange("(c p) n -> p c n", p=128), [], ["WD"], "w4")
                def op_load(i):
                    s3 = i % 3
                    T.dma(SP, mxs[s3][:], mix_scr[i * 128:(i + 1) * 128, :], mix_keys if i < 3 else [], ["mx%d" % s3], "m%d" % s3)
                    T.dma(SP, xos[s3][:], x_own[i * 128:(i + 1) * 128, :], [], ["xo%d" % s3], "xo%d" % s3)

                def op_tr(i):
                    s3 = i % 3
                    s = i % 2
                    for c in range(8):
                        T.op(PE, lambda c=c: PEe.transpose(out=pbb[0][:, c * 128:(c + 1) * 128],
                                                           in_=mxs[s3][:, c * 128:(c + 1) * 128], identity=identb[:]),
                             ["mx%d" % s3, "identb"], ["pb0"], inc=(c == 7))
                    T.op(ACT, lambda: A.copy(out=mxT[s][:], in_=pbb[0][:, 0:1024]), ["pb0"], ["mxT%d" % s])

                for i in range(min(2, NO)):
                    op_load(i)
                op_tr(0)
                for i in range(NO):
                    s = i % 2
                    s3 = i % 3
                    if i + 2 < NO:
                        op_load(i + 2)
                    if i + 1 < NO:
                        op_tr(i + 1)
                    for nb_ in range(2):
                        bank = 1 + nb_
                        for c in range(8):
                            T.op(PE, lambda c=c, s=s, nb_=nb_, bank=bank:
                                 PEe.matmul(pb[bank][:, 0:512], lhsT=mxT[s][:, c * 128:(c + 1) * 128],
                                            rhs=WO[:, c, nb_ * 512:(nb_ + 1) * 512], start=(c == 0), stop=(c == 7)),
                                 ["mxT%d" % s, "WO"], ["pb%d" % bank], inc=(c == 7))
                        T.op(DVE, lambda s=s, s3=s3, nb_=nb_, bank=bank:
                             V.tensor_tensor(out=hbs[s][:, nb_ * 512:(nb_ + 1) * 512], in0=pb[bank][:, 0:512],
                                             in1=xos[s3][:, nb_ * 512:(nb_ + 1) * 512], op=ALU.add),
                             ["pb%d" % bank, "xo%d" % s3], ["hb%d" % s])
                    T.dma(SP, h_scr[i * 128:(i + 1) * 128, :], hbs[s][:], ["hb%d" % s], ["hscr%d" % i], "hs%d" % s)
                    if dbg:
                        T.dma(SP, dbg_outs["d_h"][i * 128:(i + 1) * 128, :], hbs[s][:], ["hb%d" % s], [], "dbg")
                        T.op(DVE, lambda s3=s3: V.tensor_copy(out=xos[s3][:], in_=mxs[s3][:]), ["mx%d" % s3, "xo%d" % s3], ["xo%d" % s3])
                        T.dma(SP, dbg_outs["d_mix"][i * 128:(i + 1) * 128, :], xos[s3][:], ["xo%d" % s3], [], "dbg")

            T.barrier()
            mT = sb("mT", [128, 8, 512], BF16)
            actT = sb("actT", [128, NFF, 512], BF16)
            ee = sb("ee", [128, 512], F32)
            for gI in range(NU):
                for tt in range(4):
                    i = gI * 4 + tt
                    s = i % 2
                    T.dma(SP, hbs[s][:], h_scr[i * 128:(i + 1) * 128, :], ["hscr%d" % i], ["hb%d" % s], "hl%d" % s)
                    T.op(DVE, lambda s=s: V.scalar_tensor_tensor(out=junk[:], in0=hbs[s][:], scalar=1.0, in1=hbs[s][:],
                                                                 op0=ALU.mult, op1=ALU.mult, accum_out=st[:, 0:1]),
                         ["hb%d" % s], ["junk", "st0"])
                    rstd_from_ss(st[:, 0:1], st[:, 1:2], float(D), ["st0"])
                    T.op(DVE, lambda s=s: V.scalar_tensor_tensor(out=fnbs[s][:], in0=hbs[s][:], scalar=st[:, 1:2],
                                                                 in1=gff[:], op0=ALU.mult, op1=ALU.mult),
                         ["hb%d" % s, "st0", "gff"], ["fnb%d" % s])
                    for c in range(8):
                        T.op(PE, lambda c=c, s=s: PEe.transpose(out=pbb[0][:, c * 128:(c + 1) * 128],
                                                                in_=fnbs[s][:, c * 128:(c + 1) * 128], identity=identb[:]),
                             ["fnb%d" % s, "identb"], ["pb0"], inc=(c == 7))
                    T.op(ACT, lambda tt=tt: A.copy(out=mT[:, :, tt * 128:(tt + 1) * 128],
                                                   in_=pbb[0][:, 0:1024].rearrange("p (a b) -> p a b", b=128)),
                         ["pb0"], ["mT"])
                for f in range(NFF):
                    gb = 1 + (f % 2)
                    ub = 3 + (f % 2)
                    for c in range(8):
                        T.op(PE, lambda c=c, f=f, gb=gb: PEe.matmul(pb[gb][:, 0:512], lhsT=WG[:, c, f * 128:(f + 1) * 128],
                                                                    rhs=mT[:, c, :], start=(c == 0), stop=(c == 7)),
                             ["WG", "mT"], ["pb%d" % gb], inc=(c == 7))
                    for c in range(8):
                        T.op(PE, lambda c=c, f=f, ub=ub: PEe.matmul(pb[ub][:, 0:512], lhsT=WU[:, c, f * 128:(f + 1) * 128],
                                                                    rhs=mT[:, c, :], start=(c == 0), stop=(c == 7)),
                             ["WU", "mT"], ["pb%d" % ub], inc=(c == 7))
                    T.op(ACT, lambda gb=gb: A.activation(out=ee[:], in_=pb[gb][:, 0:512], func=AF.Silu),
                         ["pb%d" % gb], ["ee"])
                    T.op(DVE, lambda f=f, ub=ub: V.tensor_tensor(out=actT[:, f, :], in0=pb[ub][:, 0:512], in1=ee[:], op=ALU.mult),
                         ["pb%d" % ub, "ee"], ["actT"])
                for tt in range(4):
                    i = gI * 4 + tt
                    s = i % 2
                    T.dma(SP, hbs[s][:], h_scr[i * 128:(i + 1) * 128, :], ["hscr%d" % i], ["hb%d" % s], "hl%d" % s)
                    for nb_ in range(2):
                        bank = 5 + nb_
                        for f in range(NFF):
                            T.op(PE, lambda f=f, tt=tt, nb_=nb_, bank=bank:
                                 PEe.matmul(pb[bank][:, 0:512], lhsT=actT[:, f, tt * 128:(tt + 1) * 128],
                                            rhs=WD[:, f, nb_ * 512:(nb_ + 1) * 512], start=(f == 0), stop=(f == NFF - 1)),
                                 ["actT", "WD"], ["pb%d" % bank], inc=(f == NFF - 1))
                        T.op(DVE, lambda s=s, nb_=nb_, bank=bank:
                             V.tensor_tensor(out=obs[s][:, nb_ * 512:(nb_ + 1) * 512], in0=pb[bank][:, 0:512],
                                             in1=hbs[s][:, nb_ * 512:(nb_ + 1) * 512], op=ALU.add),
                             ["pb%d" % bank, "hb%d" % s], ["ob0"])
                    T.dma(SP, out[i * 128:(i + 1) * 128, :], obs[s][:], ["ob0"], [], "o0")

        for n in list(T.dsems):
            d = T.dsem(n)
            nc.sync.wait_ge(d.sem, d.count)
    return nc


def _consts():
    c = np.zeros((128, 514), np.float32)
    c[:, 0:128] = np.eye(128, dtype=np.float32)
    s = np.arange(128)[:, None]
    t = np.arange(128)[None, :]
    same = (s // 64) == (t // 64)
    c[:, 128:256] = np.where(same & (s <= t), -1.0 / 16.0, 0.0)
    c[:, 256:384] = np.where(same & (s > t), -1.0 / 16.0, 0.0)
    c[:, 384:512] = np.where(same & (s <= t), 1.0, 0.0)
    c[0:64, 512] = -1.0 / 16.0
    c[64:128, 513] = -1.0 / 16.0
    return c


def _tabs(j):
    tb = np.zeros((128, 260), np.float32)
    kl = np.arange(128, dtype=np.float64)
    for h in range(4):
        for ei in range(64):
            e = ei - 48
            if e <= 4 * j + 3:
                tb[:, h * 64 + ei] = SLOPES[h] * (128.0 * (e - 4 * j) + kl - 256.0)
            else:
                tb[:, h * 64 + ei] = NEG
    tb[:, 256 + j] = 1.0
    return tb


def _masks(j):
    m = np.zeros((128, 16, 4, 128), np.float32)
    k = np.arange(128)[:, None]
    q = np.arange(128)[None, :]
    tri = (k <= q).astype(np.float32)
    for e in range(16):
        for c in range(4):
            cs = 4 * j + c
            if e < cs:
                m[:, e, c, :] = 1.0
            elif e == cs:
                m[:, e, c, :] = tri
    return m.reshape(128, 16 * 512)


def _rep(v, n=128):
    return np.ascontiguousarray(np.broadcast_to(np.asarray(v, np.float32).reshape(1, -1), (n, v.size)))


def prep_inputs(inp, NU=4):
    T_ = NU * 2048
    f = lambda a: np.ascontiguousarray(np.asarray(a, dtype=np.float32))
    x = f(inp["x"])
    w_in = f(inp["w_in"])[0]
    cols = lambda a, b: list(range(a, b))
    dq, dk, dv = 0, 512, 1024
    gq, gk, gv, go, gl = 1536, 1792, 2048, 2560, 3072
    c_p0 = cols(dk, dk + 256) + cols(dv, dv + 256) + cols(gk, gk + 256) + cols(gl, gl + 16) + cols(gv, gv + 512)
    c_p1 = cols(dk + 256, dk + 512) + cols(dv + 256, dv + 512)
    c_ow = cols(dq, dq + 512) + cols(gq, gq + 256) + cols(gk, gk + 256) + cols(gv, gv + 512) + cols(go, go + 512) + cols(gl, gl + 16)
    shared = {
        "w_p0": np.ascontiguousarray(w_in[:, c_p0]),
        "w_p1": np.ascontiguousarray(w_in[:, c_p1]),
        "w_ow": np.ascontiguousarray(w_in[:, c_ow]),
        "w_up": f(inp["w_gla_gate_up"])[0],
        "w_o": f(inp["w_out"])[0],
        "w_g": f(inp["w_ffn_gate"])[0],
        "w_u": f(inp["w_ffn_up"])[0],
        "w_d": f(inp["w_ffn_down"])[0],
        "g_attn": _rep(f(inp["attn_norm_gain"])[0]),
        "g_ffn": _rep(f(inp["ffn_norm_gain"])[0]),
        "qk_rep": np.concatenate([_rep(np.tile(f(inp["q_norm_gain"])[0], 8)),
                                  _rep(np.tile(f(inp["k_norm_gain"])[0], 8))], axis=1),
        "b_rep": _rep(f(inp["b_gla_gate"])[0]),
        "gn_rep": np.concatenate([_rep(f(inp["diff_out_norm_gain"])[0]), _rep(f(inp["gla_out_norm_gain"])[0])], axis=1),
        "lamv": np.concatenate([_rep(f(inp[k])[0]) for k in ("lambda_q1", "lambda_k1", "lambda_q2", "lambda_k2")], axis=1),
        "cst": _consts(),
    }
    in_maps = []
    for core in range(8):
        b, j = core // 4, core % 4
        own_rows = np.concatenate([np.arange((4 * u + j) * 512, (4 * u + j + 1) * 512) for u in range(NU)])
        m = dict(shared)
        m["x_all"] = np.ascontiguousarray(x[b, :T_])
        m["x_own"] = np.ascontiguousarray(x[b, own_rows])
        m["tabs"] = _tabs(j)
        m["msk"] = _masks(j)
        in_maps.append(m)
    return in_maps


def assemble(results, NU=4, B=2, key="out"):
    T_ = NU * 2048
    outp = np.zeros((B, T_, D), np.float32)
    for core in range(8):
        b, j = core // 4, core % 4
        r = np.asarray(results[core][key])
        for u in range(NU):
            outp[b, (4 * u + j) * 512:(4 * u + j + 1) * 512] = r[u * 512:(u + 1) * 512]
    return outp


_NC_CACHE = {}


def kernel(**inputs):
    NU = 4
    if NU not in _NC_CACHE:
        _NC_CACHE[NU] = build(NU)
    nc = _NC_CACHE[NU]
    in_maps = prep_inputs(inputs, NU)
    res = run_bass_kernel_spmd(nc, in_maps, core_ids=list(range(8)))
    return assemble(res.results, NU)
```
